# Optimizing a Trainium2 kernel written in Bass

```python
import numpy as np
import jax
import jax.numpy as jnp
from jax import lax

D_MODEL = 4096
BATCH = 1
SEQ = 16384
DEPTH = 2

GRID_W = 64
CTX_LEN = 256
EPS = 1e-6
ROPE_BASE = 10000.0
NEG = -1e30
BLOCK = 128

W_GROUP = D_MODEL // 4
D_MIX = 4 * W_GROUP

LRU_W = W_GROUP
LRU_BLOCKS = 8
LRU_BS = LRU_W // LRU_BLOCKS
LRU_CONV = 4
LRU_C = 8.0

SWA_HEADS = 8
SWA_KV_HEADS = 2
SWA_HD = W_GROUP // SWA_HEADS
SWA_WINDOW = 128

MLA_HEADS = 8
MLA_NOPE = 128
MLA_ROPE = 64
MLA_V = W_GROUP // MLA_HEADS
MLA_Q_RANK = 768
MLA_KV_RANK = 256

ML_HEADS = 8
ML_HD = W_GROUP // ML_HEADS
ML_CHUNK = 128

IN_SIZES = (LRU_W, LRU_W,
            SWA_HEADS * SWA_HD, SWA_KV_HEADS * SWA_HD, SWA_KV_HEADS * SWA_HD, W_GROUP,
            MLA_Q_RANK, MLA_KV_RANK, MLA_ROPE, W_GROUP,
            W_GROUP, W_GROUP, W_GROUP, W_GROUP, 4 * ML_HEADS, W_GROUP)
N_IN = sum(IN_SIZES)

kernel_name = 'hybrid_group_diffusion_block'


def rms_norm(x, g):
    xf = x.astype(jnp.float32)
    y = xf * lax.rsqrt(jnp.mean(xf * xf, axis=-1, keepdims=True) + EPS)
    return (y * g.astype(jnp.float32)).astype(x.dtype)


def split_cols(p):
    return jnp.split(p, np.cumsum(IN_SIZES)[:-1].tolist(), axis=-1)


def grid_positions(n_tok):
    rows = n_tok // GRID_W
    row = jnp.repeat(jnp.arange(rows, dtype=jnp.int32), GRID_W)
    col = jnp.tile(jnp.arange(GRID_W, dtype=jnp.int32), rows)
    return row, col


def rope_1d(x, pos):
    half = x.shape[-1] // 2
    freqs = jnp.power(ROPE_BASE, -jnp.arange(half, dtype=jnp.float32) / half)
    ang = pos.astype(jnp.float32)[:, None] * freqs[None, :]
    cos, sin = jnp.cos(ang), jnp.sin(ang)
    xf = x.astype(jnp.float32)
    x1, x2 = xf[..., :half], xf[..., half:]
    return jnp.concatenate([x1 * cos - x2 * sin, x2 * cos + x1 * sin], axis=-1).astype(x.dtype)


def rope_2d(x, row, col):
    h = x.shape[-1] // 2
    return jnp.concatenate([rope_1d(x[..., :h], row), rope_1d(x[..., h:], col)], axis=-1)


def centred_conv(x, w, b):
    k_w = w.shape[0]
    left = k_w // 2
    right = k_w - 1 - left
    t = x.shape[1]
    xp = jnp.pad(x, ((0, 0), (left, right), (0, 0)))
    out = b
    for j in range(k_w):
        out = out + xp[:, j:j + t, :] * w[j]
    return out


def block_diag(u, w, b):
    bsz, t, _ = u.shape
    ub = u.reshape(bsz, t, LRU_BLOCKS, LRU_BS)
    y = jnp.einsum('btnk,nkj->btnj', ub, w.astype(jnp.float32)).reshape(bsz, t, LRU_W)
    return y + b.astype(jnp.float32)


def rglru_coeffs(u, lam, wa, ba, wx, bx):
    r = jax.nn.sigmoid(block_diag(u, wa, ba))
    i = jax.nn.sigmoid(block_diag(u, wx, bx))
    log_a = -LRU_C * r * jax.nn.softplus(-lam.astype(jnp.float32))
    a = jnp.exp(log_a)
    return a, jnp.sqrt(-jnp.expm1(2.0 * log_a)) * (i * u)


def linear_scan(a, b, h0, reverse):
    idx = -1 if reverse else 0
    b = b.at[:, idx].add(a[:, idx] * h0)

    def combine(left, right):
        a_l, b_l = left
        a_r, b_r = right
        return a_l * a_r, a_r * b_l + b_r

    return lax.associative_scan(combine, (a, b), reverse=reverse, axis=1)[1]


def rglru_mixer(u, uc, conv_w, conv_b, lam, wa, ba, wx, bx, need_ctx):
    u = centred_conv(u, conv_w, conv_b).astype(jnp.float32)
    uc = centred_conv(uc, conv_w, conv_b).astype(jnp.float32)
    zero = jnp.zeros((u.shape[0], LRU_W), jnp.float32)
    ys, ycs = [], []
    for d in range(2):
        rev = d == 1
        a_c, b_c = rglru_coeffs(uc, lam[d], wa[d], ba[d], wx[d], bx[d])
        h_c = linear_scan(a_c, b_c, zero, rev)
        h0 = h_c[:, 0] if rev else h_c[:, -1]
        a, b = rglru_coeffs(u, lam[d], wa[d], ba[d], wx[d], bx[d])
        ys.append(linear_scan(a, b, h0, rev))
        ycs.append(h_c)
    y = ys[0] + ys[1]
    yc = ycs[0] + ycs[1] if need_ctx else None
    return y, yc


def swa_mixer(q, k, v, qc, kc, vc, sink, row, col, need_ctx):
    bsz, s, _ = q.shape
    grp = SWA_HEADS // SWA_KV_HEADS
    nblk = s // BLOCK
    scale = SWA_HD ** -0.5

    def heads_q(t):
        return t.reshape(bsz, t.shape[1], SWA_KV_HEADS, grp, SWA_HD).transpose(0, 2, 3, 1, 4)

    def heads_kv(t):
        return t.reshape(bsz, t.shape[1], SWA_KV_HEADS, SWA_HD).transpose(0, 2, 1, 3)

    ql = rope_2d(heads_q(q), row, col)
    kl = rope_2d(heads_kv(k), row, col)
    vl = heads_kv(v)
    kcx, vcx = heads_kv(kc), heads_kv(vc)
    n_ctx = kcx.shape[2]
    sink_l = sink.reshape(SWA_KV_HEADS, grp).astype(jnp.float32)

    def band(t):
        tp = jnp.pad(t, ((0, 0), (0, 0), (BLOCK, BLOCK), (0, 0)))
        tb = tp.reshape(bsz, SWA_KV_HEADS, nblk + 2, BLOCK, SWA_HD)
        return jnp.concatenate([tb[:, :, :-2], tb[:, :, 1:-1], tb[:, :, 2:]], axis=3)

    kw, vw = band(kl), band(vl)
    qb = ql.reshape(bsz, SWA_KV_HEADS, grp, nblk, BLOCK, SWA_HD)
    qi = jnp.arange(nblk)[:, None] * BLOCK + jnp.arange(BLOCK)[None, :]
    kj = (jnp.arange(nblk)[:, None] - 1) * BLOCK + jnp.arange(3 * BLOCK)[None, :]
    valid = ((jnp.abs(qi[:, :, None] - kj[:, None, :]) <= SWA_WINDOW)
             & (kj[:, None, :] >= 0) & (kj[:, None, :] < s))
    s_loc = jnp.einsum('bkgnqd,bkntd->bkgnqt', qb, kw).astype(jnp.float32) * scale
    s_loc = jnp.where(valid, s_loc, NEG)
    s_ctx = jnp.einsum('bkgnqd,bkcd->bkgnqc', qb, kcx).astype(jnp.float32) * scale
    s_sink = jnp.broadcast_to(sink_l[None, :, :, None, None, None], s_ctx.shape[:-1] + (1,))
    p = jax.nn.softmax(jnp.concatenate([s_loc, s_ctx, s_sink], axis=-1), axis=-1).astype(vl.dtype)
    t3 = 3 * BLOCK
    o = (jnp.einsum('bkgnqt,bkntd->bkgnqd', p[..., :t3], vw)
         + jnp.einsum('bkgnqc,bkcd->bkgnqd', p[..., t3:t3 + n_ctx], vcx))
    y = o.reshape(bsz, SWA_KV_HEADS, grp, s, SWA_HD).transpose(0, 3, 1, 2, 4).reshape(bsz, s, SWA_HEADS * SWA_HD)
    yc = None
    if need_ctx:
        qcx = heads_q(qc)
        sc = jnp.einsum('bkgqd,bkcd->bkgqc', qcx, kcx).astype(jnp.float32) * scale
        sc_sink = jnp.broadcast_to(sink_l[None, :, :, None, None], sc.shape[:-1] + (1,))
        pc = jax.nn.softmax(jnp.concatenate([sc, sc_sink], axis=-1), axis=-1).astype(vcx.dtype)
        oc = jnp.einsum('bkgqc,bkcd->bkgqd', pc[..., :n_ctx], vcx)
        yc = oc.transpose(0, 3, 1, 2, 4).reshape(bsz, n_ctx, SWA_HEADS * SWA_HD)
    return y, yc


def mla_mixer(cq, ckv, kr, cqc, ckvc, krc, q_norm, w_uq, kv_norm, w_ukv, row, col, need_ctx):
    bsz, s, _ = cq.shape
    nh = MLA_HEADS
    scale = (MLA_NOPE + MLA_ROPE) ** -0.5

    def q_heads(c_):
        t = c_.shape[1]
        qf = (rms_norm(c_, q_norm) @ w_uq).reshape(bsz, t, nh, MLA_NOPE + MLA_ROPE).transpose(0, 2, 1, 3)
        return qf[..., :MLA_NOPE], qf[..., MLA_NOPE:]

    def kv_heads(c_):
        t = c_.shape[1]
        kv = (rms_norm(c_, kv_norm) @ w_ukv).reshape(bsz, t, nh, MLA_NOPE + MLA_V).transpose(0, 2, 1, 3)
        return kv[..., :MLA_NOPE], kv[..., MLA_NOPE:]

    qn, qr = q_heads(cq)
    qr = rope_2d(qr, row, col)
    kn, vl = kv_heads(ckv)
    krl = rope_2d(kr, row, col)
    knc, vcx = kv_heads(ckvc)
    k_nope = jnp.concatenate([kn, knc], axis=2)
    v_all = jnp.concatenate([vl, vcx], axis=2)
    k_rope = jnp.concatenate([krl, krc], axis=1)

    def attend(blk):
        qn_b, qr_b = blk
        sc = (jnp.einsum('bhqd,bhkd->bhqk', qn_b, k_nope)
              + jnp.einsum('bhqd,bkd->bhqk', qr_b, k_rope)).astype(jnp.float32) * scale
        p = jax.nn.softmax(sc, axis=-1).astype(v_all.dtype)
        return jnp.einsum('bhqk,bhkd->bhqd', p, v_all)

    nblk = s // BLOCK

    def to_blocks(t):
        return t.reshape(bsz, nh, nblk, BLOCK, t.shape[-1]).transpose(2, 0, 1, 3, 4)

    o = lax.map(attend, (to_blocks(qn), to_blocks(qr)))
    y = o.transpose(1, 0, 3, 2, 4).reshape(bsz, s, nh * MLA_V)
    yc = None
    if need_ctx:
        qnc, qrc = q_heads(cqc)
        sc = (jnp.einsum('bhqd,bhkd->bhqk', qnc, knc)
              + jnp.einsum('bhqd,bkd->bhqk', qrc, krc)).astype(jnp.float32) * scale
        pc = jax.nn.softmax(sc, axis=-1).astype(vcx.dtype)
        oc = jnp.einsum('bhqk,bhkd->bhqd', pc, vcx)
        yc = oc.transpose(0, 2, 1, 3).reshape(bsz, oc.shape[2], nh * MLA_V)
    return y, yc


def mlstm_chunked(q, k, v, ig, lf, state):
    c0, n0, m0 = state
    bsz, nh, t, d = q.shape
    nc = t // ML_CHUNK

    def chunks(z):
        return z.reshape((bsz, nh, nc, ML_CHUNK) + z.shape[3:])

    qc, kc, vc, igc, lfc = chunks(q), chunks(k), chunks(v), chunks(ig), chunks(lf)
    b = jnp.cumsum(lfc, axis=-1)
    g = b[..., -1]
    w_log = g[..., None] - b + igc
    m_loc = jnp.max(w_log, axis=-1)
    w = jnp.exp(w_log - m_loc[..., None])
    c_loc = jnp.einsum('bhnl,bhnld,bhnle->bhnde', w, kc, vc)
    n_loc = jnp.einsum('bhnl,bhnld->bhnd', w, kc)

    def step(carry, inp):
        c_, n_, m_ = carry
        g_, m_l, c_l, n_l = inp
        m_new = jnp.maximum(g_ + m_, m_l)
        a = jnp.exp(g_ + m_ - m_new)
        e = jnp.exp(m_l - m_new)
        new = (a[..., None, None] * c_ + e[..., None, None] * c_l,
               a[..., None] * n_ + e[..., None] * n_l, m_new)
        return new, carry

    xs = tuple(jnp.moveaxis(z, 2, 0) for z in (g, m_loc, c_loc, n_loc))
    final, (c_in, n_in, m_in) = lax.scan(step, (c0, n0, m0), xs)
    c_in = jnp.moveaxis(c_in, 0, 2)
    n_in = jnp.moveaxis(n_in, 0, 2)
    m_in = jnp.moveaxis(m_in, 0, 2)
    tri = jnp.tril(jnp.ones((ML_CHUNK, ML_CHUNK), dtype=bool))
    dmat = jnp.where(tri, b[..., :, None] - b[..., None, :] + igc[..., None, :], NEG)
    inter = b + m_in[..., None]
    m_j = jnp.maximum(inter, jnp.max(dmat, axis=-1))
    p = jnp.exp(dmat - m_j[..., None]) * jnp.einsum('bhnjd,bhnsd->bhnjs', qc, kc)
    e = jnp.exp(inter - m_j)
    num = e[..., None] * jnp.einsum('bhnjd,bhnde->bhnje', qc, c_in) + jnp.einsum('bhnjs,bhnse->bhnje', p, vc)
    den = e * jnp.einsum('bhnjd,bhnd->bhnj', qc, n_in) + jnp.sum(p, axis=-1)
    h = num / jnp.maximum(jnp.abs(den), jnp.exp(-m_j))[..., None]
    return h.reshape(bsz, nh, t, d), final


def mlstm_mixer(q, k, v, o, g, qc, kc, vc, oc, gc, gate_b, head_norm, need_ctx):
    bsz = q.shape[0]
    k_scale = ML_HD ** -0.5

    def heads(t):
        return t.reshape(bsz, t.shape[1], ML_HEADS, ML_HD).transpose(0, 2, 1, 3).astype(jnp.float32)

    def gate_pre(gg):
        t = gg.shape[1]
        z = (gg.astype(jnp.float32).reshape(bsz, t, 4, ML_HEADS) + gate_b.astype(jnp.float32)).transpose(2, 0, 3, 1)
        return z[0], jax.nn.log_sigmoid(z[1]), z[2], jax.nn.log_sigmoid(z[3])

    def flip(t):
        return jnp.flip(t, axis=2)

    lat = (heads(q), heads(k) * k_scale, heads(v))
    con = (heads(qc), heads(kc) * k_scale, heads(vc))
    ig_f, lf_f, ig_b, lf_b = gate_pre(g)
    igc_f, lfc_f, igc_b, lfc_b = gate_pre(gc)
    zero_state = (jnp.zeros((bsz, ML_HEADS, ML_HD, ML_HD), jnp.float32),
                  jnp.zeros((bsz, ML_HEADS, ML_HD), jnp.float32),
                  jnp.zeros((bsz, ML_HEADS), jnp.float32))
    hc_f, st_f = mlstm_chunked(con[0], con[1], con[2], igc_f, lfc_f, zero_state)
    h_f, _ = mlstm_chunked(lat[0], lat[1], lat[2], ig_f, lf_f, st_f)
    hc_b, st_b = mlstm_chunked(flip(con[0]), flip(con[1]), flip(con[2]), flip(igc_b), flip(lfc_b), zero_state)
    h_b, _ = mlstm_chunked(flip(lat[0]), flip(lat[1]), flip(lat[2]), flip(ig_b), flip(lf_b), st_b)

    def finish(h, o_):
        t = h.shape[2]
        hn = rms_norm(h, head_norm.reshape(ML_HEADS, 1, ML_HD))
        hn = hn.transpose(0, 2, 1, 3).reshape(bsz, t, W_GROUP)
        return (jax.nn.sigmoid(o_.astype(jnp.float32)) * hn).astype(o_.dtype)

    y = finish(h_f + flip(h_b), o)
    yc = finish(hc_f + flip(hc_b), oc) if need_ctx else None
    return y, yc


def gated_concat(ys, gates):
    return jnp.concatenate([y.astype(gt.dtype) * jax.nn.silu(gt) for y, gt in zip(ys, gates)], axis=-1)


def setup_inputs(seed: int = 0) -> dict:
    key = jax.random.key(seed)
    ks = jax.random.split(key, 28)
    f32 = jnp.float32

    def nrm(k, shape, scale):
        return jax.random.normal(k, shape, f32) * scale

    x = nrm(ks[0], (BATCH, SEQ, D_MODEL), 1.0)
    c = nrm(ks[1], (BATCH, D_MODEL), 1.0)
    ctx = nrm(ks[2], (BATCH, CTX_LEN, D_MODEL), 1.0)
    c_ctx = nrm(ks[3], (D_MODEL,), 1.0)
    w_mod = nrm(ks[4], (DEPTH, D_MODEL, 3 * D_MODEL), 0.5 * D_MODEL ** -0.5)
    b_mod = nrm(ks[5], (DEPTH, 3 * D_MODEL), 0.02)
    g_pre = 1.0 + nrm(ks[6], (DEPTH, D_MODEL), 0.02)
    g_post = 1.0 + nrm(ks[7], (DEPTH, D_MODEL), 0.02)
    w_in = nrm(ks[8], (DEPTH, D_MODEL, N_IN), D_MODEL ** -0.5)
    w_out = nrm(ks[9], (DEPTH, D_MIX, D_MODEL), D_MIX ** -0.5)
    lru_conv_w = nrm(ks[10], (DEPTH, LRU_CONV, LRU_W), LRU_CONV ** -0.5)
    lru_conv_b = nrm(ks[11], (DEPTH, LRU_W), 0.02)
    u = jax.random.uniform(ks[12], (DEPTH, 2, LRU_W), f32, 0.9, 0.999)
    a0 = u ** (1.0 / LRU_C)
    lru_lambda = jnp.log(a0) - jnp.log1p(-a0)
    lru_wa = nrm(ks[13], (DEPTH, 2, LRU_BLOCKS, LRU_BS, LRU_BS), LRU_BS ** -0.5)
    lru_ba = nrm(ks[14], (DEPTH, 2, LRU_W), 0.02)
    lru_wx = nrm(ks[15], (DEPTH, 2, LRU_BLOCKS, LRU_BS, LRU_BS), LRU_BS ** -0.5)
    lru_bx = nrm(ks[16], (DEPTH, 2, LRU_W), 0.02)
    swa_sink = nrm(ks[17], (DEPTH, SWA_HEADS), 0.5)
    mla_q_norm = 1.0 + nrm(ks[18], (DEPTH, MLA_Q_RANK), 0.02)
    mla_w_uq = nrm(ks[19], (DEPTH, MLA_Q_RANK, MLA_HEADS * (MLA_NOPE + MLA_ROPE)), MLA_Q_RANK ** -0.5)
    mla_kv_norm = 1.0 + nrm(ks[20], (DEPTH, MLA_KV_RANK), 0.02)
    mla_w_ukv = nrm(ks[21], (DEPTH, MLA_KV_RANK, MLA_HEADS * (MLA_NOPE + MLA_V)), MLA_KV_RANK ** -0.5)
    ig_f = nrm(ks[22], (DEPTH, ML_HEADS), 0.1)
    fg_f = jax.random.uniform(ks[23], (DEPTH, ML_HEADS), f32, 3.0, 6.0)
    ig_b = nrm(ks[24], (DEPTH, ML_HEADS), 0.1)
    fg_b = jax.random.uniform(ks[25], (DEPTH, ML_HEADS), f32, 3.0, 6.0)
    ml_gate_b = jnp.stack([ig_f, fg_f, ig_b, fg_b], axis=1)
    ml_head_norm = 1.0 + nrm(ks[26], (DEPTH, W_GROUP), 0.02)
    return {'x': x, 'c': c, 'ctx': ctx, 'c_ctx': c_ctx, 'w_mod': w_mod, 'b_mod': b_mod,
            'g_pre': g_pre, 'g_post': g_post, 'w_in': w_in, 'w_out': w_out,
            'lru_conv_w': lru_conv_w, 'lru_conv_b': lru_conv_b, 'lru_lambda': lru_lambda,
            'lru_wa': lru_wa, 'lru_ba': lru_ba, 'lru_wx': lru_wx, 'lru_bx': lru_bx,
            'swa_sink': swa_sink, 'mla_q_norm': mla_q_norm, 'mla_w_uq': mla_w_uq,
            'mla_kv_norm': mla_kv_norm, 'mla_w_ukv': mla_w_ukv,
            'ml_gate_b': ml_gate_b, 'ml_head_norm': ml_head_norm}


def reference(x, c, ctx, c_ctx, w_mod, b_mod, g_pre, g_post, w_in, w_out,
              lru_conv_w, lru_conv_b, lru_lambda, lru_wa, lru_ba, lru_wx, lru_bx,
              swa_sink, mla_q_norm, mla_w_uq, mla_kv_norm, mla_w_ukv,
              ml_gate_b, ml_head_norm):
    row, col = grid_positions(x.shape[1])
    xc = ctx
    for l in range(DEPTH):
        need_ctx = l < DEPTH - 1
        mod = jax.nn.silu(c) @ w_mod[l] + b_mod[l]
        mod_c = jax.nn.silu(c_ctx) @ w_mod[l] + b_mod[l]
        shift, scale, gate = jnp.split(mod[:, None, :], 3, axis=-1)
        shift_c, scale_c, gate_c = jnp.split(mod_c, 3, axis=-1)
        h = rms_norm(x, g_pre[l]) * (1.0 + scale) + shift
        hc = rms_norm(xc, g_pre[l]) * (1.0 + scale_c) + shift_c
        p = split_cols(h @ w_in[l])
        pc = split_cols(hc @ w_in[l])
        y_a, yc_a = rglru_mixer(p[0], pc[0], lru_conv_w[l], lru_conv_b[l], lru_lambda[l],
                                lru_wa[l], lru_ba[l], lru_wx[l], lru_bx[l], need_ctx)
        y_b, yc_b = swa_mixer(p[2], p[3], p[4], pc[2], pc[3], pc[4], swa_sink[l], row, col, need_ctx)
        y_c, yc_c = mla_mixer(p[6], p[7], p[8], pc[6], pc[7], pc[8], mla_q_norm[l], mla_w_uq[l],
                              mla_kv_norm[l], mla_w_ukv[l], row, col, need_ctx)
        y_d, yc_d = mlstm_mixer(p[10], p[11], p[12], p[13], p[14], pc[10], pc[11], pc[12], pc[13], pc[14],
                                ml_gate_b[l], ml_head_norm[l], need_ctx)
        y = gated_concat((y_a, y_b, y_c, y_d), (p[1], p[5], p[9], p[15]))
        x = x + gate * rms_norm(y @ w_out[l], g_post[l])
        if need_ctx:
            yc = gated_concat((yc_a, yc_b, yc_c, yc_d), (pc[1], pc[5], pc[9], pc[15]))
            xc = xc + gate_c * rms_norm(yc @ w_out[l], g_post[l])
    return x
```

```python
import numpy as np
import ml_dtypes
from contextlib import ExitStack
import concourse.bass as bass
import concourse.mybir as mybir
from concourse.bass_utils import run_bass_kernel_spmd

F32 = mybir.dt.float32
BF16 = mybir.dt.bfloat16
AF = mybir.ActivationFunctionType
ALU = mybir.AluOpType
NPBF = ml_dtypes.bfloat16

NCORE = 8
D = 4096
T = 16384
TC = 256
TL = T // NCORE
CL = TC // NCORE
NT = TL + CL
TT = T + TC
NB = TT // 128
EPS = 1e-6
KC = D // 128
HALVES = ((0, 1024), (1024, NT))

O_AU, O_AG, O_BQ, O_BK, O_BV, O_BG = 0, 1024, 2048, 3072, 3328, 3584
O_CQ, O_CKV, O_KR, O_CG = 4608, 5376, 5632, 5696
O_DQ, O_DK, O_DV, O_DO, O_DGT, O_DG = 6720, 7744, 8768, 9792, 10816, 10848

SEM_CH = 16000
N_DMA_SEMS = 24


class Res:
    __slots__ = ("name", "w", "rs")

    def __init__(self, name):
        self.name = name
        self.w = None
        self.rs = []


class Op:
    __slots__ = ("eng", "fn", "deps", "sig", "sem", "semv", "dma", "gi")

    def __init__(self, eng, fn, dma, gi):
        self.eng = eng
        self.fn = fn
        self.dma = dma
        self.deps = set()
        self.sig = False
        self.sem = None
        self.semv = 0
        self.gi = gi


class Prog:
    ENGS = ("pe", "act", "dve", "pool", "sp")

    def __init__(self, nc):
        self.nc = nc
        self.ops = {e: [] for e in self.ENGS}
        self.gi = 0
        self.all_ops = []

    def res(self, name=""):
        return Res(name)

    def op(self, eng, fn, reads=(), writes=(), dma=False):
        o = Op(eng, fn, dma, self.gi)
        self.gi += 1
        deps = o.deps
        for r in reads:
            if r.w is not None:
                deps.add(r.w)
        for r in writes:
            if r.w is not None:
                deps.add(r.w)
            deps.update(r.rs)
        for r in reads:
            r.rs.append(o)
        for r in writes:
            r.w = o
            r.rs = []
        deps.discard(o)
        self.ops[eng].append(o)
        self.all_ops.append(o)
        return o

    def dma(self, eng, out, in_, reads=(), writes=(), **kw):
        return self.op(eng, lambda e: e.dma_start(out=out, in_=in_, **kw), reads, writes, dma=True)

    def emit(self, stack):
        nc = self.nc
        dma_last, dma_cnt, dma_sems = {}, {}, {}
        rr = {e: 0 for e in self.ENGS}
        for o in self.all_ops:
            if o.dma:
                key = (o.eng, rr[o.eng] % N_DMA_SEMS)
                rr[o.eng] += 1
                if key not in dma_sems:
                    dma_sems[key] = stack.enter_context(nc.semaphore(f"d_{o.eng}_{key[1]}"))
                    dma_cnt[key] = 0
                prev = dma_last.get(key)
                if prev is not None:
                    o.deps.add(prev)
                dma_last[key] = o
                dma_cnt[key] += 16
                o.sem = dma_sems[key]
                o.semv = dma_cnt[key]
                o.sig = True
        for o in self.all_ops:
            for d in o.deps:
                if d.eng == "pe" and o.eng == "pe" and not d.dma:
                    continue
                d.sig = True
        for e in self.ENGS:
            cnt = 0
            cur = None
            for o in self.ops[e]:
                if o.dma or not o.sig or o.fn is None:
                    continue
                if cnt % SEM_CH == 0:
                    cur = stack.enter_context(nc.semaphore(f"c_{e}_{cnt // SEM_CH}"))
                o.sem = cur
                o.semv = cnt % SEM_CH + 1
                cnt += 1
        block = stack.enter_context(nc.Block())

        def run(ename, eng):
            waited = {}
            for o in self.ops[ename]:
                for d in sorted(o.deps, key=lambda d: d.gi):
                    if d.eng == "pe" and ename == "pe" and not d.dma:
                        continue
                    k = id(d.sem)
                    if waited.get(k, 0) < d.semv:
                        eng.wait_ge(d.sem, d.semv)
                        waited[k] = d.semv
                if o.fn is None:
                    continue
                ins = o.fn(eng)
                if o.sig:
                    ins.then_inc(o.sem, 16 if o.dma else 1)

        if self.ops["sp"]:
            @block.sync
            def _(e):
                run("sp", e)
        if self.ops["pe"]:
            @block.tensor
            def _(e):
                run("pe", e)
        if self.ops["act"]:
            @block.scalar
            def _(e):
                run("act", e)
        if self.ops["dve"]:
            @block.vector
            def _(e):
                run("dve", e)
        if self.ops["pool"]:
            @block.gpsimd
            def _(e):
                run("pool", e)


class Ctx:
    def __init__(self, nc, st):
        self.nc = nc
        self.st = st
        self.P = Prog(nc)
        self.outs = []

    def dram_in(self, name, shape, dt=F32):
        return self.nc.dram_tensor(name, list(shape), dt, kind="ExternalInput").ap()

    def dram_out(self, name, shape, dt=F32):
        return self.nc.dram_tensor(name, list(shape), dt, kind="ExternalOutput").ap()

    def sb(self, name, shape, dt=F32):
        t = self.st.enter_context(self.nc.sbuf_tensor(name, list(shape), dt))
        return t, self.P.res(name)

    def ps(self, name, shape, dt=F32):
        t = self.st.enter_context(self.nc.psum_tensor(name, list(shape), dt))
        return t, self.P.res(name)

    def psum_banks(self, n, cols=512, prefix="pb"):
        self.banks = [self.ps(f"{prefix}{i}", [128, cols]) for i in range(n)]
        self.bank_i = 0

    def bank(self):
        b = self.banks[self.bank_i % len(self.banks)]
        self.bank_i += 1
        return b

    def out_dma(self, eng, out_ap, in_ap, reads):
        r = self.P.res("o")
        self.P.dma(eng, out_ap, in_ap, reads=reads, writes=[r])
        self.outs.append(r)

    def finish(self):
        self.P.op("sp", None, reads=self.outs)
        self.P.emit(self.st)


NJ0 = 24


def build_p0():
    nc = bass.Bass("TRN2", target_bir_lowering=False)
    with ExitStack() as st:
        C = Ctx(nc, st)
        P = C.P
        wm = C.dram_in("wm", [NJ0, 128, KC * 128])
        cvec = C.dram_in("cvec", [128, KC * 2])
        bm = C.dram_in("bm", [128, NJ0])
        o = C.dram_out("modT", [128, NJ0 * 2])
        cv, r_cv = C.sb("cv", [128, KC * 2])
        sc, r_sc = C.sb("sc", [128, KC * 2])
        bmt, r_bm = C.sb("bmt", [128, NJ0])
        ot, r_ot = C.sb("ot", [128, NJ0 * 2])
        wts = [C.sb(f"w{i}", [128, KC * 128]) for i in range(2)]
        C.psum_banks(2, cols=2)
        P.dma("sp", cv[:], cvec, writes=[r_cv])
        P.dma("sp", bmt[:], bm, writes=[r_bm])
        P.op("act", lambda e: e.activation(sc[:], cv[:], AF.Silu), [r_cv], [r_sc])
        for i in range(NJ0):
            wt, r_w = wts[i % 2]
            P.dma("sp" if i % 2 == 0 else "pool", wt[:], wm[i], writes=[r_w])
            pb, r_pb = C.bank()
            for kc in range(KC):
                P.op("pe", lambda e, wt=wt, pb=pb, kc=kc: e.matmul(
                    pb[:, 0:2], wt[:, kc * 128:(kc + 1) * 128], sc[:, kc * 2:kc * 2 + 2],
                    start=(kc == 0), stop=(kc == KC - 1)), [r_w, r_sc], [r_pb])
            P.op("dve", lambda e, pb=pb, i=i: e.tensor_scalar(
                ot[:, 2 * i:2 * i + 2], pb[:, 0:2], bmt[:, i:i + 1], None, ALU.add), [r_pb, r_bm], [r_ot])
        C.out_dma("sp", o, ot[:], [r_ot])
        C.finish()
    return nc


def _perm(idx, half):
    return np.asarray(idx) ^ half


def p1_groups():
    g = []
    r128 = np.arange(128)
    for j in range(6):
        g.append((f"cq{j}", O_CQ + j * 128 + r128, "cq", j))
    for j in range(2):
        g.append((f"ckv{j}", O_CKV + j * 128 + r128, "ckv", j))
    for h in range(8):
        g.append((f"A_u{h}", O_AU + h * 128 + r128, "copy", None))
    for kv in range(2):
        g.append((f"B_v{kv}", O_BV + kv * 128 + r128, "copy", None))
    for h in range(8):
        g.append((f"D_q{h}", O_DQ + h * 128 + r128, "copy", None))
    for h in range(8):
        g.append((f"D_v{h}", O_DV + h * 128 + r128, "copy", None))
    for h in range(8):
        g.append((f"D_k{h}", O_DK + h * 128 + r128, "scale", 128.0 ** -0.5))
    for nm, off in (("A_g", O_AG), ("B_g", O_BG), ("C_g", O_CG), ("D_g", O_DG)):
        for h in range(8):
            g.append((f"{nm}{h}", off + h * 128 + r128, "silu", None))
    for h in range(8):
        g.append((f"D_o{h}", O_DO + h * 128 + r128, "sigmoid", None))
    for h in range(8):
        g.append((f"B_q{h}", O_BQ + h * 128 + r128, "ropeA", "B"))
        g.append((f"B_q{h}", O_BQ + h * 128 + _perm(r128, 32), "ropeB", "B"))
    for kv in range(2):
        g.append((f"B_k{kv}", O_BK + kv * 128 + r128, "ropeA", "B"))
        g.append((f"B_k{kv}", O_BK + kv * 128 + _perm(r128, 32), "ropeB", "B"))
    r64 = np.arange(64)
    g.append(("C_kr", O_KR + r64, "ropeA", "C"))
    g.append(("C_kr", O_KR + _perm(r64, 16), "ropeB", "C"))
    r8 = np.arange(8)
    g.append(("G_i", np.concatenate([O_DGT + 0 * 8 + r8, O_DGT + 2 * 8 + r8]), "gi", None))
    g.append(("G_f", np.concatenate([O_DGT + 1 * 8 + r8, O_DGT + 3 * 8 + r8]), "gf", None))
    return g


def p1_groups2():
    g = []
    r128 = np.arange(128)
    r64 = np.arange(64)
    for h in range(8):
        g.append((f"C_qn{h}", "q", h * 192 + r128, "mulr", "q"))
    for h in range(8):
        g.append((f"C_kn{h}", "kv", h * 256 + r128, "mulr", "kv"))
    for h in range(8):
        g.append((f"C_v{h}", "kv", h * 256 + 128 + r128, "mulr", "kv"))
    for hp in range(4):
        base = np.concatenate([(2 * hp) * 192 + 128 + r64, (2 * hp + 1) * 192 + 128 + r64])
        perm = np.concatenate([(2 * hp) * 192 + 128 + _perm(r64, 16), (2 * hp + 1) * 192 + 128 + _perm(r64, 16)])
        g.append((f"C_qr{hp}", "q", base, "ropeA", "Cq"))
        g.append((f"C_qr{hp}", "q", perm, "ropeB", "Cq"))
    return g


def p1_out_rows():
    rows = {}
    r = 0
    for name, cols, epi, arg in p1_groups():
        if epi in ("cq", "ckv", "gi", "gf", "ropeA"):
            continue
        rows[name] = (r, len(cols))
        r += len(cols)
    for name, which, cols, epi, arg in p1_groups2():
        if epi == "ropeA":
            continue
        rows[name] = (r, len(cols))
        r += len(cols)
    return rows, r


def build_p1():
    g1 = p1_groups()
    g2 = p1_groups2()
    rows, nrows = p1_out_rows()
    NG1 = len(g1)
    g2q = [x for x in g2 if x[1] == "q"]
    g2kv = [x for x in g2 if x[1] == "kv"]
    nc = bass.Bass("TRN2", target_bir_lowering=False)
    with ExitStack() as st:
        C = Ctx(nc, st)
        P = C.P
        x = C.dram_in("x", [NT, D])
        modT = C.dram_in("modT", [128, 96 * 2])
        gpre = C.dram_in("gpre", [128, KC])
        w1 = C.dram_in("w1", [NG1, 128, KC * 128])
        w2q = C.dram_in("w2q", [len(g2q), 128, 6 * 128])
        w2kv = C.dram_in("w2kv", [len(g2kv), 128, 2 * 128])
        gq = C.dram_in("gq", [128, 6])
        gkv = C.dram_in("gkv", [128, 2])
        gbias = C.dram_in("gbias", [16, 2])
        cosB = C.dram_in("cosB", [128, NT])
        sinB = C.dram_in("sinB", [128, NT])
        cosC = C.dram_in("cosC", [128, NT])
        sinC = C.dram_in("sinC", [128, NT])
        identd = C.dram_in("ident", [128, 128])
        O1 = C.dram_out("O1", [nrows, NT], BF16)
        O1g = C.dram_out("O1g", [32, NT])

        HW = 1056
        hT, r_hT = C.sb("hT", [128, KC, HW], BF16)
        xt, r_xt = C.sb("xt", [128, D])
        junk, r_junk = C.sb("junk", [128, D], BF16)
        wst, r_wst = C.sb("wst", [128, KC * 128])
        wbf = [C.sb(f"wbf{i}", [128, KC, 128], BF16) for i in range(2)]
        cqT, r_cqT = C.sb("cqT", [128, 6, HW], BF16)
        ckvT, r_ckvT = C.sb("ckvT", [128, 2, HW], BF16)
        ssq_q, r_ssq_q = C.sb("ssq_q", [128, HW])
        ssq_kv, r_ssq_kv = C.sb("ssq_kv", [128, HW])
        tcos = {k: C.sb(f"tcos{k}", [128, HW]) for k in ("B", "C", "Cq")}
        tsin = {k: C.sb(f"tsin{k}", [128, HW]) for k in ("B", "C", "Cq")}
        tmpA, r_tmpA = C.sb("tmpA", [128, HW])
        tmpB = [C.sb(f"tmpB{i}", [128, 512]) for i in range(2)]
        sqt = [C.sb(f"sqt{i}", [128, 512], BF16) for i in range(2)]
        ost = [C.sb(f"ost{i}", [128, HW], BF16) for i in range(2)]
        ostg, r_ostg = C.sb("ostg", [16, HW])
        gtmp, r_gtmp = C.sb("gtmp", [16, HW])
        ident, r_ident = C.sb("identt", [128, 128])
        ones_bf, r_ones = C.sb("ones_bf", [128, 128], BF16)
        mod, r_mod = C.sb("mod", [128, 96 * 2])
        gp, r_gp = C.sb("gp", [128, KC])
        s1p, r_s1p = C.sb("s1p", [128, KC * 2])
        gqt, r_gqt = C.sb("gqt", [128, 6])
        gkvt, r_gkvt = C.sb("gkvt", [128, 2])
        gbt, r_gbt = C.sb("gbt", [16, 2])
        ngbt, r_ngbt = C.sb("ngbt", [16, 2])
        ssq, r_ssq = C.sb("ssq", [128, 1])
        epsc, r_eps = C.sb("epsc", [128, 1])
        onec, r_onec = C.sb("onec", [128, 1])
        C.psum_banks(8)

        P.dma("sp", ident[:], identd, writes=[r_ident])
        P.dma("sp", mod[:], modT, writes=[r_mod])
        P.dma("sp", gp[:], gpre, writes=[r_gp])
        P.dma("sp", gqt[:], gq, writes=[r_gqt])
        P.dma("sp", gkvt[:], gkv, writes=[r_gkvt])
        P.dma("sp", gbt[:], gbias, writes=[r_gbt])
        P.op("pool", lambda e: e.memset(ones_bf[:], 1.0), [], [r_ones])
        P.op("pool", lambda e: e.memset(epsc[:], EPS), [], [r_eps])
        P.op("pool", lambda e: e.memset(onec[:], 1.0), [], [r_onec])
        P.op("dve", lambda e: e.tensor_scalar(ngbt[:], gbt[:], -1.0, None, ALU.mult), [r_gbt], [r_ngbt])
        mod3 = mod[:].rearrange("p (j v) -> p j v", v=2)
        s1p3 = s1p[:].rearrange("p (j v) -> p j v", v=2)
        for v in range(2):
            P.op("dve", lambda e, v=v: e.scalar_tensor_tensor(
                s1p3[:, :, v], mod3[:, 32:64, v], 1.0, gp[:], ALU.add, ALU.mult), [r_mod, r_gp], [r_s1p])

        wq_i = 0
        for (h0, h1) in HALVES:
            nh = h1 - h0
            subs = [(c0, min(c0 + 512, nh)) for c0 in range(0, nh, 512)]
            for k, (cd, sd) in (("B", (cosB, sinB)), ("C", (cosC, sinC))):
                P.dma("pool", tcos[k][0][:, 0:nh], cd[:, h0:h1], writes=[tcos[k][1]])
                P.dma("pool", tsin[k][0][:, 0:nh], sd[:, h0:h1], writes=[tsin[k][1]])
            tok = h0
            while tok < h1:
                n = min(128, h1 - tok, (TL - tok) if tok < TL else 128)
                v = 0 if tok < TL else 1
                P.dma("sp", xt[0:n, :], x[tok:tok + n, :], writes=[r_xt])
                P.op("pool", lambda e: e.memset(ssq[:], 0.0), [], [r_ssq])
                P.op("act", lambda e, n=n: e.activation(junk[0:n, :], xt[0:n, :], AF.Square, accum_out=ssq[0:n, :]),
                     [r_xt], [r_junk, r_ssq])
                P.op("act", lambda e, n=n: e.activation(ssq[0:n, :], ssq[0:n, :], AF.Sqrt, bias=epsc[0:n, :], scale=1.0 / D),
                     [r_ssq, r_eps], [r_ssq])
                P.op("dve", lambda e, n=n: e.reciprocal(ssq[0:n, :], ssq[0:n, :]), [r_ssq], [r_ssq])
                P.op("dve", lambda e, n=n: e.tensor_scalar(xt[0:n, :], xt[0:n, :], ssq[0:n, :], None, ALU.mult),
                     [r_xt, r_ssq], [r_xt])
                for kc0 in range(0, KC, 4):
                    pb, r_pb = C.bank()
                    for q in range(4):
                        kc = kc0 + q
                        P.op("pe", lambda e, pb=pb, q=q, kc=kc, n=n: e.transpose(
                            pb[:, q * 128:q * 128 + n], xt[0:n, kc * 128:(kc + 1) * 128], ident[0:n, 0:n]),
                            [r_xt, r_ident], [r_pb])
                    for q in range(4):
                        kc = kc0 + q
                        hdst = hT[:, kc, tok - h0:tok - h0 + n]
                        psrc = pb[:, q * 128:q * 128 + n]
                        sc_ = s1p[:, kc * 2 + v:kc * 2 + v + 1]
                        bi_ = mod[:, kc * 2 + v:kc * 2 + v + 1]
                        if q % 2 == 0:
                            P.op("dve", lambda e, hdst=hdst, psrc=psrc, sc_=sc_, bi_=bi_: e.tensor_scalar(
                                hdst, psrc, sc_, bi_, ALU.mult, ALU.add), [r_pb, r_s1p, r_mod], [r_hT])
                        else:
                            P.op("act", lambda e, hdst=hdst, psrc=psrc, sc_=sc_, bi_=bi_: e.activation(
                                hdst, psrc, AF.Identity, bias=bi_, scale=sc_), [r_pb, r_s1p, r_mod], [r_hT])
                tok += n

            def load_w(src_ap, ncol, gi_):
                wb, r_wb = wbf[gi_ % 2]
                P.dma("sp", wst[:, 0:ncol], src_ap, writes=[r_wst])
                P.op("pool", lambda e: e.tensor_copy(wb[:].rearrange("p k n -> p (k n)")[:, 0:ncol], wst[:, 0:ncol]),
                     [r_wst], [r_wb])
                return wb, r_wb

            for gi_, (name, cols, epi, arg) in enumerate(g1):
                M = len(cols)
                wb, r_wb = load_w(w1[gi_], KC * 128, gi_)
                osb, r_osb = ost[gi_ % 2]
                for si, (c0, c1) in enumerate(subs):
                    n = c1 - c0
                    pb, r_pb = C.bank()
                    for kc in range(KC):
                        P.op("pe", lambda e, pb=pb, wb=wb, kc=kc, M=M, c0=c0, c1=c1, n=n: e.matmul(
                            pb[0:M, 0:n], wb[:, kc, 0:M], hT[:, kc, c0:c1], start=(kc == 0), stop=(kc == KC - 1)),
                            [r_wb, r_hT], [r_pb])
                    src = pb[0:M, 0:n]
                    dst = osb[0:M, c0:c1]
                    if epi == "copy":
                        P.op("act", lambda e, src=src, dst=dst: e.copy(dst, src), [r_pb], [r_osb])
                    elif epi == "scale":
                        P.op("act", lambda e, src=src, dst=dst, arg=arg: e.mul(dst, src, arg), [r_pb], [r_osb])
                    elif epi == "silu":
                        P.op("act", lambda e, src=src, dst=dst: e.activation(dst, src, AF.Silu), [r_pb], [r_osb])
                    elif epi == "sigmoid":
                        P.op("act", lambda e, src=src, dst=dst: e.activation(dst, src, AF.Sigmoid), [r_pb], [r_osb])
                    elif epi in ("cq", "ckv"):
                        tgt, r_tgt = (cqT, r_cqT) if epi == "cq" else (ckvT, r_ckvT)
                        acc, r_acc = (ssq_q, r_ssq_q) if epi == "cq" else (ssq_kv, r_ssq_kv)
                        sq, r_sq = sqt[si % 2]
                        P.op("act", lambda e, src=src, tgt=tgt, arg=arg, c0=c0, c1=c1: e.copy(tgt[:, arg, c0:c1], src),
                             [r_pb], [r_tgt])
                        P.op("act", lambda e, src=src, sq=sq, n=n: e.activation(sq[:, 0:n], src, AF.Square),
                             [r_pb], [r_sq])
                        pb2, r_pb2 = C.bank()
                        P.op("pe", lambda e, pb2=pb2, sq=sq, n=n: e.matmul(pb2[:, 0:n], ones_bf[:], sq[:, 0:n],
                                                                          start=True, stop=True), [r_sq, r_ones], [r_pb2])
                        if arg == 0:
                            P.op("dve", lambda e, acc=acc, pb2=pb2, c0=c0, c1=c1, n=n: e.tensor_copy(
                                acc[:, c0:c1], pb2[:, 0:n]), [r_pb2], [r_acc])
                        else:
                            P.op("dve", lambda e, acc=acc, pb2=pb2, c0=c0, c1=c1, n=n: e.tensor_tensor(
                                acc[:, c0:c1], acc[:, c0:c1], pb2[:, 0:n], ALU.add), [r_pb2, r_acc], [r_acc])
                    elif epi == "ropeA":
                        ct, r_ct = tcos[arg]
                        P.op("dve", lambda e, src=src, ct=ct, M=M, c0=c0, c1=c1: e.tensor_tensor(
                            tmpA[0:M, c0:c1], src, ct[0:M, c0:c1], ALU.mult), [r_pb, r_ct], [r_tmpA])
                    elif epi == "ropeB":
                        stt, r_stt = tsin[arg]
                        tb, r_tb = tmpB[si % 2]
                        P.op("dve", lambda e, src=src, stt=stt, tb=tb, M=M, c0=c0, c1=c1, n=n: e.tensor_tensor(
                            tb[0:M, 0:n], src, stt[0:M, c0:c1], ALU.mult), [r_pb, r_stt], [r_tb])
                        P.op("pool", lambda e, dst=dst, tb=tb, M=M, c0=c0, c1=c1, n=n: e.tensor_tensor(
                            dst, tb[0:M, 0:n], tmpA[0:M, c0:c1], ALU.add), [r_tb, r_tmpA], [r_osb])
                    elif epi == "gi":
                        P.op("act", lambda e, src=src, c0=c0, c1=c1: e.activation(
                            ostg[0:16, c0:c1], src, AF.Identity, bias=gbt[:, 0:1]), [r_pb, r_gbt], [r_ostg])
                    elif epi == "gf":
                        P.op("act", lambda e, src=src, c0=c0, c1=c1: e.activation(
                            gtmp[0:16, c0:c1], src, AF.Exp, bias=ngbt[:, 1:2], scale=-1.0), [r_pb, r_ngbt], [r_gtmp])
                        P.op("act", lambda e, c0=c0, c1=c1: e.activation(
                            gtmp[0:16, c0:c1], gtmp[0:16, c0:c1], AF.Ln, bias=onec[0:16, :]), [r_gtmp, r_onec], [r_gtmp])
                        P.op("dve", lambda e, c0=c0, c1=c1: e.tensor_scalar(
                            ostg[0:16, c0:c1], gtmp[0:16, c0:c1], -1.0, None, ALU.mult), [r_gtmp], [r_ostg])
                if epi in ("copy", "scale", "silu", "sigmoid", "ropeB"):
                    r0, m = rows[name]
                    C.out_dma("pool", O1[r0:r0 + m, h0:h1], osb[0:m, 0:nh], [r_osb])
                elif epi == "gi":
                    C.out_dma("pool", O1g[0:16, h0:h1], ostg[0:16, 0:nh], [r_ostg])
                elif epi == "gf":
                    C.out_dma("pool", O1g[16:32, h0:h1], ostg[0:16, 0:nh], [r_ostg])

            for acc, r_acc, dim in ((ssq_q, r_ssq_q, 768), (ssq_kv, r_ssq_kv, 256)):
                a_ = acc[:, 0:nh]
                P.op("act", lambda e, a_=a_, dim=dim: e.activation(
                    a_, a_, AF.Sqrt, bias=epsc[:], scale=1.0 / dim), [r_acc, r_eps], [r_acc])
                P.op("dve", lambda e, a_=a_: e.reciprocal(a_, a_), [r_acc], [r_acc])
            for tab in (tcos, tsin):
                o_ = tab["Cq"][0][:, 0:nh]
                i_ = tab["C"][0][:, 0:nh]
                q_ = ssq_q[:, 0:nh]
                P.op("dve", lambda e, o_=o_, i_=i_, q_=q_: e.tensor_tensor(o_, i_, q_, ALU.mult),
                     [tab["C"][1], r_ssq_q], [tab["Cq"][1]])

            iq = ikv = 0
            for gj, (name, which, cols, epi, arg) in enumerate(g2):
                M = len(cols)
                nk = 6 if which == "q" else 2
                src_ap = w2q[iq] if which == "q" else w2kv[ikv]
                gt = gqt if which == "q" else gkvt
                r_gt = r_gqt if which == "q" else r_gkvt
                if which == "q":
                    iq += 1
                else:
                    ikv += 1
                wb, r_wb = wbf[gj % 2]
                P.dma("sp", wst[:, 0:nk * 128], src_ap, writes=[r_wst])
                for kc in range(nk):
                    P.op("pool", lambda e, wb=wb, kc=kc, gt=gt: e.tensor_scalar(
                        wb[:, kc, :], wst[:, kc * 128:(kc + 1) * 128], gt[:, kc:kc + 1], None, ALU.mult),
                        [r_wst, r_gt], [r_wb])
                rhsT, r_rhs = (cqT, r_cqT) if which == "q" else (ckvT, r_ckvT)
                rr, r_rr = (ssq_q, r_ssq_q) if which == "q" else (ssq_kv, r_ssq_kv)
                osb, r_osb = ost[gj % 2]
                for si, (c0, c1) in enumerate(subs):
                    n = c1 - c0
                    pb, r_pb = C.bank()
                    for kc in range(nk):
                        P.op("pe", lambda e, pb=pb, wb=wb, kc=kc, M=M, c0=c0, c1=c1, n=n, rhsT=rhsT, nk=nk: e.matmul(
                            pb[0:M, 0:n], wb[:, kc, 0:M], rhsT[:, kc, c0:c1], start=(kc == 0), stop=(kc == nk - 1)),
                            [r_wb, r_rhs], [r_pb])
                    src = pb[0:M, 0:n]
                    dst = osb[0:M, c0:c1]
                    if epi == "mulr":
                        P.op("dve", lambda e, src=src, dst=dst, rr=rr, M=M, c0=c0, c1=c1: e.tensor_tensor(
                            dst, src, rr[0:M, c0:c1], ALU.mult), [r_pb, r_rr], [r_osb])
                    elif epi == "ropeA":
                        ct, r_ct = tcos[arg]
                        P.op("dve", lambda e, src=src, ct=ct, M=M, c0=c0, c1=c1: e.tensor_tensor(
                            tmpA[0:M, c0:c1], src, ct[0:M, c0:c1], ALU.mult), [r_pb, r_ct], [r_tmpA])
                    elif epi == "ropeB":
                        stt, r_stt = tsin[arg]
                        tb, r_tb = tmpB[si % 2]
                        P.op("dve", lambda e, src=src, stt=stt, tb=tb, M=M, c0=c0, c1=c1, n=n: e.tensor_tensor(
                            tb[0:M, 0:n], src, stt[0:M, c0:c1], ALU.mult), [r_pb, r_stt], [r_tb])
                        P.op("pool", lambda e, dst=dst, tb=tb, M=M, c0=c0, c1=c1, n=n: e.tensor_tensor(
                            dst, tb[0:M, 0:n], tmpA[0:M, c0:c1], ALU.add), [r_tb, r_tmpA], [r_osb])
                if epi in ("mulr", "ropeB"):
                    r0, m = rows[name]
                    C.out_dma("pool", O1[r0:r0 + m, h0:h1], osb[0:m, 0:nh], [r_osb])
        C.finish()
    return nc


def host_p0_inputs(c, c_ctx, w_mod, b_mod):
    cv = np.stack([np.asarray(c).reshape(D), np.asarray(c_ctx).reshape(D)], axis=-1)
    cvec = np.ascontiguousarray(cv.reshape(KC, 128, 2).transpose(1, 0, 2)).reshape(128, KC * 2)
    maps = []
    for core in range(NCORE):
        wm = np.empty((NJ0, 128, KC, 128), np.float32)
        bm = np.empty((128, NJ0), np.float32)
        for l in range(2):
            w3 = w_mod[l].reshape(KC, 128, 96, 128)
            for jj in range(12):
                j = core * 12 + jj
                wm[l * 12 + jj] = w3[:, :, j, :].transpose(1, 0, 2)
                bm[:, l * 12 + jj] = b_mod[l, j * 128:(j + 1) * 128]
        maps.append({"wm": wm.reshape(NJ0, 128, KC * 128), "cvec": cvec, "bm": bm})
    return maps


def host_p0_gather(results):
    out = np.empty((2, 128, 96, 2), np.float32)
    for core in range(NCORE):
        o = results[core]["modT"].reshape(128, NJ0, 2)
        for l in range(2):
            out[l, :, core * 12:(core + 1) * 12, :] = o[:, l * 12:(l + 1) * 12, :]
    return out.reshape(2, 128, 192)


def host_w1(w_in_l):
    g1 = p1_groups()
    out = np.zeros((len(g1), 128, KC, 128), np.float32)
    w3 = w_in_l.reshape(KC, 128, -1)
    for i, (name, cols, epi, arg) in enumerate(g1):
        out[i, :, :, :len(cols)] = w3[:, :, cols].transpose(1, 0, 2)
    return out.reshape(len(g1), 128, KC * 128)


def host_w2(w_uq_l, w_ukv_l):
    g2 = p1_groups2()
    gq = [x for x in g2 if x[1] == "q"]
    gkv = [x for x in g2 if x[1] == "kv"]
    oq = np.zeros((len(gq), 128, 6, 128), np.float32)
    okv = np.zeros((len(gkv), 128, 2, 128), np.float32)
    wq3 = w_uq_l.reshape(6, 128, -1)
    wkv3 = w_ukv_l.reshape(2, 128, -1)
    for i, (name, which, cols, epi, arg) in enumerate(gq):
        oq[i, :, :, :len(cols)] = wq3[:, :, cols].transpose(1, 0, 2)
    for i, (name, which, cols, epi, arg) in enumerate(gkv):
        okv[i, :, :, :len(cols)] = wkv3[:, :, cols].transpose(1, 0, 2)
    return oq.reshape(len(gq), 128, 6 * 128), okv.reshape(len(gkv), 128, 2 * 128)


def pk(vec, nchunk):
    return np.ascontiguousarray(np.asarray(vec, np.float32).reshape(nchunk, 128).T)


def rope_tables(core):
    t = core * TL + np.arange(TL)
    row = (t // 64).astype(np.float64)
    col = (t % 64).astype(np.float64)

    def tab(dim):
        h = dim // 2
        half = h // 2
        cos = np.ones((dim, NT), np.float64)
        sin = np.zeros((dim, NT), np.float64)
        for i in range(dim):
            pos = row if i < h else col
            j = i % half
            f = 10000.0 ** (-j / half)
            sgn = -1.0 if (i % h) < half else 1.0
            cos[i, :TL] = np.cos(pos * f)
            sin[i, :TL] = sgn * np.sin(pos * f)
        return cos.astype(np.float32), sin.astype(np.float32)

    cB, sB = tab(128)
    c64, s64 = tab(64)
    cC = np.concatenate([c64, c64], axis=0)
    sC = np.concatenate([s64, s64], axis=0)
    return cB, sB, cC, sC


def host_p1_inputs(x_full, xc_full, modT_l, l, inp):
    w1 = host_w1(inp["w_in"][l])
    w2q, w2kv = host_w2(inp["mla_w_uq"][l], inp["mla_w_ukv"][l])
    gb = inp["ml_gate_b"][l]
    gbias = np.stack([np.concatenate([gb[0], gb[2]]), np.concatenate([gb[1], gb[3]])], axis=1).astype(np.float32)
    common = {
        "modT": modT_l, "gpre": pk(inp["g_pre"][l], KC), "w1": w1, "w2q": w2q, "w2kv": w2kv,
        "gq": pk(inp["mla_q_norm"][l], 6), "gkv": pk(inp["mla_kv_norm"][l], 2), "gbias": gbias,
        "ident": np.eye(128, dtype=np.float32),
    }
    maps = []
    for core in range(NCORE):
        cB, sB, cC, sC = rope_tables(core)
        m = dict(common)
        m["x"] = np.concatenate([x_full[core * TL:(core + 1) * TL], xc_full[core * CL:(core + 1) * CL]], axis=0)
        m.update({"cosB": cB, "sinB": sB, "cosC": cC, "sinC": sC})
        maps.append(m)
    return maps


NG3 = 16
GW3 = D // NG3


def build_p3():
    nc = bass.Bass("TRN2", target_bir_lowering=False)
    with ExitStack() as st:
        C = Ctx(nc, st)
        P = C.P
        yT = C.dram_in("yT", [9, 128, KC * 256], BF16)
        wo = C.dram_in("wo", [NG3, 128, KC * GW3])
        x = C.dram_in("x", [NT, D])
        grow = C.dram_in("grow", [2, D])
        gpost = C.dram_in("gpost", [1, D])
        xo = C.dram_out("xo", [NT, D])
        yts = [C.sb(f"yt{i}", [128, KC, 256], BF16) for i in range(2)]
        wst, r_wst = C.sb("wst", [128, KC * GW3])
        wbf = [C.sb(f"wbf{i}", [128, KC, GW3], BF16) for i in range(2)]
        ots = [C.sb(f"ot{i}", [128, D]) for i in range(2)]
        xt, r_xt = C.sb("xt", [128, D])
        ggs = [C.sb(f"gg{i}", [128, D]) for i in range(2)]
        junk, r_junk = C.sb("junk", [128, D], BF16)
        ssq, r_ssq = C.sb("ssq", [128, 1])
        epsc, r_eps = C.sb("epsc", [128, 1])
        C.psum_banks(8)
        P.op("pool", lambda e: e.memset(epsc[:], EPS), [], [r_eps])
        P.dma("sp", xt[:], gpost[0:1, :].to_broadcast([128, D]), writes=[r_xt])
        for v in range(2):
            gg, r_gg = ggs[v]
            P.dma("sp", gg[:], grow[v:v + 1, :].to_broadcast([128, D]), writes=[r_gg])
            P.op("dve", lambda e, gg=gg: e.tensor_tensor(gg[:], gg[:], xt[:], ALU.mult), [r_gg, r_xt], [r_gg])
        sts = [(s, s + 256) for s in range(0, TL, 256)] + [(TL, NT)]
        wi = 0
        for sti, (s0, s1) in enumerate(sts):
            v = 0 if s0 < TL else 1
            ns = s1 - s0
            yt, r_yt = yts[sti % 2]
            P.dma("pool", yt[:].rearrange("p k n -> p (k n)"), yT[sti], writes=[r_yt])
            tiles = [(a, min(a + 128, ns)) for a in range(0, ns, 128)]
            for g in range(NG3):
                wb, r_wb = wbf[wi % 2]
                wi += 1
                P.dma("sp", wst[:], wo[g], writes=[r_wst])
                P.op("pool", lambda e, wb=wb: e.tensor_copy(wb[:].rearrange("p k n -> p (k n)"), wst[:]), [r_wst], [r_wb])
                for ti, (a, b) in enumerate(tiles):
                    n = b - a
                    pb, r_pb = C.bank()
                    ot, r_ot = ots[ti]
                    for kc in range(KC):
                        P.op("pe", lambda e, pb=pb, yt=yt, wb=wb, kc=kc, a=a, b=b, n=n: e.matmul(
                            pb[0:n, 0:GW3], yt[:, kc, a:b], wb[:, kc, :], start=(kc == 0), stop=(kc == KC - 1)),
                            [r_yt, r_wb], [r_pb])
                    P.op("act", lambda e, pb=pb, ot=ot, n=n, g=g: e.copy(ot[0:n, g * GW3:(g + 1) * GW3], pb[0:n, 0:GW3]),
                         [r_pb], [r_ot])
            for ti, (a, b) in enumerate(tiles):
                n = b - a
                ot, r_ot = ots[ti]
                gg, r_gg = ggs[v]
                P.dma("sp", xt[0:n, :], x[s0 + a:s0 + b, :], writes=[r_xt])
                P.op("pool", lambda e: e.memset(ssq[:], 0.0), [], [r_ssq])
                P.op("act", lambda e, ot=ot, n=n: e.activation(junk[0:n, :], ot[0:n, :], AF.Square, accum_out=ssq[0:n, :]),
                     [r_ot], [r_junk, r_ssq])
                P.op("act", lambda e, n=n: e.activation(ssq[0:n, :], ssq[0:n, :], AF.Sqrt, bias=epsc[0:n, :], scale=1.0 / D),
                     [r_ssq, r_eps], [r_ssq])
                P.op("dve", lambda e, n=n: e.reciprocal(ssq[0:n, :], ssq[0:n, :]), [r_ssq], [r_ssq])
                P.op("dve", lambda e, ot=ot, gg=gg, n=n: e.scalar_tensor_tensor(
                    ot[0:n, :], ot[0:n, :], ssq[0:n, :], gg[0:n, :], ALU.mult, ALU.mult), [r_ot, r_ssq, r_gg], [r_ot])
                P.op("dve", lambda e, ot=ot, n=n: e.tensor_tensor(ot[0:n, :], ot[0:n, :], xt[0:n, :], ALU.add),
                     [r_ot, r_xt], [r_ot])
                C.out_dma("pool", xo[s0 + a:s0 + b, :], ot[0:n, :], [r_ot])
        C.finish()
    return nc


def host_p3_common(modT_l, l, inp):
    wo = inp["w_out"][l].reshape(KC, 128, NG3, GW3).transpose(2, 1, 0, 3)
    wo = np.ascontiguousarray(wo).reshape(NG3, 128, KC * GW3)
    m3 = modT_l.reshape(128, 96, 2)
    grow = np.ascontiguousarray(m3[:, 64:96, :].transpose(2, 1, 0)).reshape(2, D)
    return {"wo": wo, "grow": grow, "gpost": np.asarray(inp["g_post"][l], np.float32).reshape(1, D)}


def host_yT_tiles(y_loc):
    out = np.zeros((9, 128, KC, 256), y_loc.dtype)
    for i in range(9):
        s0 = i * 256
        s1 = min(s0 + 256, NT)
        out[i, :, :, :s1 - s0] = y_loc[s0:s1].reshape(s1 - s0, KC, 128).transpose(2, 1, 0)
    return out.reshape(9, 128, KC * 256)


BIGW = 16896
LCH = 512


def build_p2():
    nc = bass.Bass("TRN2", target_bir_lowering=False)
    with ExitStack() as st:
        C = Ctx(nc, st)
        P = C.P
        FA = C.dram_in("FA", [12, 128, TT], BF16)
        FB = C.dram_in("FB", [2, 64, TT], BF16)
        TM = C.dram_in("TM", [3, 128, NB * 128], BF16)
        TMV = C.dram_in("TMV", [128, NB * 129], BF16)
        GT = C.dram_in("GT", [128, 4 * NB])
        LP = C.dram_in("LP", [128, 16])
        LW = C.dram_in("LW", [4, 128, 128])
        HN = C.dram_in("HN", [128, 128])
        MK = C.dram_in("MK", [2, 128, 128])
        identd = C.dram_in("ident", [128, 128])
        Y = C.dram_out("Y", [4, 128, TT], BF16)

        bigs = [C.sb(f"big{i}", [128, BIGW], BF16) for i in range(5)]
        lp, r_lp = C.sb("lp", [128, 16])
        lwbf, r_lwbf = C.sb("lwbf", [128, 4, 128], BF16)
        hn, r_hn = C.sb("hnrow", [128, 128])
        mk32, r_mk32 = C.sb("mk32", [128, 2, 128])
        mkbf, r_mkbf = C.sb("mkbf", [128, 2, 128], BF16)
        ident, r_ident = C.sb("identt", [128, 128])
        ones_bf, r_ones = C.sb("ones_bf", [128, 128], BF16)
        ones32, r_ones32 = C.sb("ones32", [128, 128])
        onec, r_onec = C.sb("onec", [128, 1])
        epsc, r_eps = C.sb("epsc", [128, 1])
        cc, r_cc = C.sb("cc", [128, 8])
        tf = [C.sb(f"tf{i}", [128, LCH]) for i in range(6)]
        tb = [C.sb(f"tbf{i}", [128, 1024], BF16) for i in range(3)]
        S = [C.ps(f"S{i}", [128, 1024]) for i in range(2)]
        PO = [C.ps(f"PO{i}", [128, 512]) for i in range(2)]
        PD = [C.ps(f"PD{i}", [128, 512]) for i in range(2)]

        P.dma("sp", lp[:], LP, writes=[r_lp])
        P.dma("sp", hn[:], HN, writes=[r_hn])
        P.dma("sp", mk32[:], MK.rearrange("g k j -> k g j"), writes=[r_mk32])
        P.dma("sp", ident[:], identd, writes=[r_ident])
        P.op("pool", lambda e: e.memset(ones_bf[:], 1.0), [], [r_ones])
        P.op("pool", lambda e: e.memset(ones32[:], 1.0), [], [r_ones32])
        P.op("pool", lambda e: e.memset(onec[:], 1.0), [], [r_onec])
        P.op("pool", lambda e: e.memset(epsc[:], EPS), [], [r_eps])
        lwst = tf[0][0][:].rearrange("p (g j) -> p g j", j=128)
        P.dma("sp", lwst, LW.rearrange("g k j -> k g j"), writes=[tf[0][1]])
        P.op("dve", lambda e: e.tensor_copy(lwbf[:], lwst), [tf[0][1]], [r_lwbf])
        P.op("dve", lambda e: e.tensor_copy(mkbf[:], mk32[:]), [r_mk32], [r_mkbf])
        P.op("act", lambda e: e.activation(cc[:, 0:2], lp[:, 5:7], AF.Exp, scale=-1.0), [r_lp], [r_cc])
        P.op("act", lambda e: e.activation(cc[:, 0:2], cc[:, 0:2], AF.Ln, bias=onec[:]), [r_cc, r_onec], [r_cc])
        P.op("dve", lambda e: e.tensor_scalar(cc[:, 2:4], cc[:, 0:2], -16.0, None, ALU.mult), [r_cc], [r_cc])
        P.op("dve", lambda e: e.tensor_scalar(cc[:, 0:2], cc[:, 0:2], -8.0, None, ALU.mult), [r_cc], [r_cc])
        P.op("act", lambda e: e.activation(cc[:, 4:5], lp[:, 11:12], AF.Exp), [r_lp], [r_cc])

        def load_big(i, src, ncol, part=128, eng="sp"):
            t, r = bigs[i]
            P.dma(eng, t[0:part, 0:ncol], src, writes=[r])
            return t, r

        u_bf, r_u = load_big(0, FA[0], TT)
        hfl = bigs[1][0][:].bitcast(F32)
        hfh = bigs[2][0][:].bitcast(F32)
        r_hfl, r_hfh = bigs[1][1], bigs[2][1]

        def hf_ap(s, e):
            if e <= 8448:
                return hfl[:, s:e], r_hfl
            return hfh[:, s - 8448:e - 8448], r_hfh

        chunks_ctx = [(0, TC, 0, TC)]
        chunks_lat = [(TC + i * LCH, TC + (i + 1) * LCH, TC, TT) for i in range(T // LCH)]
        (v32, r_v32), (rb, r_rb), (ib, r_ib), (t2, r_t2), (hb0, r_hb0), (hb1, r_hb1) = tf
        (vbf, r_vbf), (sgt, r_sgt), (yst, r_yst) = tb[0], tb[1], tb[2]
        hbs = [(hb0, r_hb0), (hb1, r_hb1)]

        def lru_chunk(s, e, S_, E_, d, prev_ap, prev_res, hbi):
            n = e - s
            P.op("dve", lambda e_: e_.tensor_scalar(v32[:, 0:n], u_bf[:, s:e], lp[:, 2:3], lp[:, 4:5], ALU.mult, ALU.add),
                 [r_u, r_lp], [r_v32])
            for j in (0, 1, 3):
                o = j - 2
                lo, hi = max(s, S_ - o), min(e, E_ - o)
                if lo < hi:
                    P.op("dve", lambda e_, lo=lo, hi=hi, o=o, j=j: e_.scalar_tensor_tensor(
                        v32[:, lo - s:hi - s], u_bf[:, lo + o:hi + o], lp[:, j:j + 1], v32[:, lo - s:hi - s],
                        ALU.mult, ALU.add), [r_u, r_lp, r_v32], [r_v32])
            P.op("pool", lambda e_: e_.tensor_copy(vbf[:, 0:n], v32[:, 0:n]), [r_v32], [r_vbf])
            pa, r_pa = S[0]
            px, r_px = S[1]
            for c0 in range(0, n, 512):
                c1_ = min(c0 + 512, n)
                P.op("pe", lambda e_, c0=c0, c1_=c1_: e_.matmul(pa[:, c0:c1_], lwbf[:, 2 * d, :], vbf[:, c0:c1_],
                                                               start=True, stop=True), [r_lwbf, r_vbf], [r_pa])
                P.op("pe", lambda e_, c0=c0, c1_=c1_: e_.matmul(px[:, c0:c1_], lwbf[:, 2 * d + 1, :], vbf[:, c0:c1_],
                                                               start=True, stop=True), [r_lwbf, r_vbf], [r_px])
            P.op("act", lambda e_: e_.activation(ib[:, 0:n], px[:, 0:n], AF.Sigmoid, bias=lp[:, 9 + d:10 + d]), [r_px, r_lp], [r_ib])
            P.op("act", lambda e_: e_.activation(rb[:, 0:n], pa[:, 0:n], AF.Sigmoid, bias=lp[:, 7 + d:8 + d]), [r_pa, r_lp], [r_rb])
            P.op("act", lambda e_: e_.activation(t2[:, 0:n], rb[:, 0:n], AF.Exp, scale=cc[:, 2 + d:3 + d]), [r_rb, r_cc], [r_t2])
            P.op("act", lambda e_: e_.activation(rb[:, 0:n], rb[:, 0:n], AF.Exp, scale=cc[:, d:d + 1]), [r_rb, r_cc], [r_rb])
            P.op("act", lambda e_: e_.activation(t2[:, 0:n], t2[:, 0:n], AF.Sqrt, bias=onec[:], scale=-1.0), [r_t2, r_onec], [r_t2])
            P.op("dve", lambda e_: e_.tensor_tensor(ib[:, 0:n], ib[:, 0:n], v32[:, 0:n], ALU.mult), [r_ib, r_v32], [r_ib])
            P.op("dve", lambda e_: e_.tensor_tensor(ib[:, 0:n], ib[:, 0:n], t2[:, 0:n], ALU.mult), [r_ib, r_t2], [r_ib])
            init = 0.0 if prev_ap is None else prev_ap
            rd = [r_rb, r_ib] + ([prev_res] if prev_res is not None else [])
            if d == 0:
                out, r_out = hf_ap(s, e)
                P.op("dve", lambda e_: e_.tensor_tensor_scan(out, rb[:, 0:n], ib[:, 0:n], init, ALU.mult, ALU.add), rd, [r_out])
                return out[:, n - 1:n], r_out
            hb, r_hb = hbs[hbi]
            P.op("dve", lambda e_: e_.tensor_tensor_scan(hb[:, 0:n][:, ::-1], rb[:, 0:n][:, ::-1], ib[:, 0:n][:, ::-1],
                                                         init, ALU.mult, ALU.add), rd, [r_hb])
            hfa, r_hfa = hf_ap(s, e)
            P.dma("sp", sgt[:, 0:n], FA[1][:, s:e], writes=[r_sgt])
            P.op("pool", lambda e_: e_.tensor_tensor(v32[:, 0:n], hb[:, 0:n], hfa, ALU.add), [r_hb, r_hfa], [r_v32])
            P.op("pool", lambda e_: e_.tensor_tensor(yst[:, 0:n], v32[:, 0:n], sgt[:, 0:n], ALU.mult), [r_v32, r_sgt], [r_yst])
            C.out_dma("sp", Y[0][:, s:e], yst[:, 0:n], [r_yst])
            return hb[:, 0:1], r_hb

        pa_, pr_ = None, None
        for (s, e, S_, E_) in chunks_ctx + chunks_lat:
            pa_, pr_ = lru_chunk(s, e, S_, E_, 0, pa_, pr_, 0)
        pa_, pr_ = None, None
        hbi = 0
        for (s, e, S_, E_) in chunks_ctx + chunks_lat[::-1]:
            pa_, pr_ = lru_chunk(s, e, S_, E_, 1, pa_, pr_, hbi)
            hbi ^= 1

        qT, r_q = load_big(0, FA[2], TT)
        kT, r_k = load_big(1, FA[3], TT)
        vtm, r_v = load_big(2, TM[0], NB * 128)
        sgB, r_sgB = load_big(3, FA[4], TT)
        vtm3 = vtm[:, 0:NB * 128].rearrange("p (b d) -> p b d", d=128)
        pTs = [tb[0], tb[1]]
        ystB, r_ystB = tb[2]
        (dn, r_dn), (ob, r_ob) = tf[0], tf[1]
        sc_swa = 128.0 ** -0.5
        qblocks = [(0, [(0, None), (1, None)]), (1, [(0, None), (1, None)])]
        for nq in range(128):
            keys = []
            if nq > 0:
                keys.append((2 + nq - 1, 1))
            keys.append((2 + nq, None))
            if nq < 127:
                keys.append((2 + nq + 1, 0))
            keys += [(0, None), (1, None)]
            qblocks.append((2 + nq, keys))
        for bi, (qb, keys) in enumerate(qblocks):
            Sx, r_S = S[bi % 2]
            po, r_po = PO[bi % 2]
            pd, r_pd = PD[bi % 2]
            pT, r_pT = pTs[bi % 2]
            nk = len(keys)
            qs = slice(qb * 128, (qb + 1) * 128)
            for sl, (kb, mk) in enumerate(keys):
                P.op("pe", lambda e_, Sx=Sx, sl=sl, kb=kb, qs=qs: e_.matmul(
                    Sx[:, sl * 128:(sl + 1) * 128], kT[:, kb * 128:(kb + 1) * 128], qT[:, qs], start=True, stop=True),
                    [r_k, r_q], [r_S])
            P.op("act", lambda e_, Sx=Sx, pT=pT, nk=nk: e_.activation(pT[:, 0:nk * 128], Sx[:, 0:nk * 128], AF.Exp, scale=sc_swa),
                 [r_S], [r_pT])
            for sl, (kb, mk) in enumerate(keys):
                if mk is not None:
                    P.op("dve", lambda e_, pT=pT, sl=sl, mk=mk: e_.tensor_tensor(
                        pT[:, sl * 128:(sl + 1) * 128], pT[:, sl * 128:(sl + 1) * 128], mkbf[:, mk, :], ALU.mult),
                        [r_pT, r_mkbf], [r_pT])
            for sl, (kb, mk) in enumerate(keys):
                P.op("pe", lambda e_, po=po, pT=pT, sl=sl, kb=kb, nk=nk: e_.matmul(
                    po[:, 0:128], vtm3[:, kb, :], pT[:, sl * 128:(sl + 1) * 128], start=(sl == 0), stop=(sl == nk - 1)),
                    [r_v, r_pT], [r_po])
            for sl, (kb, mk) in enumerate(keys):
                P.op("pe", lambda e_, pd=pd, pT=pT, sl=sl, nk=nk: e_.matmul(
                    pd[:, 0:128], ones_bf[:], pT[:, sl * 128:(sl + 1) * 128], start=(sl == 0), stop=(sl == nk - 1)),
                    [r_ones, r_pT], [r_pd])
            P.op("dve", lambda e_, pd=pd: e_.tensor_scalar(dn[:, 0:128], pd[:, 0:128], cc[:, 4:5], None, ALU.add), [r_pd, r_cc], [r_dn])
            P.op("dve", lambda e_: e_.reciprocal(dn[:, 0:128], dn[:, 0:128]), [r_dn], [r_dn])
            P.op("dve", lambda e_, po=po: e_.tensor_tensor(ob[:, 0:128], po[:, 0:128], dn[:, 0:128], ALU.mult), [r_po, r_dn], [r_ob])
            j8 = bi % 8
            P.op("dve", lambda e_, qs=qs, j8=j8: e_.tensor_tensor(ystB[:, j8 * 128:(j8 + 1) * 128], ob[:, 0:128], sgB[:, qs], ALU.mult),
                 [r_ob, r_sgB], [r_ystB])
            if j8 == 7 or bi == len(qblocks) - 1:
                b0 = bi - j8
                q0 = qblocks[b0][0]
                C.out_dma("sp", Y[1][:, q0 * 128:(qb + 1) * 128], ystB[:, 0:(j8 + 1) * 128], [r_ystB])

        knT, r_kn = load_big(0, FA[6], TT)
        krT, r_kr = load_big(1, FB[1], TT, part=64)
        vtmC, r_vC = load_big(2, TM[1], NB * 128)
        vtmC3 = vtmC[:, 0:NB * 128].rearrange("p (b d) -> p b d", d=128)
        sc_mla = 192.0 ** -0.5
        qns = [C.sb(f"qn{i}", [128, 512], BF16) for i in range(1)] * 2
        qrs = [C.sb(f"qr{i}", [64, 512], BF16) for i in range(1)] * 2
        sgs = [C.sb(f"sgc{i}", [128, 512], BF16) for i in range(1)] * 2
        ystC = [C.sb(f"ystc{i}", [128, 512], BF16) for i in range(1)] * 2
        pTC = [tb[0], tb[1]]
        (rec, r_rec), (oc, r_oc) = tf[0], tf[1]
        qtiles = [(0, TC, [(0, 1)])] + [(TC + i * 512, TC + (i + 1) * 512, [(2 * j, 2 * j + 1) for j in range(NB // 2)])
                                        for i in range(32)]
        pcount = 0
        for qi, (q0, q1, pairs) in enumerate(qtiles):
            n = q1 - q0
            qn, r_qn = qns[qi % 2]
            qr, r_qr = qrs[qi % 2]
            sg, r_sg = sgs[qi % 2]
            yc, r_yc = ystC[qi % 2]
            po, r_po = PO[qi % 2]
            pd, r_pd = PD[qi % 2]
            P.dma("sp", qn[:, 0:n], FA[5][:, q0:q1], writes=[r_qn])
            P.dma("sp", qr[:, 0:n], FB[0][:, q0:q1], writes=[r_qr])
            P.dma("sp", sg[:, 0:n], FA[7][:, q0:q1], writes=[r_sg])
            npair = len(pairs)

            def qk(i):
                Sx, r_S = S[(pcount + i) % 2]
                for hh in range(2):
                    kb = pairs[i][hh]
                    P.op("pe", lambda e_, Sx=Sx, hh=hh, kb=kb, n=n, qn=qn, qr=qr: e_.matmul(
                        Sx[:, hh * 512:hh * 512 + n], knT[:, kb * 128:(kb + 1) * 128], qn[:, 0:n], start=True, stop=False),
                        [r_kn, r_qn], [r_S])
                    P.op("pe", lambda e_, Sx=Sx, hh=hh, kb=kb, n=n, qn=qn, qr=qr: e_.matmul(
                        Sx[:, hh * 512:hh * 512 + n], krT[0:64, kb * 128:(kb + 1) * 128], qr[0:64, 0:n], start=False, stop=True),
                        [r_kr, r_qr], [r_S])
                pT, r_pT = pTC[(pcount + i) % 2]
                if n == 512:
                    P.op("act", lambda e_, Sx=Sx, pT=pT: e_.activation(pT[:, :], Sx[:, :], AF.Exp, scale=sc_mla), [r_S], [r_pT])
                else:
                    for hh in range(2):
                        P.op("act", lambda e_, Sx=Sx, pT=pT, hh=hh, n=n: e_.activation(
                            pT[:, hh * 512:hh * 512 + n], Sx[:, hh * 512:hh * 512 + n], AF.Exp, scale=sc_mla), [r_S], [r_pT])

            def pv(i):
                pT, r_pT = pTC[(pcount + i) % 2]
                for hh in range(2):
                    kb = pairs[i][hh]
                    first = (i == 0 and hh == 0)
                    last = (i == npair - 1 and hh == 1)
                    P.op("pe", lambda e_, pT=pT, hh=hh, kb=kb, first=first, last=last, n=n, po=po: e_.matmul(
                        po[:, 0:n], vtmC3[:, kb, :], pT[:, hh * 512:hh * 512 + n], start=first, stop=last),
                        [r_vC, r_pT], [r_po])
                    P.op("pe", lambda e_, pT=pT, hh=hh, first=first, last=last, n=n, pd=pd: e_.matmul(
                        pd[:, 0:n], ones_bf[:], pT[:, hh * 512:hh * 512 + n], start=first, stop=last),
                        [r_ones, r_pT], [r_pd])

            qk(0)
            for i in range(npair):
                if i + 1 < npair:
                    qk(i + 1)
                pv(i)
            pcount += npair
            P.op("dve", lambda e_, pd=pd, n=n: e_.reciprocal(rec[:, 0:n], pd[:, 0:n]), [r_pd], [r_rec])
            P.op("dve", lambda e_, po=po, n=n: e_.tensor_tensor(oc[:, 0:n], po[:, 0:n], rec[:, 0:n], ALU.mult), [r_po, r_rec], [r_oc])
            P.op("pool", lambda e_, yc=yc, sg=sg, n=n: e_.tensor_tensor(yc[:, 0:n], oc[:, 0:n], sg[:, 0:n], ALU.mult), [r_oc, r_sg], [r_yc])
            C.out_dma("sp", Y[2][:, q0:q1], yc[:, 0:n], [r_yc])

        qT, r_q = load_big(0, FA[8], TT)
        kT, r_k = load_big(1, FA[9], TT)
        ktm, r_ktm = load_big(2, TM[2], NB * 128)
        vex, r_vex = load_big(3, TMV, NB * 129)
        hfs, r_hfs = bigs[4]
        ktm3 = ktm[:, 0:NB * 128].rearrange("p (b d) -> p b d", d=128)
        vex3 = vex[:, 0:NB * 129].rearrange("p (b d) -> p b d", d=129)
        hfs3 = hfs[:, 0:NB * 128].rearrange("p (b d) -> p b d", d=128)
        gt, r_gt = C.sb("gt", [128, 4 * NB])
        P.dma("sp", gt[:], GT, writes=[r_gt])
        gt3 = gt[:].rearrange("p (g n) -> p g n", n=NB)
        sc_t = {}
        for nm in ("b", "G", "u", "eb", "eg", "ueg"):
            for d in range(2):
                sc_t[(nm, d)] = C.sb(f"sc_{nm}{d}", [128, NB])
        for d in range(2):
            ig = gt3[:, d, :]
            lf = gt3[:, 2 + d, :]
            pb_, r_pb_ = PO[0]
            pg_, r_pg_ = PO[1]
            (b_, r_b), (G_, r_G), (u_, r_uu), (eb_, r_eb), (eg_, r_eg), (ueg_, r_ueg) = [sc_t[(nm, d)] for nm in ("b", "G", "u", "eb", "eg", "ueg")]
            P.op("pe", lambda e_, d=d, lf=lf, pb_=pb_: e_.matmul(pb_[:, 0:NB], mk32[:, d, :], lf, start=True, stop=True), [r_mk32, r_gt], [r_pb_])
            P.op("pe", lambda e_, lf=lf, pg_=pg_: e_.matmul(pg_[:, 0:NB], ones32[:], lf, start=True, stop=True), [r_ones32, r_gt], [r_pg_])
            P.op("dve", lambda e_, b_=b_, pb_=pb_: e_.tensor_copy(b_[:], pb_[:, 0:NB]), [r_pb_], [r_b])
            P.op("dve", lambda e_, G_=G_, pg_=pg_: e_.tensor_copy(G_[:], pg_[:, 0:NB]), [r_pg_], [r_G])
            P.op("dve", lambda e_, u_=u_, ig=ig, b_=b_: e_.tensor_tensor(u_[:], ig, b_[:], ALU.subtract), [r_gt, r_b], [r_uu])
            P.op("dve", lambda e_, ueg_=ueg_, u_=u_, G_=G_: e_.tensor_tensor(ueg_[:], u_[:], G_[:], ALU.add), [r_uu, r_G], [r_ueg])
            P.op("act", lambda e_, u_=u_: e_.activation(u_[:], u_[:], AF.Exp), [r_uu], [r_uu])
            P.op("act", lambda e_, ueg_=ueg_: e_.activation(ueg_[:], ueg_[:], AF.Exp), [r_ueg], [r_ueg])
            P.op("act", lambda e_, eb_=eb_, b_=b_: e_.activation(eb_[:], b_[:], AF.Exp), [r_b], [r_eb])
            P.op("act", lambda e_, eg_=eg_, G_=G_: e_.activation(eg_[:], G_[:], AF.Exp), [r_G], [r_eg])

        C32, r_C32 = C.sb("C32", [128, 129])
        Cbfs = [C.sb(f"Cbf{i}", [128, 129], BF16) for i in range(2)]
        pTd = [C.sb(f"pTd{i}", [128, 128], BF16) for i in range(2)]
        vps = [C.sb(f"vp{i}", [128, 129], BF16) for i in range(2)]
        vpps = [C.sb(f"vpp{i}", [128, 129], BF16) for i in range(2)]
        dds = [C.sb(f"dd{i}", [128, 2]) for i in range(2)]
        hss = [C.sb(f"hs{i}", [128, 128]) for i in range(2)]
        hns = [C.sb(f"hn{i}", [128, 128]) for i in range(2)]
        ssd = [C.sb(f"ssd{i}", [128, 1]) for i in range(2)]
        junkd, r_junkd = C.sb("junkd", [128, 128], BF16)
        (yt1, r_yt1) = tf[2]
        so_t, r_so = tb[0]
        sg_t, r_sgd = tb[1]
        ystD, r_ystD = tb[2]

        def mlstm_dir(d, order, groups):
            (b_, r_b), (G_, r_G), (u_, r_uu), (eb_, r_eb), (eg_, r_eg), (ueg_, r_ueg) = [sc_t[(nm, d)] for nm in ("b", "G", "u", "eb", "eg", "ueg")]
            P.op("pool", lambda e_: e_.memset(C32[:], 0.0), [], [r_C32])
            P.op("pool", lambda e_: e_.memset(Cbfs[0][0][:], 0.0), [], [Cbfs[0][1]])
            grp_of = {}
            for (lo, hi) in groups:
                for c in range(lo, hi):
                    grp_of[c] = (lo, hi)
            for ci, n in enumerate(order):
                par = ci % 2
                cs = slice(n * 128, (n + 1) * 128)
                Sx, r_S = S[par]
                pk_, r_pk = PO[par]
                ptt, r_ptt = PD[par]
                Cb, r_Cb = Cbfs[par]
                Cn, r_Cn = Cbfs[1 - par]
                pT, r_pT = pTd[par]
                vp, r_vp = vps[par]
                vpp, r_vpp = vpps[par]
                dd, r_dd = dds[par]
                if d == 1 and n in grp_of and (ci == 0 or grp_of[order[ci - 1]] != grp_of[n]):
                    lo, hi = grp_of[n]
                    P.dma("sp", so_t[:, 0:(hi - lo) * 128], FA[10][:, lo * 128:hi * 128], writes=[r_so])
                    P.dma("sp", sg_t[:, 0:(hi - lo) * 128], FA[11][:, lo * 128:hi * 128], writes=[r_sgd])
                P.op("pe", lambda e_, Sx=Sx, cs=cs: e_.matmul(Sx[:, 0:128], kT[:, cs], qT[:, cs], start=True, stop=True), [r_k, r_q], [r_S])
                P.op("dve", lambda e_, Sx=Sx, pT=pT: e_.tensor_tensor(pT[:], Sx[:, 0:128], mkbf[:, d, :], ALU.mult), [r_S, r_mkbf], [r_pT])
                P.op("pool", lambda e_, vp=vp, n=n: e_.tensor_scalar(vp[:], vex3[:, n, :], u_[:, n:n + 1], None, ALU.mult), [r_vex, r_uu], [r_vp])
                P.op("pool", lambda e_, vpp=vpp, n=n: e_.tensor_scalar(vpp[:], vex3[:, n, :], ueg_[:, n:n + 1], None, ALU.mult), [r_vex, r_ueg], [r_vpp])
                P.op("pe", lambda e_, Sx=Sx, cs=cs, Cb=Cb: e_.matmul(Sx[:, 512:641], qT[:, cs], Cb[:], start=True, stop=False), [r_q, r_Cb], [r_S])
                P.op("pe", lambda e_, Sx=Sx, pT=pT, vp=vp: e_.matmul(Sx[:, 512:641], pT[:], vp[:], start=False, stop=True), [r_pT, r_vp], [r_S])
                P.op("pe", lambda e_, pk_=pk_, n=n, vpp=vpp: e_.matmul(pk_[:, 0:129], ktm3[:, n, :], vpp[:], start=True, stop=True), [r_ktm, r_vpp], [r_pk])
                P.op("dve", lambda e_, pk_=pk_, n=n: e_.scalar_tensor_tensor(C32[:], C32[:], eg_[:, n:n + 1], pk_[:, 0:129], ALU.mult, ALU.add),
                     [r_C32, r_eg, r_pk], [r_C32])
                P.op("act", lambda e_, Cn=Cn: e_.copy(Cn[:], C32[:]), [r_C32], [r_Cn])
                P.op("dve", lambda e_, dd=dd, Sx=Sx, n=n: e_.tensor_scalar(dd[:, 0:1], Sx[:, 640:641], eb_[:, n:n + 1], None, ALU.mult),
                     [r_S, r_eb], [r_dd])
                P.op("dve", lambda e_, dd=dd: e_.tensor_scalar(dd[:, 1:2], dd[:, 0:1], -1.0, -1.0, ALU.min, ALU.mult), [r_dd], [r_dd])
                P.op("dve", lambda e_, dd=dd: e_.scalar_tensor_tensor(dd[:, 0:1], dd[:, 0:1], 1.0, dd[:, 1:2], ALU.max, ALU.max), [r_dd], [r_dd])
                P.op("dve", lambda e_, dd=dd: e_.reciprocal(dd[:, 0:1], dd[:, 0:1]), [r_dd], [r_dd])
                P.op("dve", lambda e_, dd=dd, n=n: e_.tensor_tensor(dd[:, 1:2], dd[:, 0:1], eb_[:, n:n + 1], ALU.mult), [r_dd, r_eb], [r_dd])
                if d == 0:
                    P.op("dve", lambda e_, Sx=Sx, dd=dd, n=n: e_.tensor_scalar(hfs3[:, n, :], Sx[:, 512:640], dd[:, 1:2], None, ALU.mult),
                         [r_S, r_dd], [r_hfs])
                    continue
                hs, r_hs = hss[par]
                hnn, r_hnn = hns[par]
                ss_, r_ss = ssd[par]
                P.op("dve", lambda e_, Sx=Sx, dd=dd, n=n, hs=hs: e_.scalar_tensor_tensor(
                    hs[:], Sx[:, 512:640], dd[:, 1:2], hfs3[:, n, :], ALU.mult, ALU.add), [r_S, r_dd, r_hfs], [r_hs])
                P.op("pool", lambda e_, ss_=ss_: e_.memset(ss_[:], 0.0), [], [r_ss])
                P.op("act", lambda e_, hs=hs, ss_=ss_: e_.activation(junkd[:], hs[:], AF.Square, accum_out=ss_[:]), [r_hs], [r_junkd, r_ss])
                P.op("act", lambda e_, ss_=ss_: e_.activation(ss_[:], ss_[:], AF.Sqrt, bias=epsc[:], scale=1.0 / 128), [r_ss, r_eps], [r_ss])
                P.op("dve", lambda e_, ss_=ss_: e_.reciprocal(ss_[:], ss_[:]), [r_ss], [r_ss])
                P.op("dve", lambda e_, hnn=hnn, hs=hs, ss_=ss_: e_.scalar_tensor_tensor(hnn[:], hs[:], ss_[:], hn[:], ALU.mult, ALU.mult),
                     [r_hs, r_ss, r_hn], [r_hnn])
                P.op("pe", lambda e_, ptt=ptt, hnn=hnn: e_.transpose(ptt[:, 0:128], hnn[:], ident[:]), [r_hnn, r_ident], [r_ptt])
                lo, hi = grp_of[n]
                off = (n - lo) * 128
                P.op("dve", lambda e_, ptt=ptt, off=off: e_.tensor_tensor(yt1[:, off:off + 128], ptt[:, 0:128], so_t[:, off:off + 128], ALU.mult),
                     [r_ptt, r_so], [r_yt1])
                P.op("pool", lambda e_, off=off: e_.tensor_tensor(ystD[:, off:off + 128], yt1[:, off:off + 128], sg_t[:, off:off + 128], ALU.mult),
                     [r_yt1, r_sgd], [r_ystD])
                if ci == len(order) - 1 or grp_of[order[ci + 1]] != (lo, hi):
                    C.out_dma("sp", Y[3][:, lo * 128:hi * 128], ystD[:, 0:(hi - lo) * 128], [r_ystD])

        groups = [(0, 2)] + [(2 + 4 * i, 2 + 4 * (i + 1)) for i in range(32)]
        mlstm_dir(0, list(range(NB)), groups)
        mlstm_dir(1, [1, 0] + list(range(NB - 1, 1, -1)), groups)
        C.finish()
    return nc


def _gather_tokens(per_core):
    ctx = np.concatenate([a[:, TL:NT] for a in per_core], axis=1)
    lat = np.concatenate([a[:, 0:TL] for a in per_core], axis=1)
    return np.concatenate([ctx, lat], axis=1)


def _tok_major(item):
    return np.ascontiguousarray(item.T.reshape(NB, 128, 128).transpose(1, 0, 2)).reshape(128, NB * 128)


def host_p2_inputs(res1, l, inp):
    rows, nrows = p1_out_rows()
    G = _gather_tokens([r["O1"] for r in res1])
    Gg = _gather_tokens([r["O1g"] for r in res1])

    def item(name, sub=None):
        r0, m = rows[name]
        if sub is not None:
            return G[r0 + sub[0]:r0 + sub[1]]
        return G[r0:r0 + m]

    p_ = np.arange(128)
    MK = np.stack([(p_[:, None] <= p_[None, :]), (p_[:, None] >= p_[None, :])]).astype(np.float32)
    ident = np.eye(128, dtype=np.float32)
    maps = []
    for d in range(NCORE):
        FA = np.stack([item(f"A_u{d}"), item(f"A_g{d}"), item(f"B_q{d}"), item(f"B_k{d // 4}"), item(f"B_g{d}"),
                       item(f"C_qn{d}"), item(f"C_kn{d}"), item(f"C_g{d}"), item(f"D_q{d}"), item(f"D_k{d}"),
                       item(f"D_o{d}"), item(f"D_g{d}")])
        FB = np.stack([item(f"C_qr{d // 2}", ((d % 2) * 64, (d % 2) * 64 + 64)), item("C_kr")])
        TM = np.stack([_tok_major(item(f"B_v{d // 4}")), _tok_major(item(f"C_v{d}")), _tok_major(item(f"D_k{d}"))])
        vex = np.ones((TT, 129), NPBF)
        vex[:, :128] = item(f"D_v{d}").T
        TMV = np.ascontiguousarray(vex.reshape(NB, 128, 129).transpose(1, 0, 2)).reshape(128, NB * 129)
        gsel = Gg[[d, 8 + d, 16 + d, 24 + d]]
        GT = np.ascontiguousarray(gsel.reshape(4, NB, 128).transpose(2, 0, 1)).reshape(128, 4 * NB)
        ch = slice(d * 128, (d + 1) * 128)
        LP = np.zeros((128, 16), np.float32)
        LP[:, 0:4] = inp["lru_conv_w"][l][:, ch].T
        LP[:, 4] = inp["lru_conv_b"][l][ch]
        LP[:, 5:7] = inp["lru_lambda"][l][:, ch].T
        LP[:, 7:9] = inp["lru_ba"][l][:, ch].T
        LP[:, 9:11] = inp["lru_bx"][l][:, ch].T
        LP[:, 11] = inp["swa_sink"][l][d]
        LW = np.stack([inp["lru_wa"][l][0, d], inp["lru_wx"][l][0, d], inp["lru_wa"][l][1, d], inp["lru_wx"][l][1, d]]).astype(np.float32)
        HN = np.ascontiguousarray(np.broadcast_to(inp["ml_head_norm"][l][ch].astype(np.float32), (128, 128)))
        maps.append({"FA": FA, "FB": FB, "TM": TM, "TMV": TMV, "GT": GT, "LP": LP, "LW": LW, "HN": HN, "MK": MK, "ident": ident})
    return maps


def host_p3_inputs(res2, x_full, xc_full, modT_l, l, inp):
    Yall = np.stack([r["Y"] for r in res2])
    Yt = np.ascontiguousarray(Yall.transpose(3, 1, 0, 2)).reshape(TT, D)
    common = host_p3_common(modT_l, l, inp)
    maps = []
    for core in range(NCORE):
        y_loc = np.concatenate([Yt[TC + core * TL:TC + (core + 1) * TL], Yt[core * CL:(core + 1) * CL]], axis=0)
        m = dict(common)
        m["yT"] = host_yT_tiles(y_loc)
        m["x"] = np.concatenate([x_full[core * TL:(core + 1) * TL], xc_full[core * CL:(core + 1) * CL]], axis=0)
        maps.append(m)
    return maps


def kernel(**inputs):
    inp = {k: np.asarray(v) for k, v in inputs.items()}
    cores = list(range(NCORE))
    x_full = np.ascontiguousarray(inp["x"][0], dtype=np.float32)
    xc_full = np.ascontiguousarray(inp["ctx"][0], dtype=np.float32)
    res0 = run_bass_kernel_spmd(build_p0(), host_p0_inputs(inp["c"], inp["c_ctx"], inp["w_mod"], inp["b_mod"]), core_ids=cores)
    modT = host_p0_gather(res0.results)
    nc1, nc2, nc3 = build_p1(), build_p2(), build_p3()
    for l in range(2):
        res1 = run_bass_kernel_spmd(nc1, host_p1_inputs(x_full, xc_full, modT[l], l, inp), core_ids=cores).results
        maps2 = host_p2_inputs(res1, l, inp)
        del res1
        res2 = run_bass_kernel_spmd(nc2, maps2, core_ids=cores).results
        del maps2
        maps3 = host_p3_inputs(res2, x_full, xc_full, modT[l], l, inp)
        del res2
        res3 = run_bass_kernel_spmd(nc3, maps3, core_ids=cores).results
        del maps3
        x_full = np.concatenate([r["xo"][0:TL] for r in res3], axis=0)
        xc_full = np.concatenate([r["xo"][TL:NT] for r in res3], axis=0)
    return x_full.reshape(1, T, D).astype(np.float32)
```

```python
import numpy as np
import ml_dtypes
from contextlib import ExitStack
import concourse.bass as bass
import concourse.mybir as mybir
from concourse.bass_utils import run_bass_kernel_spmd

F32 = mybir.dt.float32
BF16 = mybir.dt.bfloat16
AF = mybir.ActivationFunctionType
ALU = mybir.AluOpType
NPBF = ml_dtypes.bfloat16

NCORE = 8
D = 4096
T = 16384
TC = 256
TL = T // NCORE
CL = TC // NCORE
NT = TL + CL
TT = T + TC
NB = TT // 128
EPS = 1e-6
KC = D // 128
HALVES = ((0, 1024), (1024, NT))

O_AU, O_AG, O_BQ, O_BK, O_BV, O_BG = 0, 1024, 2048, 3072, 3328, 3584
O_CQ, O_CKV, O_KR, O_CG = 4608, 5376, 5632, 5696
O_DQ, O_DK, O_DV, O_DO, O_DGT, O_DG = 6720, 7744, 8768, 9792, 10816, 10848

SEM_CH = 16000
N_DMA_SEMS = 24


class Res:
    __slots__ = ("name", "w", "rs")

    def __init__(self, name):
        self.name = name
        self.w = None
        self.rs = []


class Op:
    __slots__ = ("eng", "fn", "deps", "sig", "sem", "semv", "dma", "gi")

    def __init__(self, eng, fn, dma, gi):
        self.eng = eng
        self.fn = fn
        self.dma = dma
        self.deps = set()
        self.sig = False
        self.sem = None
        self.semv = 0
        self.gi = gi


class Prog:
    ENGS = ("pe", "act", "dve", "pool", "sp")

    def __init__(self, nc):
        self.nc = nc
        self.ops = {e: [] for e in self.ENGS}
        self.gi = 0
        self.all_ops = []

    def res(self, name=""):
        return Res(name)

    def op(self, eng, fn, reads=(), writes=(), dma=False):
        o = Op(eng, fn, dma, self.gi)
        self.gi += 1
        deps = o.deps
        for r in reads:
            if r.w is not None:
                deps.add(r.w)
        for r in writes:
            if r.w is not None:
                deps.add(r.w)
            deps.update(r.rs)
        for r in reads:
            r.rs.append(o)
        for r in writes:
            r.w = o
            r.rs = []
        deps.discard(o)
        self.ops[eng].append(o)
        self.all_ops.append(o)
        return o

    def dma(self, eng, out, in_, reads=(), writes=(), **kw):
        return self.op(eng, lambda e: e.dma_start(out=out, in_=in_, **kw), reads, writes, dma=True)

    def emit(self, stack):
        nc = self.nc
        dma_last, dma_cnt, dma_sems = {}, {}, {}
        rr = {e: 0 for e in self.ENGS}
        for o in self.all_ops:
            if o.dma:
                key = (o.eng, rr[o.eng] % N_DMA_SEMS)
                rr[o.eng] += 1
                if key not in dma_sems:
                    dma_sems[key] = stack.enter_context(nc.semaphore(f"d_{o.eng}_{key[1]}"))
                    dma_cnt[key] = 0
                prev = dma_last.get(key)
                if prev is not None:
                    o.deps.add(prev)
                dma_last[key] = o
                dma_cnt[key] += 16
                o.sem = dma_sems[key]
                o.semv = dma_cnt[key]
                o.sig = True
        for o in self.all_ops:
            for d in o.deps:
                if d.eng == "pe" and o.eng == "pe" and not d.dma:
                    continue
                d.sig = True
        for e in self.ENGS:
            cnt = 0
            cur = None
            for o in self.ops[e]:
                if o.dma or not o.sig or o.fn is None:
                    continue
                if cnt % SEM_CH == 0:
                    cur = stack.enter_context(nc.semaphore(f"c_{e}_{cnt // SEM_CH}"))
                o.sem = cur
                o.semv = cnt % SEM_CH + 1
                cnt += 1
        block = stack.enter_context(nc.Block())

        def run(ename, eng):
            waited = {}
            for o in self.ops[ename]:
                for d in sorted(o.deps, key=lambda d: d.gi):
                    if d.eng == "pe" and ename == "pe" and not d.dma:
                        continue
                    k = id(d.sem)
                    if waited.get(k, 0) < d.semv:
                        eng.wait_ge(d.sem, d.semv)
                        waited[k] = d.semv
                if o.fn is None:
                    continue
                ins = o.fn(eng)
                if o.sig:
                    ins.then_inc(o.sem, 16 if o.dma else 1)

        if self.ops["sp"]:
            @block.sync
            def _(e):
                run("sp", e)
        if self.ops["pe"]:
            @block.tensor
            def _(e):
                run("pe", e)
        if self.ops["act"]:
            @block.scalar
            def _(e):
                run("act", e)
        if self.ops["dve"]:
            @block.vector
            def _(e):
                run("dve", e)
        if self.ops["pool"]:
            @block.gpsimd
            def _(e):
                run("pool", e)


class Ctx:
    def __init__(self, nc, st):
        self.nc = nc
        self.st = st
        self.P = Prog(nc)
        self.outs = []

    def dram_in(self, name, shape, dt=F32):
        return self.nc.dram_tensor(name, list(shape), dt, kind="ExternalInput").ap()

    def dram_out(self, name, shape, dt=F32):
        return self.nc.dram_tensor(name, list(shape), dt, kind="ExternalOutput").ap()

    def sb(self, name, shape, dt=F32):
        t = self.st.enter_context(self.nc.sbuf_tensor(name, list(shape), dt))
        return t, self.P.res(name)

    def ps(self, name, shape, dt=F32):
        t = self.st.enter_context(self.nc.psum_tensor(name, list(shape), dt))
        return t, self.P.res(name)

    def psum_banks(self, n, cols=512, prefix="pb"):
        self.banks = [self.ps(f"{prefix}{i}", [128, cols]) for i in range(n)]
        self.bank_i = 0

    def bank(self):
        b = self.banks[self.bank_i % len(self.banks)]
        self.bank_i += 1
        return b

    def out_dma(self, eng, out_ap, in_ap, reads):
        r = self.P.res("o")
        self.P.dma(eng, out_ap, in_ap, reads=reads, writes=[r])
        self.outs.append(r)

    def finish(self):
        self.P.op("sp", None, reads=self.outs)
        self.P.emit(self.st)


NJ0 = 24


def build_p0():
    nc = bass.Bass("TRN2", target_bir_lowering=False)
    with ExitStack() as st:
        C = Ctx(nc, st)
        P = C.P
        wm = C.dram_in("wm", [NJ0, 128, KC * 128])
        cvec = C.dram_in("cvec", [128, KC * 2])
        bm = C.dram_in("bm", [128, NJ0])
        o = C.dram_out("modT", [128, NJ0 * 2])
        cv, r_cv = C.sb("cv", [128, KC * 2])
        sc, r_sc = C.sb("sc", [128, KC * 2])
        bmt, r_bm = C.sb("bmt", [128, NJ0])
        ot, r_ot = C.sb("ot", [128, NJ0 * 2])
        wts = [C.sb(f"w{i}", [128, KC * 128]) for i in range(2)]
        C.psum_banks(2, cols=2)
        P.dma("sp", cv[:], cvec, writes=[r_cv])
        P.dma("sp", bmt[:], bm, writes=[r_bm])
        P.op("act", lambda e: e.activation(sc[:], cv[:], AF.Silu), [r_cv], [r_sc])
        for i in range(NJ0):
            wt, r_w = wts[i % 2]
            P.dma("sp" if i % 2 == 0 else "pool", wt[:], wm[i], writes=[r_w])
            pb, r_pb = C.bank()
            for kc in range(KC):
                P.op("pe", lambda e, wt=wt, pb=pb, kc=kc: e.matmul(
                    pb[:, 0:2], wt[:, kc * 128:(kc + 1) * 128], sc[:, kc * 2:kc * 2 + 2],
                    start=(kc == 0), stop=(kc == KC - 1)), [r_w, r_sc], [r_pb])
            P.op("dve", lambda e, pb=pb, i=i: e.tensor_scalar(
                ot[:, 2 * i:2 * i + 2], pb[:, 0:2], bmt[:, i:i + 1], None, ALU.add), [r_pb, r_bm], [r_ot])
        C.out_dma("sp", o, ot[:], [r_ot])
        C.finish()
    return nc


def _perm(idx, half):
    return np.asarray(idx) ^ half


def p1_groups():
    g = []
    r128 = np.arange(128)
    for j in range(6):
        g.append((f"cq{j}", O_CQ + j * 128 + r128, "cq", j))
    for j in range(2):
        g.append((f"ckv{j}", O_CKV + j * 128 + r128, "ckv", j))
    for h in range(8):
        g.append((f"A_u{h}", O_AU + h * 128 + r128, "copy", None))
    for kv in range(2):
        g.append((f"B_v{kv}", O_BV + kv * 128 + r128, "copy", None))
    for h in range(8):
        g.append((f"D_q{h}", O_DQ + h * 128 + r128, "copy", None))
    for h in range(8):
        g.append((f"D_v{h}", O_DV + h * 128 + r128, "copy", None))
    for h in range(8):
        g.append((f"D_k{h}", O_DK + h * 128 + r128, "scale", 128.0 ** -0.5))
    for nm, off in (("A_g", O_AG), ("B_g", O_BG), ("C_g", O_CG), ("D_g", O_DG)):
        for h in range(8):
            g.append((f"{nm}{h}", off + h * 128 + r128, "silu", None))
    for h in range(8):
        g.append((f"D_o{h}", O_DO + h * 128 + r128, "sigmoid", None))
    for h in range(8):
        g.append((f"B_q{h}", O_BQ + h * 128 + r128, "ropeA", "B"))
        g.append((f"B_q{h}", O_BQ + h * 128 + _perm(r128, 32), "ropeB", "B"))
    for kv in range(2):
        g.append((f"B_k{kv}", O_BK + kv * 128 + r128, "ropeA", "B"))
        g.append((f"B_k{kv}", O_BK + kv * 128 + _perm(r128, 32), "ropeB", "B"))
    r64 = np.arange(64)
    g.append(("C_kr", O_KR + r64, "ropeA", "C"))
    g.append(("C_kr", O_KR + _perm(r64, 16), "ropeB", "C"))
    r8 = np.arange(8)
    g.append(("G_i", np.concatenate([O_DGT + 0 * 8 + r8, O_DGT + 2 * 8 + r8]), "gi", None))
    g.append(("G_f", np.concatenate([O_DGT + 1 * 8 + r8, O_DGT + 3 * 8 + r8]), "gf", None))
    return g


def p1_groups2():
    g = []
    r128 = np.arange(128)
    r64 = np.arange(64)
    for h in range(8):
        g.append((f"C_qn{h}", "q", h * 192 + r128, "mulr", "q"))
    for h in range(8):
        g.append((f"C_kn{h}", "kv", h * 256 + r128, "mulr", "kv"))
    for h in range(8):
        g.append((f"C_v{h}", "kv", h * 256 + 128 + r128, "mulr", "kv"))
    for hp in range(4):
        base = np.concatenate([(2 * hp) * 192 + 128 + r64, (2 * hp + 1) * 192 + 128 + r64])
        perm = np.concatenate([(2 * hp) * 192 + 128 + _perm(r64, 16), (2 * hp + 1) * 192 + 128 + _perm(r64, 16)])
        g.append((f"C_qr{hp}", "q", base, "ropeA", "Cq"))
        g.append((f"C_qr{hp}", "q", perm, "ropeB", "Cq"))
    return g


def p1_out_rows():
    rows = {}
    r = 0
    for name, cols, epi, arg in p1_groups():
        if epi in ("cq", "ckv", "gi", "gf", "ropeA"):
            continue
        rows[name] = (r, len(cols))
        r += len(cols)
    for name, which, cols, epi, arg in p1_groups2():
        if epi == "ropeA":
            continue
        rows[name] = (r, len(cols))
        r += len(cols)
    return rows, r


def build_p1():
    g1 = p1_groups()
    g2 = p1_groups2()
    rows, nrows = p1_out_rows()
    NG1 = len(g1)
    g2q = [x for x in g2 if x[1] == "q"]
    g2kv = [x for x in g2 if x[1] == "kv"]
    nc = bass.Bass("TRN2", target_bir_lowering=False)
    with ExitStack() as st:
        C = Ctx(nc, st)
        P = C.P
        x = C.dram_in("x", [NT, D])
        modT = C.dram_in("modT", [128, 96 * 2])
        gpre = C.dram_in("gpre", [128, KC])
        w1 = C.dram_in("w1", [NG1, 128, KC * 128])
        w2q = C.dram_in("w2q", [len(g2q), 128, 6 * 128])
        w2kv = C.dram_in("w2kv", [len(g2kv), 128, 2 * 128])
        gq = C.dram_in("gq", [128, 6])
        gkv = C.dram_in("gkv", [128, 2])
        gbias = C.dram_in("gbias", [16, 2])
        cosB = C.dram_in("cosB", [128, NT])
        sinB = C.dram_in("sinB", [128, NT])
        cosC = C.dram_in("cosC", [128, NT])
        sinC = C.dram_in("sinC", [128, NT])
        identd = C.dram_in("ident", [128, 128])
        O1 = C.dram_out("O1", [nrows, NT], BF16)
        O1g = C.dram_out("O1g", [32, NT])

        HW = 1056
        hT, r_hT = C.sb("hT", [128, KC, HW], BF16)
        xt, r_xt = C.sb("xt", [128, D])
        junk, r_junk = C.sb("junk", [128, D // 2], BF16)
        wsts = [C.sb(f"wst{i}", [128, KC * 128]) for i in range(2)]
        wbf = [C.sb(f"wbf{i}", [128, KC, 128], BF16) for i in range(2)]
        wbf_hi = [P.res(f"wbfhi{i}") for i in range(2)]
        wld = [0]
        cqT, r_cqT = C.sb("cqT", [128, 6, HW], BF16)
        ckvT, r_ckvT = C.sb("ckvT", [128, 2, HW], BF16)
        ssq_q, r_ssq_q = C.sb("ssq_q", [128, HW])
        ssq_kv, r_ssq_kv = C.sb("ssq_kv", [128, HW])
        tcos = {k: C.sb(f"tcos{k}", [128, HW]) for k in ("B", "C", "Cq")}
        tsin = {k: C.sb(f"tsin{k}", [128, HW]) for k in ("B", "C", "Cq")}
        tmpA, r_tmpA = C.sb("tmpA", [128, HW])
        gtmp, r_gtmp = tmpA, r_tmpA
        tmpB = [C.sb(f"tmpB{i}", [128, 512]) for i in range(2)]
        sqt = [C.sb(f"sqt{i}", [128, 512], BF16) for i in range(2)]
        ost = [C.sb(f"ost{i}", [128, HW], BF16) for i in range(2)]
        ostg, r_ostg = C.sb("ostg", [16, HW])
        ident, r_ident = C.sb("identt", [128, 128])
        ones_bf, r_ones = C.sb("ones_bf", [128, 128], BF16)
        mod, r_mod = C.sb("mod", [128, 96 * 2])
        gp, r_gp = C.sb("gp", [128, KC])
        s1p, r_s1p = C.sb("s1p", [128, KC * 2])
        gqt, r_gqt = C.sb("gqt", [128, 6])
        gkvt, r_gkvt = C.sb("gkvt", [128, 2])
        gbt, r_gbt = C.sb("gbt", [16, 2])
        ngbt, r_ngbt = C.sb("ngbt", [16, 2])
        ssq, r_ssq = C.sb("ssq", [128, 2])
        epsc, r_eps = C.sb("epsc", [128, 1])
        onec, r_onec = C.sb("onec", [128, 1])
        C.psum_banks(8)

        P.dma("sp", ident[:], identd, writes=[r_ident])
        P.dma("sp", mod[:], modT, writes=[r_mod])
        P.dma("sp", gp[:], gpre, writes=[r_gp])
        P.dma("sp", gqt[:], gq, writes=[r_gqt])
        P.dma("sp", gkvt[:], gkv, writes=[r_gkvt])
        P.dma("sp", gbt[:], gbias, writes=[r_gbt])
        P.op("pool", lambda e: e.memset(ones_bf[:], 1.0), [], [r_ones])
        P.op("pool", lambda e: e.memset(epsc[:], EPS), [], [r_eps])
        P.op("pool", lambda e: e.memset(onec[:], 1.0), [], [r_onec])
        P.op("dve", lambda e: e.tensor_scalar(ngbt[:], gbt[:], -1.0, None, ALU.mult), [r_gbt], [r_ngbt])
        mod3 = mod[:].rearrange("p (j v) -> p j v", v=2)
        s1p3 = s1p[:].rearrange("p (j v) -> p j v", v=2)
        for v in range(2):
            P.op("dve", lambda e, v=v: e.scalar_tensor_tensor(
                s1p3[:, :, v], mod3[:, 32:64, v], 1.0, gp[:], ALU.add, ALU.mult), [r_mod, r_gp], [r_s1p])

        wq_i = 0
        for (h0, h1) in HALVES:
            nh = h1 - h0
            subs = [(c0, min(c0 + 512, nh)) for c0 in range(0, nh, 512)]
            for k, (cd, sd) in (("B", (cosB, sinB)), ("C", (cosC, sinC))):
                P.dma("pool", tcos[k][0][:, 0:nh], cd[:, h0:h1], writes=[tcos[k][1]])
                P.dma("pool", tsin[k][0][:, 0:nh], sd[:, h0:h1], writes=[tsin[k][1]])
            tok = h0
            while tok < h1:
                n = min(128, h1 - tok, (TL - tok) if tok < TL else 128)
                v = 0 if tok < TL else 1
                P.dma("sp", xt[0:n, :], x[tok:tok + n, :], writes=[r_xt])
                P.op("pool", lambda e: e.memset(ssq[:], 0.0), [], [r_ssq])
                for hh in range(2):
                    P.op("act", lambda e, n=n, hh=hh: e.activation(
                        junk[0:n, :], xt[0:n, hh * (D // 2):(hh + 1) * (D // 2)], AF.Square, accum_out=ssq[0:n, hh:hh + 1]),
                        [r_xt], [r_junk, r_ssq])
                P.op("dve", lambda e, n=n: e.tensor_tensor(ssq[0:n, 0:1], ssq[0:n, 0:1], ssq[0:n, 1:2], ALU.add), [r_ssq], [r_ssq])
                P.op("act", lambda e, n=n: e.activation(ssq[0:n, 0:1], ssq[0:n, 0:1], AF.Sqrt, bias=epsc[0:n, :], scale=1.0 / D),
                     [r_ssq, r_eps], [r_ssq])
                P.op("dve", lambda e, n=n: e.reciprocal(ssq[0:n, 0:1], ssq[0:n, 0:1]), [r_ssq], [r_ssq])
                P.op("dve", lambda e, n=n: e.tensor_scalar(xt[0:n, :], xt[0:n, :], ssq[0:n, 0:1], None, ALU.mult),
                     [r_xt, r_ssq], [r_xt])
                for kc0 in range(0, KC, 4):
                    pb, r_pb = C.bank()
                    for q in range(4):
                        kc = kc0 + q
                        P.op("pe", lambda e, pb=pb, q=q, kc=kc, n=n: e.transpose(
                            pb[:, q * 128:q * 128 + n], xt[0:n, kc * 128:(kc + 1) * 128], ident[0:n, 0:n]),
                            [r_xt, r_ident], [r_pb])
                    for q in range(4):
                        kc = kc0 + q
                        hdst = hT[:, kc, tok - h0:tok - h0 + n]
                        psrc = pb[:, q * 128:q * 128 + n]
                        sc_ = s1p[:, kc * 2 + v:kc * 2 + v + 1]
                        bi_ = mod[:, kc * 2 + v:kc * 2 + v + 1]
                        if q % 2 == 0:
                            P.op("dve", lambda e, hdst=hdst, psrc=psrc, sc_=sc_, bi_=bi_: e.tensor_scalar(
                                hdst, psrc, sc_, bi_, ALU.mult, ALU.add), [r_pb, r_s1p, r_mod], [r_hT])
                        else:
                            P.op("act", lambda e, hdst=hdst, psrc=psrc, sc_=sc_, bi_=bi_: e.activation(
                                hdst, psrc, AF.Identity, bias=bi_, scale=sc_), [r_pb, r_s1p, r_mod], [r_hT])
                tok += n

            def load_w(src_ap, ncol, gi_):
                wb, r_wb = wbf[gi_ % 2]
                r_hi = wbf_hi[gi_ % 2]
                wst, r_wst = wsts[wld[0] % 2]
                wld[0] += 1
                P.dma("sp", wst[:, 0:ncol], src_ap, writes=[r_wst])
                wflat = wb[:].rearrange("p k n -> p (k n)")
                hc = ncol // 2
                P.op("act", lambda e: e.copy(wflat[:, 0:hc], wst[:, 0:hc]), [r_wst], [r_wb])
                P.op("dve", lambda e: e.tensor_copy(wflat[:, hc:ncol], wst[:, hc:ncol]), [r_wst], [r_hi])
                return wb, r_wb, r_hi

            for gi_, (name, cols, epi, arg) in enumerate(g1):
                M = len(cols)
                wb, r_wb, r_whi = load_w(w1[gi_], KC * 128, gi_)
                osb, r_osb = ost[gi_ % 2]
                for si, (c0, c1) in enumerate(subs):
                    n = c1 - c0
                    pb, r_pb = C.bank()
                    for kc in range(KC):
                        P.op("pe", lambda e, pb=pb, wb=wb, kc=kc, M=M, c0=c0, c1=c1, n=n: e.matmul(
                            pb[0:M, 0:n], wb[:, kc, 0:M], hT[:, kc, c0:c1], start=(kc == 0), stop=(kc == KC - 1)),
                            [r_wb if kc < KC // 2 else r_whi, r_hT], [r_pb])
                    src = pb[0:M, 0:n]
                    dst = osb[0:M, c0:c1]
                    if epi == "copy":
                        P.op("act", lambda e, src=src, dst=dst: e.copy(dst, src), [r_pb], [r_osb])
                    elif epi == "scale":
                        P.op("act", lambda e, src=src, dst=dst, arg=arg: e.mul(dst, src, arg), [r_pb], [r_osb])
                    elif epi == "silu":
                        P.op("act", lambda e, src=src, dst=dst: e.activation(dst, src, AF.Silu), [r_pb], [r_osb])
                    elif epi == "sigmoid":
                        P.op("act", lambda e, src=src, dst=dst: e.activation(dst, src, AF.Sigmoid), [r_pb], [r_osb])
                    elif epi in ("cq", "ckv"):
                        tgt, r_tgt = (cqT, r_cqT) if epi == "cq" else (ckvT, r_ckvT)
                        acc, r_acc = (ssq_q, r_ssq_q) if epi == "cq" else (ssq_kv, r_ssq_kv)
                        sq, r_sq = sqt[si % 2]
                        P.op("act", lambda e, src=src, tgt=tgt, arg=arg, c0=c0, c1=c1: e.copy(tgt[:, arg, c0:c1], src),
                             [r_pb], [r_tgt])
                        P.op("act", lambda e, src=src, sq=sq, n=n: e.activation(sq[:, 0:n], src, AF.Square),
                             [r_pb], [r_sq])
                        pb2, r_pb2 = C.bank()
                        P.op("pe", lambda e, pb2=pb2, sq=sq, n=n: e.matmul(pb2[:, 0:n], ones_bf[:], sq[:, 0:n],
                                                                          start=True, stop=True), [r_sq, r_ones], [r_pb2])
                        if arg == 0:
                            P.op("dve", lambda e, acc=acc, pb2=pb2, c0=c0, c1=c1, n=n: e.tensor_copy(
                                acc[:, c0:c1], pb2[:, 0:n]), [r_pb2], [r_acc])
                        else:
                            P.op("dve", lambda e, acc=acc, pb2=pb2, c0=c0, c1=c1, n=n: e.tensor_tensor(
                                acc[:, c0:c1], acc[:, c0:c1], pb2[:, 0:n], ALU.add), [r_pb2, r_acc], [r_acc])
                    elif epi == "ropeA":
                        ct, r_ct = tcos[arg]
                        P.op("dve", lambda e, src=src, ct=ct, M=M, c0=c0, c1=c1: e.tensor_tensor(
                            tmpA[0:M, c0:c1], src, ct[0:M, c0:c1], ALU.mult), [r_pb, r_ct], [r_tmpA])
                    elif epi == "ropeB":
                        stt, r_stt = tsin[arg]
                        tb, r_tb = tmpB[si % 2]
                        P.op("dve", lambda e, src=src, stt=stt, tb=tb, M=M, c0=c0, c1=c1, n=n: e.tensor_tensor(
                            tb[0:M, 0:n], src, stt[0:M, c0:c1], ALU.mult), [r_pb, r_stt], [r_tb])
                        P.op("pool", lambda e, dst=dst, tb=tb, M=M, c0=c0, c1=c1, n=n: e.tensor_tensor(
                            dst, tb[0:M, 0:n], tmpA[0:M, c0:c1], ALU.add), [r_tb, r_tmpA], [r_osb])
                    elif epi == "gi":
                        P.op("act", lambda e, src=src, c0=c0, c1=c1: e.activation(
                            ostg[0:16, c0:c1], src, AF.Identity, bias=gbt[:, 0:1]), [r_pb, r_gbt], [r_ostg])
                    elif epi == "gf":
                        P.op("act", lambda e, src=src, c0=c0, c1=c1: e.activation(
                            gtmp[0:16, c0:c1], src, AF.Exp, bias=ngbt[:, 1:2], scale=-1.0), [r_pb, r_ngbt], [r_gtmp])
                        P.op("act", lambda e, c0=c0, c1=c1: e.activation(
                            gtmp[0:16, c0:c1], gtmp[0:16, c0:c1], AF.Ln, bias=onec[0:16, :]), [r_gtmp, r_onec], [r_gtmp])
                        P.op("dve", lambda e, c0=c0, c1=c1: e.tensor_scalar(
                            ostg[0:16, c0:c1], gtmp[0:16, c0:c1], -1.0, None, ALU.mult), [r_gtmp], [r_ostg])
                if epi in ("copy", "scale", "silu", "sigmoid", "ropeB"):
                    r0, m = rows[name]
                    C.out_dma("pool", O1[r0:r0 + m, h0:h1], osb[0:m, 0:nh], [r_osb])
                elif epi == "gi":
                    C.out_dma("pool", O1g[0:16, h0:h1], ostg[0:16, 0:nh], [r_ostg])
                elif epi == "gf":
                    C.out_dma("pool", O1g[16:32, h0:h1], ostg[0:16, 0:nh], [r_ostg])

            for acc, r_acc, dim in ((ssq_q, r_ssq_q, 768), (ssq_kv, r_ssq_kv, 256)):
                a_ = acc[:, 0:nh]
                P.op("act", lambda e, a_=a_, dim=dim: e.activation(
                    a_, a_, AF.Sqrt, bias=epsc[:], scale=1.0 / dim), [r_acc, r_eps], [r_acc])
                P.op("dve", lambda e, a_=a_: e.reciprocal(a_, a_), [r_acc], [r_acc])
            for tab in (tcos, tsin):
                o_ = tab["Cq"][0][:, 0:nh]
                i_ = tab["C"][0][:, 0:nh]
                q_ = ssq_q[:, 0:nh]
                P.op("dve", lambda e, o_=o_, i_=i_, q_=q_: e.tensor_tensor(o_, i_, q_, ALU.mult),
                     [tab["C"][1], r_ssq_q], [tab["Cq"][1]])

            iq = ikv = 0
            for gj, (name, which, cols, epi, arg) in enumerate(g2):
                M = len(cols)
                nk = 6 if which == "q" else 2
                src_ap = w2q[iq] if which == "q" else w2kv[ikv]
                gt = gqt if which == "q" else gkvt
                r_gt = r_gqt if which == "q" else r_gkvt
                if which == "q":
                    iq += 1
                else:
                    ikv += 1
                wb, r_wb = wbf[gj % 2]
                wst, r_wst = wsts[wld[0] % 2]
                wld[0] += 1
                P.dma("sp", wst[:, 0:nk * 128], src_ap, writes=[r_wst])
                for kc in range(nk):
                    P.op("dve", lambda e, wb=wb, kc=kc, gt=gt, wst=wst: e.tensor_scalar(
                        wb[:, kc, :], wst[:, kc * 128:(kc + 1) * 128], gt[:, kc:kc + 1], None, ALU.mult),
                        [r_wst, r_gt], [r_wb, wbf_hi[gj % 2]])
                rhsT, r_rhs = (cqT, r_cqT) if which == "q" else (ckvT, r_ckvT)
                rr, r_rr = (ssq_q, r_ssq_q) if which == "q" else (ssq_kv, r_ssq_kv)
                osb, r_osb = ost[gj % 2]
                for si, (c0, c1) in enumerate(subs):
                    n = c1 - c0
                    pb, r_pb = C.bank()
                    for kc in range(nk):
                        P.op("pe", lambda e, pb=pb, wb=wb, kc=kc, M=M, c0=c0, c1=c1, n=n, rhsT=rhsT, nk=nk: e.matmul(
                            pb[0:M, 0:n], wb[:, kc, 0:M], rhsT[:, kc, c0:c1], start=(kc == 0), stop=(kc == nk - 1)),
                            [r_wb, r_rhs], [r_pb])
                    src = pb[0:M, 0:n]
                    dst = osb[0:M, c0:c1]
                    if epi == "mulr":
                        P.op("dve", lambda e, src=src, dst=dst, rr=rr, M=M, c0=c0, c1=c1: e.tensor_tensor(
                            dst, src, rr[0:M, c0:c1], ALU.mult), [r_pb, r_rr], [r_osb])
                    elif epi == "ropeA":
                        ct, r_ct = tcos[arg]
                        P.op("dve", lambda e, src=src, ct=ct, M=M, c0=c0, c1=c1: e.tensor_tensor(
                            tmpA[0:M, c0:c1], src, ct[0:M, c0:c1], ALU.mult), [r_pb, r_ct], [r_tmpA])
                    elif epi == "ropeB":
                        stt, r_stt = tsin[arg]
                        tb, r_tb = tmpB[si % 2]
                        P.op("dve", lambda e, src=src, stt=stt, tb=tb, M=M, c0=c0, c1=c1, n=n: e.tensor_tensor(
                            tb[0:M, 0:n], src, stt[0:M, c0:c1], ALU.mult), [r_pb, r_stt], [r_tb])
                        P.op("pool", lambda e, dst=dst, tb=tb, M=M, c0=c0, c1=c1, n=n: e.tensor_tensor(
                            dst, tb[0:M, 0:n], tmpA[0:M, c0:c1], ALU.add), [r_tb, r_tmpA], [r_osb])
                if epi in ("mulr", "ropeB"):
                    r0, m = rows[name]
                    C.out_dma("pool", O1[r0:r0 + m, h0:h1], osb[0:m, 0:nh], [r_osb])
        C.finish()
    return nc


def host_p0_inputs(c, c_ctx, w_mod, b_mod):
    cv = np.stack([np.asarray(c).reshape(D), np.asarray(c_ctx).reshape(D)], axis=-1)
    cvec = np.ascontiguousarray(cv.reshape(KC, 128, 2).transpose(1, 0, 2)).reshape(128, KC * 2)
    maps = []
    for core in range(NCORE):
        wm = np.empty((NJ0, 128, KC, 128), np.float32)
        bm = np.empty((128, NJ0), np.float32)
        for l in range(2):
            w3 = w_mod[l].reshape(KC, 128, 96, 128)
            for jj in range(12):
                j = core * 12 + jj
                wm[l * 12 + jj] = w3[:, :, j, :].transpose(1, 0, 2)
                bm[:, l * 12 + jj] = b_mod[l, j * 128:(j + 1) * 128]
        maps.append({"wm": wm.reshape(NJ0, 128, KC * 128), "cvec": cvec, "bm": bm})
    return maps


def host_p0_gather(results):
    out = np.empty((2, 128, 96, 2), np.float32)
    for core in range(NCORE):
        o = results[core]["modT"].reshape(128, NJ0, 2)
        for l in range(2):
            out[l, :, core * 12:(core + 1) * 12, :] = o[:, l * 12:(l + 1) * 12, :]
    return out.reshape(2, 128, 192)


def host_w1(w_in_l):
    g1 = p1_groups()
    out = np.zeros((len(g1), 128, KC, 128), np.float32)
    w3 = w_in_l.reshape(KC, 128, -1)
    for i, (name, cols, epi, arg) in enumerate(g1):
        out[i, :, :, :len(cols)] = w3[:, :, cols].transpose(1, 0, 2)
    return out.reshape(len(g1), 128, KC * 128)


def host_w2(w_uq_l, w_ukv_l):
    g2 = p1_groups2()
    gq = [x for x in g2 if x[1] == "q"]
    gkv = [x for x in g2 if x[1] == "kv"]
    oq = np.zeros((len(gq), 128, 6, 128), np.float32)
    okv = np.zeros((len(gkv), 128, 2, 128), np.float32)
    wq3 = w_uq_l.reshape(6, 128, -1)
    wkv3 = w_ukv_l.reshape(2, 128, -1)
    for i, (name, which, cols, epi, arg) in enumerate(gq):
        oq[i, :, :, :len(cols)] = wq3[:, :, cols].transpose(1, 0, 2)
    for i, (name, which, cols, epi, arg) in enumerate(gkv):
        okv[i, :, :, :len(cols)] = wkv3[:, :, cols].transpose(1, 0, 2)
    return oq.reshape(len(gq), 128, 6 * 128), okv.reshape(len(gkv), 128, 2 * 128)


def pk(vec, nchunk):
    return np.ascontiguousarray(np.asarray(vec, np.float32).reshape(nchunk, 128).T)


def rope_tables(core):
    t = core * TL + np.arange(TL)
    row = (t // 64).astype(np.float64)
    col = (t % 64).astype(np.float64)

    def tab(dim):
        h = dim // 2
        half = h // 2
        cos = np.ones((dim, NT), np.float64)
        sin = np.zeros((dim, NT), np.float64)
        for i in range(dim):
            pos = row if i < h else col
            j = i % half
            f = 10000.0 ** (-j / half)
            sgn = -1.0 if (i % h) < half else 1.0
            cos[i, :TL] = np.cos(pos * f)
            sin[i, :TL] = sgn * np.sin(pos * f)
        return cos.astype(np.float32), sin.astype(np.float32)

    cB, sB = tab(128)
    c64, s64 = tab(64)
    cC = np.concatenate([c64, c64], axis=0)
    sC = np.concatenate([s64, s64], axis=0)
    return cB, sB, cC, sC


def host_p1_inputs(x_full, xc_full, modT_l, l, inp):
    w1 = host_w1(inp["w_in"][l])
    w2q, w2kv = host_w2(inp["mla_w_uq"][l], inp["mla_w_ukv"][l])
    gb = inp["ml_gate_b"][l]
    gbias = np.stack([np.concatenate([gb[0], gb[2]]), np.concatenate([gb[1], gb[3]])], axis=1).astype(np.float32)
    common = {
        "modT": modT_l, "gpre": pk(inp["g_pre"][l], KC), "w1": w1, "w2q": w2q, "w2kv": w2kv,
        "gq": pk(inp["mla_q_norm"][l], 6), "gkv": pk(inp["mla_kv_norm"][l], 2), "gbias": gbias,
        "ident": np.eye(128, dtype=np.float32),
    }
    maps = []
    for core in range(NCORE):
        cB, sB, cC, sC = rope_tables(core)
        m = dict(common)
        m["x"] = np.concatenate([x_full[core * TL:(core + 1) * TL], xc_full[core * CL:(core + 1) * CL]], axis=0)
        m.update({"cosB": cB, "sinB": sB, "cosC": cC, "sinC": sC})
        maps.append(m)
    return maps


NG3 = 16
GW3 = D // NG3


def build_p3():
    nc = bass.Bass("TRN2", target_bir_lowering=False)
    with ExitStack() as st:
        C = Ctx(nc, st)
        P = C.P
        yT = C.dram_in("yT", [9, 128, KC * 256], BF16)
        wo = C.dram_in("wo", [NG3, 128, KC * GW3])
        x = C.dram_in("x", [NT, D])
        grow = C.dram_in("grow", [2, D])
        gpost = C.dram_in("gpost", [1, D])
        xo = C.dram_out("xo", [NT, D])
        yts = [C.sb(f"yt{i}", [128, KC, 256], BF16) for i in range(1)] * 2
        wsts = [C.sb(f"wst{i}", [128, KC * GW3]) for i in range(2)]
        wbf = [C.sb(f"wbf{i}", [128, KC, GW3], BF16) for i in range(2)]
        wbf_hi = [P.res(f"wbfhi{i}") for i in range(2)]
        ots = [C.sb(f"ot{i}", [128, D]) for i in range(2)]
        xt, r_xt = C.sb("xt", [128, D])
        ggs = [C.sb(f"gg{i}", [128, D]) for i in range(2)]
        junk, r_junk = C.sb("junk", [128, D // 2], BF16)
        ssq, r_ssq = C.sb("ssq", [128, 2])
        epsc, r_eps = C.sb("epsc", [128, 1])
        C.psum_banks(8)
        P.op("pool", lambda e: e.memset(epsc[:], EPS), [], [r_eps])
        P.dma("sp", xt[:], gpost[0:1, :].to_broadcast([128, D]), writes=[r_xt])
        for v in range(2):
            gg, r_gg = ggs[v]
            P.dma("sp", gg[:], grow[v:v + 1, :].to_broadcast([128, D]), writes=[r_gg])
            P.op("dve", lambda e, gg=gg: e.tensor_tensor(gg[:], gg[:], xt[:], ALU.mult), [r_gg, r_xt], [r_gg])
        sts = [(s, s + 256) for s in range(0, TL, 256)] + [(TL, NT)]
        wi = 0
        for sti, (s0, s1) in enumerate(sts):
            v = 0 if s0 < TL else 1
            ns = s1 - s0
            yt, r_yt = yts[sti % 2]
            P.dma("pool", yt[:].rearrange("p k n -> p (k n)"), yT[sti], writes=[r_yt])
            tiles = [(a, min(a + 128, ns)) for a in range(0, ns, 128)]
            for g in range(NG3):
                wb, r_wb = wbf[wi % 2]
                r_whi = wbf_hi[wi % 2]
                wst, r_wst = wsts[wi % 2]
                wi += 1
                P.dma("sp", wst[:], wo[g], writes=[r_wst])
                wflat = wb[:].rearrange("p k n -> p (k n)")
                hc = KC * GW3 // 2
                P.op("act", lambda e, wflat=wflat, wst=wst: e.copy(wflat[:, 0:hc], wst[:, 0:hc]), [r_wst], [r_wb])
                P.op("dve", lambda e, wflat=wflat, wst=wst: e.tensor_copy(wflat[:, hc:], wst[:, hc:]), [r_wst], [r_whi])
                for ti, (a, b) in enumerate(tiles):
                    n = b - a
                    pb, r_pb = C.bank()
                    ot, r_ot = ots[ti]
                    for kc in range(KC):
                        P.op("pe", lambda e, pb=pb, yt=yt, wb=wb, kc=kc, a=a, b=b, n=n: e.matmul(
                            pb[0:n, 0:GW3], yt[:, kc, a:b], wb[:, kc, :], start=(kc == 0), stop=(kc == KC - 1)),
                            [r_yt, r_wb if kc < KC // 2 else r_whi], [r_pb])
                    P.op("act", lambda e, pb=pb, ot=ot, n=n, g=g: e.copy(ot[0:n, g * GW3:(g + 1) * GW3], pb[0:n, 0:GW3]),
                         [r_pb], [r_ot])
            for ti, (a, b) in enumerate(tiles):
                n = b - a
                ot, r_ot = ots[ti]
                gg, r_gg = ggs[v]
                P.dma("sp", xt[0:n, :], x[s0 + a:s0 + b, :], writes=[r_xt])
                P.op("pool", lambda e: e.memset(ssq[:], 0.0), [], [r_ssq])
                for hh in range(2):
                    P.op("act", lambda e, ot=ot, n=n, hh=hh: e.activation(
                        junk[0:n, :], ot[0:n, hh * (D // 2):(hh + 1) * (D // 2)], AF.Square, accum_out=ssq[0:n, hh:hh + 1]),
                        [r_ot], [r_junk, r_ssq])
                P.op("dve", lambda e, n=n: e.tensor_tensor(ssq[0:n, 0:1], ssq[0:n, 0:1], ssq[0:n, 1:2], ALU.add), [r_ssq], [r_ssq])
                P.op("act", lambda e, n=n: e.activation(ssq[0:n, 0:1], ssq[0:n, 0:1], AF.Sqrt, bias=epsc[0:n, :], scale=1.0 / D),
                     [r_ssq, r_eps], [r_ssq])
                P.op("dve", lambda e, n=n: e.reciprocal(ssq[0:n, 0:1], ssq[0:n, 0:1]), [r_ssq], [r_ssq])
                P.op("dve", lambda e, ot=ot, gg=gg, n=n: e.scalar_tensor_tensor(
                    ot[0:n, :], ot[0:n, :], ssq[0:n, 0:1], gg[0:n, :], ALU.mult, ALU.mult), [r_ot, r_ssq, r_gg], [r_ot])
                P.op("dve", lambda e, ot=ot, n=n: e.tensor_tensor(ot[0:n, :], ot[0:n, :], xt[0:n, :], ALU.add),
                     [r_ot, r_xt], [r_ot])
                C.out_dma("pool", xo[s0 + a:s0 + b, :], ot[0:n, :], [r_ot])
        C.finish()
    return nc


def host_p3_common(modT_l, l, inp):
    wo = inp["w_out"][l].reshape(KC, 128, NG3, GW3).transpose(2, 1, 0, 3)
    wo = np.ascontiguousarray(wo).reshape(NG3, 128, KC * GW3)
    m3 = modT_l.reshape(128, 96, 2)
    grow = np.ascontiguousarray(m3[:, 64:96, :].transpose(2, 1, 0)).reshape(2, D)
    return {"wo": wo, "grow": grow, "gpost": np.asarray(inp["g_post"][l], np.float32).reshape(1, D)}


def host_yT_tiles(y_loc):
    out = np.zeros((9, 128, KC, 256), y_loc.dtype)
    for i in range(9):
        s0 = i * 256
        s1 = min(s0 + 256, NT)
        out[i, :, :, :s1 - s0] = y_loc[s0:s1].reshape(s1 - s0, KC, 128).transpose(2, 1, 0)
    return out.reshape(9, 128, KC * 256)


BIGW = 16896
LCH = 512


def build_p2():
    nc = bass.Bass("TRN2", target_bir_lowering=False)
    with ExitStack() as st:
        C = Ctx(nc, st)
        P = C.P
        FA = C.dram_in("FA", [12, 128, TT], BF16)
        FB = C.dram_in("FB", [2, 64, TT], BF16)
        TM = C.dram_in("TM", [3, 128, NB * 128], BF16)
        TMV = C.dram_in("TMV", [128, NB * 129], BF16)
        GT = C.dram_in("GT", [128, 4 * NB])
        LP = C.dram_in("LP", [128, 16])
        LW = C.dram_in("LW", [4, 128, 128])
        HN = C.dram_in("HN", [128, 128])
        MK = C.dram_in("MK", [2, 128, 128])
        identd = C.dram_in("ident", [128, 128])
        Y = C.dram_out("Y", [4, 128, TT], BF16)

        bigs = [C.sb(f"big{i}", [128, BIGW], BF16) for i in range(5)]
        lp, r_lp = C.sb("lp", [128, 16])
        lwbf, r_lwbf = C.sb("lwbf", [128, 4, 128], BF16)
        hn, r_hn = C.sb("hnrow", [128, 128])
        mk32, r_mk32 = C.sb("mk32", [128, 2, 128])
        mkbf, r_mkbf = C.sb("mkbf", [128, 2, 128], BF16)
        ident, r_ident = C.sb("identt", [128, 128])
        ones_bf, r_ones = C.sb("ones_bf", [128, 128], BF16)
        ones32, r_ones32 = C.sb("ones32", [128, 128])
        onec, r_onec = C.sb("onec", [128, 1])
        epsc, r_eps = C.sb("epsc", [128, 1])
        cc, r_cc = C.sb("cc", [128, 8])
        tf = [C.sb(f"tf{i}", [128, LCH]) for i in range(6)]
        tb = [C.sb(f"tbf{i}", [128, 1024], BF16) for i in range(3)]
        S = [C.ps(f"S{i}", [128, 1024]) for i in range(3)]
        PO = [C.ps("PO0", [128, 512]), (S[2][0][:, 0:512], S[2][1])]
        PD = [C.ps("PD0", [128, 512]), (S[2][0][:, 512:1024], S[2][1])]

        P.dma("sp", lp[:], LP, writes=[r_lp])
        P.dma("sp", hn[:], HN, writes=[r_hn])
        P.dma("sp", mk32[:], MK.rearrange("g k j -> k g j"), writes=[r_mk32])
        P.dma("sp", ident[:], identd, writes=[r_ident])
        P.op("pool", lambda e: e.memset(ones_bf[:], 1.0), [], [r_ones])
        P.op("pool", lambda e: e.memset(ones32[:], 1.0), [], [r_ones32])
        P.op("pool", lambda e: e.memset(onec[:], 1.0), [], [r_onec])
        P.op("pool", lambda e: e.memset(epsc[:], EPS), [], [r_eps])
        lwst = tf[0][0][:].rearrange("p (g j) -> p g j", j=128)
        P.dma("sp", lwst, LW.rearrange("g k j -> k g j"), writes=[tf[0][1]])
        P.op("dve", lambda e: e.tensor_copy(lwbf[:], lwst), [tf[0][1]], [r_lwbf])
        P.op("dve", lambda e: e.tensor_copy(mkbf[:], mk32[:]), [r_mk32], [r_mkbf])
        P.op("act", lambda e: e.activation(cc[:, 0:2], lp[:, 5:7], AF.Exp, scale=-1.0), [r_lp], [r_cc])
        P.op("act", lambda e: e.activation(cc[:, 0:2], cc[:, 0:2], AF.Ln, bias=onec[:]), [r_cc, r_onec], [r_cc])
        P.op("dve", lambda e: e.tensor_scalar(cc[:, 2:4], cc[:, 0:2], -16.0, None, ALU.mult), [r_cc], [r_cc])
        P.op("dve", lambda e: e.tensor_scalar(cc[:, 0:2], cc[:, 0:2], -8.0, None, ALU.mult), [r_cc], [r_cc])
        P.op("act", lambda e: e.activation(cc[:, 4:5], lp[:, 11:12], AF.Exp), [r_lp], [r_cc])

        def load_big(i, src, ncol, part=128, eng="sp"):
            t, r = bigs[i]
            P.dma(eng, t[0:part, 0:ncol], src, writes=[r])
            return t, r

        u_bf, r_u = load_big(0, FA[0], TT)
        hfl = bigs[1][0][:].bitcast(F32)
        hfh = bigs[2][0][:].bitcast(F32)
        r_hfl, r_hfh = bigs[1][1], bigs[2][1]

        def hf_ap(s, e):
            if e <= 8448:
                return hfl[:, s:e], r_hfl
            return hfh[:, s - 8448:e - 8448], r_hfh

        chunks_ctx = [(0, TC, 0, TC)]
        chunks_lat = [(TC + i * LCH, TC + (i + 1) * LCH, TC, TT) for i in range(T // LCH)]
        (v32, r_v32), (rb, r_rb), (ib, r_ib), (t2, r_t2), (hb0, r_hb0), (hb1, r_hb1) = tf
        (vbf, r_vbf), (sgt, r_sgt), (yst, r_yst) = tb[0], tb[1], tb[2]
        hbs = [(hb0, r_hb0), (hb1, r_hb1)]

        def lru_chunk(s, e, S_, E_, d, prev_ap, prev_res, hbi):
            n = e - s
            P.op("dve", lambda e_: e_.tensor_scalar(v32[:, 0:n], u_bf[:, s:e], lp[:, 2:3], lp[:, 4:5], ALU.mult, ALU.add),
                 [r_u, r_lp], [r_v32])
            for j in (0, 1, 3):
                o = j - 2
                lo, hi = max(s, S_ - o), min(e, E_ - o)
                if lo < hi:
                    P.op("dve", lambda e_, lo=lo, hi=hi, o=o, j=j: e_.scalar_tensor_tensor(
                        v32[:, lo - s:hi - s], u_bf[:, lo + o:hi + o], lp[:, j:j + 1], v32[:, lo - s:hi - s],
                        ALU.mult, ALU.add), [r_u, r_lp, r_v32], [r_v32])
            P.op("pool", lambda e_: e_.tensor_copy(vbf[:, 0:n], v32[:, 0:n]), [r_v32], [r_vbf])
            pa, r_pa = S[0]
            px, r_px = S[1]
            for c0 in range(0, n, 512):
                c1_ = min(c0 + 512, n)
                P.op("pe", lambda e_, c0=c0, c1_=c1_: e_.matmul(pa[:, c0:c1_], lwbf[:, 2 * d, :], vbf[:, c0:c1_],
                                                               start=True, stop=True), [r_lwbf, r_vbf], [r_pa])
                P.op("pe", lambda e_, c0=c0, c1_=c1_: e_.matmul(px[:, c0:c1_], lwbf[:, 2 * d + 1, :], vbf[:, c0:c1_],
                                                               start=True, stop=True), [r_lwbf, r_vbf], [r_px])
            P.op("act", lambda e_: e_.activation(ib[:, 0:n], px[:, 0:n], AF.Sigmoid, bias=lp[:, 9 + d:10 + d]), [r_px, r_lp], [r_ib])
            P.op("act", lambda e_: e_.activation(rb[:, 0:n], pa[:, 0:n], AF.Sigmoid, bias=lp[:, 7 + d:8 + d]), [r_pa, r_lp], [r_rb])
            P.op("act", lambda e_: e_.activation(t2[:, 0:n], rb[:, 0:n], AF.Exp, scale=cc[:, 2 + d:3 + d]), [r_rb, r_cc], [r_t2])
            P.op("act", lambda e_: e_.activation(rb[:, 0:n], rb[:, 0:n], AF.Exp, scale=cc[:, d:d + 1]), [r_rb, r_cc], [r_rb])
            P.op("act", lambda e_: e_.activation(t2[:, 0:n], t2[:, 0:n], AF.Sqrt, bias=onec[:], scale=-1.0), [r_t2, r_onec], [r_t2])
            P.op("dve", lambda e_: e_.tensor_tensor(ib[:, 0:n], ib[:, 0:n], v32[:, 0:n], ALU.mult), [r_ib, r_v32], [r_ib])
            P.op("dve", lambda e_: e_.tensor_tensor(ib[:, 0:n], ib[:, 0:n], t2[:, 0:n], ALU.mult), [r_ib, r_t2], [r_ib])
            init = 0.0 if prev_ap is None else prev_ap
            rd = [r_rb, r_ib] + ([prev_res] if prev_res is not None else [])
            if d == 0:
                out, r_out = hf_ap(s, e)
                P.op("dve", lambda e_: e_.tensor_tensor_scan(out, rb[:, 0:n], ib[:, 0:n], init, ALU.mult, ALU.add), rd, [r_out])
                return out[:, n - 1:n], r_out
            hb, r_hb = hbs[hbi]
            P.op("dve", lambda e_: e_.tensor_tensor_scan(hb[:, 0:n][:, ::-1], rb[:, 0:n][:, ::-1], ib[:, 0:n][:, ::-1],
                                                         init, ALU.mult, ALU.add), rd, [r_hb])
            hfa, r_hfa = hf_ap(s, e)
            P.dma("sp", sgt[:, 0:n], FA[1][:, s:e], writes=[r_sgt])
            P.op("pool", lambda e_: e_.tensor_tensor(v32[:, 0:n], hb[:, 0:n], hfa, ALU.add), [r_hb, r_hfa], [r_v32])
            P.op("pool", lambda e_: e_.tensor_tensor(yst[:, 0:n], v32[:, 0:n], sgt[:, 0:n], ALU.mult), [r_v32, r_sgt], [r_yst])
            C.out_dma("sp", Y[0][:, s:e], yst[:, 0:n], [r_yst])
            return hb[:, 0:1], r_hb

        pa_, pr_ = None, None
        for (s, e, S_, E_) in chunks_ctx + chunks_lat:
            pa_, pr_ = lru_chunk(s, e, S_, E_, 0, pa_, pr_, 0)
        pa_, pr_ = None, None
        hbi = 0
        for (s, e, S_, E_) in chunks_ctx + chunks_lat[::-1]:
            pa_, pr_ = lru_chunk(s, e, S_, E_, 1, pa_, pr_, hbi)
            hbi ^= 1

        qT, r_q = load_big(0, FA[2], TT)
        kT, r_k = load_big(1, FA[3], TT)
        vtm, r_v = load_big(2, TM[0], NB * 128)
        sgB, r_sgB = load_big(3, FA[4], TT)
        vtm3 = vtm[:, 0:NB * 128].rearrange("p (b d) -> p b d", d=128)
        pTs = [tb[0], tb[1]]
        ystB, r_ystB = tb[2]
        (dn, r_dn), (ob, r_ob) = tf[0], tf[1]
        sc_swa = 128.0 ** -0.5
        qblocks = [(0, [(0, None), (1, None)]), (1, [(0, None), (1, None)])]
        for nq in range(128):
            keys = []
            if nq > 0:
                keys.append((2 + nq - 1, 1))
            keys.append((2 + nq, None))
            if nq < 127:
                keys.append((2 + nq + 1, 0))
            keys += [(0, None), (1, None)]
            qblocks.append((2 + nq, keys))
        for bi, (qb, keys) in enumerate(qblocks):
            Sx, r_S = S[bi % 2]
            po, r_po = PO[bi % 2]
            pd, r_pd = PD[bi % 2]
            pT, r_pT = pTs[bi % 2]
            nk = len(keys)
            qs = slice(qb * 128, (qb + 1) * 128)
            for sl, (kb, mk) in enumerate(keys):
                P.op("pe", lambda e_, Sx=Sx, sl=sl, kb=kb, qs=qs: e_.matmul(
                    Sx[:, sl * 128:(sl + 1) * 128], kT[:, kb * 128:(kb + 1) * 128], qT[:, qs], start=True, stop=True),
                    [r_k, r_q], [r_S])
            P.op("act", lambda e_, Sx=Sx, pT=pT, nk=nk: e_.activation(pT[:, 0:nk * 128], Sx[:, 0:nk * 128], AF.Exp, scale=sc_swa),
                 [r_S], [r_pT])
            for sl, (kb, mk) in enumerate(keys):
                if mk is not None:
                    P.op("dve", lambda e_, pT=pT, sl=sl, mk=mk: e_.tensor_tensor(
                        pT[:, sl * 128:(sl + 1) * 128], pT[:, sl * 128:(sl + 1) * 128], mkbf[:, mk, :], ALU.mult),
                        [r_pT, r_mkbf], [r_pT])
            for sl, (kb, mk) in enumerate(keys):
                P.op("pe", lambda e_, po=po, pT=pT, sl=sl, kb=kb, nk=nk: e_.matmul(
                    po[:, 0:128], vtm3[:, kb, :], pT[:, sl * 128:(sl + 1) * 128], start=(sl == 0), stop=(sl == nk - 1)),
                    [r_v, r_pT], [r_po])
            for sl, (kb, mk) in enumerate(keys):
                P.op("pe", lambda e_, pd=pd, pT=pT, sl=sl, nk=nk: e_.matmul(
                    pd[:, 0:128], ones_bf[:], pT[:, sl * 128:(sl + 1) * 128], start=(sl == 0), stop=(sl == nk - 1)),
                    [r_ones, r_pT], [r_pd])
            P.op("dve", lambda e_, pd=pd: e_.tensor_scalar(dn[:, 0:128], pd[:, 0:128], cc[:, 4:5], None, ALU.add), [r_pd, r_cc], [r_dn])
            P.op("dve", lambda e_: e_.reciprocal(dn[:, 0:128], dn[:, 0:128]), [r_dn], [r_dn])
            P.op("dve", lambda e_, po=po: e_.tensor_tensor(ob[:, 0:128], po[:, 0:128], dn[:, 0:128], ALU.mult), [r_po, r_dn], [r_ob])
            j8 = bi % 8
            P.op("dve", lambda e_, qs=qs, j8=j8: e_.tensor_tensor(ystB[:, j8 * 128:(j8 + 1) * 128], ob[:, 0:128], sgB[:, qs], ALU.mult),
                 [r_ob, r_sgB], [r_ystB])
            if j8 == 7 or bi == len(qblocks) - 1:
                b0 = bi - j8
                q0 = qblocks[b0][0]
                C.out_dma("sp", Y[1][:, q0 * 128:(qb + 1) * 128], ystB[:, 0:(j8 + 1) * 128], [r_ystB])

        knT, r_kn = load_big(0, FA[6], TT)
        krT, r_kr = load_big(1, FB[1], TT, part=64)
        P.op("pool", lambda e_: e_.memset(krT[64:128, 0:TT], 0.0), [], [r_kr])
        vtmC, r_vC = load_big(2, TM[1], NB * 128)
        vtmC3 = vtmC[:, 0:NB * 128].rearrange("p (b d) -> p b d", d=128)
        sc_mla = 192.0 ** -0.5
        qns = [C.sb(f"qn{i}", [128, 512], BF16) for i in range(1)] * 2
        qrs = [C.sb(f"qr{i}", [128, 512], BF16) for i in range(1)] * 2
        P.op("pool", lambda e_: e_.memset(qrs[0][0][64:128, :], 0.0), [], [qrs[0][1]])
        sgs = [C.sb(f"sgc{i}", [128, 512], BF16) for i in range(1)] * 2
        ystC = [C.sb(f"ystc{i}", [128, 512], BF16) for i in range(1)] * 2
        pTC = [tb[0], tb[1], tb[2]]
        (rec, r_rec), (oc, r_oc) = tf[0], tf[1]
        (accA, r_accA), (accB, r_accB) = tf[2], tf[3]
        dhi, r_dhi = C.sb("dhi", [128, 512], BF16)
        dlo, r_dlo = C.sb("dlo", [128, 512], BF16)
        qtiles = [(0, TC, [(0, 1)])] + [(TC + i * 512, TC + (i + 1) * 512, [(2 * j, 2 * j + 1) for j in range(NB // 2)])
                                        for i in range(32)]
        pcount = 0
        for qi, (q0, q1, pairs) in enumerate(qtiles):
            n = q1 - q0
            qn, r_qn = qns[qi % 2]
            qr, r_qr = qrs[qi % 2]
            sg, r_sg = sgs[qi % 2]
            yc, r_yc = ystC[qi % 2]
            po, r_po = PO[0]
            pd, r_pd = PD[0]
            P.dma("sp", qn[:, 0:n], FA[5][:, q0:q1], writes=[r_qn])
            P.dma("sp", qr[0:64, 0:n], FB[0][:, q0:q1], writes=[r_qr])
            P.dma("sp", sg[:, 0:n], FA[7][:, q0:q1], writes=[r_sg])
            npair = len(pairs)

            def qk(i):
                Sx, r_S = S[(pcount + i) % 3]
                for hh in range(2):
                    kb = pairs[i][hh]
                    P.op("pe", lambda e_, Sx=Sx, hh=hh, kb=kb, n=n, qn=qn, qr=qr: e_.matmul(
                        Sx[:, hh * 512:hh * 512 + n], knT[:, kb * 128:(kb + 1) * 128], qn[:, 0:n], start=True, stop=False),
                        [r_kn, r_qn], [r_S])
                    P.op("pe", lambda e_, Sx=Sx, hh=hh, kb=kb, n=n, qn=qn, qr=qr: e_.matmul(
                        Sx[:, hh * 512:hh * 512 + n], krT[:, kb * 128:(kb + 1) * 128], qr[:, 0:n], start=False, stop=True),
                        [r_kr, r_qr], [r_S])
                pT, r_pT = pTC[(pcount + i) % 3]
                if n == 512:
                    P.op("act", lambda e_, Sx=Sx, pT=pT: e_.activation(pT[:, :], Sx[:, :], AF.Exp, scale=sc_mla), [r_S], [r_pT])
                else:
                    for hh in range(2):
                        P.op("act", lambda e_, Sx=Sx, pT=pT, hh=hh, n=n: e_.activation(
                            pT[:, hh * 512:hh * 512 + n], Sx[:, hh * 512:hh * 512 + n], AF.Exp, scale=sc_mla), [r_S], [r_pT])

            def pv(i):
                pT, r_pT = pTC[(pcount + i) % 3]
                for hh in range(2):
                    kb = pairs[i][hh]
                    first = (i == 0 and hh == 0)
                    last = (i == npair - 1 and hh == 1)
                    P.op("pe", lambda e_, pT=pT, hh=hh, kb=kb, first=first, last=last, n=n, po=po: e_.matmul(
                        po[:, 0:n], vtmC3[:, kb, :], pT[:, hh * 512:hh * 512 + n], start=first, stop=last),
                        [r_vC, r_pT], [r_po])
                    P.op("pe", lambda e_, pT=pT, hh=hh, first=first, last=last, n=n, pd=pd: e_.matmul(
                        pd[:, 0:n], ones_bf[:], pT[:, hh * 512:hh * 512 + n], start=first, stop=last),
                        [r_ones, r_pT], [r_pd])

            qk(0)
            if npair > 1:
                qk(1)
            for i in range(npair):
                if i + 2 < npair:
                    qk(i + 2)
                pv(i)
            pcount += npair
            P.op("dve", lambda e_, pd=pd, n=n: e_.reciprocal(rec[:, 0:n], pd[:, 0:n]), [r_pd], [r_rec])
            P.op("dve", lambda e_, po=po, n=n: e_.tensor_tensor(oc[:, 0:n], po[:, 0:n], rec[:, 0:n], ALU.mult), [r_po, r_rec], [r_oc])
            P.op("pool", lambda e_, yc=yc, sg=sg, n=n: e_.tensor_tensor(yc[:, 0:n], oc[:, 0:n], sg[:, 0:n], ALU.mult), [r_oc, r_sg], [r_yc])
            C.out_dma("sp", Y[2][:, q0:q1], yc[:, 0:n], [r_yc])

        qT, r_q = load_big(0, FA[8], TT)
        kT, r_k = load_big(1, FA[9], TT)
        ktm, r_ktm = load_big(2, TM[2], NB * 128)
        vex, r_vex = load_big(3, TMV, NB * 129)
        hfs, r_hfs = bigs[4]
        ktm3 = ktm[:, 0:NB * 128].rearrange("p (b d) -> p b d", d=128)
        vex3 = vex[:, 0:NB * 129].rearrange("p (b d) -> p b d", d=129)
        hfs3 = hfs[:, 0:NB * 128].rearrange("p (b d) -> p b d", d=128)
        gt, r_gt = C.sb("gt", [128, 4 * NB])
        P.dma("sp", gt[:], GT, writes=[r_gt])
        gt3 = gt[:].rearrange("p (g n) -> p g n", n=NB)
        sc_t = {}
        for nm in ("b", "G", "u", "eb", "eg", "ueg"):
            for d in range(2):
                sc_t[(nm, d)] = C.sb(f"sc_{nm}{d}", [128, NB])
        for d in range(2):
            ig = gt3[:, d, :]
            lf = gt3[:, 2 + d, :]
            pb_, r_pb_ = PO[0]
            pg_, r_pg_ = PO[1]
            (b_, r_b), (G_, r_G), (u_, r_uu), (eb_, r_eb), (eg_, r_eg), (ueg_, r_ueg) = [sc_t[(nm, d)] for nm in ("b", "G", "u", "eb", "eg", "ueg")]
            P.op("pe", lambda e_, d=d, lf=lf, pb_=pb_: e_.matmul(pb_[:, 0:NB], mk32[:, d, :], lf, start=True, stop=True), [r_mk32, r_gt], [r_pb_])
            P.op("pe", lambda e_, lf=lf, pg_=pg_: e_.matmul(pg_[:, 0:NB], ones32[:], lf, start=True, stop=True), [r_ones32, r_gt], [r_pg_])
            P.op("dve", lambda e_, b_=b_, pb_=pb_: e_.tensor_copy(b_[:], pb_[:, 0:NB]), [r_pb_], [r_b])
            P.op("dve", lambda e_, G_=G_, pg_=pg_: e_.tensor_copy(G_[:], pg_[:, 0:NB]), [r_pg_], [r_G])
            P.op("dve", lambda e_, u_=u_, ig=ig, b_=b_: e_.tensor_tensor(u_[:], ig, b_[:], ALU.subtract), [r_gt, r_b], [r_uu])
            P.op("dve", lambda e_, ueg_=ueg_, u_=u_, G_=G_: e_.tensor_tensor(ueg_[:], u_[:], G_[:], ALU.add), [r_uu, r_G], [r_ueg])
            P.op("act", lambda e_, u_=u_: e_.activation(u_[:], u_[:], AF.Exp), [r_uu], [r_uu])
            P.op("act", lambda e_, ueg_=ueg_: e_.activation(ueg_[:], ueg_[:], AF.Exp), [r_ueg], [r_ueg])
            P.op("act", lambda e_, eb_=eb_, b_=b_: e_.activation(eb_[:], b_[:], AF.Exp), [r_b], [r_eb])
            P.op("act", lambda e_, eg_=eg_, G_=G_: e_.activation(eg_[:], G_[:], AF.Exp), [r_G], [r_eg])

        C32, r_C32 = C.sb("C32", [128, 129])
        Cbfs = [C.sb(f"Cbf{i}", [128, 129], BF16) for i in range(2)]
        pTd = [C.sb(f"pTd{i}", [128, 128], BF16) for i in range(2)]
        vps = [C.sb(f"vp{i}", [128, 129], BF16) for i in range(2)]
        vpps = [C.sb(f"vpp{i}", [128, 129], BF16) for i in range(2)]
        dds = [C.sb(f"dd{i}", [128, 2]) for i in range(2)]
        hss = [C.sb(f"hs{i}", [128, 128]) for i in range(2)]
        hns = [C.sb(f"hn{i}", [128, 128]) for i in range(2)]
        ssd = [C.sb(f"ssd{i}", [128, 1]) for i in range(2)]
        junkd, r_junkd = C.sb("junkd", [128, 128], BF16)
        (yt1, r_yt1) = tf[2]
        so_t, r_so = tb[0]
        sg_t, r_sgd = tb[1]
        ystD, r_ystD = tb[2]

        def mlstm_dir(d, order, groups):
            (b_, r_b), (G_, r_G), (u_, r_uu), (eb_, r_eb), (eg_, r_eg), (ueg_, r_ueg) = [sc_t[(nm, d)] for nm in ("b", "G", "u", "eb", "eg", "ueg")]
            P.op("pool", lambda e_: e_.memset(C32[:], 0.0), [], [r_C32])
            P.op("pool", lambda e_: e_.memset(Cbfs[0][0][:], 0.0), [], [Cbfs[0][1]])
            grp_of = {}
            for (lo, hi) in groups:
                for c in range(lo, hi):
                    grp_of[c] = (lo, hi)
            for ci, n in enumerate(order):
                par = ci % 2
                cs = slice(n * 128, (n + 1) * 128)
                Sx, r_S = S[par]
                pk_, r_pk = PO[par]
                ptt, r_ptt = PD[par]
                Cb, r_Cb = Cbfs[par]
                Cn, r_Cn = Cbfs[1 - par]
                pT, r_pT = pTd[par]
                vp, r_vp = vps[par]
                vpp, r_vpp = vpps[par]
                dd, r_dd = dds[par]
                if d == 1 and n in grp_of and (ci == 0 or grp_of[order[ci - 1]] != grp_of[n]):
                    lo, hi = grp_of[n]
                    P.dma("sp", so_t[:, 0:(hi - lo) * 128], FA[10][:, lo * 128:hi * 128], writes=[r_so])
                    P.dma("sp", sg_t[:, 0:(hi - lo) * 128], FA[11][:, lo * 128:hi * 128], writes=[r_sgd])
                P.op("pe", lambda e_, Sx=Sx, cs=cs: e_.matmul(Sx[:, 0:128], kT[:, cs], qT[:, cs], start=True, stop=True), [r_k, r_q], [r_S])
                P.op("dve", lambda e_, Sx=Sx, pT=pT: e_.tensor_tensor(pT[:], Sx[:, 0:128], mkbf[:, d, :], ALU.mult), [r_S, r_mkbf], [r_pT])
                P.op("act", lambda e_, vp=vp, n=n: e_.mul(vp[:], vex3[:, n, :], u_[:, n:n + 1]), [r_vex, r_uu], [r_vp])
                P.op("pool", lambda e_, vpp=vpp, n=n: e_.tensor_scalar(vpp[:], vex3[:, n, :], ueg_[:, n:n + 1], None, ALU.mult), [r_vex, r_ueg], [r_vpp])
                P.op("pe", lambda e_, Sx=Sx, cs=cs, Cb=Cb: e_.matmul(Sx[:, 512:641], qT[:, cs], Cb[:], start=True, stop=False), [r_q, r_Cb], [r_S])
                P.op("pe", lambda e_, Sx=Sx, pT=pT, vp=vp: e_.matmul(Sx[:, 512:641], pT[:], vp[:], start=False, stop=True), [r_pT, r_vp], [r_S])
                P.op("pe", lambda e_, pk_=pk_, n=n, vpp=vpp: e_.matmul(pk_[:, 0:129], ktm3[:, n, :], vpp[:], start=True, stop=True), [r_ktm, r_vpp], [r_pk])
                P.op("dve", lambda e_, pk_=pk_, n=n: e_.scalar_tensor_tensor(C32[:], C32[:], eg_[:, n:n + 1], pk_[:, 0:129], ALU.mult, ALU.add),
                     [r_C32, r_eg, r_pk], [r_C32])
                P.op("act", lambda e_, Cn=Cn: e_.copy(Cn[:], C32[:]), [r_C32], [r_Cn])
                P.op("dve", lambda e_, dd=dd, Sx=Sx, n=n: e_.tensor_scalar(dd[:, 0:1], Sx[:, 640:641], eb_[:, n:n + 1], None, ALU.mult),
                     [r_S, r_eb], [r_dd])
                P.op("dve", lambda e_, dd=dd: e_.tensor_scalar(dd[:, 1:2], dd[:, 0:1], -1.0, -1.0, ALU.min, ALU.mult), [r_dd], [r_dd])
                P.op("dve", lambda e_, dd=dd: e_.scalar_tensor_tensor(dd[:, 0:1], dd[:, 0:1], 1.0, dd[:, 1:2], ALU.max, ALU.max), [r_dd], [r_dd])
                P.op("dve", lambda e_, dd=dd: e_.reciprocal(dd[:, 0:1], dd[:, 0:1]), [r_dd], [r_dd])
                P.op("dve", lambda e_, dd=dd, n=n: e_.tensor_tensor(dd[:, 1:2], dd[:, 0:1], eb_[:, n:n + 1], ALU.mult), [r_dd, r_eb], [r_dd])
                if d == 0:
                    P.op("dve", lambda e_, Sx=Sx, dd=dd, n=n: e_.tensor_scalar(hfs3[:, n, :], Sx[:, 512:640], dd[:, 1:2], None, ALU.mult),
                         [r_S, r_dd], [r_hfs])
                    continue
                hs, r_hs = hss[par]
                hnn, r_hnn = hns[par]
                ss_, r_ss = ssd[par]
                P.op("dve", lambda e_, Sx=Sx, dd=dd, n=n, hs=hs: e_.scalar_tensor_tensor(
                    hs[:], Sx[:, 512:640], dd[:, 1:2], hfs3[:, n, :], ALU.mult, ALU.add), [r_S, r_dd, r_hfs], [r_hs])
                P.op("pool", lambda e_, ss_=ss_: e_.memset(ss_[:], 0.0), [], [r_ss])
                P.op("act", lambda e_, hs=hs, ss_=ss_: e_.activation(junkd[:], hs[:], AF.Square, accum_out=ss_[:]), [r_hs], [r_junkd, r_ss])
                P.op("act", lambda e_, ss_=ss_: e_.activation(ss_[:], ss_[:], AF.Sqrt, bias=epsc[:], scale=1.0 / 128), [r_ss, r_eps], [r_ss])
                P.op("dve", lambda e_, ss_=ss_: e_.reciprocal(ss_[:], ss_[:]), [r_ss], [r_ss])
                P.op("dve", lambda e_, hnn=hnn, hs=hs, ss_=ss_: e_.scalar_tensor_tensor(hnn[:], hs[:], ss_[:], hn[:], ALU.mult, ALU.mult),
                     [r_hs, r_ss, r_hn], [r_hnn])
                P.op("pe", lambda e_, ptt=ptt, hnn=hnn: e_.transpose(ptt[:, 0:128], hnn[:], ident[:]), [r_hnn, r_ident], [r_ptt])
                lo, hi = grp_of[n]
                off = (n - lo) * 128
                P.op("dve", lambda e_, ptt=ptt, off=off: e_.tensor_tensor(yt1[:, off:off + 128], ptt[:, 0:128], so_t[:, off:off + 128], ALU.mult),
                     [r_ptt, r_so], [r_yt1])
                P.op("pool", lambda e_, off=off: e_.tensor_tensor(ystD[:, off:off + 128], yt1[:, off:off + 128], sg_t[:, off:off + 128], ALU.mult),
                     [r_yt1, r_sgd], [r_ystD])
                if ci == len(order) - 1 or grp_of[order[ci + 1]] != (lo, hi):
                    C.out_dma("sp", Y[3][:, lo * 128:hi * 128], ystD[:, 0:(hi - lo) * 128], [r_ystD])

        groups = [(0, 2)] + [(2 + 4 * i, 2 + 4 * (i + 1)) for i in range(32)]
        mlstm_dir(0, list(range(NB)), groups)
        mlstm_dir(1, [1, 0] + list(range(NB - 1, 1, -1)), groups)
        C.finish()
    return nc


def _gather_tokens(per_core):
    ctx = np.concatenate([a[:, TL:NT] for a in per_core], axis=1)
    lat = np.concatenate([a[:, 0:TL] for a in per_core], axis=1)
    return np.concatenate([ctx, lat], axis=1)


def _tok_major(item):
    return np.ascontiguousarray(item.T.reshape(NB, 128, 128).transpose(1, 0, 2)).reshape(128, NB * 128)


def host_p2_inputs(res1, l, inp):
    rows, nrows = p1_out_rows()
    G = _gather_tokens([r["O1"] for r in res1])
    Gg = _gather_tokens([r["O1g"] for r in res1])

    def item(name, sub=None):
        r0, m = rows[name]
        if sub is not None:
            return G[r0 + sub[0]:r0 + sub[1]]
        return G[r0:r0 + m]

    p_ = np.arange(128)
    MK = np.stack([(p_[:, None] <= p_[None, :]), (p_[:, None] >= p_[None, :])]).astype(np.float32)
    ident = np.eye(128, dtype=np.float32)
    maps = []
    for d in range(NCORE):
        FA = np.stack([item(f"A_u{d}"), item(f"A_g{d}"), item(f"B_q{d}"), item(f"B_k{d // 4}"), item(f"B_g{d}"),
                       item(f"C_qn{d}"), item(f"C_kn{d}"), item(f"C_g{d}"), item(f"D_q{d}"), item(f"D_k{d}"),
                       item(f"D_o{d}"), item(f"D_g{d}")])
        FB = np.stack([item(f"C_qr{d // 2}", ((d % 2) * 64, (d % 2) * 64 + 64)), item("C_kr")])
        TM = np.stack([_tok_major(item(f"B_v{d // 4}")), _tok_major(item(f"C_v{d}")), _tok_major(item(f"D_k{d}"))])
        vex = np.ones((TT, 129), NPBF)
        vex[:, :128] = item(f"D_v{d}").T
        TMV = np.ascontiguousarray(vex.reshape(NB, 128, 129).transpose(1, 0, 2)).reshape(128, NB * 129)
        gsel = Gg[[d, 8 + d, 16 + d, 24 + d]]
        GT = np.ascontiguousarray(gsel.reshape(4, NB, 128).transpose(2, 0, 1)).reshape(128, 4 * NB)
        ch = slice(d * 128, (d + 1) * 128)
        LP = np.zeros((128, 16), np.float32)
        LP[:, 0:4] = inp["lru_conv_w"][l][:, ch].T
        LP[:, 4] = inp["lru_conv_b"][l][ch]
        LP[:, 5:7] = inp["lru_lambda"][l][:, ch].T
        LP[:, 7:9] = inp["lru_ba"][l][:, ch].T
        LP[:, 9:11] = inp["lru_bx"][l][:, ch].T
        LP[:, 11] = inp["swa_sink"][l][d]
        LW = np.stack([inp["lru_wa"][l][0, d], inp["lru_wx"][l][0, d], inp["lru_wa"][l][1, d], inp["lru_wx"][l][1, d]]).astype(np.float32)
        HN = np.ascontiguousarray(np.broadcast_to(inp["ml_head_norm"][l][ch].astype(np.float32), (128, 128)))
        maps.append({"FA": FA, "FB": FB, "TM": TM, "TMV": TMV, "GT": GT, "LP": LP, "LW": LW, "HN": HN, "MK": MK, "ident": ident})
    return maps


def host_p3_inputs(res2, x_full, xc_full, modT_l, l, inp):
    Yall = np.stack([r["Y"] for r in res2])
    Yt = np.ascontiguousarray(Yall.transpose(3, 1, 0, 2)).reshape(TT, D)
    common = host_p3_common(modT_l, l, inp)
    maps = []
    for core in range(NCORE):
        y_loc = np.concatenate([Yt[TC + core * TL:TC + (core + 1) * TL], Yt[core * CL:(core + 1) * CL]], axis=0)
        m = dict(common)
        m["yT"] = host_yT_tiles(y_loc)
        m["x"] = np.concatenate([x_full[core * TL:(core + 1) * TL], xc_full[core * CL:(core + 1) * CL]], axis=0)
        maps.append(m)
    return maps


def kernel(**inputs):
    inp = {k: np.asarray(v) for k, v in inputs.items()}
    cores = list(range(NCORE))
    x_full = np.ascontiguousarray(inp["x"][0], dtype=np.float32)
    xc_full = np.ascontiguousarray(inp["ctx"][0], dtype=np.float32)
    res0 = run_bass_kernel_spmd(build_p0(), host_p0_inputs(inp["c"], inp["c_ctx"], inp["w_mod"], inp["b_mod"]), core_ids=cores)
    modT = host_p0_gather(res0.results)
    nc1, nc2, nc3 = build_p1(), build_p2(), build_p3()
    for l in range(2):
        res1 = run_bass_kernel_spmd(nc1, host_p1_inputs(x_full, xc_full, modT[l], l, inp), core_ids=cores).results
        maps2 = host_p2_inputs(res1, l, inp)
        del res1
        res2 = run_bass_kernel_spmd(nc2, maps2, core_ids=cores).results
        del maps2
        maps3 = host_p3_inputs(res2, x_full, xc_full, modT[l], l, inp)
        del res2
        res3 = run_bass_kernel_spmd(nc3, maps3, core_ids=cores).results
        del maps3
        x_full = np.concatenate([r["xo"][0:TL] for r in res3], axis=0)
        xc_full = np.concatenate([r["xo"][TL:NT] for r in res3], axis=0)
    return x_full.reshape(1, T, D).astype(np.float32)
```

```python
import numpy as np
import ml_dtypes
from contextlib import ExitStack
import concourse.bass as bass
import concourse.mybir as mybir
from concourse.bass_utils import run_bass_kernel_spmd

F32 = mybir.dt.float32
BF16 = mybir.dt.bfloat16
AF = mybir.ActivationFunctionType
ALU = mybir.AluOpType
NPBF = ml_dtypes.bfloat16

NCORE = 8
D = 4096
T = 16384
TC = 256
TL = T // NCORE
CL = TC // NCORE
NT = TL + CL
TT = T + TC
NB = TT // 128
EPS = 1e-6
KC = D // 128
HALVES = ((0, 1024), (1024, NT))

O_AU, O_AG, O_BQ, O_BK, O_BV, O_BG = 0, 1024, 2048, 3072, 3328, 3584
O_CQ, O_CKV, O_KR, O_CG = 4608, 5376, 5632, 5696
O_DQ, O_DK, O_DV, O_DO, O_DGT, O_DG = 6720, 7744, 8768, 9792, 10816, 10848

SEM_CH = 16000
N_DMA_SEMS = 24


class Res:
    __slots__ = ("name", "w", "rs")

    def __init__(self, name):
        self.name = name
        self.w = None
        self.rs = []


class Op:
    __slots__ = ("eng", "fn", "deps", "sig", "sem", "semv", "dma", "gi")

    def __init__(self, eng, fn, dma, gi):
        self.eng = eng
        self.fn = fn
        self.dma = dma
        self.deps = set()
        self.sig = False
        self.sem = None
        self.semv = 0
        self.gi = gi


class Prog:
    ENGS = ("pe", "act", "dve", "pool", "sp")

    def __init__(self, nc):
        self.nc = nc
        self.ops = {e: [] for e in self.ENGS}
        self.gi = 0
        self.all_ops = []

    def res(self, name=""):
        return Res(name)

    def op(self, eng, fn, reads=(), writes=(), dma=False):
        o = Op(eng, fn, dma, self.gi)
        self.gi += 1
        deps = o.deps
        for r in reads:
            if r.w is not None:
                deps.add(r.w)
        for r in writes:
            if r.w is not None:
                deps.add(r.w)
            deps.update(r.rs)
        for r in reads:
            r.rs.append(o)
        for r in writes:
            r.w = o
            r.rs = []
        deps.discard(o)
        self.ops[eng].append(o)
        self.all_ops.append(o)
        return o

    def dma(self, eng, out, in_, reads=(), writes=(), **kw):
        return self.op(eng, lambda e: e.dma_start(out=out, in_=in_, **kw), reads, writes, dma=True)

    def emit(self, stack):
        nc = self.nc
        dma_last, dma_cnt, dma_sems = {}, {}, {}
        rr = {e: 0 for e in self.ENGS}
        for o in self.all_ops:
            if o.dma:
                key = (o.eng, rr[o.eng] % N_DMA_SEMS)
                rr[o.eng] += 1
                if key not in dma_sems:
                    dma_sems[key] = stack.enter_context(nc.semaphore(f"d_{o.eng}_{key[1]}"))
                    dma_cnt[key] = 0
                prev = dma_last.get(key)
                if prev is not None:
                    o.deps.add(prev)
                dma_last[key] = o
                dma_cnt[key] += 16
                o.sem = dma_sems[key]
                o.semv = dma_cnt[key]
                o.sig = True
        for o in self.all_ops:
            for d in o.deps:
                if d.eng == "pe" and o.eng == "pe" and not d.dma:
                    continue
                d.sig = True
        for e in self.ENGS:
            cnt = 0
            cur = None
            for o in self.ops[e]:
                if o.dma or not o.sig or o.fn is None:
                    continue
                if cnt % SEM_CH == 0:
                    cur = stack.enter_context(nc.semaphore(f"c_{e}_{cnt // SEM_CH}"))
                o.sem = cur
                o.semv = cnt % SEM_CH + 1
                cnt += 1
        block = stack.enter_context(nc.Block())

        def run(ename, eng):
            waited = {}
            for o in self.ops[ename]:
                for d in sorted(o.deps, key=lambda d: d.gi):
                    if d.eng == "pe" and ename == "pe" and not d.dma:
                        continue
                    k = id(d.sem)
                    if waited.get(k, 0) < d.semv:
                        eng.wait_ge(d.sem, d.semv)
                        waited[k] = d.semv
                if o.fn is None:
                    continue
                ins = o.fn(eng)
                if o.sig:
                    ins.then_inc(o.sem, 16 if o.dma else 1)

        if self.ops["sp"]:
            @block.sync
            def _(e):
                run("sp", e)
        if self.ops["pe"]:
            @block.tensor
            def _(e):
                run("pe", e)
        if self.ops["act"]:
            @block.scalar
            def _(e):
                run("act", e)
        if self.ops["dve"]:
            @block.vector
            def _(e):
                run("dve", e)
        if self.ops["pool"]:
            @block.gpsimd
            def _(e):
                run("pool", e)


class Ctx:
    def __init__(self, nc, st):
        self.nc = nc
        self.st = st
        self.P = Prog(nc)
        self.outs = []

    def dram_in(self, name, shape, dt=F32):
        return self.nc.dram_tensor(name, list(shape), dt, kind="ExternalInput").ap()

    def dram_out(self, name, shape, dt=F32):
        return self.nc.dram_tensor(name, list(shape), dt, kind="ExternalOutput").ap()

    def sb(self, name, shape, dt=F32):
        t = self.st.enter_context(self.nc.sbuf_tensor(name, list(shape), dt))
        return t, self.P.res(name)

    def ps(self, name, shape, dt=F32):
        t = self.st.enter_context(self.nc.psum_tensor(name, list(shape), dt))
        return t, self.P.res(name)

    def psum_banks(self, n, cols=512, prefix="pb"):
        self.banks = [self.ps(f"{prefix}{i}", [128, cols]) for i in range(n)]
        self.bank_i = 0

    def bank(self):
        b = self.banks[self.bank_i % len(self.banks)]
        self.bank_i += 1
        return b

    def out_dma(self, eng, out_ap, in_ap, reads):
        r = self.P.res("o")
        self.P.dma(eng, out_ap, in_ap, reads=reads, writes=[r])
        self.outs.append(r)

    def finish(self):
        self.P.op("sp", None, reads=self.outs)
        self.P.emit(self.st)


NJ0 = 24


def build_p0():
    nc = bass.Bass("TRN2", target_bir_lowering=False)
    with ExitStack() as st:
        C = Ctx(nc, st)
        P = C.P
        wm = C.dram_in("wm", [NJ0, 128, KC * 128])
        cvec = C.dram_in("cvec", [128, KC * 2])
        bm = C.dram_in("bm", [128, NJ0])
        o = C.dram_out("modT", [128, NJ0 * 2])
        cv, r_cv = C.sb("cv", [128, KC * 2])
        sc, r_sc = C.sb("sc", [128, KC * 2])
        bmt, r_bm = C.sb("bmt", [128, NJ0])
        ot, r_ot = C.sb("ot", [128, NJ0 * 2])
        wts = [C.sb(f"w{i}", [128, KC * 128]) for i in range(2)]
        C.psum_banks(2, cols=2)
        P.dma("sp", cv[:], cvec, writes=[r_cv])
        P.dma("sp", bmt[:], bm, writes=[r_bm])
        P.op("act", lambda e: e.activation(sc[:], cv[:], AF.Silu), [r_cv], [r_sc])
        for i in range(NJ0):
            wt, r_w = wts[i % 2]
            P.dma("sp" if i % 2 == 0 else "pool", wt[:], wm[i], writes=[r_w])
            pb, r_pb = C.bank()
            for kc in range(KC):
                P.op("pe", lambda e, wt=wt, pb=pb, kc=kc: e.matmul(
                    pb[:, 0:2], wt[:, kc * 128:(kc + 1) * 128], sc[:, kc * 2:kc * 2 + 2],
                    start=(kc == 0), stop=(kc == KC - 1)), [r_w, r_sc], [r_pb])
            P.op("dve", lambda e, pb=pb, i=i: e.tensor_scalar(
                ot[:, 2 * i:2 * i + 2], pb[:, 0:2], bmt[:, i:i + 1], None, ALU.add), [r_pb, r_bm], [r_ot])
        C.out_dma("sp", o, ot[:], [r_ot])
        C.finish()
    return nc


def _perm(idx, half):
    return np.asarray(idx) ^ half


def p1_groups():
    g = []
    r128 = np.arange(128)
    for j in range(6):
        g.append((f"cq{j}", O_CQ + j * 128 + r128, "cq", j))
    for j in range(2):
        g.append((f"ckv{j}", O_CKV + j * 128 + r128, "ckv", j))
    for h in range(8):
        g.append((f"A_u{h}", O_AU + h * 128 + r128, "copy", None))
    for kv in range(2):
        g.append((f"B_v{kv}", O_BV + kv * 128 + r128, "copy", None))
    for h in range(8):
        g.append((f"D_q{h}", O_DQ + h * 128 + r128, "copy", None))
    for h in range(8):
        g.append((f"D_v{h}", O_DV + h * 128 + r128, "copy", None))
    for h in range(8):
        g.append((f"D_k{h}", O_DK + h * 128 + r128, "scale", 128.0 ** -0.5))
    for nm, off in (("A_g", O_AG), ("B_g", O_BG), ("C_g", O_CG), ("D_g", O_DG)):
        for h in range(8):
            g.append((f"{nm}{h}", off + h * 128 + r128, "silu", None))
    for h in range(8):
        g.append((f"D_o{h}", O_DO + h * 128 + r128, "sigmoid", None))
    for h in range(8):
        g.append((f"B_q{h}", O_BQ + h * 128 + r128, "ropeA", "B"))
        g.append((f"B_q{h}", O_BQ + h * 128 + _perm(r128, 32), "ropeB", "B"))
    for kv in range(2):
        g.append((f"B_k{kv}", O_BK + kv * 128 + r128, "ropeA", "B"))
        g.append((f"B_k{kv}", O_BK + kv * 128 + _perm(r128, 32), "ropeB", "B"))
    r64 = np.arange(64)
    g.append(("C_kr", O_KR + r64, "ropeA", "C"))
    g.append(("C_kr", O_KR + _perm(r64, 16), "ropeB", "C"))
    r8 = np.arange(8)
    g.append(("G_i", np.concatenate([O_DGT + 0 * 8 + r8, O_DGT + 2 * 8 + r8]), "gi", None))
    g.append(("G_f", np.concatenate([O_DGT + 1 * 8 + r8, O_DGT + 3 * 8 + r8]), "gf", None))
    return g


def p1_groups2():
    g = []
    r128 = np.arange(128)
    r64 = np.arange(64)
    for h in range(8):
        g.append((f"C_qn{h}", "q", h * 192 + r128, "mulr", "q"))
    for h in range(8):
        g.append((f"C_kn{h}", "kv", h * 256 + r128, "mulr", "kv"))
    for h in range(8):
        g.append((f"C_v{h}", "kv", h * 256 + 128 + r128, "mulr", "kv"))
    for hp in range(4):
        base = np.concatenate([(2 * hp) * 192 + 128 + r64, (2 * hp + 1) * 192 + 128 + r64])
        perm = np.concatenate([(2 * hp) * 192 + 128 + _perm(r64, 16), (2 * hp + 1) * 192 + 128 + _perm(r64, 16)])
        g.append((f"C_qr{hp}", "q", base, "ropeA", "Cq"))
        g.append((f"C_qr{hp}", "q", perm, "ropeB", "Cq"))
    return g


def p1_out_rows():
    rows = {}
    r = 0
    for name, cols, epi, arg in p1_groups():
        if epi in ("cq", "ckv", "gi", "gf", "ropeA"):
            continue
        rows[name] = (r, len(cols))
        r += len(cols)
    for name, which, cols, epi, arg in p1_groups2():
        if epi == "ropeA":
            continue
        rows[name] = (r, len(cols))
        r += len(cols)
    return rows, r


def build_p1():
    g1 = p1_groups()
    g2 = p1_groups2()
    rows, nrows = p1_out_rows()
    NG1 = len(g1)
    g2q = [x for x in g2 if x[1] == "q"]
    g2kv = [x for x in g2 if x[1] == "kv"]
    nc = bass.Bass("TRN2", target_bir_lowering=False)
    with ExitStack() as st:
        C = Ctx(nc, st)
        P = C.P
        x = C.dram_in("x", [NT, D])
        modT = C.dram_in("modT", [128, 96 * 2])
        gpre = C.dram_in("gpre", [128, KC])
        w1 = C.dram_in("w1", [NG1, 128, KC * 128])
        w2q = C.dram_in("w2q", [len(g2q), 128, 6 * 128])
        w2kv = C.dram_in("w2kv", [len(g2kv), 128, 2 * 128])
        gq = C.dram_in("gq", [128, 6])
        gkv = C.dram_in("gkv", [128, 2])
        gbias = C.dram_in("gbias", [16, 2])
        cosB = C.dram_in("cosB", [128, NT])
        sinB = C.dram_in("sinB", [128, NT])
        cosC = C.dram_in("cosC", [128, NT])
        sinC = C.dram_in("sinC", [128, NT])
        identd = C.dram_in("ident", [128, 128])
        O1 = C.dram_out("O1", [nrows, NT], BF16)
        O1g = C.dram_out("O1g", [32, NT])

        HW = 1056
        hT, r_hT = C.sb("hT", [128, KC, HW], BF16)
        xt, r_xt = C.sb("xt", [128, D])
        junk, r_junk = C.sb("junk", [128, D // 2], BF16)
        wsts = [C.sb(f"wst{i}", [128, KC * 128]) for i in range(2)]
        wbf = [C.sb(f"wbf{i}", [128, KC, 128], BF16) for i in range(2)]
        wbf_hi = [P.res(f"wbfhi{i}") for i in range(2)]
        wld = [0]
        cqT, r_cqT = C.sb("cqT", [128, 6, HW], BF16)
        ckvT, r_ckvT = C.sb("ckvT", [128, 2, HW], BF16)
        ssq_q, r_ssq_q = C.sb("ssq_q", [128, HW])
        ssq_kv, r_ssq_kv = C.sb("ssq_kv", [128, HW])
        tcos = {k: C.sb(f"tcos{k}", [128, HW]) for k in ("B", "C", "Cq")}
        tsin = {k: C.sb(f"tsin{k}", [128, HW]) for k in ("B", "C", "Cq")}
        tmpA, r_tmpA = C.sb("tmpA", [128, HW])
        gtmp, r_gtmp = tmpA, r_tmpA
        tmpB = [C.sb(f"tmpB{i}", [128, 512]) for i in range(2)]
        sqt = [C.sb(f"sqt{i}", [128, 512], BF16) for i in range(2)]
        ost = [C.sb(f"ost{i}", [128, HW], BF16) for i in range(2)]
        ostg, r_ostg = C.sb("ostg", [16, HW])
        ident, r_ident = C.sb("identt", [128, 128])
        ones_bf, r_ones = C.sb("ones_bf", [128, 128], BF16)
        mod, r_mod = C.sb("mod", [128, 96 * 2])
        gp, r_gp = C.sb("gp", [128, KC])
        s1p, r_s1p = C.sb("s1p", [128, KC * 2])
        gqt, r_gqt = C.sb("gqt", [128, 6])
        gkvt, r_gkvt = C.sb("gkvt", [128, 2])
        gbt, r_gbt = C.sb("gbt", [16, 2])
        ngbt, r_ngbt = C.sb("ngbt", [16, 2])
        ssq, r_ssq = C.sb("ssq", [128, 2])
        epsc, r_eps = C.sb("epsc", [128, 1])
        onec, r_onec = C.sb("onec", [128, 1])
        C.psum_banks(8)

        P.dma("sp", ident[:], identd, writes=[r_ident])
        P.dma("sp", mod[:], modT, writes=[r_mod])
        P.dma("sp", gp[:], gpre, writes=[r_gp])
        P.dma("sp", gqt[:], gq, writes=[r_gqt])
        P.dma("sp", gkvt[:], gkv, writes=[r_gkvt])
        P.dma("sp", gbt[:], gbias, writes=[r_gbt])
        P.op("pool", lambda e: e.memset(ones_bf[:], 1.0), [], [r_ones])
        P.op("pool", lambda e: e.memset(epsc[:], EPS), [], [r_eps])
        P.op("pool", lambda e: e.memset(onec[:], 1.0), [], [r_onec])
        P.op("dve", lambda e: e.tensor_scalar(ngbt[:], gbt[:], -1.0, None, ALU.mult), [r_gbt], [r_ngbt])
        mod3 = mod[:].rearrange("p (j v) -> p j v", v=2)
        s1p3 = s1p[:].rearrange("p (j v) -> p j v", v=2)
        for v in range(2):
            P.op("dve", lambda e, v=v: e.scalar_tensor_tensor(
                s1p3[:, :, v], mod3[:, 32:64, v], 1.0, gp[:], ALU.add, ALU.mult), [r_mod, r_gp], [r_s1p])

        wq_i = 0
        for (h0, h1) in HALVES:
            nh = h1 - h0
            subs = [(c0, min(c0 + 512, nh)) for c0 in range(0, nh, 512)]
            for k, (cd, sd) in (("B", (cosB, sinB)), ("C", (cosC, sinC))):
                P.dma("pool", tcos[k][0][:, 0:nh], cd[:, h0:h1], writes=[tcos[k][1]])
                P.dma("pool", tsin[k][0][:, 0:nh], sd[:, h0:h1], writes=[tsin[k][1]])
            tok = h0
            while tok < h1:
                n = min(128, h1 - tok, (TL - tok) if tok < TL else 128)
                v = 0 if tok < TL else 1
                P.dma("sp", xt[0:n, :], x[tok:tok + n, :], writes=[r_xt])
                P.op("pool", lambda e: e.memset(ssq[:], 0.0), [], [r_ssq])
                for hh in range(2):
                    P.op("act", lambda e, n=n, hh=hh: e.activation(
                        junk[0:n, :], xt[0:n, hh * (D // 2):(hh + 1) * (D // 2)], AF.Square, accum_out=ssq[0:n, hh:hh + 1]),
                        [r_xt], [r_junk, r_ssq])
                P.op("dve", lambda e, n=n: e.tensor_tensor(ssq[0:n, 0:1], ssq[0:n, 0:1], ssq[0:n, 1:2], ALU.add), [r_ssq], [r_ssq])
                P.op("act", lambda e, n=n: e.activation(ssq[0:n, 0:1], ssq[0:n, 0:1], AF.Sqrt, bias=epsc[0:n, :], scale=1.0 / D),
                     [r_ssq, r_eps], [r_ssq])
                P.op("dve", lambda e, n=n: e.reciprocal(ssq[0:n, 0:1], ssq[0:n, 0:1]), [r_ssq], [r_ssq])
                P.op("dve", lambda e, n=n: e.tensor_scalar(xt[0:n, :], xt[0:n, :], ssq[0:n, 0:1], None, ALU.mult),
                     [r_xt, r_ssq], [r_xt])
                for kc0 in range(0, KC, 4):
                    pb, r_pb = C.bank()
                    for q in range(4):
                        kc = kc0 + q
                        P.op("pe", lambda e, pb=pb, q=q, kc=kc, n=n: e.transpose(
                            pb[:, q * 128:q * 128 + n], xt[0:n, kc * 128:(kc + 1) * 128], ident[0:n, 0:n]),
                            [r_xt, r_ident], [r_pb])
                    for q in range(4):
                        kc = kc0 + q
                        hdst = hT[:, kc, tok - h0:tok - h0 + n]
                        psrc = pb[:, q * 128:q * 128 + n]
                        sc_ = s1p[:, kc * 2 + v:kc * 2 + v + 1]
                        bi_ = mod[:, kc * 2 + v:kc * 2 + v + 1]
                        if q % 2 == 0:
                            P.op("dve", lambda e, hdst=hdst, psrc=psrc, sc_=sc_, bi_=bi_: e.tensor_scalar(
                                hdst, psrc, sc_, bi_, ALU.mult, ALU.add), [r_pb, r_s1p, r_mod], [r_hT])
                        else:
                            P.op("act", lambda e, hdst=hdst, psrc=psrc, sc_=sc_, bi_=bi_: e.activation(
                                hdst, psrc, AF.Identity, bias=bi_, scale=sc_), [r_pb, r_s1p, r_mod], [r_hT])
                tok += n

            def load_w(src_ap, ncol, gi_):
                wb, r_wb = wbf[gi_ % 2]
                r_hi = wbf_hi[gi_ % 2]
                wst, r_wst = wsts[wld[0] % 2]
                wld[0] += 1
                P.dma("sp", wst[:, 0:ncol], src_ap, writes=[r_wst])
                wflat = wb[:].rearrange("p k n -> p (k n)")
                hc = ncol // 2
                P.op("act", lambda e: e.copy(wflat[:, 0:hc], wst[:, 0:hc]), [r_wst], [r_wb])
                P.op("dve", lambda e: e.tensor_copy(wflat[:, hc:ncol], wst[:, hc:ncol]), [r_wst], [r_hi])
                return wb, r_wb, r_hi

            for gi_, (name, cols, epi, arg) in enumerate(g1):
                M = len(cols)
                wb, r_wb, r_whi = load_w(w1[gi_], KC * 128, gi_)
                osb, r_osb = ost[gi_ % 2]
                for si, (c0, c1) in enumerate(subs):
                    n = c1 - c0
                    pb, r_pb = C.bank()
                    for kc in range(KC):
                        P.op("pe", lambda e, pb=pb, wb=wb, kc=kc, M=M, c0=c0, c1=c1, n=n: e.matmul(
                            pb[0:M, 0:n], wb[:, kc, 0:M], hT[:, kc, c0:c1], start=(kc == 0), stop=(kc == KC - 1)),
                            [r_wb if kc < KC // 2 else r_whi, r_hT], [r_pb])
                    src = pb[0:M, 0:n]
                    dst = osb[0:M, c0:c1]
                    if epi == "copy":
                        P.op("act", lambda e, src=src, dst=dst: e.copy(dst, src), [r_pb], [r_osb])
                    elif epi == "scale":
                        P.op("act", lambda e, src=src, dst=dst, arg=arg: e.mul(dst, src, arg), [r_pb], [r_osb])
                    elif epi == "silu":
                        P.op("act", lambda e, src=src, dst=dst: e.activation(dst, src, AF.Silu), [r_pb], [r_osb])
                    elif epi == "sigmoid":
                        P.op("act", lambda e, src=src, dst=dst: e.activation(dst, src, AF.Sigmoid), [r_pb], [r_osb])
                    elif epi in ("cq", "ckv"):
                        tgt, r_tgt = (cqT, r_cqT) if epi == "cq" else (ckvT, r_ckvT)
                        acc, r_acc = (ssq_q, r_ssq_q) if epi == "cq" else (ssq_kv, r_ssq_kv)
                        sq, r_sq = sqt[si % 2]
                        P.op("act", lambda e, src=src, tgt=tgt, arg=arg, c0=c0, c1=c1: e.copy(tgt[:, arg, c0:c1], src),
                             [r_pb], [r_tgt])
                        P.op("act", lambda e, src=src, sq=sq, n=n: e.activation(sq[:, 0:n], src, AF.Square),
                             [r_pb], [r_sq])
                        pb2, r_pb2 = C.bank()
                        P.op("pe", lambda e, pb2=pb2, sq=sq, n=n: e.matmul(pb2[:, 0:n], ones_bf[:], sq[:, 0:n],
                                                                          start=True, stop=True), [r_sq, r_ones], [r_pb2])
                        if arg == 0:
                            P.op("dve", lambda e, acc=acc, pb2=pb2, c0=c0, c1=c1, n=n: e.tensor_copy(
                                acc[:, c0:c1], pb2[:, 0:n]), [r_pb2], [r_acc])
                        else:
                            P.op("dve", lambda e, acc=acc, pb2=pb2, c0=c0, c1=c1, n=n: e.tensor_tensor(
                                acc[:, c0:c1], acc[:, c0:c1], pb2[:, 0:n], ALU.add), [r_pb2, r_acc], [r_acc])
                    elif epi == "ropeA":
                        ct, r_ct = tcos[arg]
                        P.op("dve", lambda e, src=src, ct=ct, M=M, c0=c0, c1=c1: e.tensor_tensor(
                            tmpA[0:M, c0:c1], src, ct[0:M, c0:c1], ALU.mult), [r_pb, r_ct], [r_tmpA])
                    elif epi == "ropeB":
                        stt, r_stt = tsin[arg]
                        tb, r_tb = tmpB[si % 2]
                        P.op("dve", lambda e, src=src, stt=stt, tb=tb, M=M, c0=c0, c1=c1, n=n: e.tensor_tensor(
                            tb[0:M, 0:n], src, stt[0:M, c0:c1], ALU.mult), [r_pb, r_stt], [r_tb])
                        P.op("pool", lambda e, dst=dst, tb=tb, M=M, c0=c0, c1=c1, n=n: e.tensor_tensor(
                            dst, tb[0:M, 0:n], tmpA[0:M, c0:c1], ALU.add), [r_tb, r_tmpA], [r_osb])
                    elif epi == "gi":
                        P.op("act", lambda e, src=src, c0=c0, c1=c1: e.activation(
                            ostg[0:16, c0:c1], src, AF.Identity, bias=gbt[:, 0:1]), [r_pb, r_gbt], [r_ostg])
                    elif epi == "gf":
                        P.op("act", lambda e, src=src, c0=c0, c1=c1: e.activation(
                            gtmp[0:16, c0:c1], src, AF.Exp, bias=ngbt[:, 1:2], scale=-1.0), [r_pb, r_ngbt], [r_gtmp])
                        P.op("act", lambda e, c0=c0, c1=c1: e.activation(
                            gtmp[0:16, c0:c1], gtmp[0:16, c0:c1], AF.Ln, bias=onec[0:16, :]), [r_gtmp, r_onec], [r_gtmp])
                        P.op("dve", lambda e, c0=c0, c1=c1: e.tensor_scalar(
                            ostg[0:16, c0:c1], gtmp[0:16, c0:c1], -1.0, None, ALU.mult), [r_gtmp], [r_ostg])
                if epi in ("copy", "scale", "silu", "sigmoid", "ropeB"):
                    r0, m = rows[name]
                    C.out_dma("pool", O1[r0:r0 + m, h0:h1], osb[0:m, 0:nh], [r_osb])
                elif epi == "gi":
                    C.out_dma("pool", O1g[0:16, h0:h1], ostg[0:16, 0:nh], [r_ostg])
                elif epi == "gf":
                    C.out_dma("pool", O1g[16:32, h0:h1], ostg[0:16, 0:nh], [r_ostg])

            for acc, r_acc, dim in ((ssq_q, r_ssq_q, 768), (ssq_kv, r_ssq_kv, 256)):
                a_ = acc[:, 0:nh]
                P.op("act", lambda e, a_=a_, dim=dim: e.activation(
                    a_, a_, AF.Sqrt, bias=epsc[:], scale=1.0 / dim), [r_acc, r_eps], [r_acc])
                P.op("dve", lambda e, a_=a_: e.reciprocal(a_, a_), [r_acc], [r_acc])
            for tab in (tcos, tsin):
                o_ = tab["Cq"][0][:, 0:nh]
                i_ = tab["C"][0][:, 0:nh]
                q_ = ssq_q[:, 0:nh]
                P.op("dve", lambda e, o_=o_, i_=i_, q_=q_: e.tensor_tensor(o_, i_, q_, ALU.mult),
                     [tab["C"][1], r_ssq_q], [tab["Cq"][1]])

            iq = ikv = 0
            for gj, (name, which, cols, epi, arg) in enumerate(g2):
                M = len(cols)
                nk = 6 if which == "q" else 2
                src_ap = w2q[iq] if which == "q" else w2kv[ikv]
                gt = gqt if which == "q" else gkvt
                r_gt = r_gqt if which == "q" else r_gkvt
                if which == "q":
                    iq += 1
                else:
                    ikv += 1
                wb, r_wb = wbf[gj % 2]
                wst, r_wst = wsts[wld[0] % 2]
                wld[0] += 1
                P.dma("sp", wst[:, 0:nk * 128], src_ap, writes=[r_wst])
                for kc in range(nk):
                    P.op("dve", lambda e, wb=wb, kc=kc, gt=gt, wst=wst: e.tensor_scalar(
                        wb[:, kc, :], wst[:, kc * 128:(kc + 1) * 128], gt[:, kc:kc + 1], None, ALU.mult),
                        [r_wst, r_gt], [r_wb, wbf_hi[gj % 2]])
                rhsT, r_rhs = (cqT, r_cqT) if which == "q" else (ckvT, r_ckvT)
                rr, r_rr = (ssq_q, r_ssq_q) if which == "q" else (ssq_kv, r_ssq_kv)
                osb, r_osb = ost[gj % 2]
                for si, (c0, c1) in enumerate(subs):
                    n = c1 - c0
                    pb, r_pb = C.bank()
                    for kc in range(nk):
                        P.op("pe", lambda e, pb=pb, wb=wb, kc=kc, M=M, c0=c0, c1=c1, n=n, rhsT=rhsT, nk=nk: e.matmul(
                            pb[0:M, 0:n], wb[:, kc, 0:M], rhsT[:, kc, c0:c1], start=(kc == 0), stop=(kc == nk - 1)),
                            [r_wb, r_rhs], [r_pb])
                    src = pb[0:M, 0:n]
                    dst = osb[0:M, c0:c1]
                    if epi == "mulr":
                        P.op("dve", lambda e, src=src, dst=dst, rr=rr, M=M, c0=c0, c1=c1: e.tensor_tensor(
                            dst, src, rr[0:M, c0:c1], ALU.mult), [r_pb, r_rr], [r_osb])
                    elif epi == "ropeA":
                        ct, r_ct = tcos[arg]
                        P.op("dve", lambda e, src=src, ct=ct, M=M, c0=c0, c1=c1: e.tensor_tensor(
                            tmpA[0:M, c0:c1], src, ct[0:M, c0:c1], ALU.mult), [r_pb, r_ct], [r_tmpA])
                    elif epi == "ropeB":
                        stt, r_stt = tsin[arg]
                        tb, r_tb = tmpB[si % 2]
                        P.op("dve", lambda e, src=src, stt=stt, tb=tb, M=M, c0=c0, c1=c1, n=n: e.tensor_tensor(
                            tb[0:M, 0:n], src, stt[0:M, c0:c1], ALU.mult), [r_pb, r_stt], [r_tb])
                        P.op("pool", lambda e, dst=dst, tb=tb, M=M, c0=c0, c1=c1, n=n: e.tensor_tensor(
                            dst, tb[0:M, 0:n], tmpA[0:M, c0:c1], ALU.add), [r_tb, r_tmpA], [r_osb])
                if epi in ("mulr", "ropeB"):
                    r0, m = rows[name]
                    C.out_dma("pool", O1[r0:r0 + m, h0:h1], osb[0:m, 0:nh], [r_osb])
        C.finish()
    return nc


def host_p0_inputs(c, c_ctx, w_mod, b_mod):
    cv = np.stack([np.asarray(c).reshape(D), np.asarray(c_ctx).reshape(D)], axis=-1)
    cvec = np.ascontiguousarray(cv.reshape(KC, 128, 2).transpose(1, 0, 2)).reshape(128, KC * 2)
    maps = []
    for core in range(NCORE):
        wm = np.empty((NJ0, 128, KC, 128), np.float32)
        bm = np.empty((128, NJ0), np.float32)
        for l in range(2):
            w3 = w_mod[l].reshape(KC, 128, 96, 128)
            for jj in range(12):
                j = core * 12 + jj
                wm[l * 12 + jj] = w3[:, :, j, :].transpose(1, 0, 2)
                bm[:, l * 12 + jj] = b_mod[l, j * 128:(j + 1) * 128]
        maps.append({"wm": wm.reshape(NJ0, 128, KC * 128), "cvec": cvec, "bm": bm})
    return maps


def host_p0_gather(results):
    out = np.empty((2, 128, 96, 2), np.float32)
    for core in range(NCORE):
        o = results[core]["modT"].reshape(128, NJ0, 2)
        for l in range(2):
            out[l, :, core * 12:(core + 1) * 12, :] = o[:, l * 12:(l + 1) * 12, :]
    return out.reshape(2, 128, 192)


def host_w1(w_in_l):
    g1 = p1_groups()
    out = np.zeros((len(g1), 128, KC, 128), np.float32)
    w3 = w_in_l.reshape(KC, 128, -1)
    for i, (name, cols, epi, arg) in enumerate(g1):
        out[i, :, :, :len(cols)] = w3[:, :, cols].transpose(1, 0, 2)
    return out.reshape(len(g1), 128, KC * 128)


def host_w2(w_uq_l, w_ukv_l):
    g2 = p1_groups2()
    gq = [x for x in g2 if x[1] == "q"]
    gkv = [x for x in g2 if x[1] == "kv"]
    oq = np.zeros((len(gq), 128, 6, 128), np.float32)
    okv = np.zeros((len(gkv), 128, 2, 128), np.float32)
    wq3 = w_uq_l.reshape(6, 128, -1)
    wkv3 = w_ukv_l.reshape(2, 128, -1)
    for i, (name, which, cols, epi, arg) in enumerate(gq):
        oq[i, :, :, :len(cols)] = wq3[:, :, cols].transpose(1, 0, 2)
    for i, (name, which, cols, epi, arg) in enumerate(gkv):
        okv[i, :, :, :len(cols)] = wkv3[:, :, cols].transpose(1, 0, 2)
    return oq.reshape(len(gq), 128, 6 * 128), okv.reshape(len(gkv), 128, 2 * 128)


def pk(vec, nchunk):
    return np.ascontiguousarray(np.asarray(vec, np.float32).reshape(nchunk, 128).T)


def rope_tables(core):
    t = core * TL + np.arange(TL)
    row = (t // 64).astype(np.float64)
    col = (t % 64).astype(np.float64)

    def tab(dim):
        h = dim // 2
        half = h // 2
        cos = np.ones((dim, NT), np.float64)
        sin = np.zeros((dim, NT), np.float64)
        for i in range(dim):
            pos = row if i < h else col
            j = i % half
            f = 10000.0 ** (-j / half)
            sgn = -1.0 if (i % h) < half else 1.0
            cos[i, :TL] = np.cos(pos * f)
            sin[i, :TL] = sgn * np.sin(pos * f)
        return cos.astype(np.float32), sin.astype(np.float32)

    cB, sB = tab(128)
    c64, s64 = tab(64)
    cC = np.concatenate([c64, c64], axis=0)
    sC = np.concatenate([s64, s64], axis=0)
    return cB, sB, cC, sC


def host_p1_inputs(x_full, xc_full, modT_l, l, inp):
    w1 = host_w1(inp["w_in"][l])
    w2q, w2kv = host_w2(inp["mla_w_uq"][l], inp["mla_w_ukv"][l])
    gb = inp["ml_gate_b"][l]
    gbias = np.stack([np.concatenate([gb[0], gb[2]]), np.concatenate([gb[1], gb[3]])], axis=1).astype(np.float32)
    common = {
        "modT": modT_l, "gpre": pk(inp["g_pre"][l], KC), "w1": w1, "w2q": w2q, "w2kv": w2kv,
        "gq": pk(inp["mla_q_norm"][l], 6), "gkv": pk(inp["mla_kv_norm"][l], 2), "gbias": gbias,
        "ident": np.eye(128, dtype=np.float32),
    }
    maps = []
    for core in range(NCORE):
        cB, sB, cC, sC = rope_tables(core)
        m = dict(common)
        m["x"] = np.concatenate([x_full[core * TL:(core + 1) * TL], xc_full[core * CL:(core + 1) * CL]], axis=0)
        m.update({"cosB": cB, "sinB": sB, "cosC": cC, "sinC": sC})
        maps.append(m)
    return maps


NG3 = 16
GW3 = D // NG3


def build_p3():
    nc = bass.Bass("TRN2", target_bir_lowering=False)
    with ExitStack() as st:
        C = Ctx(nc, st)
        P = C.P
        yT = C.dram_in("yT", [5, 128, KC * 512], BF16)
        wo = C.dram_in("wo", [NG3, 128, KC * GW3])
        x = C.dram_in("x", [NT, D])
        grow = C.dram_in("grow", [2, D])
        gpost = C.dram_in("gpost", [1, D])
        xo = C.dram_out("xo", [NT, D])
        yts = [C.sb(f"yt{i}", [128, KC, 512], BF16) for i in range(1)] * 2
        wsts = [C.sb(f"wst{i}", [128, KC * GW3]) for i in range(1)] * 2
        wbf = [C.sb(f"wbf{i}", [128, KC, GW3], BF16) for i in range(2)]
        wbf_hi = [P.res(f"wbfhi{i}") for i in range(2)]
        ots = [C.sb(f"ot{i}", [128, D]) for i in range(4)]
        xt, r_xt = C.sb("xt", [128, D])
        ggs = [C.sb(f"gg{i}", [128, D]) for i in range(1)] * 2
        junk, r_junk = C.sb("junk", [128, D // 2], BF16)
        ssq, r_ssq = C.sb("ssq", [128, 2])
        epsc, r_eps = C.sb("epsc", [128, 1])
        C.psum_banks(8)
        P.op("pool", lambda e: e.memset(epsc[:], EPS), [], [r_eps])
        def load_gg(v):
            gg, r_gg = ggs[v]
            P.dma("sp", xt[:], gpost[0:1, :].to_broadcast([128, D]), writes=[r_xt])
            P.dma("sp", gg[:], grow[v:v + 1, :].to_broadcast([128, D]), writes=[r_gg])
            P.op("dve", lambda e, gg=gg: e.tensor_tensor(gg[:], gg[:], xt[:], ALU.mult), [r_gg, r_xt], [r_gg])

        load_gg(0)
        sts = [(s, s + 512) for s in range(0, TL, 512)] + [(TL, NT)]
        wi = 0
        for sti, (s0, s1) in enumerate(sts):
            v = 0 if s0 < TL else 1
            ns = s1 - s0
            yt, r_yt = yts[sti % 2]
            P.dma("pool", yt[:].rearrange("p k n -> p (k n)"), yT[sti], writes=[r_yt])
            tiles = [(a, min(a + 128, ns)) for a in range(0, ns, 128)]
            for g in range(NG3):
                wb, r_wb = wbf[wi % 2]
                r_whi = wbf_hi[wi % 2]
                wst, r_wst = wsts[wi % 2]
                wi += 1
                P.dma("sp", wst[:], wo[g], writes=[r_wst])
                wflat = wb[:].rearrange("p k n -> p (k n)")
                hc = KC * GW3 // 2
                P.op("act", lambda e, wflat=wflat, wst=wst: e.copy(wflat[:, 0:hc], wst[:, 0:hc]), [r_wst], [r_wb])
                P.op("dve", lambda e, wflat=wflat, wst=wst: e.tensor_copy(wflat[:, hc:], wst[:, hc:]), [r_wst], [r_whi])
                for ti, (a, b) in enumerate(tiles):
                    n = b - a
                    pb, r_pb = C.bank()
                    ot, r_ot = ots[ti]
                    for kc in range(KC):
                        P.op("pe", lambda e, pb=pb, yt=yt, wb=wb, kc=kc, a=a, b=b, n=n: e.matmul(
                            pb[0:n, 0:GW3], yt[:, kc, a:b], wb[:, kc, :], start=(kc == 0), stop=(kc == KC - 1)),
                            [r_yt, r_wb if kc < KC // 2 else r_whi], [r_pb])
                    P.op("act", lambda e, pb=pb, ot=ot, n=n, g=g: e.copy(ot[0:n, g * GW3:(g + 1) * GW3], pb[0:n, 0:GW3]),
                         [r_pb], [r_ot])
            if v == 1:
                load_gg(1)
            for ti, (a, b) in enumerate(tiles):
                n = b - a
                ot, r_ot = ots[ti]
                gg, r_gg = ggs[v]
                P.dma("sp", xt[0:n, :], x[s0 + a:s0 + b, :], writes=[r_xt])
                P.op("pool", lambda e: e.memset(ssq[:], 0.0), [], [r_ssq])
                for hh in range(2):
                    P.op("act", lambda e, ot=ot, n=n, hh=hh: e.activation(
                        junk[0:n, :], ot[0:n, hh * (D // 2):(hh + 1) * (D // 2)], AF.Square, accum_out=ssq[0:n, hh:hh + 1]),
                        [r_ot], [r_junk, r_ssq])
                P.op("dve", lambda e, n=n: e.tensor_tensor(ssq[0:n, 0:1], ssq[0:n, 0:1], ssq[0:n, 1:2], ALU.add), [r_ssq], [r_ssq])
                P.op("act", lambda e, n=n: e.activation(ssq[0:n, 0:1], ssq[0:n, 0:1], AF.Sqrt, bias=epsc[0:n, :], scale=1.0 / D),
                     [r_ssq, r_eps], [r_ssq])
                P.op("dve", lambda e, n=n: e.reciprocal(ssq[0:n, 0:1], ssq[0:n, 0:1]), [r_ssq], [r_ssq])
                P.op("dve", lambda e, ot=ot, gg=gg, n=n: e.scalar_tensor_tensor(
                    ot[0:n, :], ot[0:n, :], ssq[0:n, 0:1], gg[0:n, :], ALU.mult, ALU.mult), [r_ot, r_ssq, r_gg], [r_ot])
                P.op("dve", lambda e, ot=ot, n=n: e.tensor_tensor(ot[0:n, :], ot[0:n, :], xt[0:n, :], ALU.add),
                     [r_ot, r_xt], [r_ot])
                C.out_dma("pool", xo[s0 + a:s0 + b, :], ot[0:n, :], [r_ot])
        C.finish()
    return nc


def host_p3_common(modT_l, l, inp):
    wo = inp["w_out"][l].reshape(KC, 128, NG3, GW3).transpose(2, 1, 0, 3)
    wo = np.ascontiguousarray(wo).reshape(NG3, 128, KC * GW3)
    m3 = modT_l.reshape(128, 96, 2)
    grow = np.ascontiguousarray(m3[:, 64:96, :].transpose(2, 1, 0)).reshape(2, D)
    return {"wo": wo, "grow": grow, "gpost": np.asarray(inp["g_post"][l], np.float32).reshape(1, D)}


def host_yT_tiles(y_loc):
    out = np.zeros((5, 128, KC, 512), y_loc.dtype)
    for i in range(5):
        s0 = i * 512
        s1 = min(s0 + 512, NT)
        out[i, :, :, :s1 - s0] = y_loc[s0:s1].reshape(s1 - s0, KC, 128).transpose(2, 1, 0)
    return out.reshape(5, 128, KC * 512)


BIGW = 16896
LCH = 512


def build_p2():
    nc = bass.Bass("TRN2", target_bir_lowering=False)
    with ExitStack() as st:
        C = Ctx(nc, st)
        P = C.P
        FA = C.dram_in("FA", [12, 128, TT], BF16)
        FB = C.dram_in("FB", [2, 64, TT], BF16)
        TM = C.dram_in("TM", [3, 128, NB * 128], BF16)
        TMV = C.dram_in("TMV", [128, NB * 129], BF16)
        GT = C.dram_in("GT", [128, 4 * NB])
        LP = C.dram_in("LP", [128, 16])
        LW = C.dram_in("LW", [4, 128, 128])
        HN = C.dram_in("HN", [128, 128])
        MK = C.dram_in("MK", [2, 128, 128])
        identd = C.dram_in("ident", [128, 128])
        Y = C.dram_out("Y", [4, 128, TT], BF16)

        bigs = [C.sb(f"big{i}", [128, BIGW], BF16) for i in range(5)]
        lp, r_lp = C.sb("lp", [128, 16])
        lwbf, r_lwbf = C.sb("lwbf", [128, 4, 128], BF16)
        hn, r_hn = C.sb("hnrow", [128, 128])
        mk32, r_mk32 = C.sb("mk32", [128, 2, 128])
        mkbf, r_mkbf = C.sb("mkbf", [128, 2, 128], BF16)
        ident, r_ident = C.sb("identt", [128, 128])
        ones_bf, r_ones = C.sb("ones_bf", [128, 128], BF16)
        ones32, r_ones32 = C.sb("ones32", [128, 128])
        onec, r_onec = C.sb("onec", [128, 1])
        epsc, r_eps = C.sb("epsc", [128, 1])
        cc, r_cc = C.sb("cc", [128, 8])
        tf = [C.sb(f"tf{i}", [128, LCH]) for i in range(6)]
        tb = [C.sb(f"tbf{i}", [128, 1024], BF16) for i in range(3)]
        S = [C.ps(f"S{i}", [128, 1024]) for i in range(3)]
        PO = [C.ps("PO0", [128, 512]), (S[2][0][:, 0:512], S[2][1])]
        PD = [C.ps("PD0", [128, 512]), (S[2][0][:, 512:1024], S[2][1])]

        P.dma("sp", lp[:], LP, writes=[r_lp])
        P.dma("sp", hn[:], HN, writes=[r_hn])
        P.dma("sp", mk32[:], MK.rearrange("g k j -> k g j"), writes=[r_mk32])
        P.dma("sp", ident[:], identd, writes=[r_ident])
        P.op("pool", lambda e: e.memset(ones_bf[:], 1.0), [], [r_ones])
        P.op("pool", lambda e: e.memset(ones32[:], 1.0), [], [r_ones32])
        P.op("pool", lambda e: e.memset(onec[:], 1.0), [], [r_onec])
        P.op("pool", lambda e: e.memset(epsc[:], EPS), [], [r_eps])
        lwst = tf[0][0][:].rearrange("p (g j) -> p g j", j=128)
        P.dma("sp", lwst, LW.rearrange("g k j -> k g j"), writes=[tf[0][1]])
        P.op("dve", lambda e: e.tensor_copy(lwbf[:], lwst), [tf[0][1]], [r_lwbf])
        P.op("dve", lambda e: e.tensor_copy(mkbf[:], mk32[:]), [r_mk32], [r_mkbf])
        P.op("act", lambda e: e.activation(cc[:, 0:2], lp[:, 5:7], AF.Exp, scale=-1.0), [r_lp], [r_cc])
        P.op("act", lambda e: e.activation(cc[:, 0:2], cc[:, 0:2], AF.Ln, bias=onec[:]), [r_cc, r_onec], [r_cc])
        P.op("dve", lambda e: e.tensor_scalar(cc[:, 2:4], cc[:, 0:2], -16.0, None, ALU.mult), [r_cc], [r_cc])
        P.op("dve", lambda e: e.tensor_scalar(cc[:, 0:2], cc[:, 0:2], -8.0, None, ALU.mult), [r_cc], [r_cc])
        P.op("act", lambda e: e.activation(cc[:, 4:5], lp[:, 11:12], AF.Exp), [r_lp], [r_cc])

        def load_big(i, src, ncol, part=128, eng="sp"):
            t, r = bigs[i]
            P.dma(eng, t[0:part, 0:ncol], src, writes=[r])
            return t, r

        u_bf, r_u = load_big(0, FA[0], TT)
        hfl = bigs[1][0][:].bitcast(F32)
        hfh = bigs[2][0][:].bitcast(F32)
        r_hfl, r_hfh = bigs[1][1], bigs[2][1]

        def hf_ap(s, e):
            if e <= 8448:
                return hfl[:, s:e], r_hfl
            return hfh[:, s - 8448:e - 8448], r_hfh

        chunks_ctx = [(0, TC, 0, TC)]
        chunks_lat = [(TC + i * LCH, TC + (i + 1) * LCH, TC, TT) for i in range(T // LCH)]
        (v32, r_v32), (rb, r_rb), (ib, r_ib), (t2, r_t2), (hb0, r_hb0), (hb1, r_hb1) = tf
        (vbf, r_vbf), (sgt, r_sgt), (yst, r_yst) = tb[0], tb[1], tb[2]
        hbs = [(hb0, r_hb0), (hb1, r_hb1)]

        def lru_chunk(s, e, S_, E_, d, prev_ap, prev_res, hbi):
            n = e - s
            P.op("dve", lambda e_: e_.tensor_scalar(v32[:, 0:n], u_bf[:, s:e], lp[:, 2:3], lp[:, 4:5], ALU.mult, ALU.add),
                 [r_u, r_lp], [r_v32])
            for j in (0, 1, 3):
                o = j - 2
                lo, hi = max(s, S_ - o), min(e, E_ - o)
                if lo < hi:
                    P.op("dve", lambda e_, lo=lo, hi=hi, o=o, j=j: e_.scalar_tensor_tensor(
                        v32[:, lo - s:hi - s], u_bf[:, lo + o:hi + o], lp[:, j:j + 1], v32[:, lo - s:hi - s],
                        ALU.mult, ALU.add), [r_u, r_lp, r_v32], [r_v32])
            P.op("act", lambda e_: e_.copy(vbf[:, 0:n], v32[:, 0:n]), [r_v32], [r_vbf])
            pa, r_pa = S[0]
            px, r_px = S[1]
            for c0 in range(0, n, 512):
                c1_ = min(c0 + 512, n)
                P.op("pe", lambda e_, c0=c0, c1_=c1_: e_.matmul(pa[:, c0:c1_], lwbf[:, 2 * d, :], vbf[:, c0:c1_],
                                                               start=True, stop=True), [r_lwbf, r_vbf], [r_pa])
                P.op("pe", lambda e_, c0=c0, c1_=c1_: e_.matmul(px[:, c0:c1_], lwbf[:, 2 * d + 1, :], vbf[:, c0:c1_],
                                                               start=True, stop=True), [r_lwbf, r_vbf], [r_px])
            P.op("act", lambda e_: e_.activation(ib[:, 0:n], px[:, 0:n], AF.Sigmoid, bias=lp[:, 9 + d:10 + d]), [r_px, r_lp], [r_ib])
            P.op("act", lambda e_: e_.activation(rb[:, 0:n], pa[:, 0:n], AF.Sigmoid, bias=lp[:, 7 + d:8 + d]), [r_pa, r_lp], [r_rb])
            P.op("act", lambda e_: e_.activation(t2[:, 0:n], rb[:, 0:n], AF.Exp, scale=cc[:, 2 + d:3 + d]), [r_rb, r_cc], [r_t2])
            P.op("act", lambda e_: e_.activation(rb[:, 0:n], rb[:, 0:n], AF.Exp, scale=cc[:, d:d + 1]), [r_rb, r_cc], [r_rb])
            P.op("act", lambda e_: e_.activation(t2[:, 0:n], t2[:, 0:n], AF.Sqrt, bias=onec[:], scale=-1.0), [r_t2, r_onec], [r_t2])
            P.op("dve", lambda e_: e_.tensor_tensor(ib[:, 0:n], ib[:, 0:n], v32[:, 0:n], ALU.mult), [r_ib, r_v32], [r_ib])
            P.op("dve", lambda e_: e_.tensor_tensor(ib[:, 0:n], ib[:, 0:n], t2[:, 0:n], ALU.mult), [r_ib, r_t2], [r_ib])
            init = 0.0 if prev_ap is None else prev_ap
            rd = [r_rb, r_ib] + ([prev_res] if prev_res is not None else [])
            if d == 0:
                out, r_out = hf_ap(s, e)
                P.op("dve", lambda e_: e_.tensor_tensor_scan(out, rb[:, 0:n], ib[:, 0:n], init, ALU.mult, ALU.add), rd, [r_out])
                return out[:, n - 1:n], r_out
            hb, r_hb = hbs[hbi]
            P.op("dve", lambda e_: e_.tensor_tensor_scan(hb[:, 0:n][:, ::-1], rb[:, 0:n][:, ::-1], ib[:, 0:n][:, ::-1],
                                                         init, ALU.mult, ALU.add), rd, [r_hb])
            hfa, r_hfa = hf_ap(s, e)
            P.dma("sp", sgt[:, 0:n], FA[1][:, s:e], writes=[r_sgt])
            P.op("dve", lambda e_: e_.tensor_tensor(v32[:, 0:n], hb[:, 0:n], hfa, ALU.add), [r_hb, r_hfa], [r_v32])
            P.op("dve", lambda e_: e_.tensor_tensor(yst[:, 0:n], v32[:, 0:n], sgt[:, 0:n], ALU.mult), [r_v32, r_sgt], [r_yst])
            C.out_dma("sp", Y[0][:, s:e], yst[:, 0:n], [r_yst])
            return hb[:, 0:1], r_hb

        pa_, pr_ = None, None
        for (s, e, S_, E_) in chunks_ctx + chunks_lat:
            pa_, pr_ = lru_chunk(s, e, S_, E_, 0, pa_, pr_, 0)
        pa_, pr_ = None, None
        hbi = 0
        for (s, e, S_, E_) in chunks_ctx + chunks_lat[::-1]:
            pa_, pr_ = lru_chunk(s, e, S_, E_, 1, pa_, pr_, hbi)
            hbi ^= 1

        qT, r_q = load_big(0, FA[2], TT)
        kT, r_k = load_big(1, FA[3], TT)
        vtm, r_v = load_big(2, TM[0], NB * 128)
        sgB, r_sgB = load_big(3, FA[4], TT)
        vtm3 = vtm[:, 0:NB * 128].rearrange("p (b d) -> p b d", d=128)
        pTs = [tb[0], tb[1]]
        ystB, r_ystB = tb[2]
        (dn, r_dn), (ob, r_ob) = tf[0], tf[1]
        sc_swa = 128.0 ** -0.5
        qblocks = [(0, [(0, None), (1, None)]), (1, [(0, None), (1, None)])]
        for nq in range(128):
            keys = []
            if nq > 0:
                keys.append((2 + nq - 1, 1))
            keys.append((2 + nq, None))
            if nq < 127:
                keys.append((2 + nq + 1, 0))
            keys += [(0, None), (1, None)]
            qblocks.append((2 + nq, keys))
        for bi, (qb, keys) in enumerate(qblocks):
            Sx, r_S = S[bi % 2]
            po, r_po = PO[bi % 2]
            pd, r_pd = PD[bi % 2]
            pT, r_pT = pTs[bi % 2]
            nk = len(keys)
            qs = slice(qb * 128, (qb + 1) * 128)
            for sl, (kb, mk) in enumerate(keys):
                P.op("pe", lambda e_, Sx=Sx, sl=sl, kb=kb, qs=qs: e_.matmul(
                    Sx[:, sl * 128:(sl + 1) * 128], kT[:, kb * 128:(kb + 1) * 128], qT[:, qs], start=True, stop=True),
                    [r_k, r_q], [r_S])
            P.op("act", lambda e_, Sx=Sx, pT=pT, nk=nk: e_.activation(pT[:, 0:nk * 128], Sx[:, 0:nk * 128], AF.Exp, scale=sc_swa),
                 [r_S], [r_pT])
            for sl, (kb, mk) in enumerate(keys):
                if mk is not None:
                    P.op("dve", lambda e_, pT=pT, sl=sl, mk=mk: e_.tensor_tensor(
                        pT[:, sl * 128:(sl + 1) * 128], pT[:, sl * 128:(sl + 1) * 128], mkbf[:, mk, :], ALU.mult),
                        [r_pT, r_mkbf], [r_pT])
            for sl, (kb, mk) in enumerate(keys):
                P.op("pe", lambda e_, po=po, pT=pT, sl=sl, kb=kb, nk=nk: e_.matmul(
                    po[:, 0:128], vtm3[:, kb, :], pT[:, sl * 128:(sl + 1) * 128], start=(sl == 0), stop=(sl == nk - 1)),
                    [r_v, r_pT], [r_po])
            for sl, (kb, mk) in enumerate(keys):
                P.op("pe", lambda e_, pd=pd, pT=pT, sl=sl, nk=nk: e_.matmul(
                    pd[:, 0:128], ones_bf[:], pT[:, sl * 128:(sl + 1) * 128], start=(sl == 0), stop=(sl == nk - 1)),
                    [r_ones, r_pT], [r_pd])
            P.op("dve", lambda e_, pd=pd: e_.tensor_scalar(dn[:, 0:128], pd[:, 0:128], cc[:, 4:5], None, ALU.add), [r_pd, r_cc], [r_dn])
            P.op("dve", lambda e_: e_.reciprocal(dn[:, 0:128], dn[:, 0:128]), [r_dn], [r_dn])
            P.op("dve", lambda e_, po=po: e_.tensor_tensor(ob[:, 0:128], po[:, 0:128], dn[:, 0:128], ALU.mult), [r_po, r_dn], [r_ob])
            j8 = bi % 8
            P.op("dve", lambda e_, qs=qs, j8=j8: e_.tensor_tensor(ystB[:, j8 * 128:(j8 + 1) * 128], ob[:, 0:128], sgB[:, qs], ALU.mult),
                 [r_ob, r_sgB], [r_ystB])
            if j8 == 7 or bi == len(qblocks) - 1:
                b0 = bi - j8
                q0 = qblocks[b0][0]
                C.out_dma("sp", Y[1][:, q0 * 128:(qb + 1) * 128], ystB[:, 0:(j8 + 1) * 128], [r_ystB])

        knT, r_kn = load_big(0, FA[6], TT)
        krT, r_kr = load_big(1, FB[1], TT, part=64)
        P.op("pool", lambda e_: e_.memset(krT[64:128, 0:TT], 0.0), [], [r_kr])
        vtmC, r_vC = load_big(2, TM[1], NB * 128)
        vtmC3 = vtmC[:, 0:NB * 128].rearrange("p (b d) -> p b d", d=128)
        sc_mla = 192.0 ** -0.5
        qns = [C.sb(f"qn{i}", [128, 512], BF16) for i in range(1)] * 2
        qrs = [C.sb(f"qr{i}", [128, 512], BF16) for i in range(1)] * 2
        P.op("pool", lambda e_: e_.memset(qrs[0][0][64:128, :], 0.0), [], [qrs[0][1]])
        sgs = [C.sb(f"sgc{i}", [128, 512], BF16) for i in range(1)] * 2
        ystC = [C.sb(f"ystc{i}", [128, 512], BF16) for i in range(1)] * 2
        pTC = [tb[0], tb[1], tb[2]]
        (rec, r_rec), (oc, r_oc) = tf[0], tf[1]
        (accA, r_accA), (accB, r_accB) = tf[2], tf[3]
        qtiles = [(0, TC, [(0, 1)])] + [(TC + i * 512, TC + (i + 1) * 512, [(2 * j, 2 * j + 1) for j in range(NB // 2)])
                                        for i in range(32)]
        pcount = 0
        for qi, (q0, q1, pairs) in enumerate(qtiles):
            n = q1 - q0
            qn, r_qn = qns[qi % 2]
            qr, r_qr = qrs[qi % 2]
            sg, r_sg = sgs[qi % 2]
            yc, r_yc = ystC[qi % 2]
            po, r_po = PO[0]
            pd, r_pd = PD[0]
            P.dma("sp", qn[:, 0:n], FA[5][:, q0:q1], writes=[r_qn])
            P.dma("sp", qr[0:64, 0:n], FB[0][:, q0:q1], writes=[r_qr])
            P.dma("sp", sg[:, 0:n], FA[7][:, q0:q1], writes=[r_sg])
            npair = len(pairs)

            def qk(i):
                Sx, r_S = S[(pcount + i) % 3]
                for hh in range(2):
                    kb = pairs[i][hh]
                    P.op("pe", lambda e_, Sx=Sx, hh=hh, kb=kb, n=n, qn=qn, qr=qr: e_.matmul(
                        Sx[:, hh * 512:hh * 512 + n], knT[:, kb * 128:(kb + 1) * 128], qn[:, 0:n], start=True, stop=False),
                        [r_kn, r_qn], [r_S])
                    P.op("pe", lambda e_, Sx=Sx, hh=hh, kb=kb, n=n, qn=qn, qr=qr: e_.matmul(
                        Sx[:, hh * 512:hh * 512 + n], krT[:, kb * 128:(kb + 1) * 128], qr[:, 0:n], start=False, stop=True),
                        [r_kr, r_qr], [r_S])
                pT, r_pT = pTC[(pcount + i) % 3]
                if n == 512:
                    P.op("act", lambda e_, Sx=Sx, pT=pT: e_.activation(pT[:, :], Sx[:, :], AF.Exp, scale=sc_mla), [r_S], [r_pT])
                else:
                    for hh in range(2):
                        P.op("act", lambda e_, Sx=Sx, pT=pT, hh=hh, n=n: e_.activation(
                            pT[:, hh * 512:hh * 512 + n], Sx[:, hh * 512:hh * 512 + n], AF.Exp, scale=sc_mla), [r_S], [r_pT])

            def pv(i):
                pT, r_pT = pTC[(pcount + i) % 3]
                for hh in range(2):
                    kb = pairs[i][hh]
                    first = (i == 0 and hh == 0)
                    last = (i == npair - 1 and hh == 1)
                    P.op("pe", lambda e_, pT=pT, hh=hh, kb=kb, first=first, last=last, n=n, po=po: e_.matmul(
                        po[:, 0:n], vtmC3[:, kb, :], pT[:, hh * 512:hh * 512 + n], start=first, stop=last),
                        [r_vC, r_pT], [r_po])
                    P.op("pe", lambda e_, pT=pT, hh=hh, first=first, last=last, n=n, pd=pd: e_.matmul(
                        pd[:, 0:n], ones_bf[:], pT[:, hh * 512:hh * 512 + n], start=first, stop=last),
                        [r_ones, r_pT], [r_pd])

            qk(0)
            if npair > 1:
                qk(1)
            for i in range(npair):
                if i + 2 < npair:
                    qk(i + 2)
                pv(i)
            pcount += npair
            P.op("dve", lambda e_, pd=pd, n=n: e_.reciprocal(rec[:, 0:n], pd[:, 0:n]), [r_pd], [r_rec])
            P.op("dve", lambda e_, po=po, n=n: e_.tensor_tensor(oc[:, 0:n], po[:, 0:n], rec[:, 0:n], ALU.mult), [r_po, r_rec], [r_oc])
            P.op("pool", lambda e_, yc=yc, sg=sg, n=n: e_.tensor_tensor(yc[:, 0:n], oc[:, 0:n], sg[:, 0:n], ALU.mult), [r_oc, r_sg], [r_yc])
            C.out_dma("sp", Y[2][:, q0:q1], yc[:, 0:n], [r_yc])

        qT, r_q = load_big(0, FA[8], TT)
        kT, r_k = load_big(1, FA[9], TT)
        ktm, r_ktm = load_big(2, TM[2], NB * 128)
        vex, r_vex = load_big(3, TMV, NB * 129)
        hfs, r_hfs = bigs[4]
        ktm3 = ktm[:, 0:NB * 128].rearrange("p (b d) -> p b d", d=128)
        vex3 = vex[:, 0:NB * 129].rearrange("p (b d) -> p b d", d=129)
        hfs3 = hfs[:, 0:NB * 128].rearrange("p (b d) -> p b d", d=128)
        gt, r_gt = C.sb("gt", [128, 4 * NB])
        P.dma("sp", gt[:], GT, writes=[r_gt])
        gt3 = gt[:].rearrange("p (g n) -> p g n", n=NB)
        sc_t = {}
        for nm in ("b", "G", "u", "eb", "eg", "ueg"):
            for d in range(2):
                sc_t[(nm, d)] = C.sb(f"sc_{nm}{d}", [128, NB])
        for d in range(2):
            ig = gt3[:, d, :]
            lf = gt3[:, 2 + d, :]
            pb_, r_pb_ = PO[0]
            pg_, r_pg_ = PO[1]
            (b_, r_b), (G_, r_G), (u_, r_uu), (eb_, r_eb), (eg_, r_eg), (ueg_, r_ueg) = [sc_t[(nm, d)] for nm in ("b", "G", "u", "eb", "eg", "ueg")]
            P.op("pe", lambda e_, d=d, lf=lf, pb_=pb_: e_.matmul(pb_[:, 0:NB], mk32[:, d, :], lf, start=True, stop=True), [r_mk32, r_gt], [r_pb_])
            P.op("pe", lambda e_, lf=lf, pg_=pg_: e_.matmul(pg_[:, 0:NB], ones32[:], lf, start=True, stop=True), [r_ones32, r_gt], [r_pg_])
            P.op("dve", lambda e_, b_=b_, pb_=pb_: e_.tensor_copy(b_[:], pb_[:, 0:NB]), [r_pb_], [r_b])
            P.op("dve", lambda e_, G_=G_, pg_=pg_: e_.tensor_copy(G_[:], pg_[:, 0:NB]), [r_pg_], [r_G])
            P.op("dve", lambda e_, u_=u_, ig=ig, b_=b_: e_.tensor_tensor(u_[:], ig, b_[:], ALU.subtract), [r_gt, r_b], [r_uu])
            P.op("dve", lambda e_, ueg_=ueg_, u_=u_, G_=G_: e_.tensor_tensor(ueg_[:], u_[:], G_[:], ALU.add), [r_uu, r_G], [r_ueg])
            P.op("act", lambda e_, u_=u_: e_.activation(u_[:], u_[:], AF.Exp), [r_uu], [r_uu])
            P.op("act", lambda e_, ueg_=ueg_: e_.activation(ueg_[:], ueg_[:], AF.Exp), [r_ueg], [r_ueg])
            P.op("act", lambda e_, eb_=eb_, b_=b_: e_.activation(eb_[:], b_[:], AF.Exp), [r_b], [r_eb])
            P.op("act", lambda e_, eg_=eg_, G_=G_: e_.activation(eg_[:], G_[:], AF.Exp), [r_G], [r_eg])

        C32, r_C32 = C.sb("C32", [128, 129])
        Cbfs = [C.sb(f"Cbf{i}", [128, 129], BF16) for i in range(2)]
        pTd = [C.sb(f"pTd{i}", [128, 128], BF16) for i in range(2)]
        vps = [C.sb(f"vp{i}", [128, 129], BF16) for i in range(2)]
        vpps = [C.sb(f"vpp{i}", [128, 129], BF16) for i in range(2)]
        dds = [C.sb(f"dd{i}", [128, 2]) for i in range(2)]
        hss = [C.sb(f"hs{i}", [128, 128]) for i in range(2)]
        hns = [C.sb(f"hn{i}", [128, 128]) for i in range(2)]
        ssd = [C.sb(f"ssd{i}", [128, 1]) for i in range(2)]
        junkd, r_junkd = C.sb("junkd", [128, 128], BF16)
        (yt1, r_yt1) = tf[2]
        so_t, r_so = tb[0]
        sg_t, r_sgd = tb[1]
        ystD, r_ystD = tb[2]

        def mlstm_dir(d, order, groups):
            (b_, r_b), (G_, r_G), (u_, r_uu), (eb_, r_eb), (eg_, r_eg), (ueg_, r_ueg) = [sc_t[(nm, d)] for nm in ("b", "G", "u", "eb", "eg", "ueg")]
            P.op("pool", lambda e_: e_.memset(C32[:], 0.0), [], [r_C32])
            P.op("pool", lambda e_: e_.memset(Cbfs[0][0][:], 0.0), [], [Cbfs[0][1]])
            grp_of = {}
            for (lo, hi) in groups:
                for c in range(lo, hi):
                    grp_of[c] = (lo, hi)
            def stage1(ci):
                n = order[ci]
                par = ci % 2
                cs = slice(n * 128, (n + 1) * 128)
                Sx, r_S = S[par]
                pk_, r_pk = PO[par]
                pT, r_pT = pTd[par]
                vp, r_vp = vps[par]
                vpp, r_vpp = vpps[par]
                P.op("pe", lambda e_: e_.matmul(Sx[:, 0:128], kT[:, cs], qT[:, cs], start=True, stop=True), [r_k, r_q], [r_S])
                P.op("dve", lambda e_: e_.tensor_tensor(pT[:], Sx[:, 0:128], mkbf[:, d, :], ALU.mult), [r_S, r_mkbf], [r_pT])
                P.op("act", lambda e_: e_.mul(vp[:], vex3[:, n, :], u_[:, n:n + 1]), [r_vex, r_uu], [r_vp])
                P.op("act", lambda e_: e_.mul(vpp[:], vex3[:, n, :], ueg_[:, n:n + 1]), [r_vex, r_ueg], [r_vpp])
                P.op("pe", lambda e_: e_.matmul(pk_[:, 0:129], ktm3[:, n, :], vpp[:], start=True, stop=True), [r_ktm, r_vpp], [r_pk])

            def stage2(ci):
                n = order[ci]
                par = ci % 2
                cs = slice(n * 128, (n + 1) * 128)
                Sx, r_S = S[par]
                pk_, r_pk = PO[par]
                ptt, r_ptt = PD[par]
                Cb, r_Cb = Cbfs[par]
                Cn, r_Cn = Cbfs[1 - par]
                pT, r_pT = pTd[par]
                vp, r_vp = vps[par]
                dd, r_dd = dds[par]
                if d == 1 and n in grp_of and (ci == 0 or grp_of[order[ci - 1]] != grp_of[n]):
                    lo, hi = grp_of[n]
                    P.dma("sp", so_t[:, 0:(hi - lo) * 128], FA[10][:, lo * 128:hi * 128], writes=[r_so])
                    P.dma("sp", sg_t[:, 0:(hi - lo) * 128], FA[11][:, lo * 128:hi * 128], writes=[r_sgd])
                P.op("pe", lambda e_: e_.matmul(Sx[:, 512:641], qT[:, cs], Cb[:], start=True, stop=False), [r_q, r_Cb], [r_S])
                P.op("pe", lambda e_: e_.matmul(Sx[:, 512:641], pT[:], vp[:], start=False, stop=True), [r_pT, r_vp], [r_S])
                P.op("dve", lambda e_: e_.scalar_tensor_tensor(C32[:], C32[:], eg_[:, n:n + 1], pk_[:, 0:129], ALU.mult, ALU.add),
                     [r_C32, r_eg, r_pk], [r_C32])
                P.op("act", lambda e_: e_.copy(Cn[:], C32[:]), [r_C32], [r_Cn])
                P.op("dve", lambda e_: e_.tensor_scalar(dd[:, 0:1], Sx[:, 640:641], eb_[:, n:n + 1], None, ALU.mult),
                     [r_S, r_eb], [r_dd])
                P.op("dve", lambda e_: e_.tensor_scalar(dd[:, 1:2], dd[:, 0:1], -1.0, -1.0, ALU.min, ALU.mult), [r_dd], [r_dd])
                P.op("dve", lambda e_: e_.scalar_tensor_tensor(dd[:, 0:1], dd[:, 0:1], 1.0, dd[:, 1:2], ALU.max, ALU.max), [r_dd], [r_dd])
                P.op("dve", lambda e_: e_.reciprocal(dd[:, 0:1], dd[:, 0:1]), [r_dd], [r_dd])
                P.op("dve", lambda e_: e_.tensor_tensor(dd[:, 1:2], dd[:, 0:1], eb_[:, n:n + 1], ALU.mult), [r_dd, r_eb], [r_dd])
                if d == 0:
                    P.op("dve", lambda e_: e_.tensor_scalar(hfs3[:, n, :], Sx[:, 512:640], dd[:, 1:2], None, ALU.mult),
                         [r_S, r_dd], [r_hfs])
                    return
                hs, r_hs = hss[par]
                hnn, r_hnn = hns[par]
                ss_, r_ss = ssd[par]
                P.op("dve", lambda e_: e_.scalar_tensor_tensor(
                    hs[:], Sx[:, 512:640], dd[:, 1:2], hfs3[:, n, :], ALU.mult, ALU.add), [r_S, r_dd, r_hfs], [r_hs])
                P.op("pool", lambda e_: e_.memset(ss_[:], 0.0), [], [r_ss])
                P.op("act", lambda e_: e_.activation(junkd[:], hs[:], AF.Square, accum_out=ss_[:]), [r_hs], [r_junkd, r_ss])
                P.op("act", lambda e_: e_.activation(ss_[:], ss_[:], AF.Sqrt, bias=epsc[:], scale=1.0 / 128), [r_ss, r_eps], [r_ss])
                P.op("dve", lambda e_: e_.reciprocal(ss_[:], ss_[:]), [r_ss], [r_ss])
                P.op("dve", lambda e_: e_.scalar_tensor_tensor(hnn[:], hs[:], ss_[:], hn[:], ALU.mult, ALU.mult),
                     [r_hs, r_ss, r_hn], [r_hnn])
                P.op("pe", lambda e_: e_.transpose(ptt[:, 0:128], hnn[:], ident[:]), [r_hnn, r_ident], [r_ptt])
                lo, hi = grp_of[n]
                off = (n - lo) * 128
                P.op("dve", lambda e_: e_.tensor_tensor(yt1[:, off:off + 128], ptt[:, 0:128], so_t[:, off:off + 128], ALU.mult),
                     [r_ptt, r_so], [r_yt1])
                P.op("pool", lambda e_: e_.tensor_tensor(ystD[:, off:off + 128], yt1[:, off:off + 128], sg_t[:, off:off + 128], ALU.mult),
                     [r_yt1, r_sgd], [r_ystD])
                if ci == len(order) - 1 or grp_of[order[ci + 1]] != (lo, hi):
                    C.out_dma("sp", Y[3][:, lo * 128:hi * 128], ystD[:, 0:(hi - lo) * 128], [r_ystD])

            stage1(0)
            for ci in range(len(order)):
                if ci + 1 < len(order):
                    stage1(ci + 1)
                stage2(ci)

        groups = [(0, 2)] + [(2 + 4 * i, 2 + 4 * (i + 1)) for i in range(32)]
        mlstm_dir(0, list(range(NB)), groups)
        mlstm_dir(1, [1, 0] + list(range(NB - 1, 1, -1)), groups)
        C.finish()
    return nc


def _gather_tokens(per_core):
    ctx = np.concatenate([a[:, TL:NT] for a in per_core], axis=1)
    lat = np.concatenate([a[:, 0:TL] for a in per_core], axis=1)
    return np.concatenate([ctx, lat], axis=1)


def _tok_major(item):
    return np.ascontiguousarray(item.T.reshape(NB, 128, 128).transpose(1, 0, 2)).reshape(128, NB * 128)


def host_p2_inputs(res1, l, inp):
    rows, nrows = p1_out_rows()
    G = _gather_tokens([r["O1"] for r in res1])
    Gg = _gather_tokens([r["O1g"] for r in res1])

    def item(name, sub=None):
        r0, m = rows[name]
        if sub is not None:
            return G[r0 + sub[0]:r0 + sub[1]]
        return G[r0:r0 + m]

    p_ = np.arange(128)
    MK = np.stack([(p_[:, None] <= p_[None, :]), (p_[:, None] >= p_[None, :])]).astype(np.float32)
    ident = np.eye(128, dtype=np.float32)
    maps = []
    for d in range(NCORE):
        FA = np.stack([item(f"A_u{d}"), item(f"A_g{d}"), item(f"B_q{d}"), item(f"B_k{d // 4}"), item(f"B_g{d}"),
                       item(f"C_qn{d}"), item(f"C_kn{d}"), item(f"C_g{d}"), item(f"D_q{d}"), item(f"D_k{d}"),
                       item(f"D_o{d}"), item(f"D_g{d}")])
        FB = np.stack([item(f"C_qr{d // 2}", ((d % 2) * 64, (d % 2) * 64 + 64)), item("C_kr")])
        TM = np.stack([_tok_major(item(f"B_v{d // 4}")), _tok_major(item(f"C_v{d}")), _tok_major(item(f"D_k{d}"))])
        vex = np.ones((TT, 129), NPBF)
        vex[:, :128] = item(f"D_v{d}").T
        TMV = np.ascontiguousarray(vex.reshape(NB, 128, 129).transpose(1, 0, 2)).reshape(128, NB * 129)
        gsel = Gg[[d, 8 + d, 16 + d, 24 + d]]
        GT = np.ascontiguousarray(gsel.reshape(4, NB, 128).transpose(2, 0, 1)).reshape(128, 4 * NB)
        ch = slice(d * 128, (d + 1) * 128)
        LP = np.zeros((128, 16), np.float32)
        LP[:, 0:4] = inp["lru_conv_w"][l][:, ch].T
        LP[:, 4] = inp["lru_conv_b"][l][ch]
        LP[:, 5:7] = inp["lru_lambda"][l][:, ch].T
        LP[:, 7:9] = inp["lru_ba"][l][:, ch].T
        LP[:, 9:11] = inp["lru_bx"][l][:, ch].T
        LP[:, 11] = inp["swa_sink"][l][d]
        LW = np.stack([inp["lru_wa"][l][0, d], inp["lru_wx"][l][0, d], inp["lru_wa"][l][1, d], inp["lru_wx"][l][1, d]]).astype(np.float32)
        HN = np.ascontiguousarray(np.broadcast_to(inp["ml_head_norm"][l][ch].astype(np.float32), (128, 128)))
        maps.append({"FA": FA, "FB": FB, "TM": TM, "TMV": TMV, "GT": GT, "LP": LP, "LW": LW, "HN": HN, "MK": MK, "ident": ident})
    return maps


def host_p3_inputs(res2, x_full, xc_full, modT_l, l, inp):
    Yall = np.stack([r["Y"] for r in res2])
    Yt = np.ascontiguousarray(Yall.transpose(3, 1, 0, 2)).reshape(TT, D)
    common = host_p3_common(modT_l, l, inp)
    maps = []
    for core in range(NCORE):
        y_loc = np.concatenate([Yt[TC + core * TL:TC + (core + 1) * TL], Yt[core * CL:(core + 1) * CL]], axis=0)
        m = dict(common)
        m["yT"] = host_yT_tiles(y_loc)
        m["x"] = np.concatenate([x_full[core * TL:(core + 1) * TL], xc_full[core * CL:(core + 1) * CL]], axis=0)
        maps.append(m)
    return maps


def kernel(**inputs):
    inp = {k: np.asarray(v) for k, v in inputs.items()}
    cores = list(range(NCORE))
    x_full = np.ascontiguousarray(inp["x"][0], dtype=np.float32)
    xc_full = np.ascontiguousarray(inp["ctx"][0], dtype=np.float32)
    res0 = run_bass_kernel_spmd(build_p0(), host_p0_inputs(inp["c"], inp["c_ctx"], inp["w_mod"], inp["b_mod"]), core_ids=cores)
    modT = host_p0_gather(res0.results)
    nc1, nc2, nc3 = build_p1(), build_p2(), build_p3()
    for l in range(2):
        res1 = run_bass_kernel_spmd(nc1, host_p1_inputs(x_full, xc_full, modT[l], l, inp), core_ids=cores).results
        maps2 = host_p2_inputs(res1, l, inp)
        del res1
        res2 = run_bass_kernel_spmd(nc2, maps2, core_ids=cores).results
        del maps2
        maps3 = host_p3_inputs(res2, x_full, xc_full, modT[l], l, inp)
        del res2
        res3 = run_bass_kernel_spmd(nc3, maps3, core_ids=cores).results
        del maps3
        x_full = np.concatenate([r["xo"][0:TL] for r in res3], axis=0)
        xc_full = np.concatenate([r["xo"][TL:NT] for r in res3], axis=0)
    return x_full.reshape(1, T, D).astype(np.float32)
```

```python
import numpy as np
import ml_dtypes
from contextlib import ExitStack
import concourse.bass as bass
import concourse.mybir as mybir
from concourse.bass_utils import run_bass_kernel_spmd

F32 = mybir.dt.float32
BF16 = mybir.dt.bfloat16
AF = mybir.ActivationFunctionType
ALU = mybir.AluOpType
NPBF = ml_dtypes.bfloat16

NCORE = 8
D = 4096
T = 16384
TC = 256
TL = T // NCORE
CL = TC // NCORE
NT = TL + CL
TT = T + TC
NB = TT // 128
EPS = 1e-6
KC = D // 128
HALVES = ((0, 1024), (1024, NT))

O_AU, O_AG, O_BQ, O_BK, O_BV, O_BG = 0, 1024, 2048, 3072, 3328, 3584
O_CQ, O_CKV, O_KR, O_CG = 4608, 5376, 5632, 5696
O_DQ, O_DK, O_DV, O_DO, O_DGT, O_DG = 6720, 7744, 8768, 9792, 10816, 10848

SEM_CH = 16000
N_DMA_SEMS = 24


class Res:
    __slots__ = ("name", "w", "rs")

    def __init__(self, name):
        self.name = name
        self.w = None
        self.rs = []


class Op:
    __slots__ = ("eng", "fn", "deps", "sig", "sem", "semv", "dma", "gi")

    def __init__(self, eng, fn, dma, gi):
        self.eng = eng
        self.fn = fn
        self.dma = dma
        self.deps = set()
        self.sig = False
        self.sem = None
        self.semv = 0
        self.gi = gi


class Prog:
    ENGS = ("pe", "act", "dve", "pool", "sp")

    def __init__(self, nc):
        self.nc = nc
        self.ops = {e: [] for e in self.ENGS}
        self.gi = 0
        self.all_ops = []

    def res(self, name=""):
        return Res(name)

    def op(self, eng, fn, reads=(), writes=(), dma=False):
        o = Op(eng, fn, dma, self.gi)
        self.gi += 1
        deps = o.deps
        for r in reads:
            if r.w is not None:
                deps.add(r.w)
        for r in writes:
            if r.w is not None:
                deps.add(r.w)
            deps.update(r.rs)
        for r in reads:
            r.rs.append(o)
        for r in writes:
            r.w = o
            r.rs = []
        deps.discard(o)
        self.ops[eng].append(o)
        self.all_ops.append(o)
        return o

    def dma(self, eng, out, in_, reads=(), writes=(), **kw):
        return self.op(eng, lambda e: e.dma_start(out=out, in_=in_, **kw), reads, writes, dma=True)

    def emit(self, stack):
        nc = self.nc
        dma_last, dma_cnt, dma_sems = {}, {}, {}
        rr = {e: 0 for e in self.ENGS}
        for o in self.all_ops:
            if o.dma:
                key = (o.eng, rr[o.eng] % N_DMA_SEMS)
                rr[o.eng] += 1
                if key not in dma_sems:
                    dma_sems[key] = stack.enter_context(nc.semaphore(f"d_{o.eng}_{key[1]}"))
                    dma_cnt[key] = 0
                prev = dma_last.get(key)
                if prev is not None:
                    o.deps.add(prev)
                dma_last[key] = o
                dma_cnt[key] += 16
                o.sem = dma_sems[key]
                o.semv = dma_cnt[key]
                o.sig = True
        for o in self.all_ops:
            for d in o.deps:
                if d.eng == "pe" and o.eng == "pe" and not d.dma:
                    continue
                d.sig = True
        for e in self.ENGS:
            cnt = 0
            cur = None
            for o in self.ops[e]:
                if o.dma or not o.sig or o.fn is None:
                    continue
                if cnt % SEM_CH == 0:
                    cur = stack.enter_context(nc.semaphore(f"c_{e}_{cnt // SEM_CH}"))
                o.sem = cur
                o.semv = cnt % SEM_CH + 1
                cnt += 1
        block = stack.enter_context(nc.Block())

        def run(ename, eng):
            waited = {}
            for o in self.ops[ename]:
                for d in sorted(o.deps, key=lambda d: d.gi):
                    if d.eng == "pe" and ename == "pe" and not d.dma:
                        continue
                    k = id(d.sem)
                    if waited.get(k, 0) < d.semv:
                        eng.wait_ge(d.sem, d.semv)
                        waited[k] = d.semv
                if o.fn is None:
                    continue
                ins = o.fn(eng)
                if o.sig:
                    ins.then_inc(o.sem, 16 if o.dma else 1)

        if self.ops["sp"]:
            @block.sync
            def _(e):
                run("sp", e)
        if self.ops["pe"]:
            @block.tensor
            def _(e):
                run("pe", e)
        if self.ops["act"]:
            @block.scalar
            def _(e):
                run("act", e)
        if self.ops["dve"]:
            @block.vector
            def _(e):
                run("dve", e)
        if self.ops["pool"]:
            @block.gpsimd
            def _(e):
                run("pool", e)


class Ctx:
    def __init__(self, nc, st):
        self.nc = nc
        self.st = st
        self.P = Prog(nc)
        self.outs = []

    def dram_in(self, name, shape, dt=F32):
        return self.nc.dram_tensor(name, list(shape), dt, kind="ExternalInput").ap()

    def dram_out(self, name, shape, dt=F32):
        return self.nc.dram_tensor(name, list(shape), dt, kind="ExternalOutput").ap()

    def sb(self, name, shape, dt=F32):
        t = self.st.enter_context(self.nc.sbuf_tensor(name, list(shape), dt))
        return t, self.P.res(name)

    def ps(self, name, shape, dt=F32):
        t = self.st.enter_context(self.nc.psum_tensor(name, list(shape), dt))
        return t, self.P.res(name)

    def psum_banks(self, n, cols=512, prefix="pb"):
        self.banks = [self.ps(f"{prefix}{i}", [128, cols]) for i in range(n)]
        self.bank_i = 0

    def bank(self):
        b = self.banks[self.bank_i % len(self.banks)]
        self.bank_i += 1
        return b

    def out_dma(self, eng, out_ap, in_ap, reads):
        r = self.P.res("o")
        self.P.dma(eng, out_ap, in_ap, reads=reads, writes=[r])
        self.outs.append(r)

    def finish(self):
        self.P.op("sp", None, reads=self.outs)
        self.P.emit(self.st)


NJ0 = 24


def build_p0():
    nc = bass.Bass("TRN2", target_bir_lowering=False)
    with ExitStack() as st:
        C = Ctx(nc, st)
        P = C.P
        wm = C.dram_in("wm", [NJ0, 128, KC * 128])
        cvec = C.dram_in("cvec", [128, KC * 2])
        bm = C.dram_in("bm", [128, NJ0])
        o = C.dram_out("modT", [128, NJ0 * 2])
        cv, r_cv = C.sb("cv", [128, KC * 2])
        sc, r_sc = C.sb("sc", [128, KC * 2])
        bmt, r_bm = C.sb("bmt", [128, NJ0])
        ot, r_ot = C.sb("ot", [128, NJ0 * 2])
        wts = [C.sb(f"w{i}", [128, KC * 128]) for i in range(2)]
        C.psum_banks(2, cols=2)
        P.dma("sp", cv[:], cvec, writes=[r_cv])
        P.dma("sp", bmt[:], bm, writes=[r_bm])
        P.op("act", lambda e: e.activation(sc[:], cv[:], AF.Silu), [r_cv], [r_sc])
        for i in range(NJ0):
            wt, r_w = wts[i % 2]
            P.dma("sp" if i % 2 == 0 else "pool", wt[:], wm[i], writes=[r_w])
            pb, r_pb = C.bank()
            for kc in range(KC):
                P.op("pe", lambda e, wt=wt, pb=pb, kc=kc: e.matmul(
                    pb[:, 0:2], wt[:, kc * 128:(kc + 1) * 128], sc[:, kc * 2:kc * 2 + 2],
                    start=(kc == 0), stop=(kc == KC - 1)), [r_w, r_sc], [r_pb])
            P.op("dve", lambda e, pb=pb, i=i: e.tensor_scalar(
                ot[:, 2 * i:2 * i + 2], pb[:, 0:2], bmt[:, i:i + 1], None, ALU.add), [r_pb, r_bm], [r_ot])
        C.out_dma("sp", o, ot[:], [r_ot])
        C.finish()
    return nc


def _perm(idx, half):
    return np.asarray(idx) ^ half


def p1_groups():
    g = []
    r128 = np.arange(128)
    for j in range(6):
        g.append((f"cq{j}", O_CQ + j * 128 + r128, "cq", j))
    for j in range(2):
        g.append((f"ckv{j}", O_CKV + j * 128 + r128, "ckv", j))
    for h in range(8):
        g.append((f"A_u{h}", O_AU + h * 128 + r128, "copy", None))
    for kv in range(2):
        g.append((f"B_v{kv}", O_BV + kv * 128 + r128, "copy", None))
    for h in range(8):
        g.append((f"D_q{h}", O_DQ + h * 128 + r128, "copy", None))
    for h in range(8):
        g.append((f"D_v{h}", O_DV + h * 128 + r128, "copy", None))
    for h in range(8):
        g.append((f"D_k{h}", O_DK + h * 128 + r128, "scale", 128.0 ** -0.5))
    for nm, off in (("A_g", O_AG), ("B_g", O_BG), ("C_g", O_CG), ("D_g", O_DG)):
        for h in range(8):
            g.append((f"{nm}{h}", off + h * 128 + r128, "silu", None))
    for h in range(8):
        g.append((f"D_o{h}", O_DO + h * 128 + r128, "sigmoid", None))
    for h in range(8):
        g.append((f"B_q{h}", O_BQ + h * 128 + r128, "ropeA", "B"))
        g.append((f"B_q{h}", O_BQ + h * 128 + _perm(r128, 32), "ropeB", "B"))
    for kv in range(2):
        g.append((f"B_k{kv}", O_BK + kv * 128 + r128, "ropeA", "B"))
        g.append((f"B_k{kv}", O_BK + kv * 128 + _perm(r128, 32), "ropeB", "B"))
    r64 = np.arange(64)
    g.append(("C_kr", O_KR + r64, "ropeA", "C"))
    g.append(("C_kr", O_KR + _perm(r64, 16), "ropeB", "C"))
    r8 = np.arange(8)
    g.append(("G_i", np.concatenate([O_DGT + 0 * 8 + r8, O_DGT + 2 * 8 + r8]), "gi", None))
    g.append(("G_f", np.concatenate([O_DGT + 1 * 8 + r8, O_DGT + 3 * 8 + r8]), "gf", None))
    return g


def p1_groups2():
    g = []
    r128 = np.arange(128)
    r64 = np.arange(64)
    for h in range(8):
        g.append((f"C_qn{h}", "q", h * 192 + r128, "mulr", "q"))
    for h in range(8):
        g.append((f"C_kn{h}", "kv", h * 256 + r128, "mulr", "kv"))
    for h in range(8):
        g.append((f"C_v{h}", "kv", h * 256 + 128 + r128, "mulr", "kv"))
    for hp in range(4):
        base = np.concatenate([(2 * hp) * 192 + 128 + r64, (2 * hp + 1) * 192 + 128 + r64])
        perm = np.concatenate([(2 * hp) * 192 + 128 + _perm(r64, 16), (2 * hp + 1) * 192 + 128 + _perm(r64, 16)])
        g.append((f"C_qr{hp}", "q", base, "ropeA", "Cq"))
        g.append((f"C_qr{hp}", "q", perm, "ropeB", "Cq"))
    return g


def p1_out_rows():
    rows = {}
    r = 0
    for name, cols, epi, arg in p1_groups():
        if epi in ("cq", "ckv", "gi", "gf", "ropeA"):
            continue
        rows[name] = (r, len(cols))
        r += len(cols)
    for name, which, cols, epi, arg in p1_groups2():
        if epi == "ropeA":
            continue
        rows[name] = (r, len(cols))
        r += len(cols)
    return rows, r


def build_p1():
    g1 = p1_groups()
    g2 = p1_groups2()
    rows, nrows = p1_out_rows()
    NG1 = len(g1)
    g2q = [x for x in g2 if x[1] == "q"]
    g2kv = [x for x in g2 if x[1] == "kv"]
    nc = bass.Bass("TRN2", target_bir_lowering=False)
    with ExitStack() as st:
        C = Ctx(nc, st)
        P = C.P
        x = C.dram_in("x", [NT, D])
        modT = C.dram_in("modT", [128, 96 * 2])
        gpre = C.dram_in("gpre", [128, KC])
        w1 = C.dram_in("w1", [NG1, 128, KC * 128])
        w2q = C.dram_in("w2q", [len(g2q), 128, 6 * 128])
        w2kv = C.dram_in("w2kv", [len(g2kv), 128, 2 * 128])
        gq = C.dram_in("gq", [128, 6])
        gkv = C.dram_in("gkv", [128, 2])
        gbias = C.dram_in("gbias", [16, 2])
        cosB = C.dram_in("cosB", [128, NT])
        sinB = C.dram_in("sinB", [128, NT])
        cosC = C.dram_in("cosC", [128, NT])
        sinC = C.dram_in("sinC", [128, NT])
        identd = C.dram_in("ident", [128, 128])
        O1 = C.dram_out("O1", [nrows, NT], BF16)
        O1g = C.dram_out("O1g", [32, NT])

        HW = 1056
        hT, r_hT = C.sb("hT", [128, KC, HW], BF16)
        xt, r_xt = C.sb("xt", [128, D])
        junk, r_junk = C.sb("junk", [128, D // 2], BF16)
        wsts = [C.sb(f"wst{i}", [128, KC * 128]) for i in range(2)]
        wbf = [C.sb(f"wbf{i}", [128, KC, 128], BF16) for i in range(2)]
        wbf_hi = [P.res(f"wbfhi{i}") for i in range(2)]
        wld = [0]
        cqT, r_cqT = C.sb("cqT", [128, 6, HW], BF16)
        ckvT, r_ckvT = C.sb("ckvT", [128, 2, HW], BF16)
        ssq_q, r_ssq_q = C.sb("ssq_q", [128, HW])
        ssq_kv, r_ssq_kv = C.sb("ssq_kv", [128, HW])
        tcos = {k: C.sb(f"tcos{k}", [128, HW]) for k in ("B", "C", "Cq")}
        tsin = {k: C.sb(f"tsin{k}", [128, HW]) for k in ("B", "C", "Cq")}
        tmpA, r_tmpA = C.sb("tmpA", [128, HW])
        gtmp, r_gtmp = tmpA, r_tmpA
        tmpB = [C.sb(f"tmpB{i}", [128, 512]) for i in range(2)]
        sqt = [C.sb(f"sqt{i}", [128, 512], BF16) for i in range(2)]
        ost = [C.sb(f"ost{i}", [128, HW], BF16) for i in range(2)]
        ostg, r_ostg = C.sb("ostg", [16, HW])
        ident, r_ident = C.sb("identt", [128, 128])
        ones_bf, r_ones = C.sb("ones_bf", [128, 128], BF16)
        mod, r_mod = C.sb("mod", [128, 96 * 2])
        gp, r_gp = C.sb("gp", [128, KC])
        s1p, r_s1p = C.sb("s1p", [128, KC * 2])
        gqt, r_gqt = C.sb("gqt", [128, 6])
        gkvt, r_gkvt = C.sb("gkvt", [128, 2])
        gbt, r_gbt = C.sb("gbt", [16, 2])
        ngbt, r_ngbt = C.sb("ngbt", [16, 2])
        ssq, r_ssq = C.sb("ssq", [128, 2])
        epsc, r_eps = C.sb("epsc", [128, 1])
        onec, r_onec = C.sb("onec", [128, 1])
        C.psum_banks(8)

        P.dma("sp", ident[:], identd, writes=[r_ident])
        P.dma("sp", mod[:], modT, writes=[r_mod])
        P.dma("sp", gp[:], gpre, writes=[r_gp])
        P.dma("sp", gqt[:], gq, writes=[r_gqt])
        P.dma("sp", gkvt[:], gkv, writes=[r_gkvt])
        P.dma("sp", gbt[:], gbias, writes=[r_gbt])
        P.op("pool", lambda e: e.memset(ones_bf[:], 1.0), [], [r_ones])
        P.op("pool", lambda e: e.memset(epsc[:], EPS), [], [r_eps])
        P.op("pool", lambda e: e.memset(onec[:], 1.0), [], [r_onec])
        P.op("dve", lambda e: e.tensor_scalar(ngbt[:], gbt[:], -1.0, None, ALU.mult), [r_gbt], [r_ngbt])
        mod3 = mod[:].rearrange("p (j v) -> p j v", v=2)
        s1p3 = s1p[:].rearrange("p (j v) -> p j v", v=2)
        for v in range(2):
            P.op("dve", lambda e, v=v: e.scalar_tensor_tensor(
                s1p3[:, :, v], mod3[:, 32:64, v], 1.0, gp[:], ALU.add, ALU.mult), [r_mod, r_gp], [r_s1p])

        wq_i = 0
        for (h0, h1) in HALVES:
            nh = h1 - h0
            subs = [(c0, min(c0 + 512, nh)) for c0 in range(0, nh, 512)]
            for k, (cd, sd) in (("B", (cosB, sinB)), ("C", (cosC, sinC))):
                P.dma("pool", tcos[k][0][:, 0:nh], cd[:, h0:h1], writes=[tcos[k][1]])
                P.dma("pool", tsin[k][0][:, 0:nh], sd[:, h0:h1], writes=[tsin[k][1]])
            tok = h0
            while tok < h1:
                n = min(128, h1 - tok, (TL - tok) if tok < TL else 128)
                v = 0 if tok < TL else 1
                P.dma("sp", xt[0:n, :], x[tok:tok + n, :], writes=[r_xt])
                P.op("pool", lambda e: e.memset(ssq[:], 0.0), [], [r_ssq])
                for hh in range(2):
                    P.op("act", lambda e, n=n, hh=hh: e.activation(
                        junk[0:n, :], xt[0:n, hh * (D // 2):(hh + 1) * (D // 2)], AF.Square, accum_out=ssq[0:n, hh:hh + 1]),
                        [r_xt], [r_junk, r_ssq])
                P.op("dve", lambda e, n=n: e.tensor_tensor(ssq[0:n, 0:1], ssq[0:n, 0:1], ssq[0:n, 1:2], ALU.add), [r_ssq], [r_ssq])
                P.op("act", lambda e, n=n: e.activation(ssq[0:n, 0:1], ssq[0:n, 0:1], AF.Sqrt, bias=epsc[0:n, :], scale=1.0 / D),
                     [r_ssq, r_eps], [r_ssq])
                P.op("dve", lambda e, n=n: e.reciprocal(ssq[0:n, 0:1], ssq[0:n, 0:1]), [r_ssq], [r_ssq])
                P.op("dve", lambda e, n=n: e.tensor_scalar(xt[0:n, :], xt[0:n, :], ssq[0:n, 0:1], None, ALU.mult),
                     [r_xt, r_ssq], [r_xt])
                for kc0 in range(0, KC, 4):
                    pb, r_pb = C.bank()
                    for q in range(4):
                        kc = kc0 + q
                        P.op("pe", lambda e, pb=pb, q=q, kc=kc, n=n: e.transpose(
                            pb[:, q * 128:q * 128 + n], xt[0:n, kc * 128:(kc + 1) * 128], ident[0:n, 0:n]),
                            [r_xt, r_ident], [r_pb])
                    for q in range(4):
                        kc = kc0 + q
                        hdst = hT[:, kc, tok - h0:tok - h0 + n]
                        psrc = pb[:, q * 128:q * 128 + n]
                        sc_ = s1p[:, kc * 2 + v:kc * 2 + v + 1]
                        bi_ = mod[:, kc * 2 + v:kc * 2 + v + 1]
                        if q % 2 == 0:
                            P.op("dve", lambda e, hdst=hdst, psrc=psrc, sc_=sc_, bi_=bi_: e.tensor_scalar(
                                hdst, psrc, sc_, bi_, ALU.mult, ALU.add), [r_pb, r_s1p, r_mod], [r_hT])
                        else:
                            P.op("act", lambda e, hdst=hdst, psrc=psrc, sc_=sc_, bi_=bi_: e.activation(
                                hdst, psrc, AF.Identity, bias=bi_, scale=sc_), [r_pb, r_s1p, r_mod], [r_hT])
                tok += n

            def load_w(src_ap, ncol, gi_):
                wb, r_wb = wbf[gi_ % 2]
                r_hi = wbf_hi[gi_ % 2]
                wst, r_wst = wsts[wld[0] % 2]
                wld[0] += 1
                P.dma("sp", wst[:, 0:ncol], src_ap, writes=[r_wst])
                wflat = wb[:].rearrange("p k n -> p (k n)")
                hc = ncol // 2
                P.op("act", lambda e: e.copy(wflat[:, 0:hc], wst[:, 0:hc]), [r_wst], [r_wb])
                P.op("dve", lambda e: e.tensor_copy(wflat[:, hc:ncol], wst[:, hc:ncol]), [r_wst], [r_hi])
                return wb, r_wb, r_hi

            for gi_, (name, cols, epi, arg) in enumerate(g1):
                M = len(cols)
                wb, r_wb, r_whi = load_w(w1[gi_], KC * 128, gi_)
                osb, r_osb = ost[gi_ % 2]
                for si, (c0, c1) in enumerate(subs):
                    n = c1 - c0
                    pb, r_pb = C.bank()
                    for kc in range(KC):
                        P.op("pe", lambda e, pb=pb, wb=wb, kc=kc, M=M, c0=c0, c1=c1, n=n: e.matmul(
                            pb[0:M, 0:n], wb[:, kc, 0:M], hT[:, kc, c0:c1], start=(kc == 0), stop=(kc == KC - 1)),
                            [r_wb if kc < KC // 2 else r_whi, r_hT], [r_pb])
                    src = pb[0:M, 0:n]
                    dst = osb[0:M, c0:c1]
                    if epi == "copy":
                        P.op("act", lambda e, src=src, dst=dst: e.copy(dst, src), [r_pb], [r_osb])
                    elif epi == "scale":
                        P.op("act", lambda e, src=src, dst=dst, arg=arg: e.mul(dst, src, arg), [r_pb], [r_osb])
                    elif epi == "silu":
                        P.op("act", lambda e, src=src, dst=dst: e.activation(dst, src, AF.Silu), [r_pb], [r_osb])
                    elif epi == "sigmoid":
                        P.op("act", lambda e, src=src, dst=dst: e.activation(dst, src, AF.Sigmoid), [r_pb], [r_osb])
                    elif epi in ("cq", "ckv"):
                        tgt, r_tgt = (cqT, r_cqT) if epi == "cq" else (ckvT, r_ckvT)
                        acc, r_acc = (ssq_q, r_ssq_q) if epi == "cq" else (ssq_kv, r_ssq_kv)
                        sq, r_sq = sqt[si % 2]
                        P.op("act", lambda e, src=src, tgt=tgt, arg=arg, c0=c0, c1=c1: e.copy(tgt[:, arg, c0:c1], src),
                             [r_pb], [r_tgt])
                        P.op("act", lambda e, src=src, sq=sq, n=n: e.activation(sq[:, 0:n], src, AF.Square),
                             [r_pb], [r_sq])
                        pb2, r_pb2 = C.bank()
                        P.op("pe", lambda e, pb2=pb2, sq=sq, n=n: e.matmul(pb2[:, 0:n], ones_bf[:], sq[:, 0:n],
                                                                          start=True, stop=True), [r_sq, r_ones], [r_pb2])
                        if arg == 0:
                            P.op("dve", lambda e, acc=acc, pb2=pb2, c0=c0, c1=c1, n=n: e.tensor_copy(
                                acc[:, c0:c1], pb2[:, 0:n]), [r_pb2], [r_acc])
                        else:
                            P.op("dve", lambda e, acc=acc, pb2=pb2, c0=c0, c1=c1, n=n: e.tensor_tensor(
                                acc[:, c0:c1], acc[:, c0:c1], pb2[:, 0:n], ALU.add), [r_pb2, r_acc], [r_acc])
                    elif epi == "ropeA":
                        ct, r_ct = tcos[arg]
                        P.op("dve", lambda e, src=src, ct=ct, M=M, c0=c0, c1=c1: e.tensor_tensor(
                            tmpA[0:M, c0:c1], src, ct[0:M, c0:c1], ALU.mult), [r_pb, r_ct], [r_tmpA])
                    elif epi == "ropeB":
                        stt, r_stt = tsin[arg]
                        tb, r_tb = tmpB[si % 2]
                        P.op("dve", lambda e, src=src, stt=stt, tb=tb, M=M, c0=c0, c1=c1, n=n: e.tensor_tensor(
                            tb[0:M, 0:n], src, stt[0:M, c0:c1], ALU.mult), [r_pb, r_stt], [r_tb])
                        P.op("pool", lambda e, dst=dst, tb=tb, M=M, c0=c0, c1=c1, n=n: e.tensor_tensor(
                            dst, tb[0:M, 0:n], tmpA[0:M, c0:c1], ALU.add), [r_tb, r_tmpA], [r_osb])
                    elif epi == "gi":
                        P.op("act", lambda e, src=src, c0=c0, c1=c1: e.activation(
                            ostg[0:16, c0:c1], src, AF.Identity, bias=gbt[:, 0:1]), [r_pb, r_gbt], [r_ostg])
                    elif epi == "gf":
                        P.op("act", lambda e, src=src, c0=c0, c1=c1: e.activation(
                            gtmp[0:16, c0:c1], src, AF.Exp, bias=ngbt[:, 1:2], scale=-1.0), [r_pb, r_ngbt], [r_gtmp])
                        P.op("act", lambda e, c0=c0, c1=c1: e.activation(
                            gtmp[0:16, c0:c1], gtmp[0:16, c0:c1], AF.Ln, bias=onec[0:16, :]), [r_gtmp, r_onec], [r_gtmp])
                        P.op("dve", lambda e, c0=c0, c1=c1: e.tensor_scalar(
                            ostg[0:16, c0:c1], gtmp[0:16, c0:c1], -1.0, None, ALU.mult), [r_gtmp], [r_ostg])
                if epi in ("copy", "scale", "silu", "sigmoid", "ropeB"):
                    r0, m = rows[name]
                    C.out_dma("pool", O1[r0:r0 + m, h0:h1], osb[0:m, 0:nh], [r_osb])
                elif epi == "gi":
                    C.out_dma("pool", O1g[0:16, h0:h1], ostg[0:16, 0:nh], [r_ostg])
                elif epi == "gf":
                    C.out_dma("pool", O1g[16:32, h0:h1], ostg[0:16, 0:nh], [r_ostg])

            for acc, r_acc, dim in ((ssq_q, r_ssq_q, 768), (ssq_kv, r_ssq_kv, 256)):
                a_ = acc[:, 0:nh]
                P.op("act", lambda e, a_=a_, dim=dim: e.activation(
                    a_, a_, AF.Sqrt, bias=epsc[:], scale=1.0 / dim), [r_acc, r_eps], [r_acc])
                P.op("dve", lambda e, a_=a_: e.reciprocal(a_, a_), [r_acc], [r_acc])
            for tab in (tcos, tsin):
                o_ = tab["Cq"][0][:, 0:nh]
                i_ = tab["C"][0][:, 0:nh]
                q_ = ssq_q[:, 0:nh]
                P.op("dve", lambda e, o_=o_, i_=i_, q_=q_: e.tensor_tensor(o_, i_, q_, ALU.mult),
                     [tab["C"][1], r_ssq_q], [tab["Cq"][1]])

            iq = ikv = 0
            for gj, (name, which, cols, epi, arg) in enumerate(g2):
                M = len(cols)
                nk = 6 if which == "q" else 2
                src_ap = w2q[iq] if which == "q" else w2kv[ikv]
                gt = gqt if which == "q" else gkvt
                r_gt = r_gqt if which == "q" else r_gkvt
                if which == "q":
                    iq += 1
                else:
                    ikv += 1
                wb, r_wb = wbf[gj % 2]
                wst, r_wst = wsts[wld[0] % 2]
                wld[0] += 1
                P.dma("sp", wst[:, 0:nk * 128], src_ap, writes=[r_wst])
                for kc in range(nk):
                    P.op("dve", lambda e, wb=wb, kc=kc, gt=gt, wst=wst: e.tensor_scalar(
                        wb[:, kc, :], wst[:, kc * 128:(kc + 1) * 128], gt[:, kc:kc + 1], None, ALU.mult),
                        [r_wst, r_gt], [r_wb, wbf_hi[gj % 2]])
                rhsT, r_rhs = (cqT, r_cqT) if which == "q" else (ckvT, r_ckvT)
                rr, r_rr = (ssq_q, r_ssq_q) if which == "q" else (ssq_kv, r_ssq_kv)
                osb, r_osb = ost[gj % 2]
                for si, (c0, c1) in enumerate(subs):
                    n = c1 - c0
                    pb, r_pb = C.bank()
                    for kc in range(nk):
                        P.op("pe", lambda e, pb=pb, wb=wb, kc=kc, M=M, c0=c0, c1=c1, n=n, rhsT=rhsT, nk=nk: e.matmul(
                            pb[0:M, 0:n], wb[:, kc, 0:M], rhsT[:, kc, c0:c1], start=(kc == 0), stop=(kc == nk - 1)),
                            [r_wb, r_rhs], [r_pb])
                    src = pb[0:M, 0:n]
                    dst = osb[0:M, c0:c1]
                    if epi == "mulr":
                        P.op("dve", lambda e, src=src, dst=dst, rr=rr, M=M, c0=c0, c1=c1: e.tensor_tensor(
                            dst, src, rr[0:M, c0:c1], ALU.mult), [r_pb, r_rr], [r_osb])
                    elif epi == "ropeA":
                        ct, r_ct = tcos[arg]
                        P.op("dve", lambda e, src=src, ct=ct, M=M, c0=c0, c1=c1: e.tensor_tensor(
                            tmpA[0:M, c0:c1], src, ct[0:M, c0:c1], ALU.mult), [r_pb, r_ct], [r_tmpA])
                    elif epi == "ropeB":
                        stt, r_stt = tsin[arg]
                        tb, r_tb = tmpB[si % 2]
                        P.op("dve", lambda e, src=src, stt=stt, tb=tb, M=M, c0=c0, c1=c1, n=n: e.tensor_tensor(
                            tb[0:M, 0:n], src, stt[0:M, c0:c1], ALU.mult), [r_pb, r_stt], [r_tb])
                        P.op("pool", lambda e, dst=dst, tb=tb, M=M, c0=c0, c1=c1, n=n: e.tensor_tensor(
                            dst, tb[0:M, 0:n], tmpA[0:M, c0:c1], ALU.add), [r_tb, r_tmpA], [r_osb])
                if epi in ("mulr", "ropeB"):
                    r0, m = rows[name]
                    C.out_dma("pool", O1[r0:r0 + m, h0:h1], osb[0:m, 0:nh], [r_osb])
        C.finish()
    return nc


def host_p0_inputs(c, c_ctx, w_mod, b_mod):
    cv = np.stack([np.asarray(c).reshape(D), np.asarray(c_ctx).reshape(D)], axis=-1)
    cvec = np.ascontiguousarray(cv.reshape(KC, 128, 2).transpose(1, 0, 2)).reshape(128, KC * 2)
    maps = []
    for core in range(NCORE):
        wm = np.empty((NJ0, 128, KC, 128), np.float32)
        bm = np.empty((128, NJ0), np.float32)
        for l in range(2):
            w3 = w_mod[l].reshape(KC, 128, 96, 128)
            for jj in range(12):
                j = core * 12 + jj
                wm[l * 12 + jj] = w3[:, :, j, :].transpose(1, 0, 2)
                bm[:, l * 12 + jj] = b_mod[l, j * 128:(j + 1) * 128]
        maps.append({"wm": wm.reshape(NJ0, 128, KC * 128), "cvec": cvec, "bm": bm})
    return maps


def host_p0_gather(results):
    out = np.empty((2, 128, 96, 2), np.float32)
    for core in range(NCORE):
        o = results[core]["modT"].reshape(128, NJ0, 2)
        for l in range(2):
            out[l, :, core * 12:(core + 1) * 12, :] = o[:, l * 12:(l + 1) * 12, :]
    return out.reshape(2, 128, 192)


def host_w1(w_in_l):
    g1 = p1_groups()
    out = np.zeros((len(g1), 128, KC, 128), np.float32)
    w3 = w_in_l.reshape(KC, 128, -1)
    for i, (name, cols, epi, arg) in enumerate(g1):
        out[i, :, :, :len(cols)] = w3[:, :, cols].transpose(1, 0, 2)
    return out.reshape(len(g1), 128, KC * 128)


def host_w2(w_uq_l, w_ukv_l):
    g2 = p1_groups2()
    gq = [x for x in g2 if x[1] == "q"]
    gkv = [x for x in g2 if x[1] == "kv"]
    oq = np.zeros((len(gq), 128, 6, 128), np.float32)
    okv = np.zeros((len(gkv), 128, 2, 128), np.float32)
    wq3 = w_uq_l.reshape(6, 128, -1)
    wkv3 = w_ukv_l.reshape(2, 128, -1)
    for i, (name, which, cols, epi, arg) in enumerate(gq):
        oq[i, :, :, :len(cols)] = wq3[:, :, cols].transpose(1, 0, 2)
    for i, (name, which, cols, epi, arg) in enumerate(gkv):
        okv[i, :, :, :len(cols)] = wkv3[:, :, cols].transpose(1, 0, 2)
    return oq.reshape(len(gq), 128, 6 * 128), okv.reshape(len(gkv), 128, 2 * 128)


def pk(vec, nchunk):
    return np.ascontiguousarray(np.asarray(vec, np.float32).reshape(nchunk, 128).T)


def rope_tables(core):
    t = core * TL + np.arange(TL)
    row = (t // 64).astype(np.float64)
    col = (t % 64).astype(np.float64)

    def tab(dim):
        h = dim // 2
        half = h // 2
        cos = np.ones((dim, NT), np.float64)
        sin = np.zeros((dim, NT), np.float64)
        for i in range(dim):
            pos = row if i < h else col
            j = i % half
            f = 10000.0 ** (-j / half)
            sgn = -1.0 if (i % h) < half else 1.0
            cos[i, :TL] = np.cos(pos * f)
            sin[i, :TL] = sgn * np.sin(pos * f)
        return cos.astype(np.float32), sin.astype(np.float32)

    cB, sB = tab(128)
    c64, s64 = tab(64)
    cC = np.concatenate([c64, c64], axis=0)
    sC = np.concatenate([s64, s64], axis=0)
    return cB, sB, cC, sC


def host_p1_inputs(x_full, xc_full, modT_l, l, inp):
    w1 = host_w1(inp["w_in"][l])
    w2q, w2kv = host_w2(inp["mla_w_uq"][l], inp["mla_w_ukv"][l])
    gb = inp["ml_gate_b"][l]
    gbias = np.stack([np.concatenate([gb[0], gb[2]]), np.concatenate([gb[1], gb[3]])], axis=1).astype(np.float32)
    common = {
        "modT": modT_l, "gpre": pk(inp["g_pre"][l], KC), "w1": w1, "w2q": w2q, "w2kv": w2kv,
        "gq": pk(inp["mla_q_norm"][l], 6), "gkv": pk(inp["mla_kv_norm"][l], 2), "gbias": gbias,
        "ident": np.eye(128, dtype=np.float32),
    }
    maps = []
    for core in range(NCORE):
        cB, sB, cC, sC = rope_tables(core)
        m = dict(common)
        m["x"] = np.concatenate([x_full[core * TL:(core + 1) * TL], xc_full[core * CL:(core + 1) * CL]], axis=0)
        m.update({"cosB": cB, "sinB": sB, "cosC": cC, "sinC": sC})
        maps.append(m)
    return maps


NG3 = 16
GW3 = D // NG3


def build_p3():
    nc = bass.Bass("TRN2", target_bir_lowering=False)
    with ExitStack() as st:
        C = Ctx(nc, st)
        P = C.P
        yT = C.dram_in("yT", [5, 128, KC * 512], BF16)
        wo = C.dram_in("wo", [NG3, 128, KC * GW3])
        x = C.dram_in("x", [NT, D])
        grow = C.dram_in("grow", [2, D])
        gpost = C.dram_in("gpost", [1, D])
        xo = C.dram_out("xo", [NT, D])
        yt0, r_yt0 = C.sb("yt0", [128, KC, 512], BF16)
        wsts = [C.sb(f"wst{i}", [128, KC * GW3]) for i in range(1)] * 2
        wst_bf = wsts[0][0][:].bitcast(BF16)
        yts = [(yt0[:].rearrange("p k n -> p (k n)"), yt0, r_yt0),
               (wst_bf, wst_bf.rearrange("p (k n) -> p k n", n=512), wsts[0][1])]
        wscr = nc.dram_tensor("wscr", [NG3, 128, KC * GW3], BF16).ap()
        r_scr = [P.res(f"scr{g}") for g in range(NG3)]
        wbf = [C.sb(f"wbf{i}", [128, KC, GW3], BF16) for i in range(2)]
        wbf_hi = [P.res(f"wbfhi{i}") for i in range(2)]
        ots = [C.sb(f"ot{i}", [128, D]) for i in range(4)]
        xt, r_xt = C.sb("xt", [128, D])
        ggs = [C.sb(f"gg{i}", [128, D]) for i in range(1)] * 2
        junk, r_junk = C.sb("junk", [128, D // 2], BF16)
        ssq, r_ssq = C.sb("ssq", [128, 2])
        epsc, r_eps = C.sb("epsc", [128, 1])
        C.psum_banks(8)
        P.op("pool", lambda e: e.memset(epsc[:], EPS), [], [r_eps])
        def load_gg(v):
            gg, r_gg = ggs[v]
            P.dma("sp", xt[:], gpost[0:1, :].to_broadcast([128, D]), writes=[r_xt])
            P.dma("sp", gg[:], grow[v:v + 1, :].to_broadcast([128, D]), writes=[r_gg])
            P.op("dve", lambda e, gg=gg: e.tensor_tensor(gg[:], gg[:], xt[:], ALU.mult), [r_gg, r_xt], [r_gg])

        load_gg(0)
        sts = [(s, s + 512) for s in range(0, TL, 512)] + [(TL, NT)]
        wi = 0
        for sti, (s0, s1) in enumerate(sts):
            v = 0 if s0 < TL else 1
            ns = s1 - s0
            ytf, yt, r_yt = yts[sti % 2]
            P.dma("pool", ytf, yT[sti], writes=[r_yt])
            tiles = [(a, min(a + 128, ns)) for a in range(0, ns, 128)]
            for g in range(NG3):
                wb, r_wb = wbf[wi % 2]
                r_whi = wbf_hi[wi % 2]
                wst, r_wst = wsts[wi % 2]
                wi += 1
                wflat = wb[:].rearrange("p k n -> p (k n)")
                hc = KC * GW3 // 2
                if sti == 0:
                    P.dma("sp", wst[:], wo[g], writes=[r_wst])
                    P.op("act", lambda e, wflat=wflat, wst=wst: e.copy(wflat[:, 0:hc], wst[:, 0:hc]), [r_wst], [r_wb])
                    P.op("dve", lambda e, wflat=wflat, wst=wst: e.tensor_copy(wflat[:, hc:], wst[:, hc:]), [r_wst], [r_whi])
                    P.dma("pool", wscr[g], wflat, reads=[r_wb, r_whi], writes=[r_scr[g]])
                else:
                    P.dma("sp", wflat, wscr[g], reads=[r_scr[g]], writes=[r_wb, r_whi])
                for ti, (a, b) in enumerate(tiles):
                    n = b - a
                    pb, r_pb = C.bank()
                    ot, r_ot = ots[ti]
                    for kc in range(KC):
                        P.op("pe", lambda e, pb=pb, yt=yt, wb=wb, kc=kc, a=a, b=b, n=n: e.matmul(
                            pb[0:n, 0:GW3], yt[:, kc, a:b], wb[:, kc, :], start=(kc == 0), stop=(kc == KC - 1)),
                            [r_yt, r_wb if kc < KC // 2 else r_whi], [r_pb])
                    P.op("act", lambda e, pb=pb, ot=ot, n=n, g=g: e.copy(ot[0:n, g * GW3:(g + 1) * GW3], pb[0:n, 0:GW3]),
                         [r_pb], [r_ot])
            if v == 1:
                load_gg(1)
            for ti, (a, b) in enumerate(tiles):
                n = b - a
                ot, r_ot = ots[ti]
                gg, r_gg = ggs[v]
                P.dma("sp", xt[0:n, :], x[s0 + a:s0 + b, :], writes=[r_xt])
                P.op("pool", lambda e: e.memset(ssq[:], 0.0), [], [r_ssq])
                for hh in range(2):
                    P.op("act", lambda e, ot=ot, n=n, hh=hh: e.activation(
                        junk[0:n, :], ot[0:n, hh * (D // 2):(hh + 1) * (D // 2)], AF.Square, accum_out=ssq[0:n, hh:hh + 1]),
                        [r_ot], [r_junk, r_ssq])
                P.op("dve", lambda e, n=n: e.tensor_tensor(ssq[0:n, 0:1], ssq[0:n, 0:1], ssq[0:n, 1:2], ALU.add), [r_ssq], [r_ssq])
                P.op("act", lambda e, n=n: e.activation(ssq[0:n, 0:1], ssq[0:n, 0:1], AF.Sqrt, bias=epsc[0:n, :], scale=1.0 / D),
                     [r_ssq, r_eps], [r_ssq])
                P.op("dve", lambda e, n=n: e.reciprocal(ssq[0:n, 0:1], ssq[0:n, 0:1]), [r_ssq], [r_ssq])
                P.op("dve", lambda e, ot=ot, gg=gg, n=n: e.scalar_tensor_tensor(
                    ot[0:n, :], ot[0:n, :], ssq[0:n, 0:1], gg[0:n, :], ALU.mult, ALU.mult), [r_ot, r_ssq, r_gg], [r_ot])
                P.op("dve", lambda e, ot=ot, n=n: e.tensor_tensor(ot[0:n, :], ot[0:n, :], xt[0:n, :], ALU.add),
                     [r_ot, r_xt], [r_ot])
                C.out_dma("pool", xo[s0 + a:s0 + b, :], ot[0:n, :], [r_ot])
        C.finish()
    return nc


def host_p3_common(modT_l, l, inp):
    wo = inp["w_out"][l].reshape(KC, 128, NG3, GW3).transpose(2, 1, 0, 3)
    wo = np.ascontiguousarray(wo).reshape(NG3, 128, KC * GW3)
    m3 = modT_l.reshape(128, 96, 2)
    grow = np.ascontiguousarray(m3[:, 64:96, :].transpose(2, 1, 0)).reshape(2, D)
    return {"wo": wo, "grow": grow, "gpost": np.asarray(inp["g_post"][l], np.float32).reshape(1, D)}


def host_yT_tiles(y_loc):
    out = np.zeros((5, 128, KC, 512), y_loc.dtype)
    for i in range(5):
        s0 = i * 512
        s1 = min(s0 + 512, NT)
        out[i, :, :, :s1 - s0] = y_loc[s0:s1].reshape(s1 - s0, KC, 128).transpose(2, 1, 0)
    return out.reshape(5, 128, KC * 512)


BIGW = 16896
LCH = 512


def build_p2():
    nc = bass.Bass("TRN2", target_bir_lowering=False)
    with ExitStack() as st:
        C = Ctx(nc, st)
        P = C.P
        FA = C.dram_in("FA", [12, 128, TT], BF16)
        FB = C.dram_in("FB", [2, 64, TT], BF16)
        TM = C.dram_in("TM", [3, 128, NB * 128], BF16)
        TMV = C.dram_in("TMV", [128, NB * 129], BF16)
        GT = C.dram_in("GT", [128, 4 * NB])
        LP = C.dram_in("LP", [128, 16])
        LW = C.dram_in("LW", [4, 128, 128])
        HN = C.dram_in("HN", [128, 128])
        MK = C.dram_in("MK", [2, 128, 128])
        identd = C.dram_in("ident", [128, 128])
        Y = C.dram_out("Y", [4, 128, TT], BF16)

        bigs = [C.sb(f"big{i}", [128, BIGW], BF16) for i in range(5)]
        lp, r_lp = C.sb("lp", [128, 16])
        lwbf, r_lwbf = C.sb("lwbf", [128, 4, 128], BF16)
        hn, r_hn = C.sb("hnrow", [128, 128])
        mk32, r_mk32 = C.sb("mk32", [128, 2, 128])
        mkbf, r_mkbf = C.sb("mkbf", [128, 2, 128], BF16)
        ident, r_ident = C.sb("identt", [128, 128])
        ones_bf, r_ones = C.sb("ones_bf", [128, 128], BF16)
        ones32, r_ones32 = C.sb("ones32", [128, 128])
        onec, r_onec = C.sb("onec", [128, 1])
        epsc, r_eps = C.sb("epsc", [128, 1])
        cc, r_cc = C.sb("cc", [128, 8])
        tf = [C.sb(f"tf{i}", [128, LCH]) for i in range(6)]
        tb = [C.sb(f"tbf{i}", [128, 1024], BF16) for i in range(3)]
        S = [C.ps(f"S{i}", [128, 1024]) for i in range(3)]
        PO = [C.ps("PO0", [128, 512]), (S[2][0][:, 0:512], S[2][1])]
        PD = [C.ps("PD0", [128, 512]), (S[2][0][:, 512:1024], S[2][1])]

        P.dma("sp", lp[:], LP, writes=[r_lp])
        P.dma("sp", hn[:], HN, writes=[r_hn])
        P.dma("sp", mk32[:], MK.rearrange("g k j -> k g j"), writes=[r_mk32])
        P.dma("sp", ident[:], identd, writes=[r_ident])
        P.op("pool", lambda e: e.memset(ones_bf[:], 1.0), [], [r_ones])
        P.op("pool", lambda e: e.memset(ones32[:], 1.0), [], [r_ones32])
        P.op("pool", lambda e: e.memset(onec[:], 1.0), [], [r_onec])
        P.op("pool", lambda e: e.memset(epsc[:], EPS), [], [r_eps])
        lwst = tf[0][0][:].rearrange("p (g j) -> p g j", j=128)
        P.dma("sp", lwst, LW.rearrange("g k j -> k g j"), writes=[tf[0][1]])
        P.op("dve", lambda e: e.tensor_copy(lwbf[:], lwst), [tf[0][1]], [r_lwbf])
        P.op("dve", lambda e: e.tensor_copy(mkbf[:], mk32[:]), [r_mk32], [r_mkbf])
        P.op("act", lambda e: e.activation(cc[:, 0:2], lp[:, 5:7], AF.Exp, scale=-1.0), [r_lp], [r_cc])
        P.op("act", lambda e: e.activation(cc[:, 0:2], cc[:, 0:2], AF.Ln, bias=onec[:]), [r_cc, r_onec], [r_cc])
        P.op("dve", lambda e: e.tensor_scalar(cc[:, 2:4], cc[:, 0:2], -16.0, None, ALU.mult), [r_cc], [r_cc])
        P.op("dve", lambda e: e.tensor_scalar(cc[:, 0:2], cc[:, 0:2], -8.0, None, ALU.mult), [r_cc], [r_cc])
        P.op("act", lambda e: e.activation(cc[:, 4:5], lp[:, 11:12], AF.Exp), [r_lp], [r_cc])

        def load_big(i, src, ncol, part=128, eng="sp"):
            t, r = bigs[i]
            P.dma(eng, t[0:part, 0:ncol], src, writes=[r])
            return t, r

        u_bf, r_u = load_big(0, FA[0], TT)
        hfl = bigs[1][0][:].bitcast(F32)
        hfh = bigs[2][0][:].bitcast(F32)
        r_hfl, r_hfh = bigs[1][1], bigs[2][1]

        def hf_ap(s, e):
            if e <= 8448:
                return hfl[:, s:e], r_hfl
            return hfh[:, s - 8448:e - 8448], r_hfh

        chunks_ctx = [(0, TC, 0, TC)]
        chunks_lat = [(TC + i * LCH, TC + (i + 1) * LCH, TC, TT) for i in range(T // LCH)]
        (v32, r_v32), (rb, r_rb), (ib, r_ib), (t2, r_t2), (hb0, r_hb0), (hb1, r_hb1) = tf
        (vbf, r_vbf), (sgt, r_sgt), (yst, r_yst) = tb[0], tb[1], tb[2]
        hbs = [(hb0, r_hb0), (hb1, r_hb1)]

        def lru_chunk(s, e, S_, E_, d, prev_ap, prev_res, hbi):
            n = e - s
            P.op("dve", lambda e_: e_.tensor_scalar(v32[:, 0:n], u_bf[:, s:e], lp[:, 2:3], lp[:, 4:5], ALU.mult, ALU.add),
                 [r_u, r_lp], [r_v32])
            for j in (0, 1, 3):
                o = j - 2
                lo, hi = max(s, S_ - o), min(e, E_ - o)
                if lo < hi:
                    P.op("dve", lambda e_, lo=lo, hi=hi, o=o, j=j: e_.scalar_tensor_tensor(
                        v32[:, lo - s:hi - s], u_bf[:, lo + o:hi + o], lp[:, j:j + 1], v32[:, lo - s:hi - s],
                        ALU.mult, ALU.add), [r_u, r_lp, r_v32], [r_v32])
            P.op("act", lambda e_: e_.copy(vbf[:, 0:n], v32[:, 0:n]), [r_v32], [r_vbf])
            pa, r_pa = S[0]
            px, r_px = S[1]
            for c0 in range(0, n, 512):
                c1_ = min(c0 + 512, n)
                P.op("pe", lambda e_, c0=c0, c1_=c1_: e_.matmul(pa[:, c0:c1_], lwbf[:, 2 * d, :], vbf[:, c0:c1_],
                                                               start=True, stop=True), [r_lwbf, r_vbf], [r_pa])
                P.op("pe", lambda e_, c0=c0, c1_=c1_: e_.matmul(px[:, c0:c1_], lwbf[:, 2 * d + 1, :], vbf[:, c0:c1_],
                                                               start=True, stop=True), [r_lwbf, r_vbf], [r_px])
            P.op("act", lambda e_: e_.activation(ib[:, 0:n], px[:, 0:n], AF.Sigmoid, bias=lp[:, 9 + d:10 + d]), [r_px, r_lp], [r_ib])
            P.op("act", lambda e_: e_.activation(rb[:, 0:n], pa[:, 0:n], AF.Sigmoid, bias=lp[:, 7 + d:8 + d]), [r_pa, r_lp], [r_rb])
            P.op("act", lambda e_: e_.activation(t2[:, 0:n], rb[:, 0:n], AF.Exp, scale=cc[:, 2 + d:3 + d]), [r_rb, r_cc], [r_t2])
            P.op("act", lambda e_: e_.activation(rb[:, 0:n], rb[:, 0:n], AF.Exp, scale=cc[:, d:d + 1]), [r_rb, r_cc], [r_rb])
            P.op("act", lambda e_: e_.activation(t2[:, 0:n], t2[:, 0:n], AF.Sqrt, bias=onec[:], scale=-1.0), [r_t2, r_onec], [r_t2])
            P.op("dve", lambda e_: e_.tensor_tensor(ib[:, 0:n], ib[:, 0:n], v32[:, 0:n], ALU.mult), [r_ib, r_v32], [r_ib])
            P.op("dve", lambda e_: e_.tensor_tensor(ib[:, 0:n], ib[:, 0:n], t2[:, 0:n], ALU.mult), [r_ib, r_t2], [r_ib])
            init = 0.0 if prev_ap is None else prev_ap
            rd = [r_rb, r_ib] + ([prev_res] if prev_res is not None else [])
            if d == 0:
                out, r_out = hf_ap(s, e)
                P.op("dve", lambda e_: e_.tensor_tensor_scan(out, rb[:, 0:n], ib[:, 0:n], init, ALU.mult, ALU.add), rd, [r_out])
                return out[:, n - 1:n], r_out
            hb, r_hb = hbs[hbi]
            P.op("dve", lambda e_: e_.tensor_tensor_scan(hb[:, 0:n][:, ::-1], rb[:, 0:n][:, ::-1], ib[:, 0:n][:, ::-1],
                                                         init, ALU.mult, ALU.add), rd, [r_hb])
            hfa, r_hfa = hf_ap(s, e)
            P.dma("sp", sgt[:, 0:n], FA[1][:, s:e], writes=[r_sgt])
            P.op("dve", lambda e_: e_.tensor_tensor(v32[:, 0:n], hb[:, 0:n], hfa, ALU.add), [r_hb, r_hfa], [r_v32])
            P.op("dve", lambda e_: e_.tensor_tensor(yst[:, 0:n], v32[:, 0:n], sgt[:, 0:n], ALU.mult), [r_v32, r_sgt], [r_yst])
            C.out_dma("sp", Y[0][:, s:e], yst[:, 0:n], [r_yst])
            return hb[:, 0:1], r_hb

        pa_, pr_ = None, None
        for (s, e, S_, E_) in chunks_ctx + chunks_lat:
            pa_, pr_ = lru_chunk(s, e, S_, E_, 0, pa_, pr_, 0)
        pa_, pr_ = None, None
        hbi = 0
        for (s, e, S_, E_) in chunks_ctx + chunks_lat[::-1]:
            pa_, pr_ = lru_chunk(s, e, S_, E_, 1, pa_, pr_, hbi)
            hbi ^= 1

        qT, r_q = load_big(0, FA[2], TT)
        kT, r_k = load_big(1, FA[3], TT)
        vtm, r_v = load_big(2, TM[0], NB * 128)
        sgB, r_sgB = load_big(3, FA[4], TT)
        vtm3 = vtm[:, 0:NB * 128].rearrange("p (b d) -> p b d", d=128)
        pTs = [tb[0], tb[1]]
        ystB, r_ystB = tb[2]
        (dn, r_dn), (ob, r_ob) = tf[0], tf[1]
        sc_swa = 128.0 ** -0.5
        qblocks = [(0, [(0, None), (1, None)]), (1, [(0, None), (1, None)])]
        for nq in range(128):
            keys = []
            if nq > 0:
                keys.append((2 + nq - 1, 1))
            keys.append((2 + nq, None))
            if nq < 127:
                keys.append((2 + nq + 1, 0))
            keys += [(0, None), (1, None)]
            qblocks.append((2 + nq, keys))
        for bi, (qb, keys) in enumerate(qblocks):
            Sx, r_S = S[bi % 2]
            po, r_po = PO[bi % 2]
            pd, r_pd = PD[bi % 2]
            pT, r_pT = pTs[bi % 2]
            nk = len(keys)
            qs = slice(qb * 128, (qb + 1) * 128)
            for sl, (kb, mk) in enumerate(keys):
                P.op("pe", lambda e_, Sx=Sx, sl=sl, kb=kb, qs=qs: e_.matmul(
                    Sx[:, sl * 128:(sl + 1) * 128], kT[:, kb * 128:(kb + 1) * 128], qT[:, qs], start=True, stop=True),
                    [r_k, r_q], [r_S])
            P.op("act", lambda e_, Sx=Sx, pT=pT, nk=nk: e_.activation(pT[:, 0:nk * 128], Sx[:, 0:nk * 128], AF.Exp, scale=sc_swa),
                 [r_S], [r_pT])
            for sl, (kb, mk) in enumerate(keys):
                if mk is not None:
                    P.op("dve", lambda e_, pT=pT, sl=sl, mk=mk: e_.tensor_tensor(
                        pT[:, sl * 128:(sl + 1) * 128], pT[:, sl * 128:(sl + 1) * 128], mkbf[:, mk, :], ALU.mult),
                        [r_pT, r_mkbf], [r_pT])
            for sl, (kb, mk) in enumerate(keys):
                P.op("pe", lambda e_, po=po, pT=pT, sl=sl, kb=kb, nk=nk: e_.matmul(
                    po[:, 0:128], vtm3[:, kb, :], pT[:, sl * 128:(sl + 1) * 128], start=(sl == 0), stop=(sl == nk - 1)),
                    [r_v, r_pT], [r_po])
            for sl, (kb, mk) in enumerate(keys):
                P.op("pe", lambda e_, pd=pd, pT=pT, sl=sl, nk=nk: e_.matmul(
                    pd[:, 0:128], ones_bf[:], pT[:, sl * 128:(sl + 1) * 128], start=(sl == 0), stop=(sl == nk - 1)),
                    [r_ones, r_pT], [r_pd])
            P.op("dve", lambda e_, pd=pd: e_.tensor_scalar(dn[:, 0:128], pd[:, 0:128], cc[:, 4:5], None, ALU.add), [r_pd, r_cc], [r_dn])
            P.op("dve", lambda e_: e_.reciprocal(dn[:, 0:128], dn[:, 0:128]), [r_dn], [r_dn])
            P.op("dve", lambda e_, po=po: e_.tensor_tensor(ob[:, 0:128], po[:, 0:128], dn[:, 0:128], ALU.mult), [r_po, r_dn], [r_ob])
            j8 = bi % 8
            P.op("dve", lambda e_, qs=qs, j8=j8: e_.tensor_tensor(ystB[:, j8 * 128:(j8 + 1) * 128], ob[:, 0:128], sgB[:, qs], ALU.mult),
                 [r_ob, r_sgB], [r_ystB])
            if j8 == 7 or bi == len(qblocks) - 1:
                b0 = bi - j8
                q0 = qblocks[b0][0]
                C.out_dma("sp", Y[1][:, q0 * 128:(qb + 1) * 128], ystB[:, 0:(j8 + 1) * 128], [r_ystB])

        knT, r_kn = load_big(0, FA[6], TT)
        krT, r_kr = load_big(1, FB[1], TT, part=64)
        P.op("pool", lambda e_: e_.memset(krT[64:128, 0:TT], 0.0), [], [r_kr])
        vtmC, r_vC = load_big(2, TM[1], NB * 128)
        vtmC3 = vtmC[:, 0:NB * 128].rearrange("p (b d) -> p b d", d=128)
        sc_mla = 192.0 ** -0.5
        qns = [C.sb(f"qn{i}", [128, 512], BF16) for i in range(1)] * 2
        qrs = [C.sb(f"qr{i}", [128, 512], BF16) for i in range(1)] * 2
        P.op("pool", lambda e_: e_.memset(qrs[0][0][64:128, :], 0.0), [], [qrs[0][1]])
        sgs = [C.sb(f"sgc{i}", [128, 512], BF16) for i in range(1)] * 2
        ystC = [C.sb(f"ystc{i}", [128, 512], BF16) for i in range(1)] * 2
        pTC = [tb[0], tb[1], tb[2]]
        (rec, r_rec), (oc, r_oc) = tf[0], tf[1]
        (accA, r_accA), (accB, r_accB) = tf[2], tf[3]
        qtiles = [(0, TC, [(0, 1)])] + [(TC + i * 512, TC + (i + 1) * 512, [(2 * j, 2 * j + 1) for j in range(NB // 2)])
                                        for i in range(32)]
        pcount = 0
        for qi, (q0, q1, pairs) in enumerate(qtiles):
            n = q1 - q0
            qn, r_qn = qns[qi % 2]
            qr, r_qr = qrs[qi % 2]
            sg, r_sg = sgs[qi % 2]
            yc, r_yc = ystC[qi % 2]
            po, r_po = PO[0]
            pd, r_pd = PD[0]
            P.dma("sp", qn[:, 0:n], FA[5][:, q0:q1], writes=[r_qn])
            P.dma("sp", qr[0:64, 0:n], FB[0][:, q0:q1], writes=[r_qr])
            P.dma("sp", sg[:, 0:n], FA[7][:, q0:q1], writes=[r_sg])
            npair = len(pairs)

            def qk(i):
                Sx, r_S = S[(pcount + i) % 3]
                for hh in range(2):
                    kb = pairs[i][hh]
                    P.op("pe", lambda e_, Sx=Sx, hh=hh, kb=kb, n=n, qn=qn, qr=qr: e_.matmul(
                        Sx[:, hh * 512:hh * 512 + n], knT[:, kb * 128:(kb + 1) * 128], qn[:, 0:n], start=True, stop=False),
                        [r_kn, r_qn], [r_S])
                    P.op("pe", lambda e_, Sx=Sx, hh=hh, kb=kb, n=n, qn=qn, qr=qr: e_.matmul(
                        Sx[:, hh * 512:hh * 512 + n], krT[:, kb * 128:(kb + 1) * 128], qr[:, 0:n], start=False, stop=True),
                        [r_kr, r_qr], [r_S])
                pT, r_pT = pTC[(pcount + i) % 3]
                if n == 512:
                    P.op("act", lambda e_, Sx=Sx, pT=pT: e_.activation(pT[:, :], Sx[:, :], AF.Exp, scale=sc_mla), [r_S], [r_pT])
                else:
                    for hh in range(2):
                        P.op("act", lambda e_, Sx=Sx, pT=pT, hh=hh, n=n: e_.activation(
                            pT[:, hh * 512:hh * 512 + n], Sx[:, hh * 512:hh * 512 + n], AF.Exp, scale=sc_mla), [r_S], [r_pT])

            def pv(i):
                pT, r_pT = pTC[(pcount + i) % 3]
                for hh in range(2):
                    kb = pairs[i][hh]
                    first = (i == 0 and hh == 0)
                    last = (i == npair - 1 and hh == 1)
                    P.op("pe", lambda e_, pT=pT, hh=hh, kb=kb, first=first, last=last, n=n, po=po: e_.matmul(
                        po[:, 0:n], vtmC3[:, kb, :], pT[:, hh * 512:hh * 512 + n], start=first, stop=last),
                        [r_vC, r_pT], [r_po])
                    P.op("pe", lambda e_, pT=pT, hh=hh, first=first, last=last, n=n, pd=pd: e_.matmul(
                        pd[:, 0:n], ones_bf[:], pT[:, hh * 512:hh * 512 + n], start=first, stop=last),
                        [r_ones, r_pT], [r_pd])

            qk(0)
            if npair > 1:
                qk(1)
            for i in range(npair):
                if i + 2 < npair:
                    qk(i + 2)
                pv(i)
            pcount += npair
            P.op("dve", lambda e_, pd=pd, n=n: e_.reciprocal(rec[:, 0:n], pd[:, 0:n]), [r_pd], [r_rec])
            P.op("dve", lambda e_, po=po, n=n: e_.tensor_tensor(oc[:, 0:n], po[:, 0:n], rec[:, 0:n], ALU.mult), [r_po, r_rec], [r_oc])
            P.op("pool", lambda e_, yc=yc, sg=sg, n=n: e_.tensor_tensor(yc[:, 0:n], oc[:, 0:n], sg[:, 0:n], ALU.mult), [r_oc, r_sg], [r_yc])
            C.out_dma("sp", Y[2][:, q0:q1], yc[:, 0:n], [r_yc])

        qT, r_q = load_big(0, FA[8], TT)
        kT, r_k = load_big(1, FA[9], TT)
        ktm, r_ktm = load_big(2, TM[2], NB * 128)
        vex, r_vex = load_big(3, TMV, NB * 129)
        hfs, r_hfs = bigs[4]
        ktm3 = ktm[:, 0:NB * 128].rearrange("p (b d) -> p b d", d=128)
        vex3 = vex[:, 0:NB * 129].rearrange("p (b d) -> p b d", d=129)
        hfs3 = hfs[:, 0:NB * 128].rearrange("p (b d) -> p b d", d=128)
        gt, r_gt = C.sb("gt", [128, 4 * NB])
        P.dma("sp", gt[:], GT, writes=[r_gt])
        gt3 = gt[:].rearrange("p (g n) -> p g n", n=NB)
        sc_t = {}
        for nm in ("b", "G", "u", "eb", "eg", "ueg"):
            for d in range(2):
                sc_t[(nm, d)] = C.sb(f"sc_{nm}{d}", [128, NB])
        for d in range(2):
            ig = gt3[:, d, :]
            lf = gt3[:, 2 + d, :]
            pb_, r_pb_ = PO[0]
            pg_, r_pg_ = PO[1]
            (b_, r_b), (G_, r_G), (u_, r_uu), (eb_, r_eb), (eg_, r_eg), (ueg_, r_ueg) = [sc_t[(nm, d)] for nm in ("b", "G", "u", "eb", "eg", "ueg")]
            P.op("pe", lambda e_, d=d, lf=lf, pb_=pb_: e_.matmul(pb_[:, 0:NB], mk32[:, d, :], lf, start=True, stop=True), [r_mk32, r_gt], [r_pb_])
            P.op("pe", lambda e_, lf=lf, pg_=pg_: e_.matmul(pg_[:, 0:NB], ones32[:], lf, start=True, stop=True), [r_ones32, r_gt], [r_pg_])
            P.op("dve", lambda e_, b_=b_, pb_=pb_: e_.tensor_copy(b_[:], pb_[:, 0:NB]), [r_pb_], [r_b])
            P.op("dve", lambda e_, G_=G_, pg_=pg_: e_.tensor_copy(G_[:], pg_[:, 0:NB]), [r_pg_], [r_G])
            P.op("dve", lambda e_, u_=u_, ig=ig, b_=b_: e_.tensor_tensor(u_[:], ig, b_[:], ALU.subtract), [r_gt, r_b], [r_uu])
            P.op("dve", lambda e_, ueg_=ueg_, u_=u_, G_=G_: e_.tensor_tensor(ueg_[:], u_[:], G_[:], ALU.add), [r_uu, r_G], [r_ueg])
            P.op("act", lambda e_, u_=u_: e_.activation(u_[:], u_[:], AF.Exp), [r_uu], [r_uu])
            P.op("act", lambda e_, ueg_=ueg_: e_.activation(ueg_[:], ueg_[:], AF.Exp), [r_ueg], [r_ueg])
            P.op("act", lambda e_, eb_=eb_, b_=b_: e_.activation(eb_[:], b_[:], AF.Exp), [r_b], [r_eb])
            P.op("act", lambda e_, eg_=eg_, G_=G_: e_.activation(eg_[:], G_[:], AF.Exp), [r_G], [r_eg])

        C32, r_C32 = C.sb("C32", [128, 129])
        Cbfs = [C.sb(f"Cbf{i}", [128, 129], BF16) for i in range(2)]
        pTd = [C.sb(f"pTd{i}", [128, 128], BF16) for i in range(2)]
        vps = [C.sb(f"vp{i}", [128, 129], BF16) for i in range(2)]
        vpps = [C.sb(f"vpp{i}", [128, 129], BF16) for i in range(2)]
        dds = [C.sb(f"dd{i}", [128, 2]) for i in range(2)]
        hss = [C.sb(f"hs{i}", [128, 128]) for i in range(2)]
        hns = [C.sb(f"hn{i}", [128, 128]) for i in range(2)]
        ssd = [C.sb(f"ssd{i}", [128, 1]) for i in range(2)]
        junkd, r_junkd = C.sb("junkd", [128, 128], BF16)
        (yt1, r_yt1) = tf[2]
        so_t, r_so = tb[0]
        sg_t, r_sgd = tb[1]
        ystD, r_ystD = tb[2]

        def mlstm_dir(d, order, groups):
            (b_, r_b), (G_, r_G), (u_, r_uu), (eb_, r_eb), (eg_, r_eg), (ueg_, r_ueg) = [sc_t[(nm, d)] for nm in ("b", "G", "u", "eb", "eg", "ueg")]
            P.op("pool", lambda e_: e_.memset(C32[:], 0.0), [], [r_C32])
            P.op("pool", lambda e_: e_.memset(Cbfs[0][0][:], 0.0), [], [Cbfs[0][1]])
            grp_of = {}
            for (lo, hi) in groups:
                for c in range(lo, hi):
                    grp_of[c] = (lo, hi)
            def stage1(ci):
                n = order[ci]
                par = ci % 2
                cs = slice(n * 128, (n + 1) * 128)
                Sx, r_S = S[par]
                pk_, r_pk = PO[par]
                pT, r_pT = pTd[par]
                vp, r_vp = vps[par]
                vpp, r_vpp = vpps[par]
                P.op("pe", lambda e_: e_.matmul(Sx[:, 0:128], kT[:, cs], qT[:, cs], start=True, stop=True), [r_k, r_q], [r_S])
                P.op("dve", lambda e_: e_.tensor_tensor(pT[:], Sx[:, 0:128], mkbf[:, d, :], ALU.mult), [r_S, r_mkbf], [r_pT])
                P.op("act", lambda e_: e_.mul(vp[:], vex3[:, n, :], u_[:, n:n + 1]), [r_vex, r_uu], [r_vp])
                P.op("act", lambda e_: e_.mul(vpp[:], vex3[:, n, :], ueg_[:, n:n + 1]), [r_vex, r_ueg], [r_vpp])
                P.op("pe", lambda e_: e_.matmul(pk_[:, 0:129], ktm3[:, n, :], vpp[:], start=True, stop=True), [r_ktm, r_vpp], [r_pk])

            def stage2(ci):
                n = order[ci]
                par = ci % 2
                cs = slice(n * 128, (n + 1) * 128)
                Sx, r_S = S[par]
                pk_, r_pk = PO[par]
                ptt, r_ptt = PD[par]
                Cb, r_Cb = Cbfs[par]
                Cn, r_Cn = Cbfs[1 - par]
                pT, r_pT = pTd[par]
                vp, r_vp = vps[par]
                dd, r_dd = dds[par]
                if d == 1 and n in grp_of and (ci == 0 or grp_of[order[ci - 1]] != grp_of[n]):
                    lo, hi = grp_of[n]
                    P.dma("sp", so_t[:, 0:(hi - lo) * 128], FA[10][:, lo * 128:hi * 128], writes=[r_so])
                    P.dma("sp", sg_t[:, 0:(hi - lo) * 128], FA[11][:, lo * 128:hi * 128], writes=[r_sgd])
                P.op("pe", lambda e_: e_.matmul(Sx[:, 512:641], qT[:, cs], Cb[:], start=True, stop=False), [r_q, r_Cb], [r_S])
                P.op("pe", lambda e_: e_.matmul(Sx[:, 512:641], pT[:], vp[:], start=False, stop=True), [r_pT, r_vp], [r_S])
                P.op("dve", lambda e_: e_.scalar_tensor_tensor(C32[:], C32[:], eg_[:, n:n + 1], pk_[:, 0:129], ALU.mult, ALU.add),
                     [r_C32, r_eg, r_pk], [r_C32])
                P.op("act", lambda e_: e_.copy(Cn[:], C32[:]), [r_C32], [r_Cn])
                P.op("dve", lambda e_: e_.tensor_scalar(dd[:, 0:1], Sx[:, 640:641], eb_[:, n:n + 1], None, ALU.mult),
                     [r_S, r_eb], [r_dd])
                P.op("dve", lambda e_: e_.tensor_scalar(dd[:, 1:2], dd[:, 0:1], -1.0, -1.0, ALU.min, ALU.mult), [r_dd], [r_dd])
                P.op("dve", lambda e_: e_.scalar_tensor_tensor(dd[:, 0:1], dd[:, 0:1], 1.0, dd[:, 1:2], ALU.max, ALU.max), [r_dd], [r_dd])
                P.op("dve", lambda e_: e_.reciprocal(dd[:, 0:1], dd[:, 0:1]), [r_dd], [r_dd])
                P.op("dve", lambda e_: e_.tensor_tensor(dd[:, 1:2], dd[:, 0:1], eb_[:, n:n + 1], ALU.mult), [r_dd, r_eb], [r_dd])
                if d == 0:
                    P.op("dve", lambda e_: e_.tensor_scalar(hfs3[:, n, :], Sx[:, 512:640], dd[:, 1:2], None, ALU.mult),
                         [r_S, r_dd], [r_hfs])
                    return
                hs, r_hs = hss[par]
                hnn, r_hnn = hns[par]
                ss_, r_ss = ssd[par]
                P.op("dve", lambda e_: e_.scalar_tensor_tensor(
                    hs[:], Sx[:, 512:640], dd[:, 1:2], hfs3[:, n, :], ALU.mult, ALU.add), [r_S, r_dd, r_hfs], [r_hs])
                P.op("pool", lambda e_: e_.memset(ss_[:], 0.0), [], [r_ss])
                P.op("act", lambda e_: e_.activation(junkd[:], hs[:], AF.Square, accum_out=ss_[:]), [r_hs], [r_junkd, r_ss])
                P.op("act", lambda e_: e_.activation(ss_[:], ss_[:], AF.Sqrt, bias=epsc[:], scale=1.0 / 128), [r_ss, r_eps], [r_ss])
                P.op("dve", lambda e_: e_.reciprocal(ss_[:], ss_[:]), [r_ss], [r_ss])
                P.op("dve", lambda e_: e_.scalar_tensor_tensor(hnn[:], hs[:], ss_[:], hn[:], ALU.mult, ALU.mult),
                     [r_hs, r_ss, r_hn], [r_hnn])
                P.op("pe", lambda e_: e_.transpose(ptt[:, 0:128], hnn[:], ident[:]), [r_hnn, r_ident], [r_ptt])
                lo, hi = grp_of[n]
                off = (n - lo) * 128
                P.op("dve", lambda e_: e_.tensor_tensor(yt1[:, off:off + 128], ptt[:, 0:128], so_t[:, off:off + 128], ALU.mult),
                     [r_ptt, r_so], [r_yt1])
                P.op("pool", lambda e_: e_.tensor_tensor(ystD[:, off:off + 128], yt1[:, off:off + 128], sg_t[:, off:off + 128], ALU.mult),
                     [r_yt1, r_sgd], [r_ystD])
                if ci == len(order) - 1 or grp_of[order[ci + 1]] != (lo, hi):
                    C.out_dma("sp", Y[3][:, lo * 128:hi * 128], ystD[:, 0:(hi - lo) * 128], [r_ystD])

            stage1(0)
            for ci in range(len(order)):
                if ci + 1 < len(order):
                    stage1(ci + 1)
                stage2(ci)

        groups = [(0, 2)] + [(2 + 4 * i, 2 + 4 * (i + 1)) for i in range(32)]
        mlstm_dir(0, list(range(NB)), groups)
        mlstm_dir(1, [1, 0] + list(range(NB - 1, 1, -1)), groups)
        C.finish()
    return nc


def _gather_tokens(per_core):
    ctx = np.concatenate([a[:, TL:NT] for a in per_core], axis=1)
    lat = np.concatenate([a[:, 0:TL] for a in per_core], axis=1)
    return np.concatenate([ctx, lat], axis=1)


def _tok_major(item):
    return np.ascontiguousarray(item.T.reshape(NB, 128, 128).transpose(1, 0, 2)).reshape(128, NB * 128)


def host_p2_inputs(res1, l, inp):
    rows, nrows = p1_out_rows()
    G = _gather_tokens([r["O1"] for r in res1])
    Gg = _gather_tokens([r["O1g"] for r in res1])

    def item(name, sub=None):
        r0, m = rows[name]
        if sub is not None:
            return G[r0 + sub[0]:r0 + sub[1]]
        return G[r0:r0 + m]

    p_ = np.arange(128)
    MK = np.stack([(p_[:, None] <= p_[None, :]), (p_[:, None] >= p_[None, :])]).astype(np.float32)
    ident = np.eye(128, dtype=np.float32)
    maps = []
    for d in range(NCORE):
        FA = np.stack([item(f"A_u{d}"), item(f"A_g{d}"), item(f"B_q{d}"), item(f"B_k{d // 4}"), item(f"B_g{d}"),
                       item(f"C_qn{d}"), item(f"C_kn{d}"), item(f"C_g{d}"), item(f"D_q{d}"), item(f"D_k{d}"),
                       item(f"D_o{d}"), item(f"D_g{d}")])
        FB = np.stack([item(f"C_qr{d // 2}", ((d % 2) * 64, (d % 2) * 64 + 64)), item("C_kr")])
        TM = np.stack([_tok_major(item(f"B_v{d // 4}")), _tok_major(item(f"C_v{d}")), _tok_major(item(f"D_k{d}"))])
        vex = np.ones((TT, 129), NPBF)
        vex[:, :128] = item(f"D_v{d}").T
        TMV = np.ascontiguousarray(vex.reshape(NB, 128, 129).transpose(1, 0, 2)).reshape(128, NB * 129)
        gsel = Gg[[d, 8 + d, 16 + d, 24 + d]]
        GT = np.ascontiguousarray(gsel.reshape(4, NB, 128).transpose(2, 0, 1)).reshape(128, 4 * NB)
        ch = slice(d * 128, (d + 1) * 128)
        LP = np.zeros((128, 16), np.float32)
        LP[:, 0:4] = inp["lru_conv_w"][l][:, ch].T
        LP[:, 4] = inp["lru_conv_b"][l][ch]
        LP[:, 5:7] = inp["lru_lambda"][l][:, ch].T
        LP[:, 7:9] = inp["lru_ba"][l][:, ch].T
        LP[:, 9:11] = inp["lru_bx"][l][:, ch].T
        LP[:, 11] = inp["swa_sink"][l][d]
        LW = np.stack([inp["lru_wa"][l][0, d], inp["lru_wx"][l][0, d], inp["lru_wa"][l][1, d], inp["lru_wx"][l][1, d]]).astype(np.float32)
        HN = np.ascontiguousarray(np.broadcast_to(inp["ml_head_norm"][l][ch].astype(np.float32), (128, 128)))
        maps.append({"FA": FA, "FB": FB, "TM": TM, "TMV": TMV, "GT": GT, "LP": LP, "LW": LW, "HN": HN, "MK": MK, "ident": ident})
    return maps


def host_p3_inputs(res2, x_full, xc_full, modT_l, l, inp):
    Yall = np.stack([r["Y"] for r in res2])
    Yt = np.ascontiguousarray(Yall.transpose(3, 1, 0, 2)).reshape(TT, D)
    common = host_p3_common(modT_l, l, inp)
    maps = []
    for core in range(NCORE):
        y_loc = np.concatenate([Yt[TC + core * TL:TC + (core + 1) * TL], Yt[core * CL:(core + 1) * CL]], axis=0)
        m = dict(common)
        m["yT"] = host_yT_tiles(y_loc)
        m["x"] = np.concatenate([x_full[core * TL:(core + 1) * TL], xc_full[core * CL:(core + 1) * CL]], axis=0)
        maps.append(m)
    return maps


def kernel(**inputs):
    inp = {k: np.asarray(v) for k, v in inputs.items()}
    cores = list(range(NCORE))
    x_full = np.ascontiguousarray(inp["x"][0], dtype=np.float32)
    xc_full = np.ascontiguousarray(inp["ctx"][0], dtype=np.float32)
    res0 = run_bass_kernel_spmd(build_p0(), host_p0_inputs(inp["c"], inp["c_ctx"], inp["w_mod"], inp["b_mod"]), core_ids=cores)
    modT = host_p0_gather(res0.results)
    nc1, nc2, nc3 = build_p1(), build_p2(), build_p3()
    for l in range(2):
        res1 = run_bass_kernel_spmd(nc1, host_p1_inputs(x_full, xc_full, modT[l], l, inp), core_ids=cores).results
        maps2 = host_p2_inputs(res1, l, inp)
        del res1
        res2 = run_bass_kernel_spmd(nc2, maps2, core_ids=cores).results
        del maps2
        maps3 = host_p3_inputs(res2, x_full, xc_full, modT[l], l, inp)
        del res2
        res3 = run_bass_kernel_spmd(nc3, maps3, core_ids=cores).results
        del maps3
        x_full = np.concatenate([r["xo"][0:TL] for r in res3], axis=0)
        xc_full = np.concatenate([r["xo"][TL:NT] for r in res3], axis=0)
    return x_full.reshape(1, T, D).astype(np.float32)
```

```python
import numpy as np
import ml_dtypes
from contextlib import ExitStack
import concourse.bass as bass
import concourse.mybir as mybir
from concourse.bass_utils import run_bass_kernel_spmd

F32 = mybir.dt.float32
BF16 = mybir.dt.bfloat16
AF = mybir.ActivationFunctionType
ALU = mybir.AluOpType
NPBF = ml_dtypes.bfloat16

NCORE = 8
D = 4096
T = 16384
TC = 256
TL = T // NCORE
CL = TC // NCORE
NT = TL + CL
TT = T + TC
NB = TT // 128
EPS = 1e-6
KC = D // 128
HALVES = ((0, 1024), (1024, NT))

O_AU, O_AG, O_BQ, O_BK, O_BV, O_BG = 0, 1024, 2048, 3072, 3328, 3584
O_CQ, O_CKV, O_KR, O_CG = 4608, 5376, 5632, 5696
O_DQ, O_DK, O_DV, O_DO, O_DGT, O_DG = 6720, 7744, 8768, 9792, 10816, 10848

SEM_CH = 16000
N_DMA_SEMS = 24


class Res:
    __slots__ = ("name", "w", "rs")

    def __init__(self, name):
        self.name = name
        self.w = None
        self.rs = []


class Op:
    __slots__ = ("eng", "fn", "deps", "sig", "sem", "semv", "dma", "gi")

    def __init__(self, eng, fn, dma, gi):
        self.eng = eng
        self.fn = fn
        self.dma = dma
        self.deps = set()
        self.sig = False
        self.sem = None
        self.semv = 0
        self.gi = gi


class Prog:
    ENGS = ("pe", "act", "dve", "pool", "sp")

    def __init__(self, nc):
        self.nc = nc
        self.ops = {e: [] for e in self.ENGS}
        self.gi = 0
        self.all_ops = []

    def res(self, name=""):
        return Res(name)

    def op(self, eng, fn, reads=(), writes=(), dma=False):
        o = Op(eng, fn, dma, self.gi)
        self.gi += 1
        deps = o.deps
        for r in reads:
            if r.w is not None:
                deps.add(r.w)
        for r in writes:
            if r.w is not None:
                deps.add(r.w)
            deps.update(r.rs)
        for r in reads:
            r.rs.append(o)
        for r in writes:
            r.w = o
            r.rs = []
        deps.discard(o)
        self.ops[eng].append(o)
        self.all_ops.append(o)
        return o

    def dma(self, eng, out, in_, reads=(), writes=(), **kw):
        return self.op(eng, lambda e: e.dma_start(out=out, in_=in_, **kw), reads, writes, dma=True)

    def emit(self, stack):
        nc = self.nc
        dma_last, dma_cnt, dma_sems = {}, {}, {}
        rr = {e: 0 for e in self.ENGS}
        for o in self.all_ops:
            if o.dma:
                key = (o.eng, rr[o.eng] % N_DMA_SEMS)
                rr[o.eng] += 1
                if key not in dma_sems:
                    dma_sems[key] = stack.enter_context(nc.semaphore(f"d_{o.eng}_{key[1]}"))
                    dma_cnt[key] = 0
                prev = dma_last.get(key)
                if prev is not None:
                    o.deps.add(prev)
                dma_last[key] = o
                dma_cnt[key] += 16
                o.sem = dma_sems[key]
                o.semv = dma_cnt[key]
                o.sig = True
        for o in self.all_ops:
            for d in o.deps:
                if d.eng == "pe" and o.eng == "pe" and not d.dma:
                    continue
                d.sig = True
        for e in self.ENGS:
            cnt = 0
            cur = None
            for o in self.ops[e]:
                if o.dma or not o.sig or o.fn is None:
                    continue
                if cnt % SEM_CH == 0:
                    cur = stack.enter_context(nc.semaphore(f"c_{e}_{cnt // SEM_CH}"))
                o.sem = cur
                o.semv = cnt % SEM_CH + 1
                cnt += 1
        block = stack.enter_context(nc.Block())

        def run(ename, eng):
            waited = {}
            for o in self.ops[ename]:
                for d in sorted(o.deps, key=lambda d: d.gi):
                    if d.eng == "pe" and ename == "pe" and not d.dma:
                        continue
                    k = id(d.sem)
                    if waited.get(k, 0) < d.semv:
                        eng.wait_ge(d.sem, d.semv)
                        waited[k] = d.semv
                if o.fn is None:
                    continue
                ins = o.fn(eng)
                if o.sig:
                    ins.then_inc(o.sem, 16 if o.dma else 1)

        if self.ops["sp"]:
            @block.sync
            def _(e):
                run("sp", e)
        if self.ops["pe"]:
            @block.tensor
            def _(e):
                run("pe", e)
        if self.ops["act"]:
            @block.scalar
            def _(e):
                run("act", e)
        if self.ops["dve"]:
            @block.vector
            def _(e):
                run("dve", e)
        if self.ops["pool"]:
            @block.gpsimd
            def _(e):
                run("pool", e)


class Ctx:
    def __init__(self, nc, st):
        self.nc = nc
        self.st = st
        self.P = Prog(nc)
        self.outs = []

    def dram_in(self, name, shape, dt=F32):
        return self.nc.dram_tensor(name, list(shape), dt, kind="ExternalInput").ap()

    def dram_out(self, name, shape, dt=F32):
        return self.nc.dram_tensor(name, list(shape), dt, kind="ExternalOutput").ap()

    def sb(self, name, shape, dt=F32):
        t = self.st.enter_context(self.nc.sbuf_tensor(name, list(shape), dt))
        return t, self.P.res(name)

    def ps(self, name, shape, dt=F32):
        t = self.st.enter_context(self.nc.psum_tensor(name, list(shape), dt))
        return t, self.P.res(name)

    def psum_banks(self, n, cols=512, prefix="pb"):
        self.banks = [self.ps(f"{prefix}{i}", [128, cols]) for i in range(n)]
        self.bank_i = 0

    def bank(self):
        b = self.banks[self.bank_i % len(self.banks)]
        self.bank_i += 1
        return b

    def out_dma(self, eng, out_ap, in_ap, reads):
        r = self.P.res("o")
        self.P.dma(eng, out_ap, in_ap, reads=reads, writes=[r])
        self.outs.append(r)

    def finish(self):
        self.P.op("sp", None, reads=self.outs)
        self.P.emit(self.st)


NJ0 = 24


def build_p0():
    nc = bass.Bass("TRN2", target_bir_lowering=False)
    with ExitStack() as st:
        C = Ctx(nc, st)
        P = C.P
        wm = C.dram_in("wm", [NJ0, 128, KC * 128])
        cvec = C.dram_in("cvec", [128, KC * 2])
        bm = C.dram_in("bm", [128, NJ0])
        o = C.dram_out("modT", [128, NJ0 * 2])
        cv, r_cv = C.sb("cv", [128, KC * 2])
        sc, r_sc = C.sb("sc", [128, KC * 2])
        bmt, r_bm = C.sb("bmt", [128, NJ0])
        ot, r_ot = C.sb("ot", [128, NJ0 * 2])
        wts = [C.sb(f"w{i}", [128, KC * 128]) for i in range(2)]
        C.psum_banks(2, cols=2)
        P.dma("sp", cv[:], cvec, writes=[r_cv])
        P.dma("sp", bmt[:], bm, writes=[r_bm])
        P.op("act", lambda e: e.activation(sc[:], cv[:], AF.Silu), [r_cv], [r_sc])
        for i in range(NJ0):
            wt, r_w = wts[i % 2]
            P.dma("sp" if i % 2 == 0 else "pool", wt[:], wm[i], writes=[r_w])
            pb, r_pb = C.bank()
            for kc in range(KC):
                P.op("pe", lambda e, wt=wt, pb=pb, kc=kc: e.matmul(
                    pb[:, 0:2], wt[:, kc * 128:(kc + 1) * 128], sc[:, kc * 2:kc * 2 + 2],
                    start=(kc == 0), stop=(kc == KC - 1)), [r_w, r_sc], [r_pb])
            P.op("dve", lambda e, pb=pb, i=i: e.tensor_scalar(
                ot[:, 2 * i:2 * i + 2], pb[:, 0:2], bmt[:, i:i + 1], None, ALU.add), [r_pb, r_bm], [r_ot])
        C.out_dma("sp", o, ot[:], [r_ot])
        C.finish()
    return nc


def _perm(idx, half):
    return np.asarray(idx) ^ half


def p1_groups():
    g = []
    r128 = np.arange(128)
    for j in range(6):
        g.append((f"cq{j}", O_CQ + j * 128 + r128, "cq", j))
    for j in range(2):
        g.append((f"ckv{j}", O_CKV + j * 128 + r128, "ckv", j))
    for h in range(8):
        g.append((f"A_u{h}", O_AU + h * 128 + r128, "copy", None))
    for kv in range(2):
        g.append((f"B_v{kv}", O_BV + kv * 128 + r128, "copy", None))
    for h in range(8):
        g.append((f"D_q{h}", O_DQ + h * 128 + r128, "copy", None))
    for h in range(8):
        g.append((f"D_v{h}", O_DV + h * 128 + r128, "copy", None))
    for h in range(8):
        g.append((f"D_k{h}", O_DK + h * 128 + r128, "scale", 128.0 ** -0.5))
    for nm, off in (("A_g", O_AG), ("B_g", O_BG), ("C_g", O_CG), ("D_g", O_DG)):
        for h in range(8):
            g.append((f"{nm}{h}", off + h * 128 + r128, "silu", None))
    for h in range(8):
        g.append((f"D_o{h}", O_DO + h * 128 + r128, "sigmoid", None))
    for h in range(8):
        g.append((f"B_q{h}", O_BQ + h * 128 + r128, "ropeA", "B"))
        g.append((f"B_q{h}", O_BQ + h * 128 + _perm(r128, 32), "ropeB", "B"))
    for kv in range(2):
        g.append((f"B_k{kv}", O_BK + kv * 128 + r128, "ropeA", "B"))
        g.append((f"B_k{kv}", O_BK + kv * 128 + _perm(r128, 32), "ropeB", "B"))
    r64 = np.arange(64)
    g.append(("C_kr", O_KR + r64, "ropeA", "C"))
    g.append(("C_kr", O_KR + _perm(r64, 16), "ropeB", "C"))
    r8 = np.arange(8)
    g.append(("G_i", np.concatenate([O_DGT + 0 * 8 + r8, O_DGT + 2 * 8 + r8]), "gi", None))
    g.append(("G_f", np.concatenate([O_DGT + 1 * 8 + r8, O_DGT + 3 * 8 + r8]), "gf", None))
    return g


def p1_groups2():
    g = []
    r128 = np.arange(128)
    r64 = np.arange(64)
    for h in range(8):
        g.append((f"C_qn{h}", "q", h * 192 + r128, "mulr", "q"))
    for h in range(8):
        g.append((f"C_kn{h}", "kv", h * 256 + r128, "mulr", "kv"))
    for h in range(8):
        g.append((f"C_v{h}", "kv", h * 256 + 128 + r128, "mulr", "kv"))
    for hp in range(4):
        base = np.concatenate([(2 * hp) * 192 + 128 + r64, (2 * hp + 1) * 192 + 128 + r64])
        perm = np.concatenate([(2 * hp) * 192 + 128 + _perm(r64, 16), (2 * hp + 1) * 192 + 128 + _perm(r64, 16)])
        g.append((f"C_qr{hp}", "q", base, "ropeA", "Cq"))
        g.append((f"C_qr{hp}", "q", perm, "ropeB", "Cq"))
    return g


def p1_out_rows():
    rows = {}
    r = 0
    for name, cols, epi, arg in p1_groups():
        if epi in ("cq", "ckv", "gi", "gf", "ropeA"):
            continue
        rows[name] = (r, len(cols))
        r += len(cols)
    for name, which, cols, epi, arg in p1_groups2():
        if epi == "ropeA":
            continue
        rows[name] = (r, len(cols))
        r += len(cols)
    return rows, r


def build_p1():
    g1 = p1_groups()
    g2 = p1_groups2()
    rows, nrows = p1_out_rows()
    NG1 = len(g1)
    g2q = [x for x in g2 if x[1] == "q"]
    g2kv = [x for x in g2 if x[1] == "kv"]
    nc = bass.Bass("TRN2", target_bir_lowering=False)
    with ExitStack() as st:
        C = Ctx(nc, st)
        P = C.P
        x = C.dram_in("x", [NT, D])
        modT = C.dram_in("modT", [128, 96 * 2])
        gpre = C.dram_in("gpre", [128, KC])
        w1 = C.dram_in("w1", [NG1, 128, KC * 128])
        w2q = C.dram_in("w2q", [len(g2q), 128, 6 * 128])
        w2kv = C.dram_in("w2kv", [len(g2kv), 128, 2 * 128])
        gq = C.dram_in("gq", [128, 6])
        gkv = C.dram_in("gkv", [128, 2])
        gbias = C.dram_in("gbias", [16, 2])
        cosB = C.dram_in("cosB", [128, NT])
        sinB = C.dram_in("sinB", [128, NT])
        cosC = C.dram_in("cosC", [128, NT])
        sinC = C.dram_in("sinC", [128, NT])
        identd = C.dram_in("ident", [128, 128])
        O1 = C.dram_out("O1", [nrows, NT], BF16)
        O1g = C.dram_out("O1g", [32, NT])

        HW = 1056
        hT, r_hT = C.sb("hT", [128, KC, HW], BF16)
        xt, r_xt = C.sb("xt", [128, D])
        junk, r_junk = C.sb("junk", [128, D // 2], BF16)
        wsts = [C.sb(f"wst{i}", [128, KC * 128]) for i in range(2)]
        wbf = [C.sb(f"wbf{i}", [128, KC, 128], BF16) for i in range(2)]
        wbf_hi = [P.res(f"wbfhi{i}") for i in range(2)]
        wld = [0]
        cqT, r_cqT = C.sb("cqT", [128, 6, HW], BF16)
        ckvT, r_ckvT = C.sb("ckvT", [128, 2, HW], BF16)
        ssq_q, r_ssq_q = C.sb("ssq_q", [128, HW])
        ssq_kv, r_ssq_kv = C.sb("ssq_kv", [128, HW])
        tcos = {k: C.sb(f"tcos{k}", [128, HW]) for k in ("B", "C", "Cq")}
        tsin = {k: C.sb(f"tsin{k}", [128, HW]) for k in ("B", "C", "Cq")}
        tmpA, r_tmpA = C.sb("tmpA", [128, HW])
        gtmp, r_gtmp = tmpA, r_tmpA
        tmpB = [C.sb(f"tmpB{i}", [128, 512]) for i in range(2)]
        sqt = [C.sb(f"sqt{i}", [128, 512], BF16) for i in range(2)]
        ost = [C.sb(f"ost{i}", [128, HW], BF16) for i in range(2)]
        ostg, r_ostg = C.sb("ostg", [16, HW])
        ident, r_ident = C.sb("identt", [128, 128])
        ones_bf, r_ones = C.sb("ones_bf", [128, 128], BF16)
        mod, r_mod = C.sb("mod", [128, 96 * 2])
        gp, r_gp = C.sb("gp", [128, KC])
        s1p, r_s1p = C.sb("s1p", [128, KC * 2])
        gqt, r_gqt = C.sb("gqt", [128, 6])
        gkvt, r_gkvt = C.sb("gkvt", [128, 2])
        gbt, r_gbt = C.sb("gbt", [16, 2])
        ngbt, r_ngbt = C.sb("ngbt", [16, 2])
        ssq, r_ssq = C.sb("ssq", [128, 2])
        epsc, r_eps = C.sb("epsc", [128, 1])
        onec, r_onec = C.sb("onec", [128, 1])
        C.psum_banks(8)

        P.dma("sp", ident[:], identd, writes=[r_ident])
        P.dma("sp", mod[:], modT, writes=[r_mod])
        P.dma("sp", gp[:], gpre, writes=[r_gp])
        P.dma("sp", gqt[:], gq, writes=[r_gqt])
        P.dma("sp", gkvt[:], gkv, writes=[r_gkvt])
        P.dma("sp", gbt[:], gbias, writes=[r_gbt])
        P.op("pool", lambda e: e.memset(ones_bf[:], 1.0), [], [r_ones])
        P.op("pool", lambda e: e.memset(epsc[:], EPS), [], [r_eps])
        P.op("pool", lambda e: e.memset(onec[:], 1.0), [], [r_onec])
        P.op("dve", lambda e: e.tensor_scalar(ngbt[:], gbt[:], -1.0, None, ALU.mult), [r_gbt], [r_ngbt])
        mod3 = mod[:].rearrange("p (j v) -> p j v", v=2)
        s1p3 = s1p[:].rearrange("p (j v) -> p j v", v=2)
        for v in range(2):
            P.op("dve", lambda e, v=v: e.scalar_tensor_tensor(
                s1p3[:, :, v], mod3[:, 32:64, v], 1.0, gp[:], ALU.add, ALU.mult), [r_mod, r_gp], [r_s1p])

        wq_i = 0
        for (h0, h1) in HALVES:
            nh = h1 - h0
            nsub = (nh + 511) // 512
            bnd = [nh * i // nsub for i in range(nsub + 1)]
            subs = [(bnd[i], bnd[i + 1]) for i in range(nsub)]
            for k, (cd, sd) in (("B", (cosB, sinB)), ("C", (cosC, sinC))):
                P.dma("pool", tcos[k][0][:, 0:nh], cd[:, h0:h1], writes=[tcos[k][1]])
                P.dma("pool", tsin[k][0][:, 0:nh], sd[:, h0:h1], writes=[tsin[k][1]])
            tok = h0
            while tok < h1:
                n = min(128, h1 - tok, (TL - tok) if tok < TL else 128)
                v = 0 if tok < TL else 1
                P.dma("sp", xt[0:n, :], x[tok:tok + n, :], writes=[r_xt])
                P.op("pool", lambda e: e.memset(ssq[:], 0.0), [], [r_ssq])
                for hh in range(2):
                    P.op("act", lambda e, n=n, hh=hh: e.activation(
                        junk[0:n, :], xt[0:n, hh * (D // 2):(hh + 1) * (D // 2)], AF.Square, accum_out=ssq[0:n, hh:hh + 1]),
                        [r_xt], [r_junk, r_ssq])
                P.op("dve", lambda e, n=n: e.tensor_tensor(ssq[0:n, 0:1], ssq[0:n, 0:1], ssq[0:n, 1:2], ALU.add), [r_ssq], [r_ssq])
                P.op("act", lambda e, n=n: e.activation(ssq[0:n, 0:1], ssq[0:n, 0:1], AF.Sqrt, bias=epsc[0:n, :], scale=1.0 / D),
                     [r_ssq, r_eps], [r_ssq])
                P.op("dve", lambda e, n=n: e.reciprocal(ssq[0:n, 0:1], ssq[0:n, 0:1]), [r_ssq], [r_ssq])
                P.op("dve", lambda e, n=n: e.tensor_scalar(xt[0:n, :], xt[0:n, :], ssq[0:n, 0:1], None, ALU.mult),
                     [r_xt, r_ssq], [r_xt])
                for kc0 in range(0, KC, 4):
                    pb, r_pb = C.bank()
                    for q in range(4):
                        kc = kc0 + q
                        P.op("pe", lambda e, pb=pb, q=q, kc=kc, n=n: e.transpose(
                            pb[:, q * 128:q * 128 + n], xt[0:n, kc * 128:(kc + 1) * 128], ident[0:n, 0:n]),
                            [r_xt, r_ident], [r_pb])
                    for q in range(4):
                        kc = kc0 + q
                        hdst = hT[:, kc, tok - h0:tok - h0 + n]
                        psrc = pb[:, q * 128:q * 128 + n]
                        sc_ = s1p[:, kc * 2 + v:kc * 2 + v + 1]
                        bi_ = mod[:, kc * 2 + v:kc * 2 + v + 1]
                        if q % 2 == 0:
                            P.op("dve", lambda e, hdst=hdst, psrc=psrc, sc_=sc_, bi_=bi_: e.tensor_scalar(
                                hdst, psrc, sc_, bi_, ALU.mult, ALU.add), [r_pb, r_s1p, r_mod], [r_hT])
                        else:
                            P.op("act", lambda e, hdst=hdst, psrc=psrc, sc_=sc_, bi_=bi_: e.activation(
                                hdst, psrc, AF.Identity, bias=bi_, scale=sc_), [r_pb, r_s1p, r_mod], [r_hT])
                tok += n

            def load_w(src_ap, ncol, gi_):
                wb, r_wb = wbf[gi_ % 2]
                r_hi = wbf_hi[gi_ % 2]
                wst, r_wst = wsts[wld[0] % 2]
                wld[0] += 1
                P.dma("sp", wst[:, 0:ncol], src_ap, writes=[r_wst])
                wflat = wb[:].rearrange("p k n -> p (k n)")
                hc = ncol // 2
                P.op("act", lambda e: e.copy(wflat[:, 0:hc], wst[:, 0:hc]), [r_wst], [r_wb])
                P.op("dve", lambda e: e.tensor_copy(wflat[:, hc:ncol], wst[:, hc:ncol]), [r_wst], [r_hi])
                return wb, r_wb, r_hi

            for gi_, (name, cols, epi, arg) in enumerate(g1):
                M = len(cols)
                wb, r_wb, r_whi = load_w(w1[gi_], KC * 128, gi_)
                osb, r_osb = ost[gi_ % 2]
                for si, (c0, c1) in enumerate(subs):
                    n = c1 - c0
                    pb, r_pb = C.bank()
                    for kc in range(KC):
                        P.op("pe", lambda e, pb=pb, wb=wb, kc=kc, M=M, c0=c0, c1=c1, n=n: e.matmul(
                            pb[0:M, 0:n], wb[:, kc, 0:M], hT[:, kc, c0:c1], start=(kc == 0), stop=(kc == KC - 1)),
                            [r_wb if kc < KC // 2 else r_whi, r_hT], [r_pb])
                    src = pb[0:M, 0:n]
                    dst = osb[0:M, c0:c1]
                    if epi == "copy":
                        P.op("act", lambda e, src=src, dst=dst: e.copy(dst, src), [r_pb], [r_osb])
                    elif epi == "scale":
                        P.op("act", lambda e, src=src, dst=dst, arg=arg: e.mul(dst, src, arg), [r_pb], [r_osb])
                    elif epi == "silu":
                        P.op("act", lambda e, src=src, dst=dst: e.activation(dst, src, AF.Silu), [r_pb], [r_osb])
                    elif epi == "sigmoid":
                        P.op("act", lambda e, src=src, dst=dst: e.activation(dst, src, AF.Sigmoid), [r_pb], [r_osb])
                    elif epi in ("cq", "ckv"):
                        tgt, r_tgt = (cqT, r_cqT) if epi == "cq" else (ckvT, r_ckvT)
                        acc, r_acc = (ssq_q, r_ssq_q) if epi == "cq" else (ssq_kv, r_ssq_kv)
                        sq, r_sq = sqt[si % 2]
                        P.op("act", lambda e, src=src, tgt=tgt, arg=arg, c0=c0, c1=c1: e.copy(tgt[:, arg, c0:c1], src),
                             [r_pb], [r_tgt])
                        P.op("act", lambda e, src=src, sq=sq, n=n: e.activation(sq[:, 0:n], src, AF.Square),
                             [r_pb], [r_sq])
                        pb2, r_pb2 = C.bank()
                        P.op("pe", lambda e, pb2=pb2, sq=sq, n=n: e.matmul(pb2[:, 0:n], ones_bf[:], sq[:, 0:n],
                                                                          start=True, stop=True), [r_sq, r_ones], [r_pb2])
                        if arg == 0:
                            P.op("dve", lambda e, acc=acc, pb2=pb2, c0=c0, c1=c1, n=n: e.tensor_copy(
                                acc[:, c0:c1], pb2[:, 0:n]), [r_pb2], [r_acc])
                        else:
                            P.op("dve", lambda e, acc=acc, pb2=pb2, c0=c0, c1=c1, n=n: e.tensor_tensor(
                                acc[:, c0:c1], acc[:, c0:c1], pb2[:, 0:n], ALU.add), [r_pb2, r_acc], [r_acc])
                    elif epi == "ropeA":
                        ct, r_ct = tcos[arg]
                        P.op("dve", lambda e, src=src, ct=ct, M=M, c0=c0, c1=c1: e.tensor_tensor(
                            tmpA[0:M, c0:c1], src, ct[0:M, c0:c1], ALU.mult), [r_pb, r_ct], [r_tmpA])
                    elif epi == "ropeB":
                        stt, r_stt = tsin[arg]
                        tb, r_tb = tmpB[si % 2]
                        P.op("dve", lambda e, src=src, stt=stt, tb=tb, M=M, c0=c0, c1=c1, n=n: e.tensor_tensor(
                            tb[0:M, 0:n], src, stt[0:M, c0:c1], ALU.mult), [r_pb, r_stt], [r_tb])
                        P.op("pool", lambda e, dst=dst, tb=tb, M=M, c0=c0, c1=c1, n=n: e.tensor_tensor(
                            dst, tb[0:M, 0:n], tmpA[0:M, c0:c1], ALU.add), [r_tb, r_tmpA], [r_osb])
                    elif epi == "gi":
                        P.op("act", lambda e, src=src, c0=c0, c1=c1: e.activation(
                            ostg[0:16, c0:c1], src, AF.Identity, bias=gbt[:, 0:1]), [r_pb, r_gbt], [r_ostg])
                    elif epi == "gf":
                        P.op("act", lambda e, src=src, c0=c0, c1=c1: e.activation(
                            gtmp[0:16, c0:c1], src, AF.Exp, bias=ngbt[:, 1:2], scale=-1.0), [r_pb, r_ngbt], [r_gtmp])
                        P.op("act", lambda e, c0=c0, c1=c1: e.activation(
                            gtmp[0:16, c0:c1], gtmp[0:16, c0:c1], AF.Ln, bias=onec[0:16, :]), [r_gtmp, r_onec], [r_gtmp])
                        P.op("dve", lambda e, c0=c0, c1=c1: e.tensor_scalar(
                            ostg[0:16, c0:c1], gtmp[0:16, c0:c1], -1.0, None, ALU.mult), [r_gtmp], [r_ostg])
                if epi in ("copy", "scale", "silu", "sigmoid", "ropeB"):
                    r0, m = rows[name]
                    C.out_dma("pool", O1[r0:r0 + m, h0:h1], osb[0:m, 0:nh], [r_osb])
                elif epi == "gi":
                    C.out_dma("pool", O1g[0:16, h0:h1], ostg[0:16, 0:nh], [r_ostg])
                elif epi == "gf":
                    C.out_dma("pool", O1g[16:32, h0:h1], ostg[0:16, 0:nh], [r_ostg])

            for acc, r_acc, dim in ((ssq_q, r_ssq_q, 768), (ssq_kv, r_ssq_kv, 256)):
                a_ = acc[:, 0:nh]
                P.op("act", lambda e, a_=a_, dim=dim: e.activation(
                    a_, a_, AF.Sqrt, bias=epsc[:], scale=1.0 / dim), [r_acc, r_eps], [r_acc])
                P.op("dve", lambda e, a_=a_: e.reciprocal(a_, a_), [r_acc], [r_acc])
            for tab in (tcos, tsin):
                o_ = tab["Cq"][0][:, 0:nh]
                i_ = tab["C"][0][:, 0:nh]
                q_ = ssq_q[:, 0:nh]
                P.op("dve", lambda e, o_=o_, i_=i_, q_=q_: e.tensor_tensor(o_, i_, q_, ALU.mult),
                     [tab["C"][1], r_ssq_q], [tab["Cq"][1]])

            iq = ikv = 0
            for gj, (name, which, cols, epi, arg) in enumerate(g2):
                M = len(cols)
                nk = 6 if which == "q" else 2
                src_ap = w2q[iq] if which == "q" else w2kv[ikv]
                gt = gqt if which == "q" else gkvt
                r_gt = r_gqt if which == "q" else r_gkvt
                if which == "q":
                    iq += 1
                else:
                    ikv += 1
                wb, r_wb = wbf[gj % 2]
                wst, r_wst = wsts[wld[0] % 2]
                wld[0] += 1
                P.dma("sp", wst[:, 0:nk * 128], src_ap, writes=[r_wst])
                for kc in range(nk):
                    P.op("dve", lambda e, wb=wb, kc=kc, gt=gt, wst=wst: e.tensor_scalar(
                        wb[:, kc, :], wst[:, kc * 128:(kc + 1) * 128], gt[:, kc:kc + 1], None, ALU.mult),
                        [r_wst, r_gt], [r_wb, wbf_hi[gj % 2]])
                rhsT, r_rhs = (cqT, r_cqT) if which == "q" else (ckvT, r_ckvT)
                rr, r_rr = (ssq_q, r_ssq_q) if which == "q" else (ssq_kv, r_ssq_kv)
                osb, r_osb = ost[gj % 2]
                for si, (c0, c1) in enumerate(subs):
                    n = c1 - c0
                    pb, r_pb = C.bank()
                    for kc in range(nk):
                        P.op("pe", lambda e, pb=pb, wb=wb, kc=kc, M=M, c0=c0, c1=c1, n=n, rhsT=rhsT, nk=nk: e.matmul(
                            pb[0:M, 0:n], wb[:, kc, 0:M], rhsT[:, kc, c0:c1], start=(kc == 0), stop=(kc == nk - 1)),
                            [r_wb, r_rhs], [r_pb])
                    src = pb[0:M, 0:n]
                    dst = osb[0:M, c0:c1]
                    if epi == "mulr":
                        P.op("dve", lambda e, src=src, dst=dst, rr=rr, M=M, c0=c0, c1=c1: e.tensor_tensor(
                            dst, src, rr[0:M, c0:c1], ALU.mult), [r_pb, r_rr], [r_osb])
                    elif epi == "ropeA":
                        ct, r_ct = tcos[arg]
                        P.op("dve", lambda e, src=src, ct=ct, M=M, c0=c0, c1=c1: e.tensor_tensor(
                            tmpA[0:M, c0:c1], src, ct[0:M, c0:c1], ALU.mult), [r_pb, r_ct], [r_tmpA])
                    elif epi == "ropeB":
                        stt, r_stt = tsin[arg]
                        tb, r_tb = tmpB[si % 2]
                        P.op("dve", lambda e, src=src, stt=stt, tb=tb, M=M, c0=c0, c1=c1, n=n: e.tensor_tensor(
                            tb[0:M, 0:n], src, stt[0:M, c0:c1], ALU.mult), [r_pb, r_stt], [r_tb])
                        P.op("pool", lambda e, dst=dst, tb=tb, M=M, c0=c0, c1=c1, n=n: e.tensor_tensor(
                            dst, tb[0:M, 0:n], tmpA[0:M, c0:c1], ALU.add), [r_tb, r_tmpA], [r_osb])
                if epi in ("mulr", "ropeB"):
                    r0, m = rows[name]
                    C.out_dma("pool", O1[r0:r0 + m, h0:h1], osb[0:m, 0:nh], [r_osb])
        C.finish()
    return nc


def host_p0_inputs(c, c_ctx, w_mod, b_mod):
    cv = np.stack([np.asarray(c).reshape(D), np.asarray(c_ctx).reshape(D)], axis=-1)
    cvec = np.ascontiguousarray(cv.reshape(KC, 128, 2).transpose(1, 0, 2)).reshape(128, KC * 2)
    maps = []
    for core in range(NCORE):
        wm = np.empty((NJ0, 128, KC, 128), np.float32)
        bm = np.empty((128, NJ0), np.float32)
        for l in range(2):
            w3 = w_mod[l].reshape(KC, 128, 96, 128)
            for jj in range(12):
                j = core * 12 + jj
                wm[l * 12 + jj] = w3[:, :, j, :].transpose(1, 0, 2)
                bm[:, l * 12 + jj] = b_mod[l, j * 128:(j + 1) * 128]
        maps.append({"wm": wm.reshape(NJ0, 128, KC * 128), "cvec": cvec, "bm": bm})
    return maps


def host_p0_gather(results):
    out = np.empty((2, 128, 96, 2), np.float32)
    for core in range(NCORE):
        o = results[core]["modT"].reshape(128, NJ0, 2)
        for l in range(2):
            out[l, :, core * 12:(core + 1) * 12, :] = o[:, l * 12:(l + 1) * 12, :]
    return out.reshape(2, 128, 192)


def host_w1(w_in_l):
    g1 = p1_groups()
    out = np.zeros((len(g1), 128, KC, 128), np.float32)
    w3 = w_in_l.reshape(KC, 128, -1)
    for i, (name, cols, epi, arg) in enumerate(g1):
        out[i, :, :, :len(cols)] = w3[:, :, cols].transpose(1, 0, 2)
    return out.reshape(len(g1), 128, KC * 128)


def host_w2(w_uq_l, w_ukv_l):
    g2 = p1_groups2()
    gq = [x for x in g2 if x[1] == "q"]
    gkv = [x for x in g2 if x[1] == "kv"]
    oq = np.zeros((len(gq), 128, 6, 128), np.float32)
    okv = np.zeros((len(gkv), 128, 2, 128), np.float32)
    wq3 = w_uq_l.reshape(6, 128, -1)
    wkv3 = w_ukv_l.reshape(2, 128, -1)
    for i, (name, which, cols, epi, arg) in enumerate(gq):
        oq[i, :, :, :len(cols)] = wq3[:, :, cols].transpose(1, 0, 2)
    for i, (name, which, cols, epi, arg) in enumerate(gkv):
        okv[i, :, :, :len(cols)] = wkv3[:, :, cols].transpose(1, 0, 2)
    return oq.reshape(len(gq), 128, 6 * 128), okv.reshape(len(gkv), 128, 2 * 128)


def pk(vec, nchunk):
    return np.ascontiguousarray(np.asarray(vec, np.float32).reshape(nchunk, 128).T)


def rope_tables(core):
    t = core * TL + np.arange(TL)
    row = (t // 64).astype(np.float64)
    col = (t % 64).astype(np.float64)

    def tab(dim):
        h = dim // 2
        half = h // 2
        cos = np.ones((dim, NT), np.float64)
        sin = np.zeros((dim, NT), np.float64)
        for i in range(dim):
            pos = row if i < h else col
            j = i % half
            f = 10000.0 ** (-j / half)
            sgn = -1.0 if (i % h) < half else 1.0
            cos[i, :TL] = np.cos(pos * f)
            sin[i, :TL] = sgn * np.sin(pos * f)
        return cos.astype(np.float32), sin.astype(np.float32)

    cB, sB = tab(128)
    c64, s64 = tab(64)
    cC = np.concatenate([c64, c64], axis=0)
    sC = np.concatenate([s64, s64], axis=0)
    return cB, sB, cC, sC


def host_p1_inputs(x_full, xc_full, modT_l, l, inp):
    w1 = host_w1(inp["w_in"][l])
    w2q, w2kv = host_w2(inp["mla_w_uq"][l], inp["mla_w_ukv"][l])
    gb = inp["ml_gate_b"][l]
    gbias = np.stack([np.concatenate([gb[0], gb[2]]), np.concatenate([gb[1], gb[3]])], axis=1).astype(np.float32)
    common = {
        "modT": modT_l, "gpre": pk(inp["g_pre"][l], KC), "w1": w1, "w2q": w2q, "w2kv": w2kv,
        "gq": pk(inp["mla_q_norm"][l], 6), "gkv": pk(inp["mla_kv_norm"][l], 2), "gbias": gbias,
        "ident": np.eye(128, dtype=np.float32),
    }
    maps = []
    for core in range(NCORE):
        cB, sB, cC, sC = rope_tables(core)
        m = dict(common)
        m["x"] = np.concatenate([x_full[core * TL:(core + 1) * TL], xc_full[core * CL:(core + 1) * CL]], axis=0)
        m.update({"cosB": cB, "sinB": sB, "cosC": cC, "sinC": sC})
        maps.append(m)
    return maps


NG3 = 16
GW3 = D // NG3


def build_p3():
    nc = bass.Bass("TRN2", target_bir_lowering=False)
    with ExitStack() as st:
        C = Ctx(nc, st)
        P = C.P
        yT = C.dram_in("yT", [5, 128, KC * 512], BF16)
        wo = C.dram_in("wo", [NG3, 128, KC * GW3])
        x = C.dram_in("x", [NT, D])
        grow = C.dram_in("grow", [2, D])
        gpost = C.dram_in("gpost", [1, D])
        xo = C.dram_out("xo", [NT, D])
        yt0, r_yt0 = C.sb("yt0", [128, KC, 512], BF16)
        wsts = [C.sb(f"wst{i}", [128, KC * GW3]) for i in range(1)] * 2
        wst_bf = wsts[0][0][:].bitcast(BF16)
        yts = [(yt0[:].rearrange("p k n -> p (k n)"), yt0, r_yt0),
               (wst_bf, wst_bf.rearrange("p (k n) -> p k n", n=512), wsts[0][1])]
        wscr = nc.dram_tensor("wscr", [NG3, 128, KC * GW3], BF16).ap()
        r_scr = [P.res(f"scr{g}") for g in range(NG3)]
        wbf = [C.sb(f"wbf{i}", [128, KC, GW3], BF16) for i in range(2)]
        wbf_hi = [P.res(f"wbfhi{i}") for i in range(2)]
        ots = [C.sb(f"ot{i}", [128, D]) for i in range(4)]
        xt, r_xt = C.sb("xt", [128, D])
        ggs = [C.sb(f"gg{i}", [128, D]) for i in range(1)] * 2
        junk, r_junk = C.sb("junk", [128, D // 2], BF16)
        ssq, r_ssq = C.sb("ssq", [128, 2])
        epsc, r_eps = C.sb("epsc", [128, 1])
        C.psum_banks(8)
        P.op("pool", lambda e: e.memset(epsc[:], EPS), [], [r_eps])
        def load_gg(v):
            gg, r_gg = ggs[v]
            P.dma("sp", xt[:], gpost[0:1, :].to_broadcast([128, D]), writes=[r_xt])
            P.dma("sp", gg[:], grow[v:v + 1, :].to_broadcast([128, D]), writes=[r_gg])
            P.op("dve", lambda e, gg=gg: e.tensor_tensor(gg[:], gg[:], xt[:], ALU.mult), [r_gg, r_xt], [r_gg])

        load_gg(0)
        sts = [(s, s + 512) for s in range(0, TL, 512)] + [(TL, NT)]
        wi = 0
        for sti, (s0, s1) in enumerate(sts):
            v = 0 if s0 < TL else 1
            ns = s1 - s0
            ytf, yt, r_yt = yts[sti % 2]
            P.dma("pool", ytf, yT[sti], writes=[r_yt])
            tiles = [(a, min(a + 128, ns)) for a in range(0, ns, 128)]
            for g in range(NG3):
                wb, r_wb = wbf[wi % 2]
                r_whi = wbf_hi[wi % 2]
                wst, r_wst = wsts[wi % 2]
                wi += 1
                wflat = wb[:].rearrange("p k n -> p (k n)")
                hc = KC * GW3 // 2
                if sti == 0:
                    P.dma("sp", wst[:], wo[g], writes=[r_wst])
                    P.op("act", lambda e, wflat=wflat, wst=wst: e.copy(wflat[:, 0:hc], wst[:, 0:hc]), [r_wst], [r_wb])
                    P.op("dve", lambda e, wflat=wflat, wst=wst: e.tensor_copy(wflat[:, hc:], wst[:, hc:]), [r_wst], [r_whi])
                    P.dma("pool", wscr[g], wflat, reads=[r_wb, r_whi], writes=[r_scr[g]])
                else:
                    P.dma("sp", wflat, wscr[g], reads=[r_scr[g]], writes=[r_wb, r_whi])
                for ti, (a, b) in enumerate(tiles):
                    n = b - a
                    pb, r_pb = C.bank()
                    ot, r_ot = ots[ti]
                    for kc in range(KC):
                        P.op("pe", lambda e, pb=pb, yt=yt, wb=wb, kc=kc, a=a, b=b, n=n: e.matmul(
                            pb[0:n, 0:GW3], yt[:, kc, a:b], wb[:, kc, :], start=(kc == 0), stop=(kc == KC - 1)),
                            [r_yt, r_wb if kc < KC // 2 else r_whi], [r_pb])
                    P.op("act", lambda e, pb=pb, ot=ot, n=n, g=g: e.copy(ot[0:n, g * GW3:(g + 1) * GW3], pb[0:n, 0:GW3]),
                         [r_pb], [r_ot])
            if v == 1:
                load_gg(1)
            for ti, (a, b) in enumerate(tiles):
                n = b - a
                ot, r_ot = ots[ti]
                gg, r_gg = ggs[v]
                P.dma("sp", xt[0:n, :], x[s0 + a:s0 + b, :], writes=[r_xt])
                P.op("pool", lambda e: e.memset(ssq[:], 0.0), [], [r_ssq])
                for hh in range(2):
                    P.op("act", lambda e, ot=ot, n=n, hh=hh: e.activation(
                        junk[0:n, :], ot[0:n, hh * (D // 2):(hh + 1) * (D // 2)], AF.Square, accum_out=ssq[0:n, hh:hh + 1]),
                        [r_ot], [r_junk, r_ssq])
                P.op("dve", lambda e, n=n: e.tensor_tensor(ssq[0:n, 0:1], ssq[0:n, 0:1], ssq[0:n, 1:2], ALU.add), [r_ssq], [r_ssq])
                P.op("act", lambda e, n=n: e.activation(ssq[0:n, 0:1], ssq[0:n, 0:1], AF.Sqrt, bias=epsc[0:n, :], scale=1.0 / D),
                     [r_ssq, r_eps], [r_ssq])
                P.op("dve", lambda e, n=n: e.reciprocal(ssq[0:n, 0:1], ssq[0:n, 0:1]), [r_ssq], [r_ssq])
                P.op("dve", lambda e, ot=ot, gg=gg, n=n: e.scalar_tensor_tensor(
                    ot[0:n, :], ot[0:n, :], ssq[0:n, 0:1], gg[0:n, :], ALU.mult, ALU.mult), [r_ot, r_ssq, r_gg], [r_ot])
                P.op("dve", lambda e, ot=ot, n=n: e.tensor_tensor(ot[0:n, :], ot[0:n, :], xt[0:n, :], ALU.add),
                     [r_ot, r_xt], [r_ot])
                C.out_dma("pool", xo[s0 + a:s0 + b, :], ot[0:n, :], [r_ot])
        C.finish()
    return nc


def host_p3_common(modT_l, l, inp):
    wo = inp["w_out"][l].reshape(KC, 128, NG3, GW3).transpose(2, 1, 0, 3)
    wo = np.ascontiguousarray(wo).reshape(NG3, 128, KC * GW3)
    m3 = modT_l.reshape(128, 96, 2)
    grow = np.ascontiguousarray(m3[:, 64:96, :].transpose(2, 1, 0)).reshape(2, D)
    return {"wo": wo, "grow": grow, "gpost": np.asarray(inp["g_post"][l], np.float32).reshape(1, D)}


def host_yT_tiles(y_loc):
    out = np.zeros((5, 128, KC, 512), y_loc.dtype)
    for i in range(5):
        s0 = i * 512
        s1 = min(s0 + 512, NT)
        out[i, :, :, :s1 - s0] = y_loc[s0:s1].reshape(s1 - s0, KC, 128).transpose(2, 1, 0)
    return out.reshape(5, 128, KC * 512)


BIGW = 16896
LCH = 512


def build_p2():
    nc = bass.Bass("TRN2", target_bir_lowering=False)
    with ExitStack() as st:
        C = Ctx(nc, st)
        P = C.P
        FA = C.dram_in("FA", [12, 128, TT], BF16)
        FB = C.dram_in("FB", [2, 64, TT], BF16)
        TM = C.dram_in("TM", [3, 128, NB * 128], BF16)
        TMV = C.dram_in("TMV", [128, NB * 129], BF16)
        GT = C.dram_in("GT", [128, 4 * NB])
        LP = C.dram_in("LP", [128, 16])
        LW = C.dram_in("LW", [4, 128, 128])
        HN = C.dram_in("HN", [128, 128])
        MK = C.dram_in("MK", [2, 128, 128])
        identd = C.dram_in("ident", [128, 128])
        Y = C.dram_out("Y", [4, 128, TT], BF16)

        bigs = [C.sb(f"big{i}", [128, BIGW], BF16) for i in range(5)]
        lp, r_lp = C.sb("lp", [128, 16])
        lwbf, r_lwbf = C.sb("lwbf", [128, 4, 128], BF16)
        hn, r_hn = C.sb("hnrow", [128, 128])
        mk32, r_mk32 = C.sb("mk32", [128, 2, 128])
        mkbf, r_mkbf = C.sb("mkbf", [128, 2, 128], BF16)
        ident, r_ident = C.sb("identt", [128, 128])
        ones_bf, r_ones = C.sb("ones_bf", [128, 128], BF16)
        ones32, r_ones32 = C.sb("ones32", [128, 128])
        onec, r_onec = C.sb("onec", [128, 1])
        epsc, r_eps = C.sb("epsc", [128, 1])
        cc, r_cc = C.sb("cc", [128, 8])
        tf = [C.sb(f"tf{i}", [128, LCH]) for i in range(6)]
        tb = [C.sb(f"tbf{i}", [128, 1024], BF16) for i in range(3)]
        S = [C.ps(f"S{i}", [128, 1024]) for i in range(3)]
        PO = [C.ps("PO0", [128, 512]), (S[2][0][:, 0:512], S[2][1])]
        PD = [C.ps("PD0", [128, 512]), (S[2][0][:, 512:1024], S[2][1])]

        P.dma("sp", lp[:], LP, writes=[r_lp])
        P.dma("sp", hn[:], HN, writes=[r_hn])
        P.dma("sp", mk32[:], MK.rearrange("g k j -> k g j"), writes=[r_mk32])
        P.dma("sp", ident[:], identd, writes=[r_ident])
        P.op("pool", lambda e: e.memset(ones_bf[:], 1.0), [], [r_ones])
        P.op("pool", lambda e: e.memset(ones32[:], 1.0), [], [r_ones32])
        P.op("pool", lambda e: e.memset(onec[:], 1.0), [], [r_onec])
        P.op("pool", lambda e: e.memset(epsc[:], EPS), [], [r_eps])
        lwst = tf[0][0][:].rearrange("p (g j) -> p g j", j=128)
        P.dma("sp", lwst, LW.rearrange("g k j -> k g j"), writes=[tf[0][1]])
        P.op("dve", lambda e: e.tensor_copy(lwbf[:], lwst), [tf[0][1]], [r_lwbf])
        P.op("dve", lambda e: e.tensor_copy(mkbf[:], mk32[:]), [r_mk32], [r_mkbf])
        P.op("act", lambda e: e.activation(cc[:, 0:2], lp[:, 5:7], AF.Exp, scale=-1.0), [r_lp], [r_cc])
        P.op("act", lambda e: e.activation(cc[:, 0:2], cc[:, 0:2], AF.Ln, bias=onec[:]), [r_cc, r_onec], [r_cc])
        P.op("dve", lambda e: e.tensor_scalar(cc[:, 2:4], cc[:, 0:2], -16.0, None, ALU.mult), [r_cc], [r_cc])
        P.op("dve", lambda e: e.tensor_scalar(cc[:, 0:2], cc[:, 0:2], -8.0, None, ALU.mult), [r_cc], [r_cc])
        P.op("act", lambda e: e.activation(cc[:, 4:5], lp[:, 11:12], AF.Exp), [r_lp], [r_cc])

        def load_big(i, src, ncol, part=128, eng="sp"):
            t, r = bigs[i]
            P.dma(eng, t[0:part, 0:ncol], src, writes=[r])
            return t, r

        u_bf, r_u = load_big(0, FA[0], TT)
        hfl = bigs[1][0][:].bitcast(F32)
        hfh = bigs[2][0][:].bitcast(F32)
        r_hfl, r_hfh = bigs[1][1], bigs[2][1]

        def hf_ap(s, e):
            if e <= 8448:
                return hfl[:, s:e], r_hfl
            return hfh[:, s - 8448:e - 8448], r_hfh

        chunks_ctx = [(0, TC, 0, TC)]
        chunks_lat = [(TC + i * LCH, TC + (i + 1) * LCH, TC, TT) for i in range(T // LCH)]
        (v32, r_v32), (rb, r_rb), (ib, r_ib), (t2, r_t2), (hb0, r_hb0), (hb1, r_hb1) = tf
        (vbf, r_vbf), (sgt, r_sgt), (yst, r_yst) = tb[0], tb[1], tb[2]
        hbs = [(hb0, r_hb0), (hb1, r_hb1)]

        def lru_chunk(s, e, S_, E_, d, prev_ap, prev_res, hbi):
            n = e - s
            P.op("dve", lambda e_: e_.tensor_scalar(v32[:, 0:n], u_bf[:, s:e], lp[:, 2:3], lp[:, 4:5], ALU.mult, ALU.add),
                 [r_u, r_lp], [r_v32])
            for j in (0, 1, 3):
                o = j - 2
                lo, hi = max(s, S_ - o), min(e, E_ - o)
                if lo < hi:
                    P.op("dve", lambda e_, lo=lo, hi=hi, o=o, j=j: e_.scalar_tensor_tensor(
                        v32[:, lo - s:hi - s], u_bf[:, lo + o:hi + o], lp[:, j:j + 1], v32[:, lo - s:hi - s],
                        ALU.mult, ALU.add), [r_u, r_lp, r_v32], [r_v32])
            P.op("act", lambda e_: e_.copy(vbf[:, 0:n], v32[:, 0:n]), [r_v32], [r_vbf])
            pa, r_pa = S[0]
            px, r_px = S[1]
            for c0 in range(0, n, 512):
                c1_ = min(c0 + 512, n)
                P.op("pe", lambda e_, c0=c0, c1_=c1_: e_.matmul(pa[:, c0:c1_], lwbf[:, 2 * d, :], vbf[:, c0:c1_],
                                                               start=True, stop=True), [r_lwbf, r_vbf], [r_pa])
                P.op("pe", lambda e_, c0=c0, c1_=c1_: e_.matmul(px[:, c0:c1_], lwbf[:, 2 * d + 1, :], vbf[:, c0:c1_],
                                                               start=True, stop=True), [r_lwbf, r_vbf], [r_px])
            P.op("act", lambda e_: e_.activation(ib[:, 0:n], px[:, 0:n], AF.Sigmoid, bias=lp[:, 9 + d:10 + d]), [r_px, r_lp], [r_ib])
            P.op("act", lambda e_: e_.activation(rb[:, 0:n], pa[:, 0:n], AF.Sigmoid, bias=lp[:, 7 + d:8 + d]), [r_pa, r_lp], [r_rb])
            P.op("act", lambda e_: e_.activation(t2[:, 0:n], rb[:, 0:n], AF.Exp, scale=cc[:, 2 + d:3 + d]), [r_rb, r_cc], [r_t2])
            P.op("act", lambda e_: e_.activation(rb[:, 0:n], rb[:, 0:n], AF.Exp, scale=cc[:, d:d + 1]), [r_rb, r_cc], [r_rb])
            P.op("act", lambda e_: e_.activation(t2[:, 0:n], t2[:, 0:n], AF.Sqrt, bias=onec[:], scale=-1.0), [r_t2, r_onec], [r_t2])
            P.op("dve", lambda e_: e_.tensor_tensor(ib[:, 0:n], ib[:, 0:n], v32[:, 0:n], ALU.mult), [r_ib, r_v32], [r_ib])
            P.op("dve", lambda e_: e_.tensor_tensor(ib[:, 0:n], ib[:, 0:n], t2[:, 0:n], ALU.mult), [r_ib, r_t2], [r_ib])
            init = 0.0 if prev_ap is None else prev_ap
            rd = [r_rb, r_ib] + ([prev_res] if prev_res is not None else [])
            if d == 0:
                out, r_out = hf_ap(s, e)
                P.op("dve", lambda e_: e_.tensor_tensor_scan(out, rb[:, 0:n], ib[:, 0:n], init, ALU.mult, ALU.add), rd, [r_out])
                return out[:, n - 1:n], r_out
            hb, r_hb = hbs[hbi]
            P.op("dve", lambda e_: e_.tensor_tensor_scan(hb[:, 0:n][:, ::-1], rb[:, 0:n][:, ::-1], ib[:, 0:n][:, ::-1],
                                                         init, ALU.mult, ALU.add), rd, [r_hb])
            hfa, r_hfa = hf_ap(s, e)
            P.dma("sp", sgt[:, 0:n], FA[1][:, s:e], writes=[r_sgt])
            P.op("dve", lambda e_: e_.tensor_tensor(v32[:, 0:n], hb[:, 0:n], hfa, ALU.add), [r_hb, r_hfa], [r_v32])
            P.op("dve", lambda e_: e_.tensor_tensor(yst[:, 0:n], v32[:, 0:n], sgt[:, 0:n], ALU.mult), [r_v32, r_sgt], [r_yst])
            C.out_dma("sp", Y[0][:, s:e], yst[:, 0:n], [r_yst])
            return hb[:, 0:1], r_hb

        pa_, pr_ = None, None
        for (s, e, S_, E_) in chunks_ctx + chunks_lat:
            pa_, pr_ = lru_chunk(s, e, S_, E_, 0, pa_, pr_, 0)
        pa_, pr_ = None, None
        hbi = 0
        for (s, e, S_, E_) in chunks_ctx + chunks_lat[::-1]:
            pa_, pr_ = lru_chunk(s, e, S_, E_, 1, pa_, pr_, hbi)
            hbi ^= 1

        qT, r_q = load_big(0, FA[2], TT)
        kT, r_k = load_big(1, FA[3], TT)
        vtm, r_v = load_big(2, TM[0], NB * 128)
        sgB, r_sgB = load_big(3, FA[4], TT)
        vtm3 = vtm[:, 0:NB * 128].rearrange("p (b d) -> p b d", d=128)
        pTs = [tb[0], tb[1]]
        ystB, r_ystB = tb[2]
        (dn, r_dn), (ob, r_ob) = tf[0], tf[1]
        sc_swa = 128.0 ** -0.5
        qblocks = [(0, [(0, None), (1, None)]), (1, [(0, None), (1, None)])]
        for nq in range(128):
            keys = []
            if nq > 0:
                keys.append((2 + nq - 1, 1))
            keys.append((2 + nq, None))
            if nq < 127:
                keys.append((2 + nq + 1, 0))
            keys += [(0, None), (1, None)]
            qblocks.append((2 + nq, keys))
        def swa_stage1(bi):
            qb, keys = qblocks[bi]
            Sx, r_S = S[bi % 2]
            pT, r_pT = pTs[bi % 2]
            nk = len(keys)
            qs = slice(qb * 128, (qb + 1) * 128)
            for sl, (kb, mk) in enumerate(keys):
                P.op("pe", lambda e_, sl=sl, kb=kb: e_.matmul(
                    Sx[:, sl * 128:(sl + 1) * 128], kT[:, kb * 128:(kb + 1) * 128], qT[:, qs], start=True, stop=True),
                    [r_k, r_q], [r_S])
            P.op("act", lambda e_: e_.activation(pT[:, 0:nk * 128], Sx[:, 0:nk * 128], AF.Exp, scale=sc_swa), [r_S], [r_pT])
            for sl, (kb, mk) in enumerate(keys):
                if mk is not None:
                    P.op("dve", lambda e_, sl=sl, mk=mk: e_.tensor_tensor(
                        pT[:, sl * 128:(sl + 1) * 128], pT[:, sl * 128:(sl + 1) * 128], mkbf[:, mk, :], ALU.mult),
                        [r_pT, r_mkbf], [r_pT])

        def swa_stage2(bi):
            qb, keys = qblocks[bi]
            po, r_po = PO[bi % 2]
            pd, r_pd = PD[bi % 2]
            pT, r_pT = pTs[bi % 2]
            nk = len(keys)
            qs = slice(qb * 128, (qb + 1) * 128)
            for sl, (kb, mk) in enumerate(keys):
                P.op("pe", lambda e_, sl=sl, kb=kb: e_.matmul(
                    po[:, 0:128], vtm3[:, kb, :], pT[:, sl * 128:(sl + 1) * 128], start=(sl == 0), stop=(sl == nk - 1)),
                    [r_v, r_pT], [r_po])
            for sl, (kb, mk) in enumerate(keys):
                P.op("pe", lambda e_, sl=sl: e_.matmul(
                    pd[:, 0:128], ones_bf[:], pT[:, sl * 128:(sl + 1) * 128], start=(sl == 0), stop=(sl == nk - 1)),
                    [r_ones, r_pT], [r_pd])
            P.op("dve", lambda e_: e_.tensor_scalar(dn[:, 0:128], pd[:, 0:128], cc[:, 4:5], None, ALU.add), [r_pd, r_cc], [r_dn])
            P.op("dve", lambda e_: e_.reciprocal(dn[:, 0:128], dn[:, 0:128]), [r_dn], [r_dn])
            P.op("dve", lambda e_: e_.tensor_tensor(ob[:, 0:128], po[:, 0:128], dn[:, 0:128], ALU.mult), [r_po, r_dn], [r_ob])
            j8 = bi % 8
            P.op("dve", lambda e_: e_.tensor_tensor(ystB[:, j8 * 128:(j8 + 1) * 128], ob[:, 0:128], sgB[:, qs], ALU.mult),
                 [r_ob, r_sgB], [r_ystB])
            if j8 == 7 or bi == len(qblocks) - 1:
                b0 = bi - j8
                q0 = qblocks[b0][0]
                C.out_dma("sp", Y[1][:, q0 * 128:(qb + 1) * 128], ystB[:, 0:(j8 + 1) * 128], [r_ystB])

        swa_stage1(0)
        for bi in range(len(qblocks)):
            if bi + 1 < len(qblocks):
                swa_stage1(bi + 1)
            swa_stage2(bi)

        knT, r_kn = load_big(0, FA[6], TT)
        krT, r_kr = load_big(1, FB[1], TT, part=64)
        P.op("pool", lambda e_: e_.memset(krT[64:128, 0:TT], 0.0), [], [r_kr])
        vtmC, r_vC = load_big(2, TM[1], NB * 128)
        vtmC3 = vtmC[:, 0:NB * 128].rearrange("p (b d) -> p b d", d=128)
        sc_mla = 192.0 ** -0.5
        qns = [C.sb(f"qn{i}", [128, 512], BF16) for i in range(1)] * 2
        qrs = [C.sb(f"qr{i}", [128, 512], BF16) for i in range(1)] * 2
        P.op("pool", lambda e_: e_.memset(qrs[0][0][64:128, :], 0.0), [], [qrs[0][1]])
        sgs = [C.sb(f"sgc{i}", [128, 512], BF16) for i in range(1)] * 2
        ystC = [C.sb(f"ystc{i}", [128, 512], BF16) for i in range(1)] * 2
        pTC = [tb[0], tb[1], tb[2]]
        (rec, r_rec), (oc, r_oc) = tf[0], tf[1]
        (accA, r_accA), (accB, r_accB) = tf[2], tf[3]
        qtiles = [(0, TC, [(0, 1)])] + [(TC + i * 512, TC + (i + 1) * 512, [(2 * j, 2 * j + 1) for j in range(NB // 2)])
                                        for i in range(32)]
        pcount = 0
        for qi, (q0, q1, pairs) in enumerate(qtiles):
            n = q1 - q0
            qn, r_qn = qns[qi % 2]
            qr, r_qr = qrs[qi % 2]
            sg, r_sg = sgs[qi % 2]
            yc, r_yc = ystC[qi % 2]
            po, r_po = PO[0]
            pd, r_pd = PD[0]
            P.dma("sp", qn[:, 0:n], FA[5][:, q0:q1], writes=[r_qn])
            P.dma("sp", qr[0:64, 0:n], FB[0][:, q0:q1], writes=[r_qr])
            P.dma("sp", sg[:, 0:n], FA[7][:, q0:q1], writes=[r_sg])
            npair = len(pairs)

            def qk(i):
                Sx, r_S = S[(pcount + i) % 3]
                for hh in range(2):
                    kb = pairs[i][hh]
                    P.op("pe", lambda e_, Sx=Sx, hh=hh, kb=kb, n=n, qn=qn, qr=qr: e_.matmul(
                        Sx[:, hh * 512:hh * 512 + n], knT[:, kb * 128:(kb + 1) * 128], qn[:, 0:n], start=True, stop=False),
                        [r_kn, r_qn], [r_S])
                    P.op("pe", lambda e_, Sx=Sx, hh=hh, kb=kb, n=n, qn=qn, qr=qr: e_.matmul(
                        Sx[:, hh * 512:hh * 512 + n], krT[:, kb * 128:(kb + 1) * 128], qr[:, 0:n], start=False, stop=True),
                        [r_kr, r_qr], [r_S])
                pT, r_pT = pTC[(pcount + i) % 3]
                if n == 512:
                    P.op("act", lambda e_, Sx=Sx, pT=pT: e_.activation(pT[:, :], Sx[:, :], AF.Exp, scale=sc_mla), [r_S], [r_pT])
                else:
                    for hh in range(2):
                        P.op("act", lambda e_, Sx=Sx, pT=pT, hh=hh, n=n: e_.activation(
                            pT[:, hh * 512:hh * 512 + n], Sx[:, hh * 512:hh * 512 + n], AF.Exp, scale=sc_mla), [r_S], [r_pT])

            def pv(i):
                pT, r_pT = pTC[(pcount + i) % 3]
                for hh in range(2):
                    kb = pairs[i][hh]
                    first = (i == 0 and hh == 0)
                    last = (i == npair - 1 and hh == 1)
                    P.op("pe", lambda e_, pT=pT, hh=hh, kb=kb, first=first, last=last, n=n, po=po: e_.matmul(
                        po[:, 0:n], vtmC3[:, kb, :], pT[:, hh * 512:hh * 512 + n], start=first, stop=last),
                        [r_vC, r_pT], [r_po])
                    P.op("pe", lambda e_, pT=pT, hh=hh, first=first, last=last, n=n, pd=pd: e_.matmul(
                        pd[:, 0:n], ones_bf[:], pT[:, hh * 512:hh * 512 + n], start=first, stop=last),
                        [r_ones, r_pT], [r_pd])

            qk(0)
            if npair > 1:
                qk(1)
            for i in range(npair):
                if i + 2 < npair:
                    qk(i + 2)
                pv(i)
            pcount += npair
            P.op("dve", lambda e_, pd=pd, n=n: e_.reciprocal(rec[:, 0:n], pd[:, 0:n]), [r_pd], [r_rec])
            P.op("dve", lambda e_, po=po, n=n: e_.tensor_tensor(oc[:, 0:n], po[:, 0:n], rec[:, 0:n], ALU.mult), [r_po, r_rec], [r_oc])
            P.op("pool", lambda e_, yc=yc, sg=sg, n=n: e_.tensor_tensor(yc[:, 0:n], oc[:, 0:n], sg[:, 0:n], ALU.mult), [r_oc, r_sg], [r_yc])
            C.out_dma("sp", Y[2][:, q0:q1], yc[:, 0:n], [r_yc])

        qT, r_q = load_big(0, FA[8], TT)
        kT, r_k = load_big(1, FA[9], TT)
        ktm, r_ktm = load_big(2, TM[2], NB * 128)
        vex, r_vex = load_big(3, TMV, NB * 129)
        hfs, r_hfs = bigs[4]
        ktm3 = ktm[:, 0:NB * 128].rearrange("p (b d) -> p b d", d=128)
        vex3 = vex[:, 0:NB * 129].rearrange("p (b d) -> p b d", d=129)
        hfs3 = hfs[:, 0:NB * 128].rearrange("p (b d) -> p b d", d=128)
        gt, r_gt = C.sb("gt", [128, 4 * NB])
        P.dma("sp", gt[:], GT, writes=[r_gt])
        gt3 = gt[:].rearrange("p (g n) -> p g n", n=NB)
        sc_t = {}
        for nm in ("b", "G", "u", "eb", "eg", "ueg"):
            for d in range(2):
                sc_t[(nm, d)] = C.sb(f"sc_{nm}{d}", [128, NB])
        for d in range(2):
            ig = gt3[:, d, :]
            lf = gt3[:, 2 + d, :]
            pb_, r_pb_ = PO[0]
            pg_, r_pg_ = PO[1]
            (b_, r_b), (G_, r_G), (u_, r_uu), (eb_, r_eb), (eg_, r_eg), (ueg_, r_ueg) = [sc_t[(nm, d)] for nm in ("b", "G", "u", "eb", "eg", "ueg")]
            P.op("pe", lambda e_, d=d, lf=lf, pb_=pb_: e_.matmul(pb_[:, 0:NB], mk32[:, d, :], lf, start=True, stop=True), [r_mk32, r_gt], [r_pb_])
            P.op("pe", lambda e_, lf=lf, pg_=pg_: e_.matmul(pg_[:, 0:NB], ones32[:], lf, start=True, stop=True), [r_ones32, r_gt], [r_pg_])
            P.op("dve", lambda e_, b_=b_, pb_=pb_: e_.tensor_copy(b_[:], pb_[:, 0:NB]), [r_pb_], [r_b])
            P.op("dve", lambda e_, G_=G_, pg_=pg_: e_.tensor_copy(G_[:], pg_[:, 0:NB]), [r_pg_], [r_G])
            P.op("dve", lambda e_, u_=u_, ig=ig, b_=b_: e_.tensor_tensor(u_[:], ig, b_[:], ALU.subtract), [r_gt, r_b], [r_uu])
            P.op("dve", lambda e_, ueg_=ueg_, u_=u_, G_=G_: e_.tensor_tensor(ueg_[:], u_[:], G_[:], ALU.add), [r_uu, r_G], [r_ueg])
            P.op("act", lambda e_, u_=u_: e_.activation(u_[:], u_[:], AF.Exp), [r_uu], [r_uu])
            P.op("act", lambda e_, ueg_=ueg_: e_.activation(ueg_[:], ueg_[:], AF.Exp), [r_ueg], [r_ueg])
            P.op("act", lambda e_, eb_=eb_, b_=b_: e_.activation(eb_[:], b_[:], AF.Exp), [r_b], [r_eb])
            P.op("act", lambda e_, eg_=eg_, G_=G_: e_.activation(eg_[:], G_[:], AF.Exp), [r_G], [r_eg])

        C32, r_C32 = C.sb("C32", [128, 129])
        Cbfs = [C.sb(f"Cbf{i}", [128, 129], BF16) for i in range(2)]
        pTd = [C.sb(f"pTd{i}", [128, 128], BF16) for i in range(2)]
        vps = [C.sb(f"vp{i}", [128, 129], BF16) for i in range(2)]
        vpps = [C.sb(f"vpp{i}", [128, 129], BF16) for i in range(2)]
        dds = [C.sb(f"dd{i}", [128, 2]) for i in range(2)]
        hss = [C.sb(f"hs{i}", [128, 128]) for i in range(2)]
        hns = [C.sb(f"hn{i}", [128, 128]) for i in range(2)]
        ssd = [C.sb(f"ssd{i}", [128, 1]) for i in range(2)]
        junkd, r_junkd = C.sb("junkd", [128, 128], BF16)
        (yt1, r_yt1) = tf[2]
        so_t, r_so = tb[0]
        sg_t, r_sgd = tb[1]
        ystD, r_ystD = tb[2]

        def mlstm_dir(d, order, groups):
            (b_, r_b), (G_, r_G), (u_, r_uu), (eb_, r_eb), (eg_, r_eg), (ueg_, r_ueg) = [sc_t[(nm, d)] for nm in ("b", "G", "u", "eb", "eg", "ueg")]
            P.op("pool", lambda e_: e_.memset(C32[:], 0.0), [], [r_C32])
            P.op("pool", lambda e_: e_.memset(Cbfs[0][0][:], 0.0), [], [Cbfs[0][1]])
            grp_of = {}
            for (lo, hi) in groups:
                for c in range(lo, hi):
                    grp_of[c] = (lo, hi)
            def stage1(ci):
                n = order[ci]
                par = ci % 2
                cs = slice(n * 128, (n + 1) * 128)
                Sx, r_S = S[par]
                pk_, r_pk = PO[par]
                pT, r_pT = pTd[par]
                vp, r_vp = vps[par]
                vpp, r_vpp = vpps[par]
                P.op("pe", lambda e_: e_.matmul(Sx[:, 0:128], kT[:, cs], qT[:, cs], start=True, stop=True), [r_k, r_q], [r_S])
                P.op("dve", lambda e_: e_.tensor_tensor(pT[:], Sx[:, 0:128], mkbf[:, d, :], ALU.mult), [r_S, r_mkbf], [r_pT])
                P.op("act", lambda e_: e_.mul(vp[:], vex3[:, n, :], u_[:, n:n + 1]), [r_vex, r_uu], [r_vp])
                P.op("act", lambda e_: e_.mul(vpp[:], vex3[:, n, :], ueg_[:, n:n + 1]), [r_vex, r_ueg], [r_vpp])
                P.op("pe", lambda e_: e_.matmul(pk_[:, 0:129], ktm3[:, n, :], vpp[:], start=True, stop=True), [r_ktm, r_vpp], [r_pk])

            def stage2(ci):
                n = order[ci]
                par = ci % 2
                cs = slice(n * 128, (n + 1) * 128)
                Sx, r_S = S[par]
                pk_, r_pk = PO[par]
                ptt, r_ptt = PD[par]
                Cb, r_Cb = Cbfs[par]
                Cn, r_Cn = Cbfs[1 - par]
                pT, r_pT = pTd[par]
                vp, r_vp = vps[par]
                dd, r_dd = dds[par]
                if d == 1 and n in grp_of and (ci == 0 or grp_of[order[ci - 1]] != grp_of[n]):
                    lo, hi = grp_of[n]
                    P.dma("sp", so_t[:, 0:(hi - lo) * 128], FA[10][:, lo * 128:hi * 128], writes=[r_so])
                    P.dma("sp", sg_t[:, 0:(hi - lo) * 128], FA[11][:, lo * 128:hi * 128], writes=[r_sgd])
                P.op("pe", lambda e_: e_.matmul(Sx[:, 512:641], qT[:, cs], Cb[:], start=True, stop=False), [r_q, r_Cb], [r_S])
                P.op("pe", lambda e_: e_.matmul(Sx[:, 512:641], pT[:], vp[:], start=False, stop=True), [r_pT, r_vp], [r_S])
                P.op("dve", lambda e_: e_.scalar_tensor_tensor(C32[:], C32[:], eg_[:, n:n + 1], pk_[:, 0:129], ALU.mult, ALU.add),
                     [r_C32, r_eg, r_pk], [r_C32])
                P.op("act", lambda e_: e_.copy(Cn[:], C32[:]), [r_C32], [r_Cn])
                P.op("dve", lambda e_: e_.tensor_scalar(dd[:, 0:1], Sx[:, 640:641], eb_[:, n:n + 1], None, ALU.mult),
                     [r_S, r_eb], [r_dd])
                P.op("dve", lambda e_: e_.tensor_scalar(dd[:, 1:2], dd[:, 0:1], -1.0, -1.0, ALU.min, ALU.mult), [r_dd], [r_dd])
                P.op("dve", lambda e_: e_.scalar_tensor_tensor(dd[:, 0:1], dd[:, 0:1], 1.0, dd[:, 1:2], ALU.max, ALU.max), [r_dd], [r_dd])
                P.op("dve", lambda e_: e_.reciprocal(dd[:, 0:1], dd[:, 0:1]), [r_dd], [r_dd])
                P.op("dve", lambda e_: e_.tensor_tensor(dd[:, 1:2], dd[:, 0:1], eb_[:, n:n + 1], ALU.mult), [r_dd, r_eb], [r_dd])
                if d == 0:
                    P.op("dve", lambda e_: e_.tensor_scalar(hfs3[:, n, :], Sx[:, 512:640], dd[:, 1:2], None, ALU.mult),
                         [r_S, r_dd], [r_hfs])
                    return
                hs, r_hs = hss[par]
                hnn, r_hnn = hns[par]
                ss_, r_ss = ssd[par]
                P.op("dve", lambda e_: e_.scalar_tensor_tensor(
                    hs[:], Sx[:, 512:640], dd[:, 1:2], hfs3[:, n, :], ALU.mult, ALU.add), [r_S, r_dd, r_hfs], [r_hs])
                P.op("pool", lambda e_: e_.memset(ss_[:], 0.0), [], [r_ss])
                P.op("act", lambda e_: e_.activation(junkd[:], hs[:], AF.Square, accum_out=ss_[:]), [r_hs], [r_junkd, r_ss])
                P.op("act", lambda e_: e_.activation(ss_[:], ss_[:], AF.Sqrt, bias=epsc[:], scale=1.0 / 128), [r_ss, r_eps], [r_ss])
                P.op("dve", lambda e_: e_.reciprocal(ss_[:], ss_[:]), [r_ss], [r_ss])
                P.op("dve", lambda e_: e_.scalar_tensor_tensor(hnn[:], hs[:], ss_[:], hn[:], ALU.mult, ALU.mult),
                     [r_hs, r_ss, r_hn], [r_hnn])
                P.op("pe", lambda e_: e_.transpose(ptt[:, 0:128], hnn[:], ident[:]), [r_hnn, r_ident], [r_ptt])
                lo, hi = grp_of[n]
                off = (n - lo) * 128
                P.op("dve", lambda e_: e_.tensor_tensor(yt1[:, off:off + 128], ptt[:, 0:128], so_t[:, off:off + 128], ALU.mult),
                     [r_ptt, r_so], [r_yt1])
                P.op("pool", lambda e_: e_.tensor_tensor(ystD[:, off:off + 128], yt1[:, off:off + 128], sg_t[:, off:off + 128], ALU.mult),
                     [r_yt1, r_sgd], [r_ystD])
                if ci == len(order) - 1 or grp_of[order[ci + 1]] != (lo, hi):
                    C.out_dma("sp", Y[3][:, lo * 128:hi * 128], ystD[:, 0:(hi - lo) * 128], [r_ystD])

            stage1(0)
            for ci in range(len(order)):
                if ci + 1 < len(order):
                    stage1(ci + 1)
                stage2(ci)

        groups = [(0, 2)] + [(2 + 4 * i, 2 + 4 * (i + 1)) for i in range(32)]
        mlstm_dir(0, list(range(NB)), groups)
        mlstm_dir(1, [1, 0] + list(range(NB - 1, 1, -1)), groups)
        C.finish()
    return nc


def _gather_tokens(per_core):
    ctx = np.concatenate([a[:, TL:NT] for a in per_core], axis=1)
    lat = np.concatenate([a[:, 0:TL] for a in per_core], axis=1)
    return np.concatenate([ctx, lat], axis=1)


def _tok_major(item):
    return np.ascontiguousarray(item.T.reshape(NB, 128, 128).transpose(1, 0, 2)).reshape(128, NB * 128)


def host_p2_inputs(res1, l, inp):
    rows, nrows = p1_out_rows()
    G = _gather_tokens([r["O1"] for r in res1])
    Gg = _gather_tokens([r["O1g"] for r in res1])

    def item(name, sub=None):
        r0, m = rows[name]
        if sub is not None:
            return G[r0 + sub[0]:r0 + sub[1]]
        return G[r0:r0 + m]

    p_ = np.arange(128)
    MK = np.stack([(p_[:, None] <= p_[None, :]), (p_[:, None] >= p_[None, :])]).astype(np.float32)
    ident = np.eye(128, dtype=np.float32)
    maps = []
    for d in range(NCORE):
        FA = np.stack([item(f"A_u{d}"), item(f"A_g{d}"), item(f"B_q{d}"), item(f"B_k{d // 4}"), item(f"B_g{d}"),
                       item(f"C_qn{d}"), item(f"C_kn{d}"), item(f"C_g{d}"), item(f"D_q{d}"), item(f"D_k{d}"),
                       item(f"D_o{d}"), item(f"D_g{d}")])
        FB = np.stack([item(f"C_qr{d // 2}", ((d % 2) * 64, (d % 2) * 64 + 64)), item("C_kr")])
        TM = np.stack([_tok_major(item(f"B_v{d // 4}")), _tok_major(item(f"C_v{d}")), _tok_major(item(f"D_k{d}"))])
        vex = np.ones((TT, 129), NPBF)
        vex[:, :128] = item(f"D_v{d}").T
        TMV = np.ascontiguousarray(vex.reshape(NB, 128, 129).transpose(1, 0, 2)).reshape(128, NB * 129)
        gsel = Gg[[d, 8 + d, 16 + d, 24 + d]]
        GT = np.ascontiguousarray(gsel.reshape(4, NB, 128).transpose(2, 0, 1)).reshape(128, 4 * NB)
        ch = slice(d * 128, (d + 1) * 128)
        LP = np.zeros((128, 16), np.float32)
        LP[:, 0:4] = inp["lru_conv_w"][l][:, ch].T
        LP[:, 4] = inp["lru_conv_b"][l][ch]
        LP[:, 5:7] = inp["lru_lambda"][l][:, ch].T
        LP[:, 7:9] = inp["lru_ba"][l][:, ch].T
        LP[:, 9:11] = inp["lru_bx"][l][:, ch].T
        LP[:, 11] = inp["swa_sink"][l][d]
        LW = np.stack([inp["lru_wa"][l][0, d], inp["lru_wx"][l][0, d], inp["lru_wa"][l][1, d], inp["lru_wx"][l][1, d]]).astype(np.float32)
        HN = np.ascontiguousarray(np.broadcast_to(inp["ml_head_norm"][l][ch].astype(np.float32), (128, 128)))
        maps.append({"FA": FA, "FB": FB, "TM": TM, "TMV": TMV, "GT": GT, "LP": LP, "LW": LW, "HN": HN, "MK": MK, "ident": ident})
    return maps


def host_p3_inputs(res2, x_full, xc_full, modT_l, l, inp):
    Yall = np.stack([r["Y"] for r in res2])
    Yt = np.ascontiguousarray(Yall.transpose(3, 1, 0, 2)).reshape(TT, D)
    common = host_p3_common(modT_l, l, inp)
    maps = []
    for core in range(NCORE):
        y_loc = np.concatenate([Yt[TC + core * TL:TC + (core + 1) * TL], Yt[core * CL:(core + 1) * CL]], axis=0)
        m = dict(common)
        m["yT"] = host_yT_tiles(y_loc)
        m["x"] = np.concatenate([x_full[core * TL:(core + 1) * TL], xc_full[core * CL:(core + 1) * CL]], axis=0)
        maps.append(m)
    return maps


def kernel(**inputs):
    inp = {k: np.asarray(v) for k, v in inputs.items()}
    cores = list(range(NCORE))
    x_full = np.ascontiguousarray(inp["x"][0], dtype=np.float32)
    xc_full = np.ascontiguousarray(inp["ctx"][0], dtype=np.float32)
    res0 = run_bass_kernel_spmd(build_p0(), host_p0_inputs(inp["c"], inp["c_ctx"], inp["w_mod"], inp["b_mod"]), core_ids=cores)
    modT = host_p0_gather(res0.results)
    nc1, nc2, nc3 = build_p1(), build_p2(), build_p3()
    for l in range(2):
        res1 = run_bass_kernel_spmd(nc1, host_p1_inputs(x_full, xc_full, modT[l], l, inp), core_ids=cores).results
        maps2 = host_p2_inputs(res1, l, inp)
        del res1
        res2 = run_bass_kernel_spmd(nc2, maps2, core_ids=cores).results
        del maps2
        maps3 = host_p3_inputs(res2, x_full, xc_full, modT[l], l, inp)
        del res2
        res3 = run_bass_kernel_spmd(nc3, maps3, core_ids=cores).results
        del maps3
        x_full = np.concatenate([r["xo"][0:TL] for r in res3], axis=0)
        xc_full = np.concatenate([r["xo"][TL:NT] for r in res3], axis=0)
    return x_full.reshape(1, T, D).astype(np.float32)
```
